# Optimizing a Trainium2 kernel written in Bass

```python
import jax, jax.numpy as jnp
from jax import lax
import numpy as np


D_MODEL = 1024
BATCH = 8
SEQ = 4096
DEPTH = 2

HEAD_DIM = 64
N_HEADS_GDN = D_MODEL // 2 // HEAD_DIM
N_HEADS_RWKV = D_MODEL // 2 // HEAD_DIM
W_GDN = N_HEADS_GDN * HEAD_DIM
W_RWKV = N_HEADS_RWKV * HEAD_DIM
CONV_WIDTH = 4
GDN_CHUNK = 64
RWKV_DECAY_LORA = 64
RWKV_ICLR_LORA = 64
RWKV_GATE_LORA = 128
RWKV_GN_EPS = 64e-5
RWKV_IN = 3 * W_RWKV + RWKV_DECAY_LORA + RWKV_ICLR_LORA + RWKV_GATE_LORA
AB_IN = 4 * W_GDN + 2 * N_HEADS_GDN + RWKV_IN
AB_SPLITS = [3 * W_GDN, 4 * W_GDN, 4 * W_GDN + N_HEADS_GDN, 4 * W_GDN + 2 * N_HEADS_GDN]
RWKV_SPLITS = [W_RWKV, 2 * W_RWKV, 3 * W_RWKV, 3 * W_RWKV + RWKV_DECAY_LORA,
               3 * W_RWKV + RWKV_DECAY_LORA + RWKV_ICLR_LORA]
N_HEADS_RET = 8
RET_KEY_DIM = D_MODEL // N_HEADS_RET
RET_VAL_DIM = 2 * RET_KEY_DIM
W_RET_V = N_HEADS_RET * RET_VAL_DIM
RET_IN = 2 * D_MODEL + 2 * W_RET_V
RET_SPLITS = [D_MODEL, 2 * D_MODEL, 2 * D_MODEL + W_RET_V]
RET_CHUNK = 128
RET_GN_EPS = 1e-6
ROPE_BASE = 10000.0
D_FF = 4 * D_MODEL
RMS_EPS = 1e-6
N_EVEN = (DEPTH + 1) // 2
N_ODD = DEPTH // 2

kernel_name = "hybrid_gdn_rwkv7_retention_block"


def rmsnorm(x, w, eps=RMS_EPS):
    xf = x.astype(jnp.float32)
    return xf * lax.rsqrt(jnp.mean(xf * xf, axis=-1, keepdims=True) + eps) * w.astype(jnp.float32)


def l2norm(x, eps=1e-6):
    return x * lax.rsqrt(jnp.sum(x * x, axis=-1, keepdims=True) + eps)


def head_norm(y, eps):
    mean = jnp.mean(y, axis=-1, keepdims=True)
    var = jnp.mean(jnp.square(y - mean), axis=-1, keepdims=True)
    return (y - mean) * lax.rsqrt(var + eps)


def split_heads(x, n_heads):
    return x.reshape(*x.shape[:-1], n_heads, x.shape[-1] // n_heads)


def causal_depthwise_conv(x, w):
    K, C = w.shape
    return lax.conv_general_dilated(x, w[:, None, :].astype(x.dtype), window_strides=(1,),
                                    padding=[(K - 1, 0)], dimension_numbers=('NWC', 'WIO', 'NWC'),
                                    feature_group_count=C)


def gated_delta_rule(q, k, v, beta, g):
    Bsz, T, H, dk = q.shape
    dv = v.shape[-1]
    C = GDN_CHUNK
    N = T // C

    def chunks(a):
        a = jnp.moveaxis(a, 2, 1)
        return a.reshape(Bsz, H, N, C, *a.shape[3:])

    q, k, v, beta, g = (chunks(a) for a in (q * dk ** -0.5, k, v, beta, g))
    gcum = jnp.cumsum(g, axis=-1)
    causal = jnp.tril(jnp.ones((C, C), dtype=bool))
    strict = jnp.tril(jnp.ones((C, C), dtype=bool), -1)
    decay = jnp.exp(jnp.where(causal, gcum[..., :, None] - gcum[..., None, :], -jnp.inf))
    kb = k * beta[..., None]
    kkT = jnp.einsum('bhncd,bhnsd->bhncs', kb, k) * decay
    a_mat = jnp.where(strict, kkT, 0.0) + jnp.eye(C, dtype=kkT.dtype)
    rhs = jnp.concatenate([v * beta[..., None], kb * jnp.exp(gcum)[..., None]], axis=-1)
    sol = lax.linalg.triangular_solve(a_mat, rhs, left_side=True, lower=True, unit_diagonal=True)
    u, w = sol[..., :dv], sol[..., dv:]
    attn = jnp.einsum('bhncd,bhnsd->bhncs', q, k) * decay
    q_dec = q * jnp.exp(gcum)[..., None]
    k_dec = k * jnp.exp(gcum[..., -1:] - gcum)[..., None]
    blk_decay = jnp.exp(gcum[..., -1])

    def step(S, inp):
        u_c, w_c, attn_c, qd_c, kd_c, bd_c = inp
        v_new = u_c - jnp.einsum('bhck,bhkv->bhcv', w_c, S)
        o = jnp.einsum('bhck,bhkv->bhcv', qd_c, S) + jnp.einsum('bhcs,bhsv->bhcv', attn_c, v_new)
        S = S * bd_c[..., None, None] + jnp.einsum('bhck,bhcv->bhkv', kd_c, v_new)
        return S, o

    xs = tuple(jnp.moveaxis(a, 2, 0) for a in (u, w, attn, q_dec, k_dec, blk_decay))
    _, o = lax.scan(step, jnp.zeros((Bsz, H, dk, dv), jnp.float32), xs)
    o = jnp.moveaxis(o, 0, 2).reshape(Bsz, H, T, dv)
    return jnp.moveaxis(o, 1, 2)


def rwkv7_recurrence(r, w, k, v, kk, a):
    Bsz, T, H, d = r.shape

    def step(S, inp):
        r_t, w_t, k_t, v_t, kk_t, a_t = inp
        sa = jnp.einsum('bhvk,bhk->bhv', S, -kk_t)
        S = (S * w_t[:, :, None, :] + sa[..., None] * (kk_t * a_t)[:, :, None, :]
             + v_t[..., None] * k_t[:, :, None, :])
        return S, jnp.einsum('bhvk,bhk->bhv', S, r_t)

    xs = tuple(jnp.moveaxis(t, 1, 0) for t in (r, w, k, v, kk, a))
    _, y = lax.scan(step, jnp.zeros((Bsz, H, d, d), jnp.float32), xs)
    return jnp.moveaxis(y, 0, 1)


def hybrid_ab_mixer(u, w_in, conv_w, a_log, dt_bias, gdn_norm_w, mu, w0, w2, a0, a2, g2,
                    k_k, k_a, r_k, ln_w, ln_b, w_out):
    Bsz, T, _ = u.shape
    p = u @ w_in
    qkv, z, b_raw, alpha_raw, rp = jnp.split(p, AB_SPLITS, axis=-1)
    qkv = jax.nn.silu(causal_depthwise_conv(qkv, conv_w))
    q, k, v = (split_heads(t, N_HEADS_GDN) for t in jnp.split(qkv, 3, axis=-1))
    q, k = l2norm(q), l2norm(k)
    beta = jax.nn.sigmoid(b_raw)
    g = -jnp.exp(a_log.astype(jnp.float32)) * jax.nn.softplus(alpha_raw + dt_bias)
    o = gated_delta_rule(q, k, v, beta, g)
    o = rmsnorm(o, gdn_norm_w) * jax.nn.silu(split_heads(z, N_HEADS_GDN))
    o_a = o.reshape(Bsz, T, W_GDN)
    prev = jnp.pad(rp, ((0, 0), (1, 0), (0, 0)))[:, :-1]
    xs = rp + (prev - rp) * mu
    r, kr, vr, xw, xa, xg = jnp.split(xs, RWKV_SPLITS, axis=-1)
    w_log = -jax.nn.softplus(-(w0 + jnp.tanh(xw) @ w2)) - 0.5
    w_dec = jnp.exp(-jnp.exp(w_log))
    a = jax.nn.sigmoid(a0 + xa @ a2)
    gate = jax.nn.sigmoid(xg) @ g2
    kk = l2norm(split_heads(kr * k_k, N_HEADS_RWKV))
    kr = kr * (1.0 + (a - 1.0) * k_k * 0.0 + (a - 1.0) * k_a) if False else kr * (1.0 + (a - 1.0) * k_a)
    rh, kh, vh, ah, wh = (split_heads(t, N_HEADS_RWKV) for t in (r, kr, vr, a, w_dec))
    y = rwkv7_recurrence(rh, wh, kh, vh, kk, ah)
    y = head_norm(y, RWKV_GN_EPS).reshape(Bsz, T, W_RWKV) * ln_w + ln_b
    bonus = jnp.sum(rh * kh * r_k, axis=-1, keepdims=True) * vh
    o_b = (y + bonus.reshape(Bsz, T, W_RWKV)) * gate
    return jnp.concatenate([o_a, o_b], axis=-1) @ w_out


def rotary_every_two(x, pos):
    d = x.shape[-1]
    angle = 1.0 / (ROPE_BASE ** jnp.linspace(0.0, 1.0, d // 2, dtype=jnp.float32))
    theta = pos[:, None].astype(jnp.float32) * angle
    cos, sin = jnp.cos(theta)[:, None, :], jnp.sin(theta)[:, None, :]
    x1, x2 = x[..., 0::2], x[..., 1::2]
    return jnp.stack([x1 * cos - x2 * sin, x2 * cos + x1 * sin], axis=-1).reshape(x.shape)


def chunkwise_retention(q, k, v):
    Bsz, T, H, dk = q.shape
    dv = v.shape[-1]
    C = RET_CHUNK
    N = T // C
    log_gamma = jnp.log1p(-jnp.exp2(-5.0 - jnp.arange(H, dtype=jnp.float32)))

    def chunks(a):
        return jnp.moveaxis(a, 2, 1).reshape(Bsz, H, N, C, a.shape[-1])

    q, k, v = chunks(q), chunks(k), chunks(v)
    idx = jnp.arange(C, dtype=jnp.float32)
    rel = idx[:, None] - idx[None, :]
    intra = jnp.where(rel >= 0, jnp.exp(log_gamma[:, None, None] * jnp.maximum(rel, 0.0)), 0.0)
    scores = jnp.einsum('bhncd,bhnsd->bhncs', q, k) * intra[:, None]
    inner = jnp.einsum('bhncs,bhnsv->bhncv', scores, v)
    q_dec = q * jnp.exp(log_gamma[:, None] * (idx + 1.0))[:, None, :, None]
    k_dec = k * jnp.exp(log_gamma[:, None] * (C - 1.0 - idx))[:, None, :, None]
    blk = jnp.exp(log_gamma * C)[:, None, None]

    def step(S, inp):
        qd, kd, vc = inp
        o = jnp.einsum('bhck,bhkv->bhcv', qd, S)
        S = S * blk + jnp.einsum('bhck,bhcv->bhkv', kd, vc)
        return S, o

    xs = tuple(jnp.moveaxis(a, 2, 0) for a in (q_dec, k_dec, v))
    _, cross = lax.scan(step, jnp.zeros((Bsz, H, dk, dv), jnp.float32), xs)
    y = inner + jnp.moveaxis(cross, 0, 2)
    return jnp.moveaxis(y.reshape(Bsz, H, T, dv), 1, 2)


def retention_mixer(u, w_in, gn_w, w_out):
    Bsz, T, _ = u.shape
    q, k, v, g = jnp.split(u @ w_in, RET_SPLITS, axis=-1)
    pos = jnp.arange(T)
    q = rotary_every_two(split_heads(q, N_HEADS_RET), pos)
    k = rotary_every_two(split_heads(k, N_HEADS_RET), pos) * RET_KEY_DIM ** -0.5
    y = chunkwise_retention(q, k, split_heads(v, N_HEADS_RET))
    y = head_norm(y, RET_GN_EPS).reshape(Bsz, T, W_RET_V) * gn_w
    return (jax.nn.silu(g) * y) @ w_out


def setup_inputs(seed: int = 0) -> dict:
    key = jax.random.key(seed)
    ks = jax.random.split(key, 27)
    nrm = lambda k, shape, s: jax.random.normal(k, shape, jnp.float32) * s
    gain = lambda k, shape: 1.0 + 0.02 * jax.random.normal(k, shape, jnp.float32)
    col_scale = jnp.concatenate([jnp.ones((4 * W_GDN + N_HEADS_GDN,), jnp.float32),
                                 0.1 * jnp.ones((N_HEADS_GDN,), jnp.float32),
                                 jnp.ones((RWKV_IN,), jnp.float32)])
    dt = jnp.exp(jax.random.uniform(ks[9], (N_EVEN, N_HEADS_GDN), jnp.float32, np.log(1e-3), np.log(1e-1)))
    w0_base = -6.0 + 5.0 * jnp.linspace(0.0, 1.0, W_RWKV, dtype=jnp.float32) ** 0.85
    return {
        'x': nrm(ks[0], (BATCH, SEQ, D_MODEL), 1.0),
        'norm_mix_pre': gain(ks[1], (DEPTH, D_MODEL)),
        'norm_mix_post': gain(ks[2], (DEPTH, D_MODEL)),
        'norm_mlp_pre': gain(ks[3], (DEPTH, D_MODEL)),
        'norm_mlp_post': gain(ks[4], (DEPTH, D_MODEL)),
        'mlp_w_up': nrm(ks[5], (DEPTH, D_MODEL, D_FF), D_MODEL ** -0.5),
        'mlp_w_down': nrm(ks[6], (DEPTH, D_FF, D_MODEL), D_FF ** -0.5),
        'ab_w_in': nrm(ks[7], (N_EVEN, D_MODEL, AB_IN), D_MODEL ** -0.5) * col_scale,
        'gdn_conv_w': nrm(ks[8], (N_EVEN, CONV_WIDTH, 3 * W_GDN), CONV_WIDTH ** -0.5),
        'gdn_a_log': jnp.log(jax.random.uniform(ks[10], (N_EVEN, N_HEADS_GDN), jnp.float32, 1.0, 16.0)),
        'gdn_dt_bias': dt + jnp.log(-jnp.expm1(-dt)),
        'gdn_norm_w': gain(ks[11], (N_EVEN, HEAD_DIM)),
        'rwkv_mu': jax.random.uniform(ks[12], (N_EVEN, RWKV_IN), jnp.float32),
        'rwkv_w0': w0_base + 0.1 * jax.random.normal(ks[13], (N_EVEN, W_RWKV), jnp.float32),
        'rwkv_w2': nrm(ks[14], (N_EVEN, RWKV_DECAY_LORA, W_RWKV), 0.1 * RWKV_DECAY_LORA ** -0.5),
        'rwkv_a0': nrm(ks[15], (N_EVEN, W_RWKV), 0.1),
        'rwkv_a2': nrm(ks[16], (N_EVEN, RWKV_ICLR_LORA, W_RWKV), 0.1 * RWKV_ICLR_LORA ** -0.5),
        'rwkv_g2': nrm(ks[17], (N_EVEN, RWKV_GATE_LORA, W_RWKV), RWKV_GATE_LORA ** -0.5),
        'rwkv_k_k': 0.85 + 0.1 * jax.random.normal(ks[18], (N_EVEN, W_RWKV), jnp.float32),
        'rwkv_k_a': 1.0 + 0.1 * jax.random.normal(ks[19], (N_EVEN, W_RWKV), jnp.float32),
        'rwkv_r_k': nrm(ks[20], (N_EVEN, N_HEADS_RWKV, HEAD_DIM), 0.1),
        'rwkv_ln_w': gain(ks[21], (N_EVEN, W_RWKV)),
        'rwkv_ln_b': nrm(ks[22], (N_EVEN, W_RWKV), 0.02),
        'ab_w_out': nrm(ks[23], (N_EVEN, W_GDN + W_RWKV, D_MODEL), (W_GDN + W_RWKV) ** -0.5),
        'ret_w_in': nrm(ks[24], (N_ODD, D_MODEL, RET_IN), D_MODEL ** -0.5),
        'ret_gn_w': gain(ks[25], (N_ODD, W_RET_V)),
        'ret_w_out': nrm(ks[26], (N_ODD, W_RET_V, D_MODEL), W_RET_V ** -0.5),
    }


def reference(x, norm_mix_pre, norm_mix_post, norm_mlp_pre, norm_mlp_post, mlp_w_up, mlp_w_down,
              ab_w_in, gdn_conv_w, gdn_a_log, gdn_dt_bias, gdn_norm_w,
              rwkv_mu, rwkv_w0, rwkv_w2, rwkv_a0, rwkv_a2, rwkv_g2, rwkv_k_k, rwkv_k_a, rwkv_r_k,
              rwkv_ln_w, rwkv_ln_b, ab_w_out, ret_w_in, ret_gn_w, ret_w_out):
    h = x.astype(jnp.float32)
    for layer in range(DEPTH):
        j = layer // 2
        u = rmsnorm(h, norm_mix_pre[layer])
        if layer % 2 == 0:
            mix = hybrid_ab_mixer(u, ab_w_in[j], gdn_conv_w[j], gdn_a_log[j], gdn_dt_bias[j], gdn_norm_w[j],
                                  rwkv_mu[j], rwkv_w0[j], rwkv_w2[j], rwkv_a0[j], rwkv_a2[j], rwkv_g2[j],
                                  rwkv_k_k[j], rwkv_k_a[j], rwkv_r_k[j], rwkv_ln_w[j], rwkv_ln_b[j], ab_w_out[j])
        else:
            mix = retention_mixer(u, ret_w_in[j], ret_gn_w[j], ret_w_out[j])
        h = h + rmsnorm(mix, norm_mix_post[layer])
        u = rmsnorm(h, norm_mlp_pre[layer])
        f = jnp.square(jax.nn.relu(u @ mlp_w_up[layer])) @ mlp_w_down[layer]
        h = h + rmsnorm(f, norm_mlp_post[layer])
    return h.astype(x.dtype)
```

```python
import os
import numpy as np
from contextlib import ExitStack
import concourse.bass as bass
import concourse.mybir as mybir
from concourse.bass_utils import run_bass_kernel_spmd

F32 = mybir.dt.float32
BF16 = mybir.dt.bfloat16
AF = mybir.ActivationFunctionType
ALU = mybir.AluOpType
AX = mybir.AxisListType

SAME_ENG_SYNC = True


class Buf:
    __slots__ = ("t", "w", "r", "name")

    def __init__(self, t, name=""):
        self.t = t
        self.w = None
        self.r = {}
        self.name = name

    def __getitem__(self, key):
        return self.t[key]


class KB:
    def __init__(self, nc):
        self.nc = nc
        self.es = ExitStack()
        self.engs = {"pe": nc.tensor, "act": nc.scalar, "dve": nc.vector,
                     "pool": nc.gpsimd, "sp": nc.sync}
        self.nuniq = 0
        self.epoch = -1
        self.sem = {}
        self.cnt = {}
        self._new_epoch()

    def _new_epoch(self):
        self.epoch += 1
        self.known = {e: {} for e in self.engs}
        for e in self.engs:
            self.sem[e] = self.es.enter_context(self.nc.semaphore(f"s_{e}_{self.epoch}"))
            self.cnt[e] = 0

    def sb(self, es, name, shape, dtype):
        self.nuniq += 1
        t = es.enter_context(self.nc.sbuf_tensor(f"{name}_{self.nuniq}", list(shape), dtype))
        return Buf(t, name)

    def ps(self, es, name, shape, dtype=F32):
        self.nuniq += 1
        t = es.enter_context(self.nc.psum_tensor(f"{name}_{self.nuniq}", list(shape), dtype))
        return Buf(t, name)

    def stream(self, name):
        key = "d_" + name
        if key not in self.sem:
            self.sem[key] = self.es.enter_context(self.nc.semaphore(key))
            self.cnt[key] = 0
        return key

    def _waits(self, eng, reads, writes, extra=()):
        need = {}

        def add(ev):
            if ev is not None and ev[2] == self.epoch:
                if need.get(ev[0], 0) < ev[1]:
                    need[ev[0]] = ev[1]

        for b in reads:
            add(b.w)
        for b in writes:
            add(b.w)
            for sk, (v, ep) in b.r.items():
                add((sk, v, ep))
        for ev in extra:
            add(ev)
        e = self.engs[eng]
        kn = self.known[eng]
        for sk, v in need.items():
            if sk == eng and (eng == "pe" or eng == "sp" or not SAME_ENG_SYNC):
                continue
            if kn.get(sk, 0) >= v:
                continue
            e.wait_ge(self.sem[sk], v)
            kn[sk] = v

    def _record(self, ev, reads, writes):
        for b in reads:
            old = b.r.get(ev[0])
            if old is None or old[1] != ev[2] or old[0] < ev[1]:
                b.r[ev[0]] = (ev[1], ev[2])
        for b in writes:
            b.w = ev
            b.r = {}

    def op(self, eng, fn, reads=(), writes=()):
        self._waits(eng, reads, writes)
        ins = fn(self.engs[eng])
        self.cnt[eng] += 1
        ins.then_inc(self.sem[eng], 1)
        self._record((eng, self.cnt[eng], self.epoch), reads, writes)

    def dma(self, q, stream, out, in_, reads=(), writes=(), **kw):
        sk = self.stream(stream)
        prev = (sk, self.cnt[sk], self.epoch) if self.cnt[sk] else None
        self._waits(q, reads, writes, extra=(prev,))
        ins = self.engs[q].dma_start(out=out, in_=in_, **kw)
        self.cnt[sk] += 16
        ins.then_inc(self.sem[sk], 16)
        self._record((sk, self.cnt[sk], self.epoch), reads, writes)

    def dbg(self, name, buf, ap, shape):
        if os.environ.get("DBG", "0") != "1":
            return
        d = Buf(self.nc.dram_tensor("dbg_" + name, list(shape), F32, kind="ExternalOutput").ap(), name)
        self.dma("sp", "dbg", d.t, ap, reads=[buf], writes=[d])

    def barrier(self):
        for eng, e in self.engs.items():
            kn = self.known[eng]
            for sk, v in self.cnt.items():
                if v == 0 or kn.get(sk, 0) >= v:
                    continue
                if sk == eng and eng in ("pe", "sp"):
                    pass
                e.wait_ge(self.sem[sk], v)
                kn[sk] = v
        self._new_epoch()

    def mm(self, out, lhsT, rhs, start, stop, reads, writes, tp=None):
        kw = {}
        if tp is not None and (tp[0] == 96 or tp[1] == 96):
            kw["tile_position"] = tp
        self.op("pe", lambda e: e.matmul(out, lhsT, rhs, start=start, stop=stop, **kw), reads, writes)

    def tr(self, out, in_, ident, reads, writes):
        self.op("pe", lambda e: e.transpose(out, in_, ident), reads, writes)


D = 1024
DFF = 4096
RMS_EPS = 1e-6


def rms_rstd(k, eng_sq, x_ap, xbuf, junk, ss, rstd, n, eps=RMS_EPS):
    k.op("act", lambda e: e.activation(junk[:], x_ap, AF.Square, accum_out=ss[:]),
         reads=[xbuf], writes=[junk, ss])
    k.op("dve", lambda e: e.tensor_scalar(rstd[:], ss[:], 1.0 / n, eps, ALU.mult, ALU.add),
         reads=[ss], writes=[rstd])
    k.op("act", lambda e: e.activation(rstd[:], rstd[:], AF.Sqrt), reads=[rstd], writes=[rstd])
    k.op("dve", lambda e: e.reciprocal(rstd[:], rstd[:]), reads=[rstd], writes=[rstd])


def load_consts(k, es, c_ident):
    ident = k.sb(es, "ident", [128, 128], BF16)
    k.dma("pool", "const", ident[:], c_ident.t[:, :], reads=[c_ident], writes=[ident])
    return ident


def mlp_phase(k, nc, T, hin, hout, w_up, w_dn, g_pre, g_post, ident):
    NT = T // 128
    ST = 2
    NS = NT // ST
    es = ExitStack()
    with es:
        wup = [k.sb(es, f"wup{kc}", [128, DFF], BF16) for kc in range(8)]
        wdn = [k.sb(es, f"wdn{g}", [128, 4, D], BF16) for g in range(8)]
        for kc in range(8):
            k.dma("pool", f"wl{kc % 4}", wup[kc][:], w_up.t[kc * 128:(kc + 1) * 128, :],
                  reads=[w_up], writes=[wup[kc]])
        for g in range(8):
            k.dma("pool", f"wl{g % 4}", wdn[g][:],
                  w_dn.t[g * 512:(g + 1) * 512, :].rearrange("(fc p) d -> p fc d", p=128),
                  reads=[w_dn], writes=[wdn[g]])
        gpre = k.sb(es, "gpre", [128, 8], F32)
        k.dma("sp", "const", gpre[:], g_pre.t.rearrange("(kc p) -> p kc", p=128),
              reads=[g_pre], writes=[gpre], allow_slow_non_contiguous=True)
        gpost = k.sb(es, "gpost", [128, D], F32)
        k.dma("sp", "const", gpost[:], g_post.t.partition_broadcast(128),
              reads=[g_post], writes=[gpost])

        NB = 2
        xt = [[k.sb(es, f"xt{b}_{j}", [128, D], F32) for j in range(ST)] for b in range(NB)]
        uT = [k.sb(es, f"uT{b}", [128, 8, 128 * ST], BF16) for b in range(NB)]
        aT = [k.sb(es, f"aT{b}", [128, 32, 128 * ST], BF16) for b in range(1)]
        xs = [k.sb(es, f"xs{b}", [128, D], BF16) for b in range(2)]
        junk = k.sb(es, "junk", [128, D], BF16)
        rtmp = [k.sb(es, f"rtmp{b}", [128, 128 * ST], F32) for b in range(4)]
        ss = [k.sb(es, f"ss{b}", [128, 1], F32) for b in range(2)]
        rstd = [k.sb(es, f"rstd{b}", [128, 1], F32) for b in range(2)]
        ss2 = [k.sb(es, f"ss2{b}", [128, 1], F32) for b in range(2)]
        rstd2 = [k.sb(es, f"rstd2{b}", [128, 1], F32) for b in range(2)]
        fb = [k.sb(es, f"fb{b}", [128, D], F32) for b in range(2)]
        yb = [k.sb(es, f"yb{b}", [128, D], F32) for b in range(2)]
        pT = [k.ps(es, f"pT{b}", [128, 8, 128], BF16) for b in range(2)]
        pU = [k.ps(es, f"pU{b}", [128, 512], F32) for b in range(3)]
        pD = [k.ps(es, f"pD{b}", [128, 512], F32) for b in range(2)]

        state = {"ti": 0, "iu": 0, "idn": 0}
        NTOK = 128 * ST

        def prep(s):
            b = s % NB
            for j in range(ST):
                t = s * ST + j
                x = xt[b][j]
                ti = state["ti"]
                k.dma("sp", f"x{ti % 2}", x[:], hin[t].t, reads=[hin[t]], writes=[x])
                q = ti % 2
                rms_rstd(k, "act", x[:], x, junk, ss[q], rstd[q], D)
                k.op("dve", lambda e: e.tensor_scalar(xs[q][:], x[:], rstd[q][:, 0:1], None, ALU.mult),
                     reads=[x, rstd[q]], writes=[xs[q]])
                for kc in range(8):
                    k.tr(pT[q][:, kc, :], xs[q][:, kc * 128:(kc + 1) * 128], ident[:],
                         reads=[xs[q], ident], writes=[pT[q]])
                k.op("dve", lambda e: e.tensor_tensor(
                    uT[b][:, :, j * 128:(j + 1) * 128], pT[q][:],
                    gpre[:].unsqueeze(2).to_broadcast([128, 8, 128]), ALU.mult),
                    reads=[pT[q], gpre], writes=[uT[b]])
                state["ti"] += 1

        def up(s):
            b = s % NB
            for fc in range(32):
                pu = pU[state["iu"] % 3]
                state["iu"] += 1
                for kc in range(8):
                    k.mm(pu[:, 0:NTOK], wup[kc][:, fc * 128:(fc + 1) * 128], uT[b][:, kc, :],
                         kc == 0, kc == 7, reads=[wup[kc], uT[b]], writes=[pu])
                tm = rtmp[fc % 4]
                if fc % 2 == 0:
                    k.op("act", lambda e: e.activation(tm[:], pu[:, 0:NTOK], AF.Relu),
                         reads=[pu], writes=[tm])
                else:
                    k.op("dve", lambda e: e.tensor_scalar(tm[:], pu[:, 0:NTOK], 0.0, None, ALU.max),
                         reads=[pu], writes=[tm])
                k.op("pool", lambda e: e.tensor_tensor(aT[0][:, fc, :], tm[:], tm[:], ALU.mult),
                     reads=[tm], writes=[aT[0]])

        def down(s):
            b = s % NB
            for j in range(ST):
                t = s * ST + j
                x = xt[b][j]
                q = state["idn"] % 2
                f = fb[q]
                for nh in range(2):
                    pd = pD[nh]
                    for fc in range(32):
                        k.mm(pd[:], aT[0][:, fc, j * 128:(j + 1) * 128],
                             wdn[fc // 4][:, fc % 4, nh * 512:(nh + 1) * 512],
                             fc == 0, fc == 31, reads=[aT[0], wdn[fc // 4]], writes=[pd])
                    k.op("dve" if nh == 0 else "act",
                         (lambda e: e.tensor_copy(f[:, nh * 512:(nh + 1) * 512], pd[:])) if nh == 0 else
                         (lambda e: e.copy(f[:, nh * 512:(nh + 1) * 512], pd[:])),
                         reads=[pd], writes=[f])
                rms_rstd(k, "act", f[:], f, junk, ss2[q], rstd2[q], D)
                y = yb[q]
                k.op("dve", lambda e: e.scalar_tensor_tensor(y[:], f[:], rstd2[q][:, 0:1], gpost[:],
                                                             ALU.mult, ALU.mult),
                     reads=[f, rstd2[q], gpost], writes=[y])
                k.op("pool", lambda e: e.tensor_tensor(y[:], y[:], x[:], ALU.add),
                     reads=[y, x], writes=[y])
                k.dma("pool", f"y{q}", hout[t].t, y[:], reads=[y], writes=[hout[t]])
                state["idn"] += 1

        prep(0)
        for s in range(NS):
            up(s)
            if s + 1 < NS:
                prep(s + 1)
            down(s)
        k.barrier()


NH_RET = 8
RET_C = 128


def ret_gammas():
    return [float(1.0 - 2.0 ** (-5.0 - h)) for h in range(NH_RET)]


def ret_host_consts(T):
    pos = np.arange(T, dtype=np.float64)
    angle = 1.0 / (10000.0 ** np.linspace(0.0, 1.0, 64))
    th = pos[:, None] * angle[None, :]
    c_rot = np.concatenate([np.cos(th), np.sin(th)], axis=1).astype(np.float32)
    g = np.array(ret_gammas(), dtype=np.float64)
    idx = np.arange(128, dtype=np.float64)
    dq = g[None, :] ** idx[:, None]
    dk = g[None, :] ** (-idx[:, None]) * (128.0 ** -0.5)
    dkc = dk * (g[None, :] ** 128.0)
    c_dqk = np.concatenate([dq, dk, dkc], axis=1).astype(np.float32)
    c_maskT = (idx[:, None] <= idx[None, :]).astype(np.float32)
    return {"c_rot": c_rot, "c_dqk": c_dqk, "c_maskT": c_maskT}


def ret_phase(k, nc, T, hin, y_out, w_in, g_pre, ident, c_rot, c_dqk, c_maskT):
    NT = T // 128
    gam = ret_gammas()
    es = ExitStack()
    with es:
        win = [k.sb(es, f"rwin{kc}", [128, 6144], BF16) for kc in range(8)]
        for kc in range(8):
            k.dma("pool", f"wl{kc % 4}", win[kc][:], w_in.t[kc * 128:(kc + 1) * 128, :],
                  reads=[w_in], writes=[win[kc]])
        gpre = k.sb(es, "gpre", [128, 8], F32)
        k.dma("sp", "const", gpre[:], g_pre.t.rearrange("(kc p) -> p kc", p=128),
              reads=[g_pre], writes=[gpre], allow_slow_non_contiguous=True)
        dqk = k.sb(es, "dqk", [128, 24], F32)
        k.dma("sp", "const", dqk[:], c_dqk.t[:, :], reads=[c_dqk], writes=[dqk])
        maskT = k.sb(es, "maskT", [128, 128], F32)
        k.dma("sp", "const", maskT[:], c_maskT.t[:, :], reads=[c_maskT], writes=[maskT])

        xb = [k.sb(es, f"x{i}", [128, D], F32) for i in range(2)]
        xs = k.sb(es, "xs", [128, D], BF16)
        uT = k.sb(es, "uT", [128, 8, 128], BF16)
        cs = [k.sb(es, f"cs{b}", [128, 128], F32) for b in range(2)]
        vbb = [k.sb(es, f"vb{i}", [128, 2048], BF16) for i in range(2)]
        gsb = [k.sb(es, f"gs{i}", [128, 2048], BF16) for i in range(2)]
        tq = [[k.sb(es, f"tq{b}_{i}", [128, 4, 64], F32) for i in range(4)] for b in range(2)]
        qk_b = k.sb(es, "qk_b", [128, 16, 64, 2], BF16)
        kdb = [k.sb(es, f"kd_b{i}", [128, 8, 64, 2], BF16) for i in range(2)]
        qTb = [k.sb(es, f"qT{i}", [128, 8, 128], BF16) for i in range(2)]
        kT = k.sb(es, "kT", [128, 8, 128], BF16)
        scb = [k.sb(es, f"sc_b{i}", [128, 8, 128], BF16) for i in range(2)]
        Sg = k.sb(es, "Sg", [128, 8, 256], F32)
        Sb = k.sb(es, "Sb", [128, 8, 256], BF16)
        o_f = k.sb(es, "o_f", [128, 8, 256], F32)
        ybb = [k.sb(es, f"y_b{i}", [128, 2048], BF16) for i in range(2)]
        junk = k.sb(es, "junk", [128, D], BF16)
        st = {n: k.sb(es, n, [128, 8], F32) for n in ("s1", "s2", "mean", "msq", "var", "rstdh", "nmr")}
        ss = k.sb(es, "ss", [128, 1], F32)
        rstd = k.sb(es, "rstd", [128, 1], F32)
        pT = [k.ps(es, f"pT{b}", [128, 8, 128], BF16) for b in range(2)]
        pb = [k.ps(es, f"pb{b}", [128, 512], F32) for b in range(6)]
        ib = [0]
        print("ret sbuf bytes remaining", nc.sbuf_bytes_remaining)

        def bank():
            b = pb[ib[0] % 6]
            ib[0] += 1
            return b

        k.op("pool", lambda e: e.memset(Sg[:], 0.0), writes=[Sg])
        k.op("pool", lambda e: e.memset(Sb[:], 0.0), writes=[Sb])

        def load_x(t):
            k.dma("sp", f"x{t % 2}", xb[t % 2][:], hin[t].t, reads=[hin[t]], writes=[xb[t % 2]])
            k.dma("sp", f"cs{t % 2}", cs[t % 2][:], c_rot.t[t * 128:(t + 1) * 128, :], reads=[c_rot], writes=[cs[t % 2]])

        def prep(t):
            s = t % 2
            x, c, vb, gs, kd_b, qT, sc_b = xb[s], cs[s], vbb[s], gsb[s], kdb[s], qTb[s], scb[s]
            if t == 0:
                load_x(0)
            if t + 1 < NT:
                load_x(t + 1)
            norm_transpose(k, x, xs, junk, ss, rstd, pT[0], uT, gpre, ident)
            yield
            cosb = c[:, 0:64].unsqueeze(1).to_broadcast([128, 4, 64])
            sinb = c[:, 64:128].unsqueeze(1).to_broadcast([128, 4, 64])
            for g in range(12):
                pu = bank()
                for kc in range(8):
                    k.mm(pu[:], uT[:, kc, :], win[kc][:, g * 512:(g + 1) * 512], kc == 0, kc == 7,
                         reads=[uT, win[kc]], writes=[pu])
                if g < 4:
                    tt = tq[g % 2]
                    pv = pu[:].rearrange("p (h d t) -> p h d t", h=4, t=2)
                    x1 = pv[:, :, :, 0]
                    x2 = pv[:, :, :, 1]
                    k.op("dve", lambda e: e.tensor_tensor(tt[0][:], x1, cosb, ALU.mult), reads=[pu, c], writes=[tt[0]])
                    k.op("dve", lambda e: e.tensor_tensor(tt[1][:], x2, sinb, ALU.mult), reads=[pu, c], writes=[tt[1]])
                    k.op("dve", lambda e: e.tensor_tensor(tt[2][:], x2, cosb, ALU.mult), reads=[pu, c], writes=[tt[2]])
                    k.op("dve", lambda e: e.tensor_tensor(tt[3][:], x1, sinb, ALU.mult), reads=[pu, c], writes=[tt[3]])
                    k.op("pool", lambda e: e.tensor_tensor(tt[0][:], tt[0][:], tt[1][:], ALU.subtract),
                         reads=[tt[0], tt[1]], writes=[tt[0]])
                    k.op("pool", lambda e: e.tensor_tensor(tt[2][:], tt[2][:], tt[3][:], ALU.add),
                         reads=[tt[2], tt[3]], writes=[tt[2]])
                    h0 = 4 * g
                    sc = dqk[:, h0:h0 + 4].unsqueeze(2).to_broadcast([128, 4, 64])
                    k.op("pool", lambda e: e.tensor_tensor(qk_b[:, h0:h0 + 4, :, 0], tt[0][:], sc, ALU.mult),
                         reads=[tt[0], dqk], writes=[qk_b])
                    k.op("pool", lambda e: e.tensor_tensor(qk_b[:, h0:h0 + 4, :, 1], tt[2][:], sc, ALU.mult),
                         reads=[tt[2], dqk], writes=[qk_b])
                    if g >= 2:
                        hk = 4 * (g - 2)
                        sc2 = dqk[:, 16 + hk:16 + hk + 4].unsqueeze(2).to_broadcast([128, 4, 64])
                        k.op("pool", lambda e: e.tensor_tensor(kd_b[:, hk:hk + 4, :, 0], tt[0][:], sc2, ALU.mult),
                             reads=[tt[0], dqk], writes=[kd_b])
                        k.op("pool", lambda e: e.tensor_tensor(kd_b[:, hk:hk + 4, :, 1], tt[2][:], sc2, ALU.mult),
                             reads=[tt[2], dqk], writes=[kd_b])
                elif g < 8:
                    k.op("act", lambda e: e.copy(vb[:, (g - 4) * 512:(g - 3) * 512], pu[:]), reads=[pu], writes=[vb])
                else:
                    k.op("act", lambda e: e.activation(gs[:, (g - 8) * 512:(g - 7) * 512], pu[:], AF.Silu),
                         reads=[pu], writes=[gs])
                yield
            for i in range(16):
                k.tr(pT[i // 8][:, i % 8, :], qk_b[:, i, :, :].rearrange("p d t -> p (d t)"), ident[:],
                     reads=[qk_b, ident], writes=[pT[i // 8]])
            k.op("act", lambda e: e.copy(qT[:], pT[0][:]), reads=[pT[0]], writes=[qT])
            k.op("dve", lambda e: e.tensor_copy(kT[:], pT[1][:]), reads=[pT[1]], writes=[kT])
            yield
            for hb in range(2):
                psc = bank()
                for hh in range(4):
                    h = hb * 4 + hh
                    k.mm(psc[:, hh * 128:(hh + 1) * 128], kT[:, h, :], qT[:, h, :], True, True,
                         reads=[kT, qT], writes=[psc])
                k.op("dve", lambda e: e.tensor_tensor(
                    sc_b[:, hb * 4:hb * 4 + 4, :], psc[:].rearrange("p (h c) -> p h c", h=4),
                    maskT[:].unsqueeze(1).to_broadcast([128, 4, 128]), ALU.mult),
                    reads=[psc, maskT], writes=[sc_b])
                yield

        def fin(t):
            s = t % 2
            vb, gs, kd_b, qT, sc_b, y_b = vbb[s], gsb[s], kdb[s], qTb[s], scb[s], ybb[s]
            for hp in range(4):
                po = bank()
                for hh in range(2):
                    h = hp * 2 + hh
                    k.mm(po[:, hh * 256:(hh + 1) * 256], sc_b[:, h, :], vb[:, h * 256:(h + 1) * 256], True, False,
                         reads=[sc_b, vb], writes=[po])
                    k.mm(po[:, hh * 256:(hh + 1) * 256], qT[:, h, :], Sb[:, h, :], False, True,
                         reads=[qT, Sb], writes=[po])
                if hp % 2 == 0:
                    k.op("act", lambda e: e.copy(o_f[:, hp * 2:hp * 2 + 2, :],
                                                 po[:].rearrange("p (h v) -> p h v", h=2)), reads=[po], writes=[o_f])
                else:
                    k.op("dve", lambda e: e.tensor_copy(o_f[:, hp * 2:hp * 2 + 2, :],
                                                        po[:].rearrange("p (h v) -> p h v", h=2)),
                         reads=[po], writes=[o_f])
                yield
            for hp in range(4):
                pd = bank()
                for hh in range(2):
                    h = hp * 2 + hh
                    k.mm(pd[:, hh * 256:(hh + 1) * 256], kd_b[:, h, :, :].rearrange("p d t -> p (d t)"),
                         vb[:, h * 256:(h + 1) * 256], True, True, reads=[kd_b, vb], writes=[pd])
                for hh in range(2):
                    h = hp * 2 + hh
                    cc = gam[h] ** 128
                    k.op("dve", lambda e: e.scalar_tensor_tensor(Sg[:, h, :], Sg[:, h, :], cc,
                                                                 pd[:, hh * 256:(hh + 1) * 256], ALU.mult, ALU.add),
                         reads=[Sg, pd], writes=[Sg])
                yield
            k.op("act", lambda e: e.copy(Sb[:], Sg[:]), reads=[Sg], writes=[Sb])
            k.op("dve", lambda e: e.tensor_reduce(st["s1"][:], o_f[:], AX.X, ALU.add), reads=[o_f], writes=[st["s1"]])
            for h in range(8):
                k.op("act", lambda e: e.activation(junk[:, 0:256], o_f[:, h, :], AF.Square,
                                                   accum_out=st["s2"][:, h:h + 1]),
                     reads=[o_f], writes=[junk, st["s2"]])
            yield
            head_stats(k, st, 256, 1e-6)
            yield
            k.op("dve", lambda e: e.tensor_tensor(o_f[:], o_f[:], st["rstdh"][:].unsqueeze(2).to_broadcast([128, 8, 256]),
                                                  ALU.mult), reads=[o_f, st["rstdh"]], writes=[o_f])
            yield
            k.op("pool", lambda e: e.tensor_tensor(o_f[:], o_f[:], st["nmr"][:].unsqueeze(2).to_broadcast([128, 8, 256]),
                                                   ALU.add), reads=[o_f, st["nmr"]], writes=[o_f])
            yield
            k.op("pool", lambda e: e.tensor_tensor(y_b[:], o_f[:].rearrange("p h v -> p (h v)"), gs[:], ALU.mult),
                 reads=[o_f, gs], writes=[y_b])
            k.dma("pool", f"y{s}", y_out[t].t, y_b[:], reads=[y_b], writes=[y_out[t]])
            yield

        run_pipeline(NT, [prep, fin], [1, 1])
        k.barrier()


def outproj_phase(k, nc, T, hin, hout, srcs, K, w_out, g_post, ident, gscale=None):
    NT = T // 128
    NK = K // 128
    NG = NK // 4
    NPT = NK // 8
    es = ExitStack()
    with es:
        wout = [k.sb(es, f"owout{g}", [128, 4, D], BF16) for g in range(NG)]
        for g in range(NG):
            k.dma("pool", f"wl{g % 4}", wout[g][:],
                  w_out.t[g * 512:(g + 1) * 512, :].rearrange("(fc p) d -> p fc d", p=128),
                  reads=[w_out], writes=[wout[g]])
        gpost = k.sb(es, "gpost", [128, D], F32)
        k.dma("sp", "const", gpost[:], g_post.t.partition_broadcast(128), reads=[g_post], writes=[gpost])
        gsc = None
        if gscale is not None:
            gsc = k.sb(es, "gsc", [128, NK], F32)
            k.dma("sp", "const", gsc[:], gscale.t.rearrange("(kc p) -> p kc", p=128),
                  reads=[gscale], writes=[gsc], allow_slow_non_contiguous=True)
        NB = 3
        xb = [k.sb(es, f"x{i}", [128, D], F32) for i in range(NB)]
        need32 = any(not sr[3] for sr in srcs)
        oin = [k.sb(es, f"oin{i}", [128, K], F32) for i in range(NB)] if need32 else None
        oab = [k.sb(es, f"oab{i}", [128, K], BF16) for i in range(NB)]
        oT = [k.sb(es, f"oT{i}", [128, NK, 128], BF16) for i in range(NB)]
        f = [k.sb(es, f"f{i}", [128, D], F32) for i in range(NB)]
        junk = k.sb(es, "junk", [128, D], BF16)
        ss2 = [k.sb(es, f"ss2{i}", [128, 1], F32) for i in range(NB)]
        rstd2 = [k.sb(es, f"rstd2{i}", [128, 1], F32) for i in range(NB)]
        NPB = 3 if NPT == 1 else 2
        pT = [[k.ps(es, f"pT{b}_{j}", [128, 8, 128], BF16) for j in range(NPT)] for b in range(NPB)]
        pb = [k.ps(es, f"pb{b}", [128, 512], F32) for b in range(8 - NPB * NPT)]
        npb = len(pb)
        def front(t):
            s = t % NB
            k.dma("sp", f"x{s}", xb[s][:], hin[t].t, reads=[hin[t]], writes=[xb[s]])
            for j, (tiles_, c0, w, isb) in enumerate(srcs):
                if isb:
                    k.dma("sp", f"o{j}{s}", oab[s][:, c0:c0 + w], tiles_[t].t, reads=[tiles_[t]], writes=[oab[s]])
                else:
                    k.dma("sp", f"o{j}{s}", oin[s][:, c0:c0 + w], tiles_[t].t, reads=[tiles_[t]], writes=[oin[s]])
            if need32:
                k.op("act", lambda e: e.copy(oab[s][:], oin[s][:]), reads=[oin[s]], writes=[oab[s]])
            for i in range(NK):
                k.tr(pT[t % NPB][i // 8][:, i % 8, :], oab[s][:, i * 128:(i + 1) * 128], ident[:],
                     reads=[oab[s], ident], writes=[pT[t % NPB][i // 8]])
            for j in range(NPT):
                if gsc is None:
                    k.op("dve", lambda e: e.tensor_copy(oT[s][:, j * 8:j * 8 + 8, :], pT[t % NPB][j][:]),
                         reads=[pT[t % NPB][j]], writes=[oT[s]])
                else:
                    k.op("dve", lambda e: e.tensor_tensor(oT[s][:, j * 8:j * 8 + 8, :], pT[t % NPB][j][:],
                                                          gsc[:, j * 8:j * 8 + 8].unsqueeze(2).to_broadcast([128, 8, 128]),
                                                          ALU.mult), reads=[pT[t % NPB][j], gsc], writes=[oT[s]])

        def back(t):
            s = t % NB
            for nh in range(2):
                pm = pb[(2 * t + nh) % npb]
                for kc in range(NK):
                    k.mm(pm[:], oT[s][:, kc, :], wout[kc // 4][:, kc % 4, nh * 512:(nh + 1) * 512], kc == 0, kc == NK - 1,
                         reads=[oT[s], wout[kc // 4]], writes=[pm])
                if nh == 0:
                    k.op("dve", lambda e: e.tensor_copy(f[s][:, 0:512], pm[:]), reads=[pm], writes=[f[s]])
                else:
                    k.op("act", lambda e: e.copy(f[s][:, 512:1024], pm[:]), reads=[pm], writes=[f[s]])
            rms_rstd(k, "act", f[s][:], f[s], junk, ss2[s], rstd2[s], D)
            k.op("dve", lambda e: e.scalar_tensor_tensor(f[s][:], f[s][:], rstd2[s][:, 0:1], gpost[:], ALU.mult, ALU.mult),
                 reads=[f[s], rstd2[s], gpost], writes=[f[s]])
            k.op("pool", lambda e: e.tensor_tensor(f[s][:], f[s][:], xb[s][:], ALU.add), reads=[f[s], xb[s]], writes=[f[s]])
            k.dma("pool", f"y{s}", hout[t].t, f[s][:], reads=[f[s]], writes=[hout[t]])

        front(0)
        for t in range(NT):
            if t + 1 < NT:
                front(t + 1)
            back(t)
        k.barrier()


def head_stats(k, st, n, eps):
    k.op("dve", lambda e: e.tensor_scalar(st["mean"][:], st["s1"][:], 1.0 / n, None, ALU.mult),
         reads=[st["s1"]], writes=[st["mean"]])
    k.op("dve", lambda e: e.tensor_tensor(st["msq"][:], st["mean"][:], st["mean"][:], ALU.mult),
         reads=[st["mean"]], writes=[st["msq"]])
    k.op("dve", lambda e: e.scalar_tensor_tensor(st["var"][:], st["s2"][:], 1.0 / n, st["msq"][:],
                                                 ALU.mult, ALU.subtract),
         reads=[st["s2"], st["msq"]], writes=[st["var"]])
    k.op("dve", lambda e: e.tensor_scalar(st["var"][:], st["var"][:], eps, None, ALU.add),
         reads=[st["var"]], writes=[st["var"]])
    k.op("act", lambda e: e.activation(st["var"][:], st["var"][:], AF.Sqrt), reads=[st["var"]], writes=[st["var"]])
    k.op("dve", lambda e: e.reciprocal(st["rstdh"][:], st["var"][:]), reads=[st["var"]], writes=[st["rstdh"]])
    k.op("dve", lambda e: e.scalar_tensor_tensor(st["nmr"][:], st["mean"][:], -1.0, st["rstdh"][:],
                                                 ALU.mult, ALU.mult),
         reads=[st["mean"], st["rstdh"]], writes=[st["nmr"]])


DC = 32
C_MI, C_MX, C_MA, C_MITS, C_MITI, C_MTI, C_BLK = 0, 128, 256, 384, 512, 640, 768


def dplr_host_consts():
    idx = np.arange(128)
    ch = idx // DC
    same = ch[:, None] == ch[None, :]
    lt = idx[:, None] < idx[None, :]
    le = idx[:, None] <= idx[None, :]
    gt = idx[:, None] > idx[None, :]
    blk = ch[:, None] == np.arange(4)[None, :]
    pack = np.concatenate([same & le, same & lt, same & gt, same & lt, same & le, same & gt, blk], axis=1)
    return {"c_dplr": pack.astype(np.float32)}


def dplr_setup(k, es, c_dplr, ident):
    env = {"ident": ident}
    cd = k.sb(es, "cdplr", [128, 772], F32)
    k.dma("sp", "const", cd[:], c_dplr.t[:, :], reads=[c_dplr], writes=[cd])
    env["cd"] = cd
    W = 512
    for n in ("e1", "e2", "e3", "e4"):
        env[n] = k.sb(es, n, [128, W], F32)
    for n in ("rt", "kt", "bt", "at", "Xb", "U_b"):
        env[n] = k.sb(es, n, [128, W], BF16)
    env["kdec"] = [k.sb(es, f"kdec{s}", [128, W], BF16) for s in range(2)]
    env["bdec"] = [k.sb(es, f"bdec{s}", [128, W], BF16) for s in range(2)]
    env["wC"] = [k.sb(es, f"wC{s}", [128, 4, 4], F32) for s in range(2)]
    env["arT"] = [[k.sb(es, f"arT{s}{i}", [128, 4, 2, 128], BF16) for i in range(2)] for s in range(2)]
    env["bkT"] = [k.sb(es, f"bkT{i}", [128, 8, 128], BF16) for i in range(2)]
    env["NA"] = [k.sb(es, f"NA{s}", [128, 8, 2, 128], BF16) for s in range(2)]
    env["KA"] = [k.sb(es, f"KA{s}", [128, 8, 2, 128], BF16) for s in range(2)]
    env["Lm"] = k.sb(es, "Lm", [128, 8, 128], BF16)
    env["NP"] = [k.sb(es, f"NP{i}", [128, 8, 2, 128], BF16) for i in range(2)]
    env["Lp"] = [k.sb(es, f"Lp{i}", [128, 8, 128], BF16) for i in range(2)]
    env["Pf"] = k.sb(es, "Pf", [128, 8, 128], BF16)
    env["Ubar"] = [k.sb(es, f"Ubar{s}", [128, W], F32) for s in range(2)]
    env["AbT"] = [[k.sb(es, f"AbT{s}{i}", [128, 4, 128], BF16) for i in range(2)] for s in range(2)]
    env["St"] = k.sb(es, "St", [128, 4, 64], F32)
    env["Stb"] = k.sb(es, "Stb", [128, 4, 64], BF16)
    env["o"] = k.sb(es, "o_dplr", [128, W], F32)
    env["pT"] = [k.ps(es, f"pT{b}", [128, 8, 128], BF16) for b in range(2)]
    env["pU"] = k.ps(es, "pU", [128, 512], F32)
    env["pO"] = k.ps(es, "pO", [128, 512], F32)
    env["pS"] = env["pU"]
    env["pb"] = [k.ps(es, f"pb{b}", [128, 512], F32) for b in range(4)]
    env["ib"] = 0
    zero = [env["St"], env["Stb"], env["U_b"]] + env["bkT"]
    for s in range(2):
        zero += env["arT"][s] + env["AbT"][s]
    for b_ in zero:
        k.op("pool", lambda e: e.memset(b_[:], 0.0), writes=[b_])
    return env


def bank(env):
    b = env["pb"][env["ib"] % len(env["pb"])]
    env["ib"] += 1
    return b


def dplr_prep(k, env, s, R, Kp, A, B, Vb, ld):
    cd = env["cd"]
    ident = env["ident"]
    e1, e2, e3, e4 = env["e1"], env["e2"], env["e3"], env["e4"]
    ldb, lda = ld
    for (c0, dsts) in ((C_MI, ((e1, 1.0), (e2, -1.0))), (C_MX, ((e3, 1.0),)), (C_MA, ((e4, 1.0),))):
        pc = bank(env)
        k.mm(pc[:], cd[:, c0:c0 + 128], lda, True, True, reads=[cd, ldb], writes=[pc])
        for dst, sc in dsts:
            k.op("act", lambda e: e.activation(dst[:], pc[:], AF.Exp, scale=sc), reads=[pc], writes=[dst])
        yield
    pw = bank(env)
    for j in range(4):
        k.mm(pw[:, j * 4:(j + 1) * 4], lda[:, j * 128:(j + 1) * 128], cd[:, C_BLK:C_BLK + 4], True, True,
             reads=[ldb, cd], writes=[pw])
    wC = env["wC"][s]
    k.op("act", lambda e: e.activation(wC[:], pw[:, 0:16].rearrange("p (j c) -> p j c", j=4), AF.Exp),
         reads=[pw], writes=[wC])
    rt, kt, bt, at = (env[n] for n in ("rt", "kt", "bt", "at"))
    kdec, bdec = env["kdec"][s], env["bdec"][s]
    specs = ((at, A, e3), (rt, R, e1), (bt, B, e2), (kt, Kp, e2), (kdec, Kp, e4), (bdec, B, e4))
    for n, (dst, (sb_, sa), ee) in enumerate(specs):
        k.op("dve" if n % 2 == 0 else "pool", lambda e: e.tensor_tensor(dst[:], sa, ee[:], ALU.mult),
             reads=[sb_, ee], writes=[dst])
        if n % 2 == 1:
            yield
    pTa, pTb = env["pT"]
    arT, bkT = env["arT"][s], env["bkT"]
    for j in range(4):
        k.tr(pTa[:, j, :], at[:, j * 128:(j + 1) * 128], ident[:], reads=[at, ident], writes=[pTa])
        k.tr(pTa[:, 4 + j, :], rt[:, j * 128:(j + 1) * 128], ident[:], reads=[rt, ident], writes=[pTa])
    yield
    for j in range(4):
        k.tr(pTb[:, j, :], bt[:, j * 128:(j + 1) * 128], ident[:], reads=[bt, ident], writes=[pTb])
        k.tr(pTb[:, 4 + j, :], kt[:, j * 128:(j + 1) * 128], ident[:], reads=[kt, ident], writes=[pTb])
    for i, eng in ((0, "act"), (1, "dve")):
        pr = slice(64 * i, 64 * i + 64)
        k.op(eng, lambda e: (e.copy if eng == "act" else e.tensor_copy)(
            arT[i][pr].rearrange("p j s t -> p s j t"), pTa[pr].rearrange("p (s j) t -> p s j t", s=2)),
            reads=[pTa], writes=[arT[i]])
    yield
    for i, eng in ((0, "act"), (1, "dve")):
        pr = slice(64 * i, 64 * i + 64)
        k.op(eng, lambda e: (e.copy if eng == "act" else e.tensor_copy)(bkT[i][pr], pTb[pr]),
             reads=[pTb], writes=[bkT[i]])
    yield
    NA, KA, Lm = env["NA"][s], env["KA"][s], env["Lm"]
    mIT = cd[:, C_MITS:C_MITS + 256].rearrange("p (s t) -> p s t", s=2).unsqueeze(1).to_broadcast([128, 2, 2, 128])
    mTI = cd[:, C_MTI:C_MTI + 128].unsqueeze(1).to_broadcast([128, 4, 128])
    for hb in range(2):
        pl = bank(env)
        for hh in range(4):
            h = hb * 4 + hh
            k.mm(pl[:, hh * 128:(hh + 1) * 128], arT[h % 2][:, h // 2, 0, :], bkT[h % 2][:, h // 2, :], True, True,
                 reads=[arT[h % 2], bkT[h % 2]], writes=[pl])
        k.op("dve", lambda e: e.tensor_tensor(Lm[:, hb * 4:hb * 4 + 4, :], pl[:].rearrange("p (h t) -> p h t", h=4),
                                              mTI, ALU.mult), reads=[pl, cd], writes=[Lm])
        yield
    for dst, off in ((NA, 0), (KA, 4)):
        for hp in range(4):
            pk = bank(env)
            for hh in range(2):
                k.mm(pk[:, hh * 256:(hh + 1) * 256], bkT[hh][:, off + hp, :],
                     arT[hh][:, hp, :, :].rearrange("p s t -> p (s t)"), True, True,
                     reads=[bkT[hh], arT[hh]], writes=[pk])
            k.op("dve", lambda e: e.tensor_tensor(dst[:, 2 * hp:2 * hp + 2, :, :],
                                                  pk[:].rearrange("p (h s t) -> p h s t", h=2, s=2), mIT, ALU.mult),
                 reads=[pk, cd], writes=[dst])
            yield
    NP, Lp, Pf = env["NP"], env["Lp"], env["Pf"]
    idb = ident[:].unsqueeze(1).to_broadcast([128, 8, 128])

    def l_products(lhs_fn, rhs_fn, rbufs, dst):
        for hb in range(2):
            pn = bank(env)
            for hh in range(4):
                h = hb * 4 + hh
                k.mm(pn[:, hh * 128:(hh + 1) * 128], lhs_fn(h), rhs_fn(h), True, True, reads=rbufs, writes=[pn])
            k.op("act", lambda e: e.copy(dst[:, hb * 4:hb * 4 + 4, :], pn[:].rearrange("p (h t) -> p h t", h=4)),
                 reads=[pn], writes=[dst])

    k.op("pool", lambda e: e.tensor_tensor(NP[1][:, :, 1, :], NA[:, :, 0, :], idb, ALU.add),
         reads=[NA, ident], writes=[NP[1]])
    for hb in range(2):
        pn = bank(env)
        for hh in range(4):
            h = hb * 4 + hh
            k.mm(pn[:, hh * 128:(hh + 1) * 128], Lm[:, h, :], NA[:, h, 0, :], True, True, reads=[Lm, NA], writes=[pn])
        k.op("act", lambda e: e.copy(NP[1][:, hb * 4:hb * 4 + 4, 0, :], pn[:].rearrange("p (h t) -> p h t", h=4)),
             reads=[pn], writes=[NP[1]])
    yield
    l_products(lambda h: NA[:, h, 0, :], lambda h: Lm[:, h, :], [NA, Lm], Lp[1])
    yield
    for j in range(1, 4):
        cur_, nxt_ = NP[j % 2], NP[(j + 1) % 2]
        Lc, Ln = Lp[j % 2], Lp[(j + 1) % 2]
        for hp in range(4):
            pn = bank(env)
            for hh in range(2):
                h = hp * 2 + hh
                k.mm(pn[:, hh * 256:(hh + 1) * 256], Lc[:, h, :], cur_[:, h, :, :].rearrange("p s t -> p (s t)"),
                     True, True, reads=[Lc, cur_], writes=[pn])
            pv = pn[:].rearrange("p (h s t) -> p h s t", h=2, s=2)
            k.op("act", lambda e: e.copy(nxt_[:, 2 * hp:2 * hp + 2, 0, :], pv[:, :, 0, :]), reads=[pn], writes=[nxt_])
            k.op("dve", lambda e: e.tensor_tensor(nxt_[:, 2 * hp:2 * hp + 2, 1, :], pv[:, :, 1, :],
                                                  cur_[:, 2 * hp:2 * hp + 2, 1, :], ALU.add),
                 reads=[pn, cur_], writes=[nxt_])
            if hp % 2 == 1:
                yield
        l_products(lambda h: cur_[:, h, 0, :], lambda h: Lc[:, h, :], [cur_, Lc], Ln)
        yield
    for hb in range(2):
        pn = bank(env)
        for hh in range(4):
            h = hb * 4 + hh
            k.mm(pn[:, hh * 128:(hh + 1) * 128], Lp[0][:, h, :], NP[0][:, h, 1, :], True, True,
                 reads=[Lp[0], NP[0]], writes=[pn])
        k.op("dve", lambda e: e.tensor_tensor(Pf[:, hb * 4:hb * 4 + 4, :], pn[:].rearrange("p (h t) -> p h t", h=4),
                                              NP[0][:, hb * 4:hb * 4 + 4, 1, :], ALU.add),
             reads=[pn, NP[0]], writes=[Pf])
    yield
    P = Pf
    Xb, Ubar, AbT = env["Xb"], env["Ubar"][s], env["AbT"][s]
    px = bank(env)
    for h in range(8):
        k.mm(px[:, h * 64:(h + 1) * 64], KA[:, h, 0, :], Vb[:, h * 64:(h + 1) * 64], True, True,
             reads=[KA, Vb], writes=[px])
    k.op("act", lambda e: e.copy(Xb[:], px[:]), reads=[px], writes=[Xb])
    pab = bank(env)
    for h in range(8):
        pr = slice(64 * (h % 2), 64 * (h % 2) + 64)
        k.mm(pab[pr, (h // 2) * 128:(h // 2 + 1) * 128], at[:, h * 64:(h + 1) * 64], P[:, h, :], True, True,
             reads=[at, P], writes=[pab])
    k.op("act", lambda e: e.copy(AbT[0][0:64], pab[0:64].rearrange("p (j t) -> p j t", j=4)),
         reads=[pab], writes=[AbT[0]])
    k.op("dve", lambda e: e.tensor_copy(AbT[1][64:128], pab[64:128].rearrange("p (j t) -> p j t", j=4)),
         reads=[pab], writes=[AbT[1]])
    yield
    pub = bank(env)
    for h in range(8):
        k.mm(pub[:, h * 64:(h + 1) * 64], P[:, h, :], Xb[:, h * 64:(h + 1) * 64], True, True,
             reads=[P, Xb], writes=[pub])
    k.op("dve", lambda e: e.tensor_copy(Ubar[:], pub[:]), reads=[pub], writes=[Ubar])
    yield


def dplr_fin(k, env, s, Vb):
    arT, NA, KA, AbT, Ubar = env["arT"][s], env["NA"][s], env["KA"][s], env["AbT"][s], env["Ubar"][s]
    kdec, bdec, wC = env["kdec"][s], env["bdec"][s], env["wC"][s]
    U_b, St, Stb, pU, pO, pS = env["U_b"], env["St"], env["Stb"], env["pU"], env["pO"], env["pS"]
    for c in range(4):
        rc = slice(DC * c, DC * c + DC)
        cb = DC * c
        for h in range(8):
            k.mm(pU[rc, h * 64:(h + 1) * 64], AbT[h % 2][:, h // 2, rc], Stb[:, h // 2, :], True, True,
                 reads=[AbT[h % 2], Stb], writes=[pU], tp=(0, cb))
        k.op("dve", lambda e: e.tensor_tensor(U_b[rc, :], pU[rc, :], Ubar[rc, :], ALU.add),
             reads=[pU, Ubar], writes=[U_b])
        yield
        for h in range(8):
            pb_ = 64 * (h % 2)
            pr = slice(pb_, pb_ + 64)
            hc = slice(h * 64, (h + 1) * 64)
            jc = slice((h // 2) * 64, (h // 2 + 1) * 64)
            k.mm(pS[pr, jc], kdec[rc, hc], Vb[rc, hc], True, False, reads=[kdec, Vb], writes=[pS], tp=(cb, pb_))
            k.mm(pS[pr, jc], bdec[rc, hc], U_b[rc, hc], False, True, reads=[bdec, U_b], writes=[pS], tp=(cb, pb_))
        for h in range(8):
            hc = slice(h * 64, (h + 1) * 64)
            k.mm(pO[rc, hc], arT[h % 2][:, h // 2, 1, rc], Stb[:, h // 2, :], True, False,
                 reads=[arT[h % 2], Stb], writes=[pO], tp=(0, cb))
            k.mm(pO[rc, hc], NA[:, h, 1, rc], U_b[:, hc], False, False, reads=[NA, U_b], writes=[pO], tp=(0, cb))
            k.mm(pO[rc, hc], KA[:, h, 1, rc], Vb[:, hc], False, True, reads=[KA, Vb], writes=[pO], tp=(0, cb))
        k.op("pool", lambda e: e.tensor_tensor(St[:], St[:], wC[:, :, c].unsqueeze(2).to_broadcast([128, 4, 64]), ALU.mult),
             reads=[St, wC], writes=[St])
        k.op("dve", lambda e: e.tensor_tensor(St[:], St[:], pS[:, 0:256].rearrange("p (j v) -> p j v", j=4), ALU.add),
             reads=[St, pS], writes=[St])
        yield
        k.op("act", lambda e: e.copy(Stb[:], St[:]), reads=[St], writes=[Stb])
        yield
    o = env["o"]
    k.op("act", lambda e: e.copy(o[:], pO[:]), reads=[pO], writes=[o])
    yield


def run_pipeline(NT, stages, weights):
    ns = len(stages)
    for tau in range(NT + ns - 1):
        gens = []
        for i, st in enumerate(stages):
            t = tau - i
            if 0 <= t < NT:
                gens.append([st(t), weights[i]])
        while gens:
            for gw in list(gens):
                for _ in range(gw[1]):
                    try:
                        next(gw[0])
                    except StopIteration:
                        gens.remove(gw)
                        break


def norm_transpose(k, x, xs, junk, ss, rstd, pT, uT, gpre, ident):
    rms_rstd(k, "act", x[:], x, junk, ss, rstd, D)
    k.op("dve", lambda e: e.tensor_scalar(xs[:], x[:], rstd[:, 0:1], None, ALU.mult), reads=[x, rstd], writes=[xs])
    for kc in range(8):
        k.tr(pT[:, kc, :], xs[:, kc * 128:(kc + 1) * 128], ident[:], reads=[xs, ident], writes=[pT])
    k.op("dve", lambda e: e.tensor_tensor(uT[:], pT[:], gpre[:].unsqueeze(2).to_broadcast([128, 8, 128]), ALU.mult),
         reads=[pT, gpre], writes=[uT])


def inv_sqrt(k, dst, src, scale, eps):
    k.op("dve", lambda e: e.tensor_scalar(dst[:], src[:], scale, eps, ALU.mult, ALU.add), reads=[src], writes=[dst])
    k.op("act", lambda e: e.activation(dst[:], dst[:], AF.Sqrt), reads=[dst], writes=[dst])
    k.op("dve", lambda e: e.reciprocal(dst[:], dst[:]), reads=[dst], writes=[dst])


def bc(ap, n, w):
    return ap.unsqueeze(2).to_broadcast([128, n, w])


def gdn_phase(k, nc, T, hin, oa_out, w_in, conv_w, a_log, dt_bias, gnorm_w, g_pre, ident, c_dplr):
    NT = T // 128
    es = ExitStack()
    with es:
        win = [k.sb(es, f"gwin{kc}", [128, 2064], BF16) for kc in range(8)]
        for kc in range(8):
            k.dma("pool", f"wl{kc % 4}", win[kc][:], w_in.t[kc * 128:(kc + 1) * 128, 0:2064],
                  reads=[w_in], writes=[win[kc]])
        gpre = k.sb(es, "gpre", [128, 8], F32)
        k.dma("sp", "const", gpre[:], g_pre.t.rearrange("(kc p) -> p kc", p=128),
              reads=[g_pre], writes=[gpre], allow_slow_non_contiguous=True)
        cwt = k.sb(es, "cwt", [128, 4, 1536], F32)
        k.dma("sp", "const", cwt[:].rearrange("p j c -> p (j c)"),
              conv_w.t.rearrange("j c -> (j c)").partition_broadcast(128), reads=[conv_w], writes=[cwt])
        alb = k.sb(es, "alb", [128, 8], F32)
        dtb = k.sb(es, "dtb", [128, 8], F32)
        gnw = k.sb(es, "gnw", [128, 64], F32)
        k.dma("sp", "const", alb[:], a_log.t.partition_broadcast(128), reads=[a_log], writes=[alb])
        k.dma("sp", "const", dtb[:], dt_bias.t.partition_broadcast(128), reads=[dt_bias], writes=[dtb])
        k.dma("sp", "const", gnw[:], gnorm_w.t.partition_broadcast(128), reads=[gnorm_w], writes=[gnw])
        nea = k.sb(es, "nea", [128, 8], F32)
        k.op("act", lambda e: e.activation(nea[:], alb[:], AF.Exp), reads=[alb], writes=[nea])
        k.op("dve", lambda e: e.tensor_scalar(nea[:], nea[:], -1.0, None, ALU.mult), reads=[nea], writes=[nea])
        env = dplr_setup(k, es, c_dplr, ident)

        xb = [k.sb(es, f"x{i}", [128, D], F32) for i in range(2)]
        xs = k.sb(es, "xs", [128, D], BF16)
        junk = k.sb(es, "junk", [128, D], BF16)
        uT = k.sb(es, "uT", [128, 8, 128], BF16)
        ss = k.sb(es, "ss", [128, 1], F32)
        rstd = k.sb(es, "rstd", [128, 1], F32)
        pq = [k.sb(es, f"pq{i}", [128, 1536], F32) for i in range(2)]
        xsh = [k.sb(es, f"xsh{i}", [128, 1536], F32) for i in range(3)]
        cv = k.sb(es, "cv", [128, 1536], F32)
        zsb = [k.sb(es, f"zs{i}", [128, 512], F32) for i in range(3)]
        ba = k.sb(es, "ba", [128, 16], F32)
        tmp = xsh[2]
        tmp2 = k.sb(es, "tmp2", [128, 512], F32)
        sm = {n: k.sb(es, "g_" + n, [128, 16], F32) for n in ("ssq", "rn", "beta", "sp", "g", "eg", "coef", "rq",
                                                              "ss8", "rs8")}
        Rg = k.sb(es, "Rg", [128, 512], F32)
        Ag = k.sb(es, "Ag", [128, 512], F32)
        Kg = k.sb(es, "Kg", [128, 512], F32)
        Bg = k.sb(es, "Bg", [128, 512], F32)
        ldg = k.sb(es, "ldg", [128, 512], F32)
        Vbb = [k.sb(es, f"Vb{i}", [128, 512], BF16) for i in range(3)]
        oa = tmp2
        k.op("pool", lambda e: e.memset(pq[1][:], 0.0), writes=[pq[1]])
        print("gdn sbuf bytes remaining", nc.sbuf_bytes_remaining)
        v3 = lambda b_, lo: b_[:, lo:lo + 512].rearrange("p (h d) -> p h d", h=8)
        w3 = lambda b_: b_[:].rearrange("p (h d) -> p h d", h=8)

        def load_x(t):
            k.dma("sp", f"x{t % 2}", xb[t % 2][:], hin[t].t, reads=[hin[t]], writes=[xb[t % 2]])

        def prep(t):
            cur, prev = pq[t % 2], pq[(t + 1) % 2]
            x, zs, Vb = xb[t % 2], zsb[t % 3], Vbb[t % 3]
            if t == 0:
                load_x(0)
            if t + 1 < NT:
                load_x(t + 1)
            norm_transpose(k, x, xs, junk, ss, rstd, env["pT"][0], uT, gpre, ident)
            yield
            for g, (c0, c1) in enumerate(((0, 512), (512, 1024), (1024, 1536), (1536, 2048), (2048, 2064))):
                pu = bank(env)
                n = c1 - c0
                for kc in range(8):
                    k.mm(pu[:, 0:n], uT[:, kc, :], win[kc][:, c0:c1], kc == 0, kc == 7, reads=[uT, win[kc]], writes=[pu])
                if g < 3:
                    if g % 2 == 0:
                        k.op("act", lambda e: e.copy(cur[:, c0:c1], pu[:]), reads=[pu], writes=[cur])
                    else:
                        k.op("dve", lambda e: e.tensor_copy(cur[:, c0:c1], pu[:]), reads=[pu], writes=[cur])
                elif g == 3:
                    k.op("act", lambda e: e.activation(zs[:], pu[:], AF.Silu), reads=[pu], writes=[zs])
                else:
                    k.op("dve", lambda e: e.tensor_copy(ba[:], pu[:, 0:16]), reads=[pu], writes=[ba])
                yield
            for j in range(1, 4):
                sh = xsh[j - 1]
                k.dma("sp", f"sh{j}a", sh[j:128, :], cur[0:128 - j, :], reads=[cur], writes=[sh])
                k.dma("sp", f"sh{j}b", sh[0:j, :], prev[128 - j:128, :], reads=[prev], writes=[sh])
            k.op("act", lambda e: e.activation(sm["beta"][:, 0:8], ba[:, 0:8], AF.Sigmoid), reads=[ba], writes=[sm["beta"]])
            k.op("dve", lambda e: e.tensor_tensor(sm["sp"][:, 0:8], ba[:, 8:16], dtb[:], ALU.add),
                 reads=[ba, dtb], writes=[sm["sp"]])
            k.op("act", lambda e: e.activation(sm["sp"][:, 0:8], sm["sp"][:, 0:8], AF.Exp), reads=[sm["sp"]], writes=[sm["sp"]])
            k.op("dve", lambda e: e.tensor_scalar(sm["sp"][:, 0:8], sm["sp"][:, 0:8], 1.0, None, ALU.add),
                 reads=[sm["sp"]], writes=[sm["sp"]])
            k.op("act", lambda e: e.activation(sm["sp"][:, 0:8], sm["sp"][:, 0:8], AF.Ln), reads=[sm["sp"]], writes=[sm["sp"]])
            k.op("dve", lambda e: e.tensor_tensor(sm["g"][:, 0:8], sm["sp"][:, 0:8], nea[:], ALU.mult),
                 reads=[sm["sp"], nea], writes=[sm["g"]])
            k.op("act", lambda e: e.activation(sm["eg"][:, 0:8], sm["g"][:, 0:8], AF.Exp), reads=[sm["g"]], writes=[sm["eg"]])
            k.op("dve", lambda e: e.scalar_tensor_tensor(sm["coef"][:, 0:8], sm["eg"][:, 0:8], -1.0, sm["beta"][:, 0:8],
                                                         ALU.mult, ALU.mult), reads=[sm["eg"], sm["beta"]], writes=[sm["coef"]])
            k.op("act", lambda e: e.copy(w3(ldg), bc(sm["g"][:, 0:8], 8, 64)), reads=[sm["g"]], writes=[ldg])
            yield
            k.op("dve", lambda e: e.tensor_tensor(cv[:], cur[:], cwt[:, 3, :], ALU.mult), reads=[cur, cwt], writes=[cv])
            for j in range(1, 4):
                sh = xsh[j - 1]
                k.op("pool", lambda e: e.tensor_tensor(sh[:], sh[:], cwt[:, 3 - j, :], ALU.mult),
                     reads=[sh, cwt], writes=[sh])
                k.op("dve", lambda e: e.tensor_tensor(cv[:], cv[:], sh[:], ALU.add), reads=[cv, sh], writes=[cv])
                yield
            k.op("act", lambda e: e.activation(cv[:], cv[:], AF.Silu), reads=[cv], writes=[cv])
            yield
            k.op("act", lambda e: e.activation(tmp[:, 0:1024], cv[:, 0:1024], AF.Square), reads=[cv], writes=[tmp])
            k.op("dve", lambda e: e.tensor_reduce(sm["ssq"][:], tmp[:, 0:1024].rearrange("p (h d) -> p h d", h=16), AX.X, ALU.add),
                 reads=[tmp], writes=[sm["ssq"]])
            inv_sqrt(k, sm["rn"], sm["ssq"], 1.0, 1e-6)
            k.op("dve", lambda e: e.tensor_scalar(sm["rq"][:, 0:8], sm["rn"][:, 0:8], 0.125, None, ALU.mult),
                 reads=[sm["rn"]], writes=[sm["rq"]])
            yield
            k.op("dve", lambda e: e.tensor_tensor(w3(Rg), v3(cv, 0), bc(sm["rq"][:, 0:8], 8, 64), ALU.mult),
                 reads=[cv, sm["rq"]], writes=[Rg])
            k.op("pool", lambda e: e.tensor_tensor(w3(Ag), v3(cv, 512), bc(sm["rn"][:, 8:16], 8, 64), ALU.mult),
                 reads=[cv, sm["rn"]], writes=[Ag])
            k.op("dve", lambda e: e.tensor_tensor(w3(Kg), w3(Ag), bc(sm["beta"][:, 0:8], 8, 64), ALU.mult),
                 reads=[Ag, sm["beta"]], writes=[Kg])
            k.op("pool", lambda e: e.tensor_tensor(w3(Bg), w3(Ag), bc(sm["coef"][:, 0:8], 8, 64), ALU.mult),
                 reads=[Ag, sm["coef"]], writes=[Bg])
            k.op("act", lambda e: e.copy(Vb[:], cv[:, 1024:1536]), reads=[cv], writes=[Vb])
            yield

        def prep2(t):
            yield from dplr_prep(k, env, t % 2, (Rg, Rg[:]), (Kg, Kg[:]), (Ag, Ag[:]), (Bg, Bg[:]), Vbb[t % 3],
                                 (ldg, ldg[:]))

        def fin(t):
            zs, Vb = zsb[t % 3], Vbb[t % 3]
            yield from dplr_fin(k, env, t % 2, Vb)
            o = env["o"]
            k.op("act", lambda e: e.activation(tmp2[:], o[:], AF.Square), reads=[o], writes=[tmp2])
            k.op("dve", lambda e: e.tensor_reduce(sm["ss8"][:, 0:8], tmp2[:].rearrange("p (h d) -> p h d", h=8),
                                                  AX.X, ALU.add), reads=[tmp2], writes=[sm["ss8"]])
            yield
            inv_sqrt(k, sm["rs8"], sm["ss8"], 1.0 / 64, RMS_EPS)
            yield
            k.op("dve", lambda e: e.tensor_tensor(w3(oa), w3(o), bc(sm["rs8"][:, 0:8], 8, 64), ALU.mult),
                 reads=[o, sm["rs8"]], writes=[oa])
            k.op("pool", lambda e: e.tensor_tensor(w3(oa), w3(oa), gnw[:].unsqueeze(1).to_broadcast([128, 8, 64]), ALU.mult),
                 reads=[oa, gnw], writes=[oa])
            yield
            k.op("pool", lambda e: e.tensor_tensor(oa[:], oa[:], zs[:], ALU.mult), reads=[oa, zs], writes=[oa])
            k.dma("pool", "y0", oa_out[t].t, oa[:], reads=[oa], writes=[oa_out[t]])
            yield

        run_pipeline(NT, [prep, prep2, fin], [1, 2, 1])
        k.barrier()


def rwkv_phase(k, nc, T, hin, ob_out, w_in, vecs, w2, a2, g2, g_pre, ident, c_dplr):
    NT = T // 128
    es = ExitStack()
    with es:
        win = [k.sb(es, f"rwin{kc}", [128, 1792], BF16) for kc in range(8)]
        for kc in range(8):
            k.dma("pool", f"wl{kc % 4}", win[kc][:], w_in.t[kc * 128:(kc + 1) * 128, 2064:3856],
                  reads=[w_in], writes=[win[kc]])
        w2p = k.sb(es, "w2p", [128, 512], BF16)
        a2p = k.sb(es, "a2p", [128, 512], BF16)
        g2b = k.sb(es, "g2b", [128, 512], BF16)
        k.op("pool", lambda e: e.memset(w2p[:], 0.0), writes=[w2p])
        k.op("pool", lambda e: e.memset(a2p[:], 0.0), writes=[a2p])
        k.dma("pool", "wl0", w2p[0:64, :], w2.t[:, :], reads=[w2], writes=[w2p])
        k.dma("pool", "wl1", a2p[64:128, :], a2.t[:, :], reads=[a2], writes=[a2p])
        k.dma("pool", "wl2", g2b[:], g2.t[:, :], reads=[g2], writes=[g2b])
        gpre = k.sb(es, "gpre", [128, 8], F32)
        k.dma("sp", "const", gpre[:], g_pre.t.rearrange("(kc p) -> p kc", p=128),
              reads=[g_pre], writes=[gpre], allow_slow_non_contiguous=True)
        vb_ = {}
        for n, buf in vecs.items():
            w = 1792 if n == "mu" else 512
            vb_[n] = k.sb(es, "v_" + n, [128, w], F32)
            k.dma("sp", "const", vb_[n][:], buf.t.partition_broadcast(128), reads=[buf], writes=[vb_[n]])
        env = dplr_setup(k, es, c_dplr, ident)

        xb = [k.sb(es, f"x{i}", [128, D], F32) for i in range(2)]
        xs = k.sb(es, "xs", [128, D], BF16)
        junk = k.sb(es, "junk", [128, D], BF16)
        uT = k.sb(es, "uT", [128, 8, 128], BF16)
        ss = k.sb(es, "ss", [128, 1], F32)
        rstd = k.sb(es, "rstd", [128, 1], F32)
        prp = k.sb(es, "prp", [128, 1792], F32)
        rsh = k.sb(es, "rsh", [128, 1792], F32)
        lastrow = k.sb(es, "lastrow", [1, 1792], F32)
        lin = k.sb(es, "lin", [128, 256], BF16)
        linT = k.sb(es, "linT", [128, 2, 128], BF16)
        ldr = k.sb(es, "ldr", [128, 512], F32)
        aic = k.sb(es, "aic", [128, 512], F32)
        gateb = [k.sb(es, f"gate{i}", [128, 512], F32) for i in range(3)]
        tk = k.sb(es, "tk", [128, 512], F32)
        Kr = k.sb(es, "Kr", [128, 512], F32)
        Ar = k.sb(es, "Ar", [128, 512], F32)
        Br = k.sb(es, "Br", [128, 512], F32)
        Rr = k.sb(es, "Rr", [128, 512], F32)
        Vbb = [k.sb(es, f"Vb{i}", [128, 512], BF16) for i in range(3)]
        tmp = k.sb(es, "tmp", [128, 512], F32)
        tmp2 = k.sb(es, "tmp2", [128, 512], F32)
        bonusb = [k.sb(es, f"bonus{i}", [128, 512], F32) for i in range(3)]
        y = k.sb(es, "y", [128, 512], F32)
        sm = {n: k.sb(es, "r_" + n, [128, 8], F32) for n in ("ssq", "rn", "bs", "s1", "s2", "mean", "msq", "var",
                                                             "rstdh", "nmr")}
        k.op("pool", lambda e: e.memset(lastrow[:], 0.0), writes=[lastrow])
        print("rwkv sbuf bytes remaining", nc.sbuf_bytes_remaining)
        v3 = lambda ap: ap.rearrange("p (h d) -> p h d", h=8)

        def load_x(t):
            k.dma("sp", f"x{t % 2}", xb[t % 2][:], hin[t].t, reads=[hin[t]], writes=[xb[t % 2]])

        def prep(t):
            x, gate, Vb, bonus = xb[t % 2], gateb[t % 3], Vbb[t % 3], bonusb[t % 3]
            if t == 0:
                load_x(0)
            if t + 1 < NT:
                load_x(t + 1)
            norm_transpose(k, x, xs, junk, ss, rstd, env["pT"][0], uT, gpre, ident)
            yield
            for g, (c0, c1) in enumerate(((0, 512), (512, 1024), (1024, 1536), (1536, 1792))):
                pu = bank(env)
                n = c1 - c0
                for kc in range(8):
                    k.mm(pu[:, 0:n], uT[:, kc, :], win[kc][:, c0:c1], kc == 0, kc == 7, reads=[uT, win[kc]], writes=[pu])
                if g % 2 == 0:
                    k.op("act", lambda e: e.copy(prp[:, c0:c1], pu[:, 0:n]), reads=[pu], writes=[prp])
                else:
                    k.op("dve", lambda e: e.tensor_copy(prp[:, c0:c1], pu[:, 0:n]), reads=[pu], writes=[prp])
                yield
            k.dma("sp", "rsa", rsh[1:128, :], prp[0:127, :], reads=[prp], writes=[rsh])
            k.dma("sp", "rsb", rsh[0:1, :], lastrow[0:1, :], reads=[lastrow], writes=[rsh])
            k.dma("sp", "rsc", lastrow[0:1, :], prp[127:128, :], reads=[prp, rsh], writes=[lastrow])
            yield
            k.op("pool", lambda e: e.tensor_tensor(rsh[:], rsh[:], prp[:], ALU.subtract), reads=[rsh, prp, lastrow],
                 writes=[rsh])
            k.op("pool", lambda e: e.tensor_tensor(rsh[:], rsh[:], vb_["mu"][:], ALU.mult), reads=[rsh, vb_["mu"]],
                 writes=[rsh])
            yield
            k.op("dve", lambda e: e.tensor_tensor(prp[:], prp[:], rsh[:], ALU.add), reads=[prp, rsh, lastrow],
                 writes=[prp])
            r_ap, kr_ap, vr_ap = prp[:, 0:512], prp[:, 512:1024], prp[:, 1024:1536]
            k.op("act", lambda e: e.activation(lin[:, 0:64], prp[:, 1536:1600], AF.Tanh), reads=[prp], writes=[lin])
            k.op("act", lambda e: e.copy(lin[:, 64:128], prp[:, 1600:1664]), reads=[prp], writes=[lin])
            k.op("act", lambda e: e.activation(lin[:, 128:256], prp[:, 1664:1792], AF.Sigmoid), reads=[prp], writes=[lin])
            k.op("act", lambda e: e.copy(Vb[:], vr_ap), reads=[prp], writes=[Vb])
            k.op("act", lambda e: e.copy(Rr[:], r_ap), reads=[prp], writes=[Rr])
            yield
            pT0 = env["pT"][0]
            for i in range(2):
                k.tr(pT0[:, i, :], lin[:, i * 128:(i + 1) * 128], ident[:], reads=[lin, ident], writes=[pT0])
            k.op("act", lambda e: e.copy(linT[:], pT0[:, 0:2, :]), reads=[pT0], writes=[linT])
            yield
            pz = bank(env)
            k.mm(pz[:], linT[:, 0, :], w2p[:], True, True, reads=[linT, w2p], writes=[pz])
            k.op("dve", lambda e: e.tensor_tensor(ldr[:], pz[:], vb_["w0"][:], ALU.add), reads=[pz, vb_["w0"]], writes=[ldr])
            pa = bank(env)
            k.mm(pa[:], linT[:, 0, :], a2p[:], True, True, reads=[linT, a2p], writes=[pa])
            k.op("dve", lambda e: e.tensor_tensor(aic[:], pa[:], vb_["a0"][:], ALU.add), reads=[pa, vb_["a0"]], writes=[aic])
            yield
            k.op("act", lambda e: e.activation(ldr[:], ldr[:], AF.Sigmoid), reads=[ldr], writes=[ldr])
            k.op("act", lambda e: e.activation(aic[:], aic[:], AF.Sigmoid), reads=[aic], writes=[aic])
            pg = bank(env)
            k.mm(pg[:], linT[:, 1, :], g2b[:], True, True, reads=[linT, g2b], writes=[pg])
            k.op("act", lambda e: e.copy(gate[:], pg[:]), reads=[pg], writes=[gate])
            k.op("act", lambda e: e.mul(ldr[:], ldr[:], -0.6065306597126334), reads=[ldr], writes=[ldr])
            yield
            k.op("dve", lambda e: e.tensor_tensor(tk[:], kr_ap, vb_["k_k"][:], ALU.mult), reads=[prp, vb_["k_k"]], writes=[tk])
            k.op("act", lambda e: e.activation(tmp[:], tk[:], AF.Square), reads=[tk], writes=[tmp])
            k.op("dve", lambda e: e.tensor_reduce(sm["ssq"][:], v3(tmp[:]), AX.X, ALU.add), reads=[tmp], writes=[sm["ssq"]])
            yield
            inv_sqrt(k, sm["rn"], sm["ssq"], 1.0, 1e-6)
            yield
            k.op("dve", lambda e: e.tensor_tensor(v3(tk[:]), v3(tk[:]), bc(sm["rn"][:], 8, 64), ALU.mult),
                 reads=[tk, sm["rn"]], writes=[tk])
            k.op("dve", lambda e: e.scalar_tensor_tensor(Kr[:], aic[:], -1.0, vb_["k_a"][:], ALU.add, ALU.mult),
                 reads=[aic, vb_["k_a"]], writes=[Kr])
            k.op("dve", lambda e: e.scalar_tensor_tensor(Kr[:], Kr[:], 1.0, kr_ap, ALU.add, ALU.mult),
                 reads=[Kr, prp], writes=[Kr])
            k.op("act", lambda e: e.mul(Ar[:], tk[:], -1.0), reads=[tk], writes=[Ar])
            k.op("pool", lambda e: e.tensor_tensor(Br[:], tk[:], aic[:], ALU.mult), reads=[tk, aic], writes=[Br])
            yield
            k.op("pool", lambda e: e.tensor_tensor(tmp[:], r_ap, Kr[:], ALU.mult), reads=[prp, Kr], writes=[tmp])
            k.op("pool", lambda e: e.tensor_tensor(tmp[:], tmp[:], vb_["r_k"][:], ALU.mult), reads=[tmp, vb_["r_k"]],
                 writes=[tmp])
            k.op("dve", lambda e: e.tensor_reduce(sm["bs"][:], v3(tmp[:]), AX.X, ALU.add), reads=[tmp], writes=[sm["bs"]])
            k.op("dve", lambda e: e.tensor_tensor(v3(bonus[:]), v3(vr_ap), bc(sm["bs"][:], 8, 64), ALU.mult),
                 reads=[prp, sm["bs"]], writes=[bonus])
            yield

        def prep2(t):
            yield from dplr_prep(k, env, t % 2, (Rr, Rr[:]), (Kr, Kr[:]), (Ar, Ar[:]), (Br, Br[:]), Vbb[t % 3],
                                 (ldr, ldr[:]))

        def fin(t):
            gate, Vb, bonus = gateb[t % 3], Vbb[t % 3], bonusb[t % 3]
            yield from dplr_fin(k, env, t % 2, Vb)
            o = env["o"]
            k.op("dve", lambda e: e.tensor_reduce(sm["s1"][:], v3(o[:]), AX.X, ALU.add), reads=[o], writes=[sm["s1"]])
            k.op("act", lambda e: e.activation(tmp2[:], o[:], AF.Square), reads=[o], writes=[tmp2])
            k.op("dve", lambda e: e.tensor_reduce(sm["s2"][:], v3(tmp2[:]), AX.X, ALU.add), reads=[tmp2], writes=[sm["s2"]])
            yield
            head_stats(k, sm, 64, 64e-5)
            yield
            k.op("dve", lambda e: e.tensor_tensor(v3(y[:]), v3(o[:]), bc(sm["rstdh"][:], 8, 64), ALU.mult),
                 reads=[o, sm["rstdh"]], writes=[y])
            k.op("pool", lambda e: e.tensor_tensor(v3(y[:]), v3(y[:]), bc(sm["nmr"][:], 8, 64), ALU.add),
                 reads=[y, sm["nmr"]], writes=[y])
            yield
            k.op("dve", lambda e: e.tensor_tensor(y[:], y[:], vb_["ln_w"][:], ALU.mult), reads=[y, vb_["ln_w"]], writes=[y])
            k.op("pool", lambda e: e.tensor_tensor(y[:], y[:], vb_["ln_b"][:], ALU.add), reads=[y, vb_["ln_b"]], writes=[y])
            yield
            k.op("dve", lambda e: e.tensor_tensor(y[:], y[:], bonus[:], ALU.add), reads=[y, bonus], writes=[y])
            k.op("pool", lambda e: e.tensor_tensor(y[:], y[:], gate[:], ALU.mult), reads=[y, gate], writes=[y])
            k.dma("pool", "y0", ob_out[t].t, y[:], reads=[y], writes=[ob_out[t]])
            yield

        run_pipeline(NT, [prep, prep2, fin], [1, 2, 1])
        k.barrier()


def build_program(T, phases=("mlp",), dbg=None):
    nc = bass.Bass("TRN2", target_bir_lowering=False)
    k = KB(nc)

    dins = {}

    def din(name, shape):
        if name not in dins:
            dins[name] = Buf(nc.dram_tensor(name, list(shape), F32, kind="ExternalInput").ap(), name)
        return dins[name]

    NT = T // 128
    x = din("x", [T, D])
    c_ident = din("c_ident", [128, 128])
    out = Buf(nc.dram_tensor("out", [T, D], F32, kind="ExternalOutput").ap(), "out")

    def tiles(buf):
        return [Buf(buf.t[t * 128:(t + 1) * 128, :], f"{buf.name}{t}") for t in range(NT)]

    cur = tiles(x)
    with k.es:
        es = ExitStack()
        with es:
            ident = load_consts(k, es, c_ident)
            for pi, ph in enumerate(phases):
                last = pi == len(phases) - 1
                if ph in ("gdn", "rwkv", "ret"):
                    nxt = None
                elif last:
                    nxt = tiles(out)
                else:
                    scr = Buf(nc.dram_tensor(f"scr{pi}", [T, D], F32, kind="Internal").ap(), f"scr{pi}")
                    nxt = tiles(scr)
                if ph == "gdn":
                    oa_t = tiles(Buf(nc.dram_tensor("oa_scr", [T, 512], F32, kind="Internal").ap(), "oa_scr"))
                    gdn_phase(k, nc, T, cur, oa_t, din("ab_w_in", [D, 3856]), din("gdn_conv_w", [4, 1536]),
                              din("gdn_a_log", [8]), din("gdn_dt_bias", [8]), din("gdn_norm_w", [64]),
                              din("norm_mix_pre0", [D]), ident, din("c_dplr", [128, 772]))
                    continue
                if ph == "rwkv":
                    ob_t = tiles(Buf(nc.dram_tensor("ob_scr", [T, 512], F32, kind="Internal").ap(), "ob_scr"))
                    vecs = {"mu": din("rwkv_mu", [1792])}
                    for n in ("w0", "a0", "k_k", "k_a", "r_k", "ln_w", "ln_b"):
                        vecs[n] = din("rwkv_" + n, [512])
                    rwkv_phase(k, nc, T, cur, ob_t, din("ab_w_in", [D, 3856]), vecs,
                               din("rwkv_w2", [64, 512]), din("rwkv_a2", [64, 512]), din("rwkv_g2", [128, 512]),
                               din("norm_mix_pre0", [D]), ident, din("c_dplr", [128, 772]))
                    continue
                if ph == "abo":
                    outproj_phase(k, nc, T, cur, nxt, [(oa_t, 0, 512, False), (ob_t, 512, 512, False)], 1024,
                                  din("ab_w_out", [D, D]), din("norm_mix_post0", [D]), ident)
                    cur = nxt
                    continue
                if ph == "ret":
                    y_t = [Buf(a_, f"ys{t_}") for t_, a_ in enumerate(
                        (lambda yb: [yb[t_ * 128:(t_ + 1) * 128, :] for t_ in range(NT)])(
                            nc.dram_tensor("y_scr", [T, 2048], BF16, kind="Internal").ap()))]
                    ret_phase(k, nc, T, cur, y_t, din("ret_w_in", [D, 6144]), din("norm_mix_pre1", [D]), ident,
                              din("c_rot", [T, 128]), din("c_dqk", [128, 24]), din("c_maskT", [128, 128]))
                    continue
                if ph == "reto":
                    outproj_phase(k, nc, T, cur, nxt, [(y_t, 0, 2048, True)], 2048, din("ret_w_out", [2048, D]),
                                  din("norm_mix_post1", [D]), ident, gscale=din("ret_gn_w", [2048]))
                    cur = nxt
                    continue
                if ph.startswith("mlp"):
                    l = ph[3:]
                    mlp_phase(k, nc, T, cur, nxt, din("mlp_w_up" + l, [D, DFF]), din("mlp_w_down" + l, [DFF, D]),
                              din("norm_mlp_pre" + l, [D]), din("norm_mlp_post" + l, [D]), ident)
                cur = nxt
            k.barrier()
    return nc


PHASES = ("gdn", "rwkv", "abo", "mlp0", "ret", "reto", "mlp1")
SEQ = 4096
NCORES = 8


def host_consts(T):
    c = {"c_ident": np.eye(128, dtype=np.float32)}
    c.update(ret_host_consts(T))
    c.update(dplr_host_consts())
    return c


def kernel(**inputs):
    f32 = lambda a: np.ascontiguousarray(np.asarray(a, dtype=np.float32))
    x = f32(inputs["x"])
    B, T, _ = x.shape
    shared = dict(host_consts(T))
    for l in range(2):
        shared[f"mlp_w_up{l}"] = f32(inputs["mlp_w_up"][l])
        shared[f"mlp_w_down{l}"] = f32(inputs["mlp_w_down"][l])
        shared[f"norm_mlp_pre{l}"] = f32(inputs["norm_mlp_pre"][l])
        shared[f"norm_mlp_post{l}"] = f32(inputs["norm_mlp_post"][l])
        shared[f"norm_mix_pre{l}"] = f32(inputs["norm_mix_pre"][l])
        shared[f"norm_mix_post{l}"] = f32(inputs["norm_mix_post"][l])
    for n in ("ab_w_in", "gdn_conv_w", "gdn_a_log", "gdn_dt_bias", "gdn_norm_w", "rwkv_mu", "rwkv_w0", "rwkv_w2",
              "rwkv_a0", "rwkv_a2", "rwkv_g2", "rwkv_k_k", "rwkv_k_a", "rwkv_ln_w", "rwkv_ln_b", "ab_w_out",
              "ret_w_in", "ret_gn_w", "ret_w_out"):
        shared[n] = f32(inputs[n][0])
    shared["rwkv_r_k"] = f32(inputs["rwkv_r_k"][0]).reshape(512)
    nc = build_program(T, PHASES)
    in_maps = [dict(shared, x=x[b]) for b in range(B)]
    res = run_bass_kernel_spmd(nc, in_maps, core_ids=list(range(B)))
    return np.stack([np.asarray(r["out"], dtype=np.float32) for r in res.results], axis=0)
```

```python
import os
import numpy as np
from contextlib import ExitStack
import concourse.bass as bass
import concourse.mybir as mybir
from concourse.bass_utils import run_bass_kernel_spmd

F32 = mybir.dt.float32
BF16 = mybir.dt.bfloat16
AF = mybir.ActivationFunctionType
ALU = mybir.AluOpType
AX = mybir.AxisListType

SAME_ENG_SYNC = True


class Buf:
    __slots__ = ("t", "w", "r", "name")

    def __init__(self, t, name=""):
        self.t = t
        self.w = None
        self.r = {}
        self.name = name

    def __getitem__(self, key):
        return self.t[key]


class KB:
    def __init__(self, nc):
        self.nc = nc
        self.es = ExitStack()
        self.engs = {"pe": nc.tensor, "act": nc.scalar, "dve": nc.vector,
                     "pool": nc.gpsimd, "sp": nc.sync}
        self.nuniq = 0
        self.epoch = -1
        self.sem = {}
        self.cnt = {}
        self._new_epoch()

    def _new_epoch(self):
        self.epoch += 1
        self.known = {e: {} for e in self.engs}
        for e in self.engs:
            self.sem[e] = self.es.enter_context(self.nc.semaphore(f"s_{e}_{self.epoch}"))
            self.cnt[e] = 0

    def sb(self, es, name, shape, dtype):
        self.nuniq += 1
        t = es.enter_context(self.nc.sbuf_tensor(f"{name}_{self.nuniq}", list(shape), dtype))
        return Buf(t, name)

    def ps(self, es, name, shape, dtype=F32):
        self.nuniq += 1
        t = es.enter_context(self.nc.psum_tensor(f"{name}_{self.nuniq}", list(shape), dtype))
        return Buf(t, name)

    def stream(self, name):
        key = "d_" + name
        if key not in self.sem:
            self.sem[key] = self.es.enter_context(self.nc.semaphore(key))
            self.cnt[key] = 0
        return key

    def _waits(self, eng, reads, writes, extra=()):
        need = {}

        def add(ev):
            if ev is not None and ev[2] == self.epoch:
                if need.get(ev[0], 0) < ev[1]:
                    need[ev[0]] = ev[1]

        for b in reads:
            add(b.w)
        for b in writes:
            add(b.w)
            for sk, (v, ep) in b.r.items():
                add((sk, v, ep))
        for ev in extra:
            add(ev)
        e = self.engs[eng]
        kn = self.known[eng]
        for sk, v in need.items():
            if sk == eng and (eng == "pe" or eng == "sp" or not SAME_ENG_SYNC):
                continue
            if kn.get(sk, 0) >= v:
                continue
            e.wait_ge(self.sem[sk], v)
            kn[sk] = v

    def _record(self, ev, reads, writes):
        for b in reads:
            old = b.r.get(ev[0])
            if old is None or old[1] != ev[2] or old[0] < ev[1]:
                b.r[ev[0]] = (ev[1], ev[2])
        for b in writes:
            b.w = ev
            b.r = {}

    def op(self, eng, fn, reads=(), writes=()):
        self._waits(eng, reads, writes)
        ins = fn(self.engs[eng])
        self.cnt[eng] += 1
        ins.then_inc(self.sem[eng], 1)
        self._record((eng, self.cnt[eng], self.epoch), reads, writes)

    def dma(self, q, stream, out, in_, reads=(), writes=(), **kw):
        sk = self.stream(stream)
        prev = (sk, self.cnt[sk], self.epoch) if self.cnt[sk] else None
        self._waits(q, reads, writes, extra=(prev,))
        ins = self.engs[q].dma_start(out=out, in_=in_, **kw)
        self.cnt[sk] += 16
        ins.then_inc(self.sem[sk], 16)
        self._record((sk, self.cnt[sk], self.epoch), reads, writes)

    def dbg(self, name, buf, ap, shape):
        if os.environ.get("DBG", "0") != "1":
            return
        d = Buf(self.nc.dram_tensor("dbg_" + name, list(shape), F32, kind="ExternalOutput").ap(), name)
        self.dma("sp", "dbg", d.t, ap, reads=[buf], writes=[d])

    def barrier(self):
        for eng, e in self.engs.items():
            kn = self.known[eng]
            for sk, v in self.cnt.items():
                if v == 0 or kn.get(sk, 0) >= v:
                    continue
                if sk == eng and eng in ("pe", "sp"):
                    pass
                e.wait_ge(self.sem[sk], v)
                kn[sk] = v
        self._new_epoch()

    def mm(self, out, lhsT, rhs, start, stop, reads, writes, tp=None):
        kw = {}
        if tp is not None and (tp[0] == 96 or tp[1] == 96):
            kw["tile_position"] = tp
        self.op("pe", lambda e: e.matmul(out, lhsT, rhs, start=start, stop=stop, **kw), reads, writes)

    def tr(self, out, in_, ident, reads, writes):
        self.op("pe", lambda e: e.transpose(out, in_, ident), reads, writes)


D = 1024
DFF = 4096
RMS_EPS = 1e-6


def rms_rstd(k, eng_sq, x_ap, xbuf, junk, ss, rstd, n, eps=RMS_EPS):
    k.op("act", lambda e: e.activation(junk[:], x_ap, AF.Square, accum_out=ss[:]),
         reads=[xbuf], writes=[junk, ss])
    k.op("dve", lambda e: e.tensor_scalar(rstd[:], ss[:], 1.0 / n, eps, ALU.mult, ALU.add),
         reads=[ss], writes=[rstd])
    k.op("act", lambda e: e.activation(rstd[:], rstd[:], AF.Sqrt), reads=[rstd], writes=[rstd])
    k.op("dve", lambda e: e.reciprocal(rstd[:], rstd[:]), reads=[rstd], writes=[rstd])


def load_consts(k, es, c_ident):
    ident = k.sb(es, "ident", [128, 128], BF16)
    k.dma("pool", "const", ident[:], c_ident.t[:, :], reads=[c_ident], writes=[ident])
    return ident


def mlp_phase(k, nc, T, hin, hout, w_up, w_dn, g_pre, g_post, ident):
    NT = T // 128
    ST = 2
    NS = NT // ST
    es = ExitStack()
    with es:
        wup = [k.sb(es, f"wup{kc}", [128, DFF], BF16) for kc in range(8)]
        wdn = [k.sb(es, f"wdn{g}", [128, 4, D], BF16) for g in range(8)]
        for kc in range(8):
            k.dma("pool", f"wl{kc % 4}", wup[kc][:], w_up.t[kc * 128:(kc + 1) * 128, :],
                  reads=[w_up], writes=[wup[kc]])
        for g in range(8):
            k.dma("pool", f"wl{g % 4}", wdn[g][:],
                  w_dn.t[g * 512:(g + 1) * 512, :].rearrange("(fc p) d -> p fc d", p=128),
                  reads=[w_dn], writes=[wdn[g]])
        gpre = k.sb(es, "gpre", [128, 8], F32)
        k.dma("sp", "const", gpre[:], g_pre.t.rearrange("(kc p) -> p kc", p=128),
              reads=[g_pre], writes=[gpre], allow_slow_non_contiguous=True)
        gpost = k.sb(es, "gpost", [128, D], F32)
        k.dma("sp", "const", gpost[:], g_post.t.partition_broadcast(128),
              reads=[g_post], writes=[gpost])

        NB = 2
        xt = [[k.sb(es, f"xt{b}_{j}", [128, D], F32) for j in range(ST)] for b in range(NB)]
        uT = [k.sb(es, f"uT{b}", [128, 8, 128 * ST], BF16) for b in range(NB)]
        aT = [k.sb(es, f"aT{b}", [128, 32, 128 * ST], BF16) for b in range(1)]
        xs = [k.sb(es, f"xs{b}", [128, D], BF16) for b in range(2)]
        junk = k.sb(es, "junk", [128, D], BF16)
        rtmp = [k.sb(es, f"rtmp{b}", [128, 128 * ST], F32) for b in range(4)]
        ss = [k.sb(es, f"ss{b}", [128, 1], F32) for b in range(2)]
        rstd = [k.sb(es, f"rstd{b}", [128, 1], F32) for b in range(2)]
        ss2 = [k.sb(es, f"ss2{b}", [128, 1], F32) for b in range(2)]
        rstd2 = [k.sb(es, f"rstd2{b}", [128, 1], F32) for b in range(2)]
        fb = [k.sb(es, f"fb{b}", [128, D], F32) for b in range(2)]
        yb = [k.sb(es, f"yb{b}", [128, D], F32) for b in range(2)]
        pT = [k.ps(es, f"pT{b}", [128, 8, 128], BF16) for b in range(2)]
        pU = [k.ps(es, f"pU{b}", [128, 512], F32) for b in range(3)]
        pD = [k.ps(es, f"pD{b}", [128, 512], F32) for b in range(2)]

        state = {"ti": 0, "iu": 0, "idn": 0}
        NTOK = 128 * ST

        def prep(s):
            b = s % NB
            for j in range(ST):
                t = s * ST + j
                x = xt[b][j]
                ti = state["ti"]
                k.dma("sp", f"x{ti % 2}", x[:], hin[t].t, reads=[hin[t]], writes=[x])
                q = ti % 2
                rms_rstd(k, "act", x[:], x, junk, ss[q], rstd[q], D)
                k.op("dve", lambda e: e.tensor_scalar(xs[q][:], x[:], rstd[q][:, 0:1], None, ALU.mult),
                     reads=[x, rstd[q]], writes=[xs[q]])
                for kc in range(8):
                    k.tr(pT[q][:, kc, :], xs[q][:, kc * 128:(kc + 1) * 128], ident[:],
                         reads=[xs[q], ident], writes=[pT[q]])
                k.op("dve", lambda e: e.tensor_tensor(
                    uT[b][:, :, j * 128:(j + 1) * 128], pT[q][:],
                    gpre[:].unsqueeze(2).to_broadcast([128, 8, 128]), ALU.mult),
                    reads=[pT[q], gpre], writes=[uT[b]])
                state["ti"] += 1

        def up(s):
            b = s % NB
            for fc in range(32):
                pu = pU[state["iu"] % 3]
                state["iu"] += 1
                for kc in range(8):
                    k.mm(pu[:, 0:NTOK], wup[kc][:, fc * 128:(fc + 1) * 128], uT[b][:, kc, :],
                         kc == 0, kc == 7, reads=[wup[kc], uT[b]], writes=[pu])
                tm = rtmp[fc % 4]
                if fc % 2 == 0:
                    k.op("act", lambda e: e.activation(tm[:], pu[:, 0:NTOK], AF.Relu),
                         reads=[pu], writes=[tm])
                else:
                    k.op("dve", lambda e: e.tensor_scalar(tm[:], pu[:, 0:NTOK], 0.0, None, ALU.max),
                         reads=[pu], writes=[tm])
                k.op("pool", lambda e: e.tensor_tensor(aT[0][:, fc, :], tm[:], tm[:], ALU.mult),
                     reads=[tm], writes=[aT[0]])

        def down(s):
            b = s % NB
            for j in range(ST):
                t = s * ST + j
                x = xt[b][j]
                q = state["idn"] % 2
                f = fb[q]
                for nh in range(2):
                    pd = pD[nh]
                    for fc in range(32):
                        k.mm(pd[:], aT[0][:, fc, j * 128:(j + 1) * 128],
                             wdn[fc // 4][:, fc % 4, nh * 512:(nh + 1) * 512],
                             fc == 0, fc == 31, reads=[aT[0], wdn[fc // 4]], writes=[pd])
                    k.op("dve" if nh == 0 else "act",
                         (lambda e: e.tensor_copy(f[:, nh * 512:(nh + 1) * 512], pd[:])) if nh == 0 else
                         (lambda e: e.copy(f[:, nh * 512:(nh + 1) * 512], pd[:])),
                         reads=[pd], writes=[f])
                rms_rstd(k, "act", f[:], f, junk, ss2[q], rstd2[q], D)
                y = yb[q]
                k.op("dve", lambda e: e.scalar_tensor_tensor(y[:], f[:], rstd2[q][:, 0:1], gpost[:],
                                                             ALU.mult, ALU.mult),
                     reads=[f, rstd2[q], gpost], writes=[y])
                k.op("pool", lambda e: e.tensor_tensor(y[:], y[:], x[:], ALU.add),
                     reads=[y, x], writes=[y])
                k.dma("pool", f"y{q}", hout[t].t, y[:], reads=[y], writes=[hout[t]])
                state["idn"] += 1

        prep(0)
        for s in range(NS):
            up(s)
            if s + 1 < NS:
                prep(s + 1)
            down(s)
        k.barrier()


NH_RET = 8
RET_C = 128


def ret_gammas():
    return [float(1.0 - 2.0 ** (-5.0 - h)) for h in range(NH_RET)]


def ret_host_consts(T):
    pos = np.arange(T, dtype=np.float64)
    angle = 1.0 / (10000.0 ** np.linspace(0.0, 1.0, 64))
    th = pos[:, None] * angle[None, :]
    c_rot = np.concatenate([np.cos(th), np.sin(th)], axis=1).astype(np.float32)
    g = np.array(ret_gammas(), dtype=np.float64)
    idx = np.arange(128, dtype=np.float64)
    dq = g[None, :] ** idx[:, None]
    dk = g[None, :] ** (-idx[:, None]) * (128.0 ** -0.5)
    dkc = dk * (g[None, :] ** 128.0)
    c_dqk = np.concatenate([dq, dk, dkc], axis=1).astype(np.float32)
    c_maskT = (idx[:, None] <= idx[None, :]).astype(np.float32)
    return {"c_rot": c_rot, "c_dqk": c_dqk, "c_maskT": c_maskT}


def ret_phase(k, nc, T, hin, y_out, w_in, g_pre, ident, c_rot, c_dqk, c_maskT):
    NT = T // 128
    gam = ret_gammas()
    es = ExitStack()
    with es:
        win = [k.sb(es, f"rwin{kc}", [128, 6144], BF16) for kc in range(8)]
        for kc in range(8):
            k.dma("pool", f"wl{kc % 4}", win[kc][:], w_in.t[kc * 128:(kc + 1) * 128, :],
                  reads=[w_in], writes=[win[kc]])
        gpre = k.sb(es, "gpre", [128, 8], F32)
        k.dma("sp", "const", gpre[:], g_pre.t.rearrange("(kc p) -> p kc", p=128),
              reads=[g_pre], writes=[gpre], allow_slow_non_contiguous=True)
        dqk = k.sb(es, "dqk", [128, 24], F32)
        k.dma("sp", "const", dqk[:], c_dqk.t[:, :], reads=[c_dqk], writes=[dqk])
        maskT = k.sb(es, "maskT", [128, 128], F32)
        k.dma("sp", "const", maskT[:], c_maskT.t[:, :], reads=[c_maskT], writes=[maskT])

        xb = [k.sb(es, f"x{i}", [128, D], F32) for i in range(2)]
        xs = k.sb(es, "xs", [128, D], BF16)
        uT = k.sb(es, "uT", [128, 8, 128], BF16)
        cs = [k.sb(es, f"cs{b}", [128, 128], F32) for b in range(2)]
        vbb = [k.sb(es, f"vb{i}", [128, 2048], BF16) for i in range(2)]
        gsb = [k.sb(es, f"gs{i}", [128, 2048], BF16) for i in range(2)]
        tq = [[k.sb(es, f"tq{b}_{i}", [128, 4, 64], F32) for i in range(4)] for b in range(2)]
        qk_b = k.sb(es, "qk_b", [128, 16, 64, 2], BF16)
        kdb = [k.sb(es, f"kd_b{i}", [128, 8, 64, 2], BF16) for i in range(2)]
        qTb = [k.sb(es, f"qT{i}", [128, 8, 128], BF16) for i in range(2)]
        kT = k.sb(es, "kT", [128, 8, 128], BF16)
        scb = [k.sb(es, f"sc_b{i}", [128, 8, 128], BF16) for i in range(2)]
        Sg = k.sb(es, "Sg", [128, 8, 256], F32)
        Sb = k.sb(es, "Sb", [128, 8, 256], BF16)
        o_f = k.sb(es, "o_f", [128, 8, 256], F32)
        ybb = [k.sb(es, f"y_b{i}", [128, 2048], BF16) for i in range(2)]
        junk = k.sb(es, "junk", [128, D], BF16)
        st = {n: k.sb(es, n, [128, 8], F32) for n in ("s1", "s2", "mean", "msq", "var", "rstdh", "nmr")}
        ss = k.sb(es, "ss", [128, 1], F32)
        rstd = k.sb(es, "rstd", [128, 1], F32)
        pT = [k.ps(es, f"pT{b}", [128, 8, 128], BF16) for b in range(2)]
        pb = [k.ps(es, f"pb{b}", [128, 512], F32) for b in range(6)]
        ib = [0]
        print("ret sbuf bytes remaining", nc.sbuf_bytes_remaining)

        def bank():
            b = pb[ib[0] % 6]
            ib[0] += 1
            return b

        k.op("pool", lambda e: e.memset(Sg[:], 0.0), writes=[Sg])
        k.op("pool", lambda e: e.memset(Sb[:], 0.0), writes=[Sb])

        def load_x(t):
            k.dma("sp", f"x{t % 2}", xb[t % 2][:], hin[t].t, reads=[hin[t]], writes=[xb[t % 2]])
            k.dma("sp", f"cs{t % 2}", cs[t % 2][:], c_rot.t[t * 128:(t + 1) * 128, :], reads=[c_rot], writes=[cs[t % 2]])

        def prep(t):
            s = t % 2
            x, c, vb, gs, kd_b, qT, sc_b = xb[s], cs[s], vbb[s], gsb[s], kdb[s], qTb[s], scb[s]
            if t == 0:
                load_x(0)
            if t + 1 < NT:
                load_x(t + 1)
            norm_transpose(k, x, xs, junk, ss, rstd, pT[0], uT, gpre, ident)
            yield
            cosb = c[:, 0:64].unsqueeze(1).to_broadcast([128, 4, 64])
            sinb = c[:, 64:128].unsqueeze(1).to_broadcast([128, 4, 64])
            for g in range(12):
                pu = bank()
                for kc in range(8):
                    k.mm(pu[:], uT[:, kc, :], win[kc][:, g * 512:(g + 1) * 512], kc == 0, kc == 7,
                         reads=[uT, win[kc]], writes=[pu])
                if g < 4:
                    tt = tq[g % 2]
                    pv = pu[:].rearrange("p (h d t) -> p h d t", h=4, t=2)
                    x1 = pv[:, :, :, 0]
                    x2 = pv[:, :, :, 1]
                    k.op("dve", lambda e: e.tensor_tensor(tt[0][:], x1, cosb, ALU.mult), reads=[pu, c], writes=[tt[0]])
                    k.op("dve", lambda e: e.tensor_tensor(tt[1][:], x2, sinb, ALU.mult), reads=[pu, c], writes=[tt[1]])
                    k.op("dve", lambda e: e.tensor_tensor(tt[2][:], x2, cosb, ALU.mult), reads=[pu, c], writes=[tt[2]])
                    k.op("dve", lambda e: e.tensor_tensor(tt[3][:], x1, sinb, ALU.mult), reads=[pu, c], writes=[tt[3]])
                    k.op("pool", lambda e: e.tensor_tensor(tt[0][:], tt[0][:], tt[1][:], ALU.subtract),
                         reads=[tt[0], tt[1]], writes=[tt[0]])
                    k.op("pool", lambda e: e.tensor_tensor(tt[2][:], tt[2][:], tt[3][:], ALU.add),
                         reads=[tt[2], tt[3]], writes=[tt[2]])
                    h0 = 4 * g
                    sc = dqk[:, h0:h0 + 4].unsqueeze(2).to_broadcast([128, 4, 64])
                    k.op("pool", lambda e: e.tensor_tensor(qk_b[:, h0:h0 + 4, :, 0], tt[0][:], sc, ALU.mult),
                         reads=[tt[0], dqk], writes=[qk_b])
                    k.op("pool", lambda e: e.tensor_tensor(qk_b[:, h0:h0 + 4, :, 1], tt[2][:], sc, ALU.mult),
                         reads=[tt[2], dqk], writes=[qk_b])
                    if g >= 2:
                        hk = 4 * (g - 2)
                        sc2 = dqk[:, 16 + hk:16 + hk + 4].unsqueeze(2).to_broadcast([128, 4, 64])
                        k.op("pool", lambda e: e.tensor_tensor(kd_b[:, hk:hk + 4, :, 0], tt[0][:], sc2, ALU.mult),
                             reads=[tt[0], dqk], writes=[kd_b])
                        k.op("pool", lambda e: e.tensor_tensor(kd_b[:, hk:hk + 4, :, 1], tt[2][:], sc2, ALU.mult),
                             reads=[tt[2], dqk], writes=[kd_b])
                elif g < 8:
                    k.op("act", lambda e: e.copy(vb[:, (g - 4) * 512:(g - 3) * 512], pu[:]), reads=[pu], writes=[vb])
                else:
                    k.op("act", lambda e: e.activation(gs[:, (g - 8) * 512:(g - 7) * 512], pu[:], AF.Silu),
                         reads=[pu], writes=[gs])
                yield
            for i in range(16):
                k.tr(pT[i // 8][:, i % 8, :], qk_b[:, i, :, :].rearrange("p d t -> p (d t)"), ident[:],
                     reads=[qk_b, ident], writes=[pT[i // 8]])
            k.op("act", lambda e: e.copy(qT[:], pT[0][:]), reads=[pT[0]], writes=[qT])
            k.op("dve", lambda e: e.tensor_copy(kT[:], pT[1][:]), reads=[pT[1]], writes=[kT])
            yield
            for hb in range(2):
                psc = bank()
                for hh in range(4):
                    h = hb * 4 + hh
                    k.mm(psc[:, hh * 128:(hh + 1) * 128], kT[:, h, :], qT[:, h, :], True, True,
                         reads=[kT, qT], writes=[psc])
                k.op("dve", lambda e: e.tensor_tensor(
                    sc_b[:, hb * 4:hb * 4 + 4, :], psc[:].rearrange("p (h c) -> p h c", h=4),
                    maskT[:].unsqueeze(1).to_broadcast([128, 4, 128]), ALU.mult),
                    reads=[psc, maskT], writes=[sc_b])
                yield

        def fin(t):
            s = t % 2
            vb, gs, kd_b, qT, sc_b, y_b = vbb[s], gsb[s], kdb[s], qTb[s], scb[s], ybb[s]
            for hp in range(4):
                po = bank()
                for hh in range(2):
                    h = hp * 2 + hh
                    k.mm(po[:, hh * 256:(hh + 1) * 256], sc_b[:, h, :], vb[:, h * 256:(h + 1) * 256], True, False,
                         reads=[sc_b, vb], writes=[po])
                    k.mm(po[:, hh * 256:(hh + 1) * 256], qT[:, h, :], Sb[:, h, :], False, True,
                         reads=[qT, Sb], writes=[po])
                if hp % 2 == 0:
                    k.op("act", lambda e: e.copy(o_f[:, hp * 2:hp * 2 + 2, :],
                                                 po[:].rearrange("p (h v) -> p h v", h=2)), reads=[po], writes=[o_f])
                else:
                    k.op("dve", lambda e: e.tensor_copy(o_f[:, hp * 2:hp * 2 + 2, :],
                                                        po[:].rearrange("p (h v) -> p h v", h=2)),
                         reads=[po], writes=[o_f])
                yield
            for hp in range(4):
                pd = bank()
                for hh in range(2):
                    h = hp * 2 + hh
                    k.mm(pd[:, hh * 256:(hh + 1) * 256], kd_b[:, h, :, :].rearrange("p d t -> p (d t)"),
                         vb[:, h * 256:(h + 1) * 256], True, True, reads=[kd_b, vb], writes=[pd])
                for hh in range(2):
                    h = hp * 2 + hh
                    cc = gam[h] ** 128
                    k.op("dve", lambda e: e.scalar_tensor_tensor(Sg[:, h, :], Sg[:, h, :], cc,
                                                                 pd[:, hh * 256:(hh + 1) * 256], ALU.mult, ALU.add),
                         reads=[Sg, pd], writes=[Sg])
                yield
            k.op("act", lambda e: e.copy(Sb[:], Sg[:]), reads=[Sg], writes=[Sb])
            k.op("dve", lambda e: e.tensor_reduce(st["s1"][:], o_f[:], AX.X, ALU.add), reads=[o_f], writes=[st["s1"]])
            for h in range(8):
                k.op("act", lambda e: e.activation(junk[:, 0:256], o_f[:, h, :], AF.Square,
                                                   accum_out=st["s2"][:, h:h + 1]),
                     reads=[o_f], writes=[junk, st["s2"]])
            yield
            head_stats(k, st, 256, 1e-6)
            yield
            k.op("dve", lambda e: e.tensor_tensor(o_f[:], o_f[:], st["rstdh"][:].unsqueeze(2).to_broadcast([128, 8, 256]),
                                                  ALU.mult), reads=[o_f, st["rstdh"]], writes=[o_f])
            yield
            k.op("pool", lambda e: e.tensor_tensor(o_f[:], o_f[:], st["nmr"][:].unsqueeze(2).to_broadcast([128, 8, 256]),
                                                   ALU.add), reads=[o_f, st["nmr"]], writes=[o_f])
            yield
            k.op("pool", lambda e: e.tensor_tensor(y_b[:], o_f[:].rearrange("p h v -> p (h v)"), gs[:], ALU.mult),
                 reads=[o_f, gs], writes=[y_b])
            k.dma("pool", f"y{s}", y_out[t].t, y_b[:], reads=[y_b], writes=[y_out[t]])
            yield

        run_pipeline(NT, [prep, fin], [1, 1])
        k.barrier()


def outproj_phase(k, nc, T, hin, hout, srcs, K, w_out, g_post, ident, gscale=None):
    NT = T // 128
    NK = K // 128
    NG = NK // 4
    NPT = NK // 8
    es = ExitStack()
    with es:
        wout = [k.sb(es, f"owout{g}", [128, 4, D], BF16) for g in range(NG)]
        for g in range(NG):
            k.dma("pool", f"wl{g % 4}", wout[g][:],
                  w_out.t[g * 512:(g + 1) * 512, :].rearrange("(fc p) d -> p fc d", p=128),
                  reads=[w_out], writes=[wout[g]])
        gpost = k.sb(es, "gpost", [128, D], F32)
        k.dma("sp", "const", gpost[:], g_post.t.partition_broadcast(128), reads=[g_post], writes=[gpost])
        gsc = None
        if gscale is not None:
            gsc = k.sb(es, "gsc", [128, NK], F32)
            k.dma("sp", "const", gsc[:], gscale.t.rearrange("(kc p) -> p kc", p=128),
                  reads=[gscale], writes=[gsc], allow_slow_non_contiguous=True)
        NB = 3
        xb = [k.sb(es, f"x{i}", [128, D], F32) for i in range(NB)]
        need32 = any(not sr[3] for sr in srcs)
        oin = [k.sb(es, f"oin{i}", [128, K], F32) for i in range(NB)] if need32 else None
        oab = [k.sb(es, f"oab{i}", [128, K], BF16) for i in range(NB)]
        oT = [k.sb(es, f"oT{i}", [128, NK, 128], BF16) for i in range(NB)]
        f = [k.sb(es, f"f{i}", [128, D], F32) for i in range(NB)]
        junk = k.sb(es, "junk", [128, D], BF16)
        ss2 = [k.sb(es, f"ss2{i}", [128, 1], F32) for i in range(NB)]
        rstd2 = [k.sb(es, f"rstd2{i}", [128, 1], F32) for i in range(NB)]
        NPB = 3 if NPT == 1 else 2
        pT = [[k.ps(es, f"pT{b}_{j}", [128, 8, 128], BF16) for j in range(NPT)] for b in range(NPB)]
        pb = [k.ps(es, f"pb{b}", [128, 512], F32) for b in range(8 - NPB * NPT)]
        npb = len(pb)
        def front(t):
            s = t % NB
            k.dma("sp", f"x{s}", xb[s][:], hin[t].t, reads=[hin[t]], writes=[xb[s]])
            for j, (tiles_, c0, w, isb) in enumerate(srcs):
                if isb:
                    k.dma("sp", f"o{j}{s}", oab[s][:, c0:c0 + w], tiles_[t].t, reads=[tiles_[t]], writes=[oab[s]])
                else:
                    k.dma("sp", f"o{j}{s}", oin[s][:, c0:c0 + w], tiles_[t].t, reads=[tiles_[t]], writes=[oin[s]])
            if need32:
                k.op("act", lambda e: e.copy(oab[s][:], oin[s][:]), reads=[oin[s]], writes=[oab[s]])
            for i in range(NK):
                k.tr(pT[t % NPB][i // 8][:, i % 8, :], oab[s][:, i * 128:(i + 1) * 128], ident[:],
                     reads=[oab[s], ident], writes=[pT[t % NPB][i // 8]])
            for j in range(NPT):
                if gsc is None:
                    k.op("dve", lambda e: e.tensor_copy(oT[s][:, j * 8:j * 8 + 8, :], pT[t % NPB][j][:]),
                         reads=[pT[t % NPB][j]], writes=[oT[s]])
                else:
                    k.op("dve", lambda e: e.tensor_tensor(oT[s][:, j * 8:j * 8 + 8, :], pT[t % NPB][j][:],
                                                          gsc[:, j * 8:j * 8 + 8].unsqueeze(2).to_broadcast([128, 8, 128]),
                                                          ALU.mult), reads=[pT[t % NPB][j], gsc], writes=[oT[s]])

        def back(t):
            s = t % NB
            for nh in range(2):
                pm = pb[(2 * t + nh) % npb]
                for kc in range(NK):
                    k.mm(pm[:], oT[s][:, kc, :], wout[kc // 4][:, kc % 4, nh * 512:(nh + 1) * 512], kc == 0, kc == NK - 1,
                         reads=[oT[s], wout[kc // 4]], writes=[pm])
                if nh == 0:
                    k.op("dve", lambda e: e.tensor_copy(f[s][:, 0:512], pm[:]), reads=[pm], writes=[f[s]])
                else:
                    k.op("act", lambda e: e.copy(f[s][:, 512:1024], pm[:]), reads=[pm], writes=[f[s]])
            rms_rstd(k, "act", f[s][:], f[s], junk, ss2[s], rstd2[s], D)
            k.op("dve", lambda e: e.scalar_tensor_tensor(f[s][:], f[s][:], rstd2[s][:, 0:1], gpost[:], ALU.mult, ALU.mult),
                 reads=[f[s], rstd2[s], gpost], writes=[f[s]])
            k.op("pool", lambda e: e.tensor_tensor(f[s][:], f[s][:], xb[s][:], ALU.add), reads=[f[s], xb[s]], writes=[f[s]])
            k.dma("pool", f"y{s}", hout[t].t, f[s][:], reads=[f[s]], writes=[hout[t]])

        front(0)
        for t in range(NT):
            if t + 1 < NT:
                front(t + 1)
            back(t)
        k.barrier()


def head_stats(k, st, n, eps):
    k.op("dve", lambda e: e.tensor_scalar(st["mean"][:], st["s1"][:], 1.0 / n, None, ALU.mult),
         reads=[st["s1"]], writes=[st["mean"]])
    k.op("dve", lambda e: e.tensor_tensor(st["msq"][:], st["mean"][:], st["mean"][:], ALU.mult),
         reads=[st["mean"]], writes=[st["msq"]])
    k.op("dve", lambda e: e.scalar_tensor_tensor(st["var"][:], st["s2"][:], 1.0 / n, st["msq"][:],
                                                 ALU.mult, ALU.subtract),
         reads=[st["s2"], st["msq"]], writes=[st["var"]])
    k.op("dve", lambda e: e.tensor_scalar(st["var"][:], st["var"][:], eps, None, ALU.add),
         reads=[st["var"]], writes=[st["var"]])
    k.op("act", lambda e: e.activation(st["var"][:], st["var"][:], AF.Sqrt), reads=[st["var"]], writes=[st["var"]])
    k.op("dve", lambda e: e.reciprocal(st["rstdh"][:], st["var"][:]), reads=[st["var"]], writes=[st["rstdh"]])
    k.op("dve", lambda e: e.scalar_tensor_tensor(st["nmr"][:], st["mean"][:], -1.0, st["rstdh"][:],
                                                 ALU.mult, ALU.mult),
         reads=[st["mean"], st["rstdh"]], writes=[st["nmr"]])


DC = 32
C_MI, C_MX, C_MA, C_MITS, C_MITI, C_MTI, C_BLK = 0, 128, 256, 384, 512, 640, 768


def dplr_host_consts():
    idx = np.arange(128)
    ch = idx // DC
    same = ch[:, None] == ch[None, :]
    lt = idx[:, None] < idx[None, :]
    le = idx[:, None] <= idx[None, :]
    gt = idx[:, None] > idx[None, :]
    blk = ch[:, None] == np.arange(4)[None, :]
    pack = np.concatenate([same & le, same & lt, same & gt, same & lt, same & le, same & gt, blk], axis=1)
    return {"c_dplr": pack.astype(np.float32)}


def dplr_setup(k, es, c_dplr, ident):
    env = {"ident": ident}
    cd = k.sb(es, "cdplr", [128, 772], F32)
    k.dma("sp", "const", cd[:], c_dplr.t[:, :], reads=[c_dplr], writes=[cd])
    env["cd"] = cd
    W = 512
    for n in ("e1", "e2", "e3", "e4"):
        env[n] = k.sb(es, n, [128, W], F32)
    for n in ("rt", "kt", "bt", "at", "Xb", "U_b"):
        env[n] = k.sb(es, n, [128, W], BF16)
    env["kdec"] = [k.sb(es, f"kdec{s}", [128, W], BF16) for s in range(2)]
    env["bdec"] = [k.sb(es, f"bdec{s}", [128, W], BF16) for s in range(2)]
    env["wC"] = [k.sb(es, f"wC{s}", [128, 4, 4], F32) for s in range(2)]
    env["arT"] = [[k.sb(es, f"arT{s}{i}", [128, 4, 2, 128], BF16) for i in range(2)] for s in range(2)]
    env["bkT"] = [k.sb(es, f"bkT{i}", [128, 8, 128], BF16) for i in range(2)]
    env["NA"] = [k.sb(es, f"NA{s}", [128, 8, 2, 128], BF16) for s in range(2)]
    env["KA"] = [k.sb(es, f"KA{s}", [128, 8, 2, 128], BF16) for s in range(2)]
    env["Lm"] = k.sb(es, "Lm", [128, 8, 128], BF16)
    env["NP"] = [k.sb(es, f"NP{i}", [128, 8, 2, 128], BF16) for i in range(2)]
    env["Lp"] = [k.sb(es, f"Lp{i}", [128, 8, 128], BF16) for i in range(2)]
    env["Pf"] = k.sb(es, "Pf", [128, 8, 128], BF16)
    env["Ubar"] = [k.sb(es, f"Ubar{s}", [128, W], F32) for s in range(2)]
    env["AbT"] = [[k.sb(es, f"AbT{s}{i}", [128, 4, 128], BF16) for i in range(2)] for s in range(2)]
    env["St"] = k.sb(es, "St", [128, 4, 64], F32)
    env["Stb"] = k.sb(es, "Stb", [128, 4, 64], BF16)
    env["o"] = k.sb(es, "o_dplr", [128, W], F32)
    env["pT"] = [k.ps(es, f"pT{b}", [128, 8, 128], BF16) for b in range(2)]
    env["pU"] = k.ps(es, "pU", [128, 512], F32)
    env["pO"] = k.ps(es, "pO", [128, 512], F32)
    env["pS"] = env["pU"]
    env["pb"] = [k.ps(es, f"pb{b}", [128, 512], F32) for b in range(4)]
    env["ib"] = 0
    zero = [env["St"], env["Stb"], env["U_b"]] + env["bkT"]
    for s in range(2):
        zero += env["arT"][s] + env["AbT"][s]
    for b_ in zero:
        k.op("pool", lambda e: e.memset(b_[:], 0.0), writes=[b_])
    return env


def bank(env):
    b = env["pb"][env["ib"] % len(env["pb"])]
    env["ib"] += 1
    return b


def dplr_prep(k, env, s, R, Kp, A, B, Vb, ld):
    cd = env["cd"]
    ident = env["ident"]
    e1, e2, e3, e4 = env["e1"], env["e2"], env["e3"], env["e4"]
    ldb, lda = ld
    for (c0, dsts) in ((C_MI, ((e1, 1.0), (e2, -1.0))), (C_MX, ((e3, 1.0),)), (C_MA, ((e4, 1.0),))):
        pc = bank(env)
        k.mm(pc[:], cd[:, c0:c0 + 128], lda, True, True, reads=[cd, ldb], writes=[pc])
        for dst, sc in dsts:
            k.op("act", lambda e: e.activation(dst[:], pc[:], AF.Exp, scale=sc), reads=[pc], writes=[dst])
        yield
    pw = bank(env)
    for j in range(4):
        k.mm(pw[:, j * 4:(j + 1) * 4], lda[:, j * 128:(j + 1) * 128], cd[:, C_BLK:C_BLK + 4], True, True,
             reads=[ldb, cd], writes=[pw])
    wC = env["wC"][s]
    k.op("act", lambda e: e.activation(wC[:], pw[:, 0:16].rearrange("p (j c) -> p j c", j=4), AF.Exp),
         reads=[pw], writes=[wC])
    rt, kt, bt, at = (env[n] for n in ("rt", "kt", "bt", "at"))
    kdec, bdec = env["kdec"][s], env["bdec"][s]
    specs = ((at, A, e3), (rt, R, e1), (bt, B, e2), (kt, Kp, e2), (kdec, Kp, e4), (bdec, B, e4))
    for n, (dst, (sb_, sa), ee) in enumerate(specs):
        k.op("dve" if n % 2 == 0 else "pool", lambda e: e.tensor_tensor(dst[:], sa, ee[:], ALU.mult),
             reads=[sb_, ee], writes=[dst])
        if n % 2 == 1:
            yield
    pTa, pTb = env["pT"]
    arT, bkT = env["arT"][s], env["bkT"]
    for j in range(4):
        k.tr(pTa[:, j, :], at[:, j * 128:(j + 1) * 128], ident[:], reads=[at, ident], writes=[pTa])
        k.tr(pTa[:, 4 + j, :], rt[:, j * 128:(j + 1) * 128], ident[:], reads=[rt, ident], writes=[pTa])
    yield
    for j in range(4):
        k.tr(pTb[:, j, :], bt[:, j * 128:(j + 1) * 128], ident[:], reads=[bt, ident], writes=[pTb])
        k.tr(pTb[:, 4 + j, :], kt[:, j * 128:(j + 1) * 128], ident[:], reads=[kt, ident], writes=[pTb])
    for i, eng in ((0, "act"), (1, "dve")):
        pr = slice(64 * i, 64 * i + 64)
        k.op(eng, lambda e: (e.copy if eng == "act" else e.tensor_copy)(
            arT[i][pr].rearrange("p j s t -> p s j t"), pTa[pr].rearrange("p (s j) t -> p s j t", s=2)),
            reads=[pTa], writes=[arT[i]])
    yield
    for i, eng in ((0, "act"), (1, "dve")):
        pr = slice(64 * i, 64 * i + 64)
        k.op(eng, lambda e: (e.copy if eng == "act" else e.tensor_copy)(bkT[i][pr], pTb[pr]),
             reads=[pTb], writes=[bkT[i]])
    yield
    NA, KA, Lm = env["NA"][s], env["KA"][s], env["Lm"]
    mIT = cd[:, C_MITS:C_MITS + 256].rearrange("p (s t) -> p s t", s=2).unsqueeze(1).to_broadcast([128, 2, 2, 128])
    mTI = cd[:, C_MTI:C_MTI + 128].unsqueeze(1).to_broadcast([128, 4, 128])
    for hb in range(2):
        pl = bank(env)
        for hh in range(4):
            h = hb * 4 + hh
            k.mm(pl[:, hh * 128:(hh + 1) * 128], arT[h % 2][:, h // 2, 0, :], bkT[h % 2][:, h // 2, :], True, True,
                 reads=[arT[h % 2], bkT[h % 2]], writes=[pl])
        k.op("dve", lambda e: e.tensor_tensor(Lm[:, hb * 4:hb * 4 + 4, :], pl[:].rearrange("p (h t) -> p h t", h=4),
                                              mTI, ALU.mult), reads=[pl, cd], writes=[Lm])
        yield
    for dst, off in ((NA, 0), (KA, 4)):
        for hp in range(4):
            pk = bank(env)
            for hh in range(2):
                k.mm(pk[:, hh * 256:(hh + 1) * 256], bkT[hh][:, off + hp, :],
                     arT[hh][:, hp, :, :].rearrange("p s t -> p (s t)"), True, True,
                     reads=[bkT[hh], arT[hh]], writes=[pk])
            k.op("dve", lambda e: e.tensor_tensor(dst[:, 2 * hp:2 * hp + 2, :, :],
                                                  pk[:].rearrange("p (h s t) -> p h s t", h=2, s=2), mIT, ALU.mult),
                 reads=[pk, cd], writes=[dst])
            yield
    NP, Lp, Pf = env["NP"], env["Lp"], env["Pf"]
    idb = ident[:].unsqueeze(1).to_broadcast([128, 8, 128])

    def l_products(lhs_fn, rhs_fn, rbufs, dst):
        for hb in range(2):
            pn = bank(env)
            for hh in range(4):
                h = hb * 4 + hh
                k.mm(pn[:, hh * 128:(hh + 1) * 128], lhs_fn(h), rhs_fn(h), True, True, reads=rbufs, writes=[pn])
            k.op("act", lambda e: e.copy(dst[:, hb * 4:hb * 4 + 4, :], pn[:].rearrange("p (h t) -> p h t", h=4)),
                 reads=[pn], writes=[dst])

    k.op("pool", lambda e: e.tensor_tensor(NP[1][:, :, 1, :], NA[:, :, 0, :], idb, ALU.add),
         reads=[NA, ident], writes=[NP[1]])
    for hb in range(2):
        pn = bank(env)
        for hh in range(4):
            h = hb * 4 + hh
            k.mm(pn[:, hh * 128:(hh + 1) * 128], Lm[:, h, :], NA[:, h, 0, :], True, True, reads=[Lm, NA], writes=[pn])
        k.op("act", lambda e: e.copy(NP[1][:, hb * 4:hb * 4 + 4, 0, :], pn[:].rearrange("p (h t) -> p h t", h=4)),
             reads=[pn], writes=[NP[1]])
    yield
    l_products(lambda h: NA[:, h, 0, :], lambda h: Lm[:, h, :], [NA, Lm], Lp[1])
    yield
    for j in range(1, 4):
        cur_, nxt_ = NP[j % 2], NP[(j + 1) % 2]
        Lc, Ln = Lp[j % 2], Lp[(j + 1) % 2]
        for hp in range(4):
            pn = bank(env)
            for hh in range(2):
                h = hp * 2 + hh
                k.mm(pn[:, hh * 256:(hh + 1) * 256], Lc[:, h, :], cur_[:, h, :, :].rearrange("p s t -> p (s t)"),
                     True, True, reads=[Lc, cur_], writes=[pn])
            pv = pn[:].rearrange("p (h s t) -> p h s t", h=2, s=2)
            k.op("act", lambda e: e.copy(nxt_[:, 2 * hp:2 * hp + 2, 0, :], pv[:, :, 0, :]), reads=[pn], writes=[nxt_])
            k.op("dve", lambda e: e.tensor_tensor(nxt_[:, 2 * hp:2 * hp + 2, 1, :], pv[:, :, 1, :],
                                                  cur_[:, 2 * hp:2 * hp + 2, 1, :], ALU.add),
                 reads=[pn, cur_], writes=[nxt_])
            if hp % 2 == 1:
                yield
        l_products(lambda h: cur_[:, h, 0, :], lambda h: Lc[:, h, :], [cur_, Lc], Ln)
        yield
    for hb in range(2):
        pn = bank(env)
        for hh in range(4):
            h = hb * 4 + hh
            k.mm(pn[:, hh * 128:(hh + 1) * 128], Lp[0][:, h, :], NP[0][:, h, 1, :], True, True,
                 reads=[Lp[0], NP[0]], writes=[pn])
        k.op("dve", lambda e: e.tensor_tensor(Pf[:, hb * 4:hb * 4 + 4, :], pn[:].rearrange("p (h t) -> p h t", h=4),
                                              NP[0][:, hb * 4:hb * 4 + 4, 1, :], ALU.add),
             reads=[pn, NP[0]], writes=[Pf])
    yield
    P = Pf
    Xb, Ubar, AbT = env["Xb"], env["Ubar"][s], env["AbT"][s]
    px = bank(env)
    for h in range(8):
        k.mm(px[:, h * 64:(h + 1) * 64], KA[:, h, 0, :], Vb[:, h * 64:(h + 1) * 64], True, True,
             reads=[KA, Vb], writes=[px])
    k.op("act", lambda e: e.copy(Xb[:], px[:]), reads=[px], writes=[Xb])
    pab = bank(env)
    for h in range(8):
        pr = slice(64 * (h % 2), 64 * (h % 2) + 64)
        k.mm(pab[pr, (h // 2) * 128:(h // 2 + 1) * 128], at[:, h * 64:(h + 1) * 64], P[:, h, :], True, True,
             reads=[at, P], writes=[pab])
    k.op("act", lambda e: e.copy(AbT[0][0:64], pab[0:64].rearrange("p (j t) -> p j t", j=4)),
         reads=[pab], writes=[AbT[0]])
    k.op("dve", lambda e: e.tensor_copy(AbT[1][64:128], pab[64:128].rearrange("p (j t) -> p j t", j=4)),
         reads=[pab], writes=[AbT[1]])
    yield
    pub = bank(env)
    for h in range(8):
        k.mm(pub[:, h * 64:(h + 1) * 64], P[:, h, :], Xb[:, h * 64:(h + 1) * 64], True, True,
             reads=[P, Xb], writes=[pub])
    k.op("dve", lambda e: e.tensor_copy(Ubar[:], pub[:]), reads=[pub], writes=[Ubar])
    yield


def dplr_fin(k, env, s, Vb):
    arT, NA, KA, AbT, Ubar = env["arT"][s], env["NA"][s], env["KA"][s], env["AbT"][s], env["Ubar"][s]
    kdec, bdec, wC = env["kdec"][s], env["bdec"][s], env["wC"][s]
    U_b, St, Stb, pU, pO, pS = env["U_b"], env["St"], env["Stb"], env["pU"], env["pO"], env["pS"]
    for c in range(4):
        rc = slice(DC * c, DC * c + DC)
        cb = DC * c
        for h in range(8):
            k.mm(pU[rc, h * 64:(h + 1) * 64], AbT[h % 2][:, h // 2, rc], Stb[:, h // 2, :], True, True,
                 reads=[AbT[h % 2], Stb], writes=[pU], tp=(0, cb))
        k.op("dve", lambda e: e.tensor_tensor(U_b[rc, :], pU[rc, :], Ubar[rc, :], ALU.add),
             reads=[pU, Ubar], writes=[U_b])
        yield
        for h in range(8):
            pb_ = 64 * (h % 2)
            pr = slice(pb_, pb_ + 64)
            hc = slice(h * 64, (h + 1) * 64)
            jc = slice((h // 2) * 64, (h // 2 + 1) * 64)
            k.mm(pS[pr, jc], kdec[rc, hc], Vb[rc, hc], True, False, reads=[kdec, Vb], writes=[pS], tp=(cb, pb_))
            k.mm(pS[pr, jc], bdec[rc, hc], U_b[rc, hc], False, True, reads=[bdec, U_b], writes=[pS], tp=(cb, pb_))
        for h in range(8):
            hc = slice(h * 64, (h + 1) * 64)
            k.mm(pO[rc, hc], arT[h % 2][:, h // 2, 1, rc], Stb[:, h // 2, :], True, False,
                 reads=[arT[h % 2], Stb], writes=[pO], tp=(0, cb))
            k.mm(pO[rc, hc], NA[:, h, 1, rc], U_b[:, hc], False, False, reads=[NA, U_b], writes=[pO], tp=(0, cb))
            k.mm(pO[rc, hc], KA[:, h, 1, rc], Vb[:, hc], False, True, reads=[KA, Vb], writes=[pO], tp=(0, cb))
        k.op("pool", lambda e: e.tensor_tensor(St[:], St[:], wC[:, :, c].unsqueeze(2).to_broadcast([128, 4, 64]), ALU.mult),
             reads=[St, wC], writes=[St])
        k.op("dve", lambda e: e.tensor_tensor(St[:], St[:], pS[:, 0:256].rearrange("p (j v) -> p j v", j=4), ALU.add),
             reads=[St, pS], writes=[St])
        yield
        k.op("act", lambda e: e.copy(Stb[:], St[:]), reads=[St], writes=[Stb])
        yield
    o = env["o"]
    k.op("act", lambda e: e.copy(o[:], pO[:]), reads=[pO], writes=[o])
    yield


def run_pipeline(NT, stages, weights):
    ns = len(stages)
    for tau in range(NT + ns - 1):
        gens = []
        for i, st in enumerate(stages):
            t = tau - i
            if 0 <= t < NT:
                gens.append([st(t), weights[i]])
        while gens:
            for gw in list(gens):
                for _ in range(gw[1]):
                    try:
                        next(gw[0])
                    except StopIteration:
                        gens.remove(gw)
                        break


def norm_transpose(k, x, xs, junk, ss, rstd, pT, uT, gpre, ident):
    rms_rstd(k, "act", x[:], x, junk, ss, rstd, D)
    k.op("dve", lambda e: e.tensor_scalar(xs[:], x[:], rstd[:, 0:1], None, ALU.mult), reads=[x, rstd], writes=[xs])
    for kc in range(8):
        k.tr(pT[:, kc, :], xs[:, kc * 128:(kc + 1) * 128], ident[:], reads=[xs, ident], writes=[pT])
    k.op("dve", lambda e: e.tensor_tensor(uT[:], pT[:], gpre[:].unsqueeze(2).to_broadcast([128, 8, 128]), ALU.mult),
         reads=[pT, gpre], writes=[uT])


def inv_sqrt(k, dst, src, scale, eps):
    k.op("dve", lambda e: e.tensor_scalar(dst[:], src[:], scale, eps, ALU.mult, ALU.add), reads=[src], writes=[dst])
    k.op("act", lambda e: e.activation(dst[:], dst[:], AF.Sqrt), reads=[dst], writes=[dst])
    k.op("dve", lambda e: e.reciprocal(dst[:], dst[:]), reads=[dst], writes=[dst])


def bc(ap, n, w):
    return ap.unsqueeze(2).to_broadcast([128, n, w])


def gdn_phase(k, nc, T, hin, oa_out, w_in, conv_w, a_log, dt_bias, gnorm_w, g_pre, ident, c_dplr):
    NT = T // 128
    es = ExitStack()
    with es:
        win = [k.sb(es, f"gwin{kc}", [128, 2064], BF16) for kc in range(8)]
        for kc in range(8):
            k.dma("pool", f"wl{kc % 4}", win[kc][:], w_in.t[kc * 128:(kc + 1) * 128, 0:2064],
                  reads=[w_in], writes=[win[kc]])
        gpre = k.sb(es, "gpre", [128, 8], F32)
        k.dma("sp", "const", gpre[:], g_pre.t.rearrange("(kc p) -> p kc", p=128),
              reads=[g_pre], writes=[gpre], allow_slow_non_contiguous=True)
        cwt = k.sb(es, "cwt", [128, 4, 1536], F32)
        k.dma("sp", "const", cwt[:].rearrange("p j c -> p (j c)"),
              conv_w.t.rearrange("j c -> (j c)").partition_broadcast(128), reads=[conv_w], writes=[cwt])
        alb = k.sb(es, "alb", [128, 8], F32)
        dtb = k.sb(es, "dtb", [128, 8], F32)
        gnw = k.sb(es, "gnw", [128, 64], F32)
        k.dma("sp", "const", alb[:], a_log.t.partition_broadcast(128), reads=[a_log], writes=[alb])
        k.dma("sp", "const", dtb[:], dt_bias.t.partition_broadcast(128), reads=[dt_bias], writes=[dtb])
        k.dma("sp", "const", gnw[:], gnorm_w.t.partition_broadcast(128), reads=[gnorm_w], writes=[gnw])
        nea = k.sb(es, "nea", [128, 8], F32)
        k.op("act", lambda e: e.activation(nea[:], alb[:], AF.Exp), reads=[alb], writes=[nea])
        k.op("dve", lambda e: e.tensor_scalar(nea[:], nea[:], -1.0, None, ALU.mult), reads=[nea], writes=[nea])
        env = dplr_setup(k, es, c_dplr, ident)

        xb = [k.sb(es, f"x{i}", [128, D], F32) for i in range(2)]
        xs = k.sb(es, "xs", [128, D], BF16)
        junk = k.sb(es, "junk", [128, D], BF16)
        uT = k.sb(es, "uT", [128, 8, 128], BF16)
        ss = k.sb(es, "ss", [128, 1], F32)
        rstd = k.sb(es, "rstd", [128, 1], F32)
        pq = [k.sb(es, f"pq{i}", [128, 1536], F32) for i in range(2)]
        xsh = [k.sb(es, f"xsh{i}", [128, 1536], F32) for i in range(3)]
        cv = k.sb(es, "cv", [128, 1536], F32)
        zsb = [k.sb(es, f"zs{i}", [128, 512], F32) for i in range(3)]
        ba = k.sb(es, "ba", [128, 16], F32)
        tmp = xsh[2]
        tmp2 = k.sb(es, "tmp2", [128, 512], F32)
        sm = {n: k.sb(es, "g_" + n, [128, 16], F32) for n in ("ssq", "rn", "beta", "sp", "g", "eg", "coef", "rq",
                                                              "ss8", "rs8")}
        Rg = k.sb(es, "Rg", [128, 512], F32)
        Ag = k.sb(es, "Ag", [128, 512], F32)
        Kg = k.sb(es, "Kg", [128, 512], F32)
        Bg = k.sb(es, "Bg", [128, 512], F32)
        ldg = k.sb(es, "ldg", [128, 512], F32)
        Vbb = [k.sb(es, f"Vb{i}", [128, 512], BF16) for i in range(3)]
        oa = tmp2
        k.op("pool", lambda e: e.memset(pq[1][:], 0.0), writes=[pq[1]])
        print("gdn sbuf bytes remaining", nc.sbuf_bytes_remaining)
        v3 = lambda b_, lo: b_[:, lo:lo + 512].rearrange("p (h d) -> p h d", h=8)
        w3 = lambda b_: b_[:].rearrange("p (h d) -> p h d", h=8)

        def load_x(t):
            k.dma("sp", f"x{t % 2}", xb[t % 2][:], hin[t].t, reads=[hin[t]], writes=[xb[t % 2]])

        def prep(t):
            cur, prev = pq[t % 2], pq[(t + 1) % 2]
            x, zs, Vb = xb[t % 2], zsb[t % 3], Vbb[t % 3]
            if t == 0:
                load_x(0)
            if t + 1 < NT:
                load_x(t + 1)
            norm_transpose(k, x, xs, junk, ss, rstd, env["pT"][0], uT, gpre, ident)
            yield
            for g, (c0, c1) in enumerate(((0, 512), (512, 1024), (1024, 1536), (1536, 2048), (2048, 2064))):
                pu = bank(env)
                n = c1 - c0
                for kc in range(8):
                    k.mm(pu[:, 0:n], uT[:, kc, :], win[kc][:, c0:c1], kc == 0, kc == 7, reads=[uT, win[kc]], writes=[pu])
                if g < 3:
                    if g % 2 == 0:
                        k.op("act", lambda e: e.copy(cur[:, c0:c1], pu[:]), reads=[pu], writes=[cur])
                    else:
                        k.op("dve", lambda e: e.tensor_copy(cur[:, c0:c1], pu[:]), reads=[pu], writes=[cur])
                elif g == 3:
                    k.op("act", lambda e: e.activation(zs[:], pu[:], AF.Silu), reads=[pu], writes=[zs])
                else:
                    k.op("dve", lambda e: e.tensor_copy(ba[:], pu[:, 0:16]), reads=[pu], writes=[ba])
                yield
            for j in range(1, 4):
                sh = xsh[j - 1]
                k.dma("sp", f"sh{j}a", sh[j:128, :], cur[0:128 - j, :], reads=[cur], writes=[sh])
                k.dma("sp", f"sh{j}b", sh[0:j, :], prev[128 - j:128, :], reads=[prev], writes=[sh])
            k.op("act", lambda e: e.activation(sm["beta"][:, 0:8], ba[:, 0:8], AF.Sigmoid), reads=[ba], writes=[sm["beta"]])
            k.op("dve", lambda e: e.tensor_tensor(sm["sp"][:, 0:8], ba[:, 8:16], dtb[:], ALU.add),
                 reads=[ba, dtb], writes=[sm["sp"]])
            k.op("act", lambda e: e.activation(sm["sp"][:, 0:8], sm["sp"][:, 0:8], AF.Exp), reads=[sm["sp"]], writes=[sm["sp"]])
            k.op("dve", lambda e: e.tensor_scalar(sm["sp"][:, 0:8], sm["sp"][:, 0:8], 1.0, None, ALU.add),
                 reads=[sm["sp"]], writes=[sm["sp"]])
            k.op("act", lambda e: e.activation(sm["sp"][:, 0:8], sm["sp"][:, 0:8], AF.Ln), reads=[sm["sp"]], writes=[sm["sp"]])
            k.op("dve", lambda e: e.tensor_tensor(sm["g"][:, 0:8], sm["sp"][:, 0:8], nea[:], ALU.mult),
                 reads=[sm["sp"], nea], writes=[sm["g"]])
            k.op("act", lambda e: e.activation(sm["eg"][:, 0:8], sm["g"][:, 0:8], AF.Exp), reads=[sm["g"]], writes=[sm["eg"]])
            k.op("dve", lambda e: e.scalar_tensor_tensor(sm["coef"][:, 0:8], sm["eg"][:, 0:8], -1.0, sm["beta"][:, 0:8],
                                                         ALU.mult, ALU.mult), reads=[sm["eg"], sm["beta"]], writes=[sm["coef"]])
            k.op("act", lambda e: e.copy(w3(ldg), bc(sm["g"][:, 0:8], 8, 64)), reads=[sm["g"]], writes=[ldg])
            yield
            k.op("dve", lambda e: e.tensor_tensor(cv[:], cur[:], cwt[:, 3, :], ALU.mult), reads=[cur, cwt], writes=[cv])
            for j in range(1, 4):
                sh = xsh[j - 1]
                k.op("pool", lambda e: e.tensor_tensor(sh[:], sh[:], cwt[:, 3 - j, :], ALU.mult),
                     reads=[sh, cwt], writes=[sh])
                k.op("dve", lambda e: e.tensor_tensor(cv[:], cv[:], sh[:], ALU.add), reads=[cv, sh], writes=[cv])
                yield
            k.op("act", lambda e: e.activation(cv[:], cv[:], AF.Silu), reads=[cv], writes=[cv])
            yield
            k.op("act", lambda e: e.activation(tmp[:, 0:1024], cv[:, 0:1024], AF.Square), reads=[cv], writes=[tmp])
            k.op("dve", lambda e: e.tensor_reduce(sm["ssq"][:], tmp[:, 0:1024].rearrange("p (h d) -> p h d", h=16), AX.X, ALU.add),
                 reads=[tmp], writes=[sm["ssq"]])
            inv_sqrt(k, sm["rn"], sm["ssq"], 1.0, 1e-6)
            k.op("dve", lambda e: e.tensor_scalar(sm["rq"][:, 0:8], sm["rn"][:, 0:8], 0.125, None, ALU.mult),
                 reads=[sm["rn"]], writes=[sm["rq"]])
            yield
            k.op("dve", lambda e: e.tensor_tensor(w3(Rg), v3(cv, 0), bc(sm["rq"][:, 0:8], 8, 64), ALU.mult),
                 reads=[cv, sm["rq"]], writes=[Rg])
            k.op("pool", lambda e: e.tensor_tensor(w3(Ag), v3(cv, 512), bc(sm["rn"][:, 8:16], 8, 64), ALU.mult),
                 reads=[cv, sm["rn"]], writes=[Ag])
            k.op("dve", lambda e: e.tensor_tensor(w3(Kg), w3(Ag), bc(sm["beta"][:, 0:8], 8, 64), ALU.mult),
                 reads=[Ag, sm["beta"]], writes=[Kg])
            k.op("pool", lambda e: e.tensor_tensor(w3(Bg), w3(Ag), bc(sm["coef"][:, 0:8], 8, 64), ALU.mult),
                 reads=[Ag, sm["coef"]], writes=[Bg])
            k.op("act", lambda e: e.copy(Vb[:], cv[:, 1024:1536]), reads=[cv], writes=[Vb])
            yield

        def prep2(t):
            yield from dplr_prep(k, env, t % 2, (Rg, Rg[:]), (Kg, Kg[:]), (Ag, Ag[:]), (Bg, Bg[:]), Vbb[t % 3],
                                 (ldg, ldg[:]))

        def fin(t):
            zs, Vb = zsb[t % 3], Vbb[t % 3]
            yield from dplr_fin(k, env, t % 2, Vb)
            o = env["o"]
            k.op("act", lambda e: e.activation(tmp2[:], o[:], AF.Square), reads=[o], writes=[tmp2])
            k.op("dve", lambda e: e.tensor_reduce(sm["ss8"][:, 0:8], tmp2[:].rearrange("p (h d) -> p h d", h=8),
                                                  AX.X, ALU.add), reads=[tmp2], writes=[sm["ss8"]])
            yield
            inv_sqrt(k, sm["rs8"], sm["ss8"], 1.0 / 64, RMS_EPS)
            yield
            k.op("dve", lambda e: e.tensor_tensor(w3(oa), w3(o), bc(sm["rs8"][:, 0:8], 8, 64), ALU.mult),
                 reads=[o, sm["rs8"]], writes=[oa])
            k.op("pool", lambda e: e.tensor_tensor(w3(oa), w3(oa), gnw[:].unsqueeze(1).to_broadcast([128, 8, 64]), ALU.mult),
                 reads=[oa, gnw], writes=[oa])
            yield
            k.op("pool", lambda e: e.tensor_tensor(oa[:], oa[:], zs[:], ALU.mult), reads=[oa, zs], writes=[oa])
            k.dma("pool", "y0", oa_out[t].t, oa[:], reads=[oa], writes=[oa_out[t]])
            yield

        run_pipeline(NT, [prep, prep2, fin], [1, 6, 1])
        k.barrier()


def rwkv_phase(k, nc, T, hin, ob_out, w_in, vecs, w2, a2, g2, g_pre, ident, c_dplr):
    NT = T // 128
    es = ExitStack()
    with es:
        win = [k.sb(es, f"rwin{kc}", [128, 1792], BF16) for kc in range(8)]
        for kc in range(8):
            k.dma("pool", f"wl{kc % 4}", win[kc][:], w_in.t[kc * 128:(kc + 1) * 128, 2064:3856],
                  reads=[w_in], writes=[win[kc]])
        w2p = k.sb(es, "w2p", [128, 512], BF16)
        a2p = k.sb(es, "a2p", [128, 512], BF16)
        g2b = k.sb(es, "g2b", [128, 512], BF16)
        k.op("pool", lambda e: e.memset(w2p[:], 0.0), writes=[w2p])
        k.op("pool", lambda e: e.memset(a2p[:], 0.0), writes=[a2p])
        k.dma("pool", "wl0", w2p[0:64, :], w2.t[:, :], reads=[w2], writes=[w2p])
        k.dma("pool", "wl1", a2p[64:128, :], a2.t[:, :], reads=[a2], writes=[a2p])
        k.dma("pool", "wl2", g2b[:], g2.t[:, :], reads=[g2], writes=[g2b])
        gpre = k.sb(es, "gpre", [128, 8], F32)
        k.dma("sp", "const", gpre[:], g_pre.t.rearrange("(kc p) -> p kc", p=128),
              reads=[g_pre], writes=[gpre], allow_slow_non_contiguous=True)
        vb_ = {}
        for n, buf in vecs.items():
            w = 1792 if n == "mu" else 512
            vb_[n] = k.sb(es, "v_" + n, [128, w], F32)
            k.dma("sp", "const", vb_[n][:], buf.t.partition_broadcast(128), reads=[buf], writes=[vb_[n]])
        env = dplr_setup(k, es, c_dplr, ident)

        xb = [k.sb(es, f"x{i}", [128, D], F32) for i in range(2)]
        xs = k.sb(es, "xs", [128, D], BF16)
        junk = k.sb(es, "junk", [128, D], BF16)
        uT = k.sb(es, "uT", [128, 8, 128], BF16)
        ss = k.sb(es, "ss", [128, 1], F32)
        rstd = k.sb(es, "rstd", [128, 1], F32)
        prp = k.sb(es, "prp", [128, 1792], F32)
        rsh = k.sb(es, "rsh", [128, 1792], F32)
        lastrow = k.sb(es, "lastrow", [1, 1792], F32)
        lin = k.sb(es, "lin", [128, 256], BF16)
        linT = k.sb(es, "linT", [128, 2, 128], BF16)
        ldr = k.sb(es, "ldr", [128, 512], F32)
        aic = k.sb(es, "aic", [128, 512], F32)
        gateb = [k.sb(es, f"gate{i}", [128, 512], F32) for i in range(3)]
        tk = k.sb(es, "tk", [128, 512], F32)
        Kr = k.sb(es, "Kr", [128, 512], F32)
        Ar = k.sb(es, "Ar", [128, 512], F32)
        Br = k.sb(es, "Br", [128, 512], F32)
        Rr = k.sb(es, "Rr", [128, 512], F32)
        Vbb = [k.sb(es, f"Vb{i}", [128, 512], BF16) for i in range(3)]
        tmp = k.sb(es, "tmp", [128, 512], F32)
        tmp2 = k.sb(es, "tmp2", [128, 512], F32)
        bonusb = [k.sb(es, f"bonus{i}", [128, 512], F32) for i in range(3)]
        y = k.sb(es, "y", [128, 512], F32)
        sm = {n: k.sb(es, "r_" + n, [128, 8], F32) for n in ("ssq", "rn", "bs", "s1", "s2", "mean", "msq", "var",
                                                             "rstdh", "nmr")}
        k.op("pool", lambda e: e.memset(lastrow[:], 0.0), writes=[lastrow])
        print("rwkv sbuf bytes remaining", nc.sbuf_bytes_remaining)
        v3 = lambda ap: ap.rearrange("p (h d) -> p h d", h=8)

        def load_x(t):
            k.dma("sp", f"x{t % 2}", xb[t % 2][:], hin[t].t, reads=[hin[t]], writes=[xb[t % 2]])

        def prep(t):
            x, gate, Vb, bonus = xb[t % 2], gateb[t % 3], Vbb[t % 3], bonusb[t % 3]
            if t == 0:
                load_x(0)
            if t + 1 < NT:
                load_x(t + 1)
            norm_transpose(k, x, xs, junk, ss, rstd, env["pT"][0], uT, gpre, ident)
            yield
            for g, (c0, c1) in enumerate(((0, 512), (512, 1024), (1024, 1536), (1536, 1792))):
                pu = bank(env)
                n = c1 - c0
                for kc in range(8):
                    k.mm(pu[:, 0:n], uT[:, kc, :], win[kc][:, c0:c1], kc == 0, kc == 7, reads=[uT, win[kc]], writes=[pu])
                if g % 2 == 0:
                    k.op("act", lambda e: e.copy(prp[:, c0:c1], pu[:, 0:n]), reads=[pu], writes=[prp])
                else:
                    k.op("dve", lambda e: e.tensor_copy(prp[:, c0:c1], pu[:, 0:n]), reads=[pu], writes=[prp])
                yield
            k.dma("sp", "rsa", rsh[1:128, :], prp[0:127, :], reads=[prp], writes=[rsh])
            k.dma("sp", "rsb", rsh[0:1, :], lastrow[0:1, :], reads=[lastrow], writes=[rsh])
            k.dma("sp", "rsc", lastrow[0:1, :], prp[127:128, :], reads=[prp, rsh], writes=[lastrow])
            yield
            k.op("pool", lambda e: e.tensor_tensor(rsh[:], rsh[:], prp[:], ALU.subtract), reads=[rsh, prp, lastrow],
                 writes=[rsh])
            k.op("pool", lambda e: e.tensor_tensor(rsh[:], rsh[:], vb_["mu"][:], ALU.mult), reads=[rsh, vb_["mu"]],
                 writes=[rsh])
            yield
            k.op("dve", lambda e: e.tensor_tensor(prp[:], prp[:], rsh[:], ALU.add), reads=[prp, rsh, lastrow],
                 writes=[prp])
            r_ap, kr_ap, vr_ap = prp[:, 0:512], prp[:, 512:1024], prp[:, 1024:1536]
            k.op("act", lambda e: e.activation(lin[:, 0:64], prp[:, 1536:1600], AF.Tanh), reads=[prp], writes=[lin])
            k.op("act", lambda e: e.copy(lin[:, 64:128], prp[:, 1600:1664]), reads=[prp], writes=[lin])
            k.op("act", lambda e: e.activation(lin[:, 128:256], prp[:, 1664:1792], AF.Sigmoid), reads=[prp], writes=[lin])
            k.op("act", lambda e: e.copy(Vb[:], vr_ap), reads=[prp], writes=[Vb])
            k.op("act", lambda e: e.copy(Rr[:], r_ap), reads=[prp], writes=[Rr])
            yield
            pT0 = env["pT"][0]
            for i in range(2):
                k.tr(pT0[:, i, :], lin[:, i * 128:(i + 1) * 128], ident[:], reads=[lin, ident], writes=[pT0])
            k.op("act", lambda e: e.copy(linT[:], pT0[:, 0:2, :]), reads=[pT0], writes=[linT])
            yield
            pz = bank(env)
            k.mm(pz[:], linT[:, 0, :], w2p[:], True, True, reads=[linT, w2p], writes=[pz])
            k.op("dve", lambda e: e.tensor_tensor(ldr[:], pz[:], vb_["w0"][:], ALU.add), reads=[pz, vb_["w0"]], writes=[ldr])
            pa = bank(env)
            k.mm(pa[:], linT[:, 0, :], a2p[:], True, True, reads=[linT, a2p], writes=[pa])
            k.op("dve", lambda e: e.tensor_tensor(aic[:], pa[:], vb_["a0"][:], ALU.add), reads=[pa, vb_["a0"]], writes=[aic])
            yield
            k.op("act", lambda e: e.activation(ldr[:], ldr[:], AF.Sigmoid), reads=[ldr], writes=[ldr])
            k.op("act", lambda e: e.activation(aic[:], aic[:], AF.Sigmoid), reads=[aic], writes=[aic])
            pg = bank(env)
            k.mm(pg[:], linT[:, 1, :], g2b[:], True, True, reads=[linT, g2b], writes=[pg])
            k.op("act", lambda e: e.copy(gate[:], pg[:]), reads=[pg], writes=[gate])
            k.op("act", lambda e: e.mul(ldr[:], ldr[:], -0.6065306597126334), reads=[ldr], writes=[ldr])
            yield
            k.op("dve", lambda e: e.tensor_tensor(tk[:], kr_ap, vb_["k_k"][:], ALU.mult), reads=[prp, vb_["k_k"]], writes=[tk])
            k.op("act", lambda e: e.activation(tmp[:], tk[:], AF.Square), reads=[tk], writes=[tmp])
            k.op("dve", lambda e: e.tensor_reduce(sm["ssq"][:], v3(tmp[:]), AX.X, ALU.add), reads=[tmp], writes=[sm["ssq"]])
            yield
            inv_sqrt(k, sm["rn"], sm["ssq"], 1.0, 1e-6)
            yield
            k.op("dve", lambda e: e.tensor_tensor(v3(tk[:]), v3(tk[:]), bc(sm["rn"][:], 8, 64), ALU.mult),
                 reads=[tk, sm["rn"]], writes=[tk])
            k.op("dve", lambda e: e.scalar_tensor_tensor(Kr[:], aic[:], -1.0, vb_["k_a"][:], ALU.add, ALU.mult),
                 reads=[aic, vb_["k_a"]], writes=[Kr])
            k.op("dve", lambda e: e.scalar_tensor_tensor(Kr[:], Kr[:], 1.0, kr_ap, ALU.add, ALU.mult),
                 reads=[Kr, prp], writes=[Kr])
            k.op("act", lambda e: e.mul(Ar[:], tk[:], -1.0), reads=[tk], writes=[Ar])
            k.op("pool", lambda e: e.tensor_tensor(Br[:], tk[:], aic[:], ALU.mult), reads=[tk, aic], writes=[Br])
            yield
            k.op("pool", lambda e: e.tensor_tensor(tmp[:], r_ap, Kr[:], ALU.mult), reads=[prp, Kr], writes=[tmp])
            k.op("pool", lambda e: e.tensor_tensor(tmp[:], tmp[:], vb_["r_k"][:], ALU.mult), reads=[tmp, vb_["r_k"]],
                 writes=[tmp])
            k.op("dve", lambda e: e.tensor_reduce(sm["bs"][:], v3(tmp[:]), AX.X, ALU.add), reads=[tmp], writes=[sm["bs"]])
            k.op("dve", lambda e: e.tensor_tensor(v3(bonus[:]), v3(vr_ap), bc(sm["bs"][:], 8, 64), ALU.mult),
                 reads=[prp, sm["bs"]], writes=[bonus])
            yield

        def prep2(t):
            yield from dplr_prep(k, env, t % 2, (Rr, Rr[:]), (Kr, Kr[:]), (Ar, Ar[:]), (Br, Br[:]), Vbb[t % 3],
                                 (ldr, ldr[:]))

        def fin(t):
            gate, Vb, bonus = gateb[t % 3], Vbb[t % 3], bonusb[t % 3]
            yield from dplr_fin(k, env, t % 2, Vb)
            o = env["o"]
            k.op("dve", lambda e: e.tensor_reduce(sm["s1"][:], v3(o[:]), AX.X, ALU.add), reads=[o], writes=[sm["s1"]])
            k.op("act", lambda e: e.activation(tmp2[:], o[:], AF.Square), reads=[o], writes=[tmp2])
            k.op("dve", lambda e: e.tensor_reduce(sm["s2"][:], v3(tmp2[:]), AX.X, ALU.add), reads=[tmp2], writes=[sm["s2"]])
            yield
            head_stats(k, sm, 64, 64e-5)
            yield
            k.op("dve", lambda e: e.tensor_tensor(v3(y[:]), v3(o[:]), bc(sm["rstdh"][:], 8, 64), ALU.mult),
                 reads=[o, sm["rstdh"]], writes=[y])
            k.op("pool", lambda e: e.tensor_tensor(v3(y[:]), v3(y[:]), bc(sm["nmr"][:], 8, 64), ALU.add),
                 reads=[y, sm["nmr"]], writes=[y])
            yield
            k.op("dve", lambda e: e.tensor_tensor(y[:], y[:], vb_["ln_w"][:], ALU.mult), reads=[y, vb_["ln_w"]], writes=[y])
            k.op("pool", lambda e: e.tensor_tensor(y[:], y[:], vb_["ln_b"][:], ALU.add), reads=[y, vb_["ln_b"]], writes=[y])
            yield
            k.op("dve", lambda e: e.tensor_tensor(y[:], y[:], bonus[:], ALU.add), reads=[y, bonus], writes=[y])
            k.op("pool", lambda e: e.tensor_tensor(y[:], y[:], gate[:], ALU.mult), reads=[y, gate], writes=[y])
            k.dma("pool", "y0", ob_out[t].t, y[:], reads=[y], writes=[ob_out[t]])
            yield

        run_pipeline(NT, [prep, prep2, fin], [1, 6, 1])
        k.barrier()


def build_program(T, phases=("mlp",), dbg=None):
    nc = bass.Bass("TRN2", target_bir_lowering=False)
    k = KB(nc)

    dins = {}

    def din(name, shape):
        if name not in dins:
            dins[name] = Buf(nc.dram_tensor(name, list(shape), F32, kind="ExternalInput").ap(), name)
        return dins[name]

    NT = T // 128
    x = din("x", [T, D])
    c_ident = din("c_ident", [128, 128])
    out = Buf(nc.dram_tensor("out", [T, D], F32, kind="ExternalOutput").ap(), "out")

    def tiles(buf):
        return [Buf(buf.t[t * 128:(t + 1) * 128, :], f"{buf.name}{t}") for t in range(NT)]

    cur = tiles(x)
    with k.es:
        es = ExitStack()
        with es:
            ident = load_consts(k, es, c_ident)
            for pi, ph in enumerate(phases):
                last = pi == len(phases) - 1
                if ph in ("gdn", "rwkv", "ret"):
                    nxt = None
                elif last:
                    nxt = tiles(out)
                else:
                    scr = Buf(nc.dram_tensor(f"scr{pi}", [T, D], F32, kind="Internal").ap(), f"scr{pi}")
                    nxt = tiles(scr)
                if ph == "gdn":
                    oa_t = tiles(Buf(nc.dram_tensor("oa_scr", [T, 512], F32, kind="Internal").ap(), "oa_scr"))
                    gdn_phase(k, nc, T, cur, oa_t, din("ab_w_in", [D, 3856]), din("gdn_conv_w", [4, 1536]),
                              din("gdn_a_log", [8]), din("gdn_dt_bias", [8]), din("gdn_norm_w", [64]),
                              din("norm_mix_pre0", [D]), ident, din("c_dplr", [128, 772]))
                    continue
                if ph == "rwkv":
                    ob_t = tiles(Buf(nc.dram_tensor("ob_scr", [T, 512], F32, kind="Internal").ap(), "ob_scr"))
                    vecs = {"mu": din("rwkv_mu", [1792])}
                    for n in ("w0", "a0", "k_k", "k_a", "r_k", "ln_w", "ln_b"):
                        vecs[n] = din("rwkv_" + n, [512])
                    rwkv_phase(k, nc, T, cur, ob_t, din("ab_w_in", [D, 3856]), vecs,
                               din("rwkv_w2", [64, 512]), din("rwkv_a2", [64, 512]), din("rwkv_g2", [128, 512]),
                               din("norm_mix_pre0", [D]), ident, din("c_dplr", [128, 772]))
                    continue
                if ph == "abo":
                    outproj_phase(k, nc, T, cur, nxt, [(oa_t, 0, 512, False), (ob_t, 512, 512, False)], 1024,
                                  din("ab_w_out", [D, D]), din("norm_mix_post0", [D]), ident)
                    cur = nxt
                    continue
                if ph == "ret":
                    y_t = [Buf(a_, f"ys{t_}") for t_, a_ in enumerate(
                        (lambda yb: [yb[t_ * 128:(t_ + 1) * 128, :] for t_ in range(NT)])(
                            nc.dram_tensor("y_scr", [T, 2048], BF16, kind="Internal").ap()))]
                    ret_phase(k, nc, T, cur, y_t, din("ret_w_in", [D, 6144]), din("norm_mix_pre1", [D]), ident,
                              din("c_rot", [T, 128]), din("c_dqk", [128, 24]), din("c_maskT", [128, 128]))
                    continue
                if ph == "reto":
                    outproj_phase(k, nc, T, cur, nxt, [(y_t, 0, 2048, True)], 2048, din("ret_w_out", [2048, D]),
                                  din("norm_mix_post1", [D]), ident, gscale=din("ret_gn_w", [2048]))
                    cur = nxt
                    continue
                if ph.startswith("mlp"):
                    l = ph[3:]
                    mlp_phase(k, nc, T, cur, nxt, din("mlp_w_up" + l, [D, DFF]), din("mlp_w_down" + l, [DFF, D]),
                              din("norm_mlp_pre" + l, [D]), din("norm_mlp_post" + l, [D]), ident)
                cur = nxt
            k.barrier()
    return nc


PHASES = ("gdn", "rwkv", "abo", "mlp0", "ret", "reto", "mlp1")
SEQ = 4096
NCORES = 8


def host_consts(T):
    c = {"c_ident": np.eye(128, dtype=np.float32)}
    c.update(ret_host_consts(T))
    c.update(dplr_host_consts())
    return c


def kernel(**inputs):
    f32 = lambda a: np.ascontiguousarray(np.asarray(a, dtype=np.float32))
    x = f32(inputs["x"])
    B, T, _ = x.shape
    shared = dict(host_consts(T))
    for l in range(2):
        shared[f"mlp_w_up{l}"] = f32(inputs["mlp_w_up"][l])
        shared[f"mlp_w_down{l}"] = f32(inputs["mlp_w_down"][l])
        shared[f"norm_mlp_pre{l}"] = f32(inputs["norm_mlp_pre"][l])
        shared[f"norm_mlp_post{l}"] = f32(inputs["norm_mlp_post"][l])
        shared[f"norm_mix_pre{l}"] = f32(inputs["norm_mix_pre"][l])
        shared[f"norm_mix_post{l}"] = f32(inputs["norm_mix_post"][l])
    for n in ("ab_w_in", "gdn_conv_w", "gdn_a_log", "gdn_dt_bias", "gdn_norm_w", "rwkv_mu", "rwkv_w0", "rwkv_w2",
              "rwkv_a0", "rwkv_a2", "rwkv_g2", "rwkv_k_k", "rwkv_k_a", "rwkv_ln_w", "rwkv_ln_b", "ab_w_out",
              "ret_w_in", "ret_gn_w", "ret_w_out"):
        shared[n] = f32(inputs[n][0])
    shared["rwkv_r_k"] = f32(inputs["rwkv_r_k"][0]).reshape(512)
    nc = build_program(T, PHASES)
    in_maps = [dict(shared, x=x[b]) for b in range(B)]
    res = run_bass_kernel_spmd(nc, in_maps, core_ids=list(range(B)))
    return np.stack([np.asarray(r["out"], dtype=np.float32) for r in res.results], axis=0)
```

```python
import os
import numpy as np
from contextlib import ExitStack
import concourse.bass as bass
import concourse.mybir as mybir
from concourse.bass_utils import run_bass_kernel_spmd

F32 = mybir.dt.float32
BF16 = mybir.dt.bfloat16
AF = mybir.ActivationFunctionType
ALU = mybir.AluOpType
AX = mybir.AxisListType

SAME_ENG_SYNC = True


class Buf:
    __slots__ = ("t", "w", "r", "name")

    def __init__(self, t, name=""):
        self.t = t
        self.w = None
        self.r = {}
        self.name = name

    def __getitem__(self, key):
        return self.t[key]


class KB:
    def __init__(self, nc):
        self.nc = nc
        self.es = ExitStack()
        self.engs = {"pe": nc.tensor, "act": nc.scalar, "dve": nc.vector,
                     "pool": nc.gpsimd, "sp": nc.sync}
        self.nuniq = 0
        self.epoch = -1
        self.sem = {}
        self.cnt = {}
        self._new_epoch()

    def _new_epoch(self):
        self.epoch += 1
        self.known = {e: {} for e in self.engs}
        for e in self.engs:
            self.sem[e] = self.es.enter_context(self.nc.semaphore(f"s_{e}_{self.epoch}"))
            self.cnt[e] = 0

    def sb(self, es, name, shape, dtype):
        self.nuniq += 1
        t = es.enter_context(self.nc.sbuf_tensor(f"{name}_{self.nuniq}", list(shape), dtype))
        return Buf(t, name)

    def ps(self, es, name, shape, dtype=F32):
        self.nuniq += 1
        t = es.enter_context(self.nc.psum_tensor(f"{name}_{self.nuniq}", list(shape), dtype))
        return Buf(t, name)

    def stream(self, name):
        key = "d_" + name
        if key not in self.sem:
            self.sem[key] = self.es.enter_context(self.nc.semaphore(key))
            self.cnt[key] = 0
        return key

    def _waits(self, eng, reads, writes, extra=()):
        need = {}

        def add(ev):
            if ev is not None and ev[2] == self.epoch:
                if need.get(ev[0], 0) < ev[1]:
                    need[ev[0]] = ev[1]

        for b in reads:
            add(b.w)
        for b in writes:
            add(b.w)
            for sk, (v, ep) in b.r.items():
                add((sk, v, ep))
        for ev in extra:
            add(ev)
        e = self.engs[eng]
        kn = self.known[eng]
        for sk, v in need.items():
            if sk == eng and (eng == "pe" or eng == "sp" or not SAME_ENG_SYNC):
                continue
            if kn.get(sk, 0) >= v:
                continue
            e.wait_ge(self.sem[sk], v)
            kn[sk] = v

    def _record(self, ev, reads, writes):
        for b in reads:
            old = b.r.get(ev[0])
            if old is None or old[1] != ev[2] or old[0] < ev[1]:
                b.r[ev[0]] = (ev[1], ev[2])
        for b in writes:
            b.w = ev
            b.r = {}

    def op(self, eng, fn, reads=(), writes=()):
        self._waits(eng, reads, writes)
        ins = fn(self.engs[eng])
        self.cnt[eng] += 1
        ins.then_inc(self.sem[eng], 1)
        self._record((eng, self.cnt[eng], self.epoch), reads, writes)

    def dma(self, q, stream, out, in_, reads=(), writes=(), **kw):
        sk = self.stream(stream)
        prev = (sk, self.cnt[sk], self.epoch) if self.cnt[sk] else None
        self._waits(q, reads, writes, extra=(prev,))
        ins = self.engs[q].dma_start(out=out, in_=in_, **kw)
        self.cnt[sk] += 16
        ins.then_inc(self.sem[sk], 16)
        self._record((sk, self.cnt[sk], self.epoch), reads, writes)

    def dbg(self, name, buf, ap, shape):
        if os.environ.get("DBG", "0") != "1":
            return
        d = Buf(self.nc.dram_tensor("dbg_" + name, list(shape), F32, kind="ExternalOutput").ap(), name)
        self.dma("sp", "dbg", d.t, ap, reads=[buf], writes=[d])

    def barrier(self):
        for eng, e in self.engs.items():
            kn = self.known[eng]
            for sk, v in self.cnt.items():
                if v == 0 or kn.get(sk, 0) >= v:
                    continue
                if sk == eng and eng in ("pe", "sp"):
                    pass
                e.wait_ge(self.sem[sk], v)
                kn[sk] = v
        self._new_epoch()

    def mm(self, out, lhsT, rhs, start, stop, reads, writes, tp=None):
        kw = {}
        if tp is not None and (tp[0] == 96 or tp[1] == 96):
            kw["tile_position"] = tp
        self.op("pe", lambda e: e.matmul(out, lhsT, rhs, start=start, stop=stop, **kw), reads, writes)

    def tr(self, out, in_, ident, reads, writes):
        self.op("pe", lambda e: e.transpose(out, in_, ident), reads, writes)


D = 1024
DFF = 4096
RMS_EPS = 1e-6


def rms_rstd(k, eng_sq, x_ap, xbuf, junk, ss, rstd, n, eps=RMS_EPS):
    k.op("act", lambda e: e.activation(junk[:], x_ap, AF.Square, accum_out=ss[:]),
         reads=[xbuf], writes=[junk, ss])
    k.op("dve", lambda e: e.tensor_scalar(rstd[:], ss[:], 1.0 / n, eps, ALU.mult, ALU.add),
         reads=[ss], writes=[rstd])
    k.op("act", lambda e: e.activation(rstd[:], rstd[:], AF.Sqrt), reads=[rstd], writes=[rstd])
    k.op("dve", lambda e: e.reciprocal(rstd[:], rstd[:]), reads=[rstd], writes=[rstd])


def load_consts(k, es, c_ident):
    ident = k.sb(es, "ident", [128, 128], BF16)
    k.dma("pool", "const", ident[:], c_ident.t[:, :], reads=[c_ident], writes=[ident])
    return ident


def mlp_phase(k, nc, T, hin, hout, w_up, w_dn, g_pre, g_post, ident):
    NT = T // 128
    ST = 2
    NS = NT // ST
    es = ExitStack()
    with es:
        wup = [k.sb(es, f"wup{kc}", [128, DFF], BF16) for kc in range(8)]
        wdn = [k.sb(es, f"wdn{g}", [128, 4, D], BF16) for g in range(8)]
        for kc in range(8):
            k.dma("pool", f"wl{kc % 4}", wup[kc][:], w_up.t[kc * 128:(kc + 1) * 128, :],
                  reads=[w_up], writes=[wup[kc]])
        for g in range(8):
            k.dma("pool", f"wl{g % 4}", wdn[g][:],
                  w_dn.t[g * 512:(g + 1) * 512, :].rearrange("(fc p) d -> p fc d", p=128),
                  reads=[w_dn], writes=[wdn[g]])
        gpre = k.sb(es, "gpre", [128, 8], F32)
        k.dma("sp", "const", gpre[:], g_pre.t.rearrange("(kc p) -> p kc", p=128),
              reads=[g_pre], writes=[gpre], allow_slow_non_contiguous=True)
        gpost = k.sb(es, "gpost", [128, D], F32)
        k.dma("sp", "const", gpost[:], g_post.t.partition_broadcast(128),
              reads=[g_post], writes=[gpost])

        NB = 2
        xt = [[k.sb(es, f"xt{b}_{j}", [128, D], F32) for j in range(ST)] for b in range(NB)]
        uT = [k.sb(es, f"uT{b}", [128, 8, 128 * ST], BF16) for b in range(NB)]
        aT = [k.sb(es, f"aT{b}", [128, 32, 128 * ST], BF16) for b in range(1)]
        xs = [k.sb(es, f"xs{b}", [128, D], BF16) for b in range(2)]
        junk = k.sb(es, "junk", [128, D], BF16)
        rtmp = [k.sb(es, f"rtmp{b}", [128, 128 * ST], F32) for b in range(4)]
        ss = [k.sb(es, f"ss{b}", [128, 1], F32) for b in range(2)]
        rstd = [k.sb(es, f"rstd{b}", [128, 1], F32) for b in range(2)]
        ss2 = [k.sb(es, f"ss2{b}", [128, 1], F32) for b in range(2)]
        rstd2 = [k.sb(es, f"rstd2{b}", [128, 1], F32) for b in range(2)]
        fb = [k.sb(es, f"fb{b}", [128, D], F32) for b in range(2)]
        yb = [k.sb(es, f"yb{b}", [128, D], F32) for b in range(2)]
        pT = [k.ps(es, f"pT{b}", [128, 8, 128], BF16) for b in range(2)]
        pU = [k.ps(es, f"pU{b}", [128, 512], F32) for b in range(3)]
        pD = [k.ps(es, f"pD{b}", [128, 512], F32) for b in range(2)]

        state = {"ti": 0, "iu": 0, "idn": 0}
        NTOK = 128 * ST

        def prep(s):
            b = s % NB
            for j in range(ST):
                t = s * ST + j
                x = xt[b][j]
                ti = state["ti"]
                k.dma("sp", f"x{ti % 2}", x[:], hin[t].t, reads=[hin[t]], writes=[x])
                q = ti % 2
                rms_rstd(k, "act", x[:], x, junk, ss[q], rstd[q], D)
                k.op("dve", lambda e: e.tensor_scalar(xs[q][:], x[:], rstd[q][:, 0:1], None, ALU.mult),
                     reads=[x, rstd[q]], writes=[xs[q]])
                for kc in range(8):
                    k.tr(pT[q][:, kc, :], xs[q][:, kc * 128:(kc + 1) * 128], ident[:],
                         reads=[xs[q], ident], writes=[pT[q]])
                k.op("dve", lambda e: e.tensor_tensor(
                    uT[b][:, :, j * 128:(j + 1) * 128], pT[q][:],
                    gpre[:].unsqueeze(2).to_broadcast([128, 8, 128]), ALU.mult),
                    reads=[pT[q], gpre], writes=[uT[b]])
                state["ti"] += 1

        def up(s):
            b = s % NB
            for fc in range(32):
                pu = pU[state["iu"] % 3]
                state["iu"] += 1
                for kc in range(8):
                    k.mm(pu[:, 0:NTOK], wup[kc][:, fc * 128:(fc + 1) * 128], uT[b][:, kc, :],
                         kc == 0, kc == 7, reads=[wup[kc], uT[b]], writes=[pu])
                tm = rtmp[fc % 4]
                if fc % 2 == 0:
                    k.op("act", lambda e: e.activation(tm[:], pu[:, 0:NTOK], AF.Relu),
                         reads=[pu], writes=[tm])
                else:
                    k.op("dve", lambda e: e.tensor_scalar(tm[:], pu[:, 0:NTOK], 0.0, None, ALU.max),
                         reads=[pu], writes=[tm])
                k.op("pool", lambda e: e.tensor_tensor(aT[0][:, fc, :], tm[:], tm[:], ALU.mult),
                     reads=[tm], writes=[aT[0]])

        def down(s):
            b = s % NB
            for j in range(ST):
                t = s * ST + j
                x = xt[b][j]
                q = state["idn"] % 2
                f = fb[q]
                for nh in range(2):
                    pd = pD[nh]
                    for fc in range(32):
                        k.mm(pd[:], aT[0][:, fc, j * 128:(j + 1) * 128],
                             wdn[fc // 4][:, fc % 4, nh * 512:(nh + 1) * 512],
                             fc == 0, fc == 31, reads=[aT[0], wdn[fc // 4]], writes=[pd])
                    k.op("dve" if nh == 0 else "act",
                         (lambda e: e.tensor_copy(f[:, nh * 512:(nh + 1) * 512], pd[:])) if nh == 0 else
                         (lambda e: e.copy(f[:, nh * 512:(nh + 1) * 512], pd[:])),
                         reads=[pd], writes=[f])
                rms_rstd(k, "act", f[:], f, junk, ss2[q], rstd2[q], D)
                y = yb[q]
                k.op("dve", lambda e: e.scalar_tensor_tensor(y[:], f[:], rstd2[q][:, 0:1], gpost[:],
                                                             ALU.mult, ALU.mult),
                     reads=[f, rstd2[q], gpost], writes=[y])
                k.op("pool", lambda e: e.tensor_tensor(y[:], y[:], x[:], ALU.add),
                     reads=[y, x], writes=[y])
                k.dma("pool", f"y{q}", hout[t].t, y[:], reads=[y], writes=[hout[t]])
                state["idn"] += 1

        prep(0)
        for s in range(NS):
            up(s)
            if s + 1 < NS:
                prep(s + 1)
            down(s)
        k.barrier()


NH_RET = 8
RET_C = 128


def ret_gammas():
    return [float(1.0 - 2.0 ** (-5.0 - h)) for h in range(NH_RET)]


def ret_host_consts(T):
    pos = np.arange(T, dtype=np.float64)
    angle = 1.0 / (10000.0 ** np.linspace(0.0, 1.0, 64))
    th = pos[:, None] * angle[None, :]
    c_rot = np.concatenate([np.cos(th), np.sin(th)], axis=1).astype(np.float32)
    g = np.array(ret_gammas(), dtype=np.float64)
    idx = np.arange(128, dtype=np.float64)
    dq = g[None, :] ** idx[:, None]
    dk = g[None, :] ** (-idx[:, None]) * (128.0 ** -0.5)
    dkc = dk * (g[None, :] ** 128.0)
    c_dqk = np.concatenate([dq, dk, dkc], axis=1).astype(np.float32)
    c_maskT = (idx[:, None] <= idx[None, :]).astype(np.float32)
    return {"c_rot": c_rot, "c_dqk": c_dqk, "c_maskT": c_maskT}


def ret_phase(k, nc, T, hin, y_out, w_in, g_pre, ident, c_rot, c_dqk, c_maskT):
    NT = T // 128
    gam = ret_gammas()
    es = ExitStack()
    with es:
        win = [k.sb(es, f"rwin{kc}", [128, 6144], BF16) for kc in range(8)]
        for kc in range(8):
            k.dma("pool", f"wl{kc % 4}", win[kc][:], w_in.t[kc * 128:(kc + 1) * 128, :],
                  reads=[w_in], writes=[win[kc]])
        gpre = k.sb(es, "gpre", [128, 8], F32)
        k.dma("sp", "const", gpre[:], g_pre.t.rearrange("(kc p) -> p kc", p=128),
              reads=[g_pre], writes=[gpre], allow_slow_non_contiguous=True)
        dqk = k.sb(es, "dqk", [128, 24], F32)
        k.dma("sp", "const", dqk[:], c_dqk.t[:, :], reads=[c_dqk], writes=[dqk])
        maskT = k.sb(es, "maskT", [128, 128], F32)
        k.dma("sp", "const", maskT[:], c_maskT.t[:, :], reads=[c_maskT], writes=[maskT])

        xb = [k.sb(es, f"x{i}", [128, D], F32) for i in range(2)]
        xs = k.sb(es, "xs", [128, D], BF16)
        uT = k.sb(es, "uT", [128, 8, 128], BF16)
        cs = [k.sb(es, f"cs{b}", [128, 128], F32) for b in range(2)]
        vbb = [k.sb(es, f"vb{i}", [128, 2048], BF16) for i in range(2)]
        gsb = [k.sb(es, f"gs{i}", [128, 2048], BF16) for i in range(2)]
        tq = [[k.sb(es, f"tq{b}_{i}", [128, 4, 64], F32) for i in range(4)] for b in range(2)]
        qk_b = k.sb(es, "qk_b", [128, 16, 64, 2], BF16)
        kdb = [k.sb(es, f"kd_b{i}", [128, 8, 64, 2], BF16) for i in range(2)]
        qTb = [k.sb(es, f"qT{i}", [128, 8, 128], BF16) for i in range(2)]
        kT = k.sb(es, "kT", [128, 8, 128], BF16)
        scb = [k.sb(es, f"sc_b{i}", [128, 8, 128], BF16) for i in range(2)]
        Sg = k.sb(es, "Sg", [128, 8, 256], F32)
        Sb = k.sb(es, "Sb", [128, 8, 256], BF16)
        o_f = k.sb(es, "o_f", [128, 8, 256], F32)
        ybb = [k.sb(es, f"y_b{i}", [128, 2048], BF16) for i in range(2)]
        junk = k.sb(es, "junk", [128, D], BF16)
        st = {n: k.sb(es, n, [128, 8], F32) for n in ("s1", "s2", "mean", "msq", "var", "rstdh", "nmr")}
        ss = k.sb(es, "ss", [128, 1], F32)
        rstd = k.sb(es, "rstd", [128, 1], F32)
        pT = [k.ps(es, f"pT{b}", [128, 8, 128], BF16) for b in range(2)]
        pb = [k.ps(es, f"pb{b}", [128, 512], F32) for b in range(6)]
        ib = [0]
        print("ret sbuf bytes remaining", nc.sbuf_bytes_remaining)

        def bank():
            b = pb[ib[0] % 6]
            ib[0] += 1
            return b

        k.op("pool", lambda e: e.memset(Sg[:], 0.0), writes=[Sg])
        k.op("pool", lambda e: e.memset(Sb[:], 0.0), writes=[Sb])

        def load_x(t):
            k.dma("sp", f"x{t % 2}", xb[t % 2][:], hin[t].t, reads=[hin[t]], writes=[xb[t % 2]])
            k.dma("sp", f"cs{t % 2}", cs[t % 2][:], c_rot.t[t * 128:(t + 1) * 128, :], reads=[c_rot], writes=[cs[t % 2]])

        def prep(t):
            s = t % 2
            x, c, vb, gs, kd_b, qT, sc_b = xb[s], cs[s], vbb[s], gsb[s], kdb[s], qTb[s], scb[s]
            if t == 0:
                load_x(0)
            if t + 1 < NT:
                load_x(t + 1)
            norm_transpose(k, x, xs, junk, ss, rstd, pT[0], uT, gpre, ident)
            yield
            cosb = c[:, 0:64].unsqueeze(1).to_broadcast([128, 4, 64])
            sinb = c[:, 64:128].unsqueeze(1).to_broadcast([128, 4, 64])
            for g in range(12):
                pu = bank()
                for kc in range(8):
                    k.mm(pu[:], uT[:, kc, :], win[kc][:, g * 512:(g + 1) * 512], kc == 0, kc == 7,
                         reads=[uT, win[kc]], writes=[pu])
                if g < 4:
                    tt = tq[g % 2]
                    pv = pu[:].rearrange("p (h d t) -> p h d t", h=4, t=2)
                    x1 = pv[:, :, :, 0]
                    x2 = pv[:, :, :, 1]
                    k.op("dve", lambda e: e.tensor_tensor(tt[0][:], x1, cosb, ALU.mult), reads=[pu, c], writes=[tt[0]])
                    k.op("dve", lambda e: e.tensor_tensor(tt[1][:], x2, sinb, ALU.mult), reads=[pu, c], writes=[tt[1]])
                    k.op("dve", lambda e: e.tensor_tensor(tt[2][:], x2, cosb, ALU.mult), reads=[pu, c], writes=[tt[2]])
                    k.op("dve", lambda e: e.tensor_tensor(tt[3][:], x1, sinb, ALU.mult), reads=[pu, c], writes=[tt[3]])
                    k.op("pool", lambda e: e.tensor_tensor(tt[0][:], tt[0][:], tt[1][:], ALU.subtract),
                         reads=[tt[0], tt[1]], writes=[tt[0]])
                    k.op("pool", lambda e: e.tensor_tensor(tt[2][:], tt[2][:], tt[3][:], ALU.add),
                         reads=[tt[2], tt[3]], writes=[tt[2]])
                    h0 = 4 * g
                    sc = dqk[:, h0:h0 + 4].unsqueeze(2).to_broadcast([128, 4, 64])
                    k.op("pool", lambda e: e.tensor_tensor(qk_b[:, h0:h0 + 4, :, 0], tt[0][:], sc, ALU.mult),
                         reads=[tt[0], dqk], writes=[qk_b])
                    k.op("pool", lambda e: e.tensor_tensor(qk_b[:, h0:h0 + 4, :, 1], tt[2][:], sc, ALU.mult),
                         reads=[tt[2], dqk], writes=[qk_b])
                    if g >= 2:
                        hk = 4 * (g - 2)
                        sc2 = dqk[:, 16 + hk:16 + hk + 4].unsqueeze(2).to_broadcast([128, 4, 64])
                        k.op("pool", lambda e: e.tensor_tensor(kd_b[:, hk:hk + 4, :, 0], tt[0][:], sc2, ALU.mult),
                             reads=[tt[0], dqk], writes=[kd_b])
                        k.op("pool", lambda e: e.tensor_tensor(kd_b[:, hk:hk + 4, :, 1], tt[2][:], sc2, ALU.mult),
                             reads=[tt[2], dqk], writes=[kd_b])
                elif g < 8:
                    k.op("act", lambda e: e.copy(vb[:, (g - 4) * 512:(g - 3) * 512], pu[:]), reads=[pu], writes=[vb])
                else:
                    k.op("act", lambda e: e.activation(gs[:, (g - 8) * 512:(g - 7) * 512], pu[:], AF.Silu),
                         reads=[pu], writes=[gs])
                yield
            for i in range(16):
                k.tr(pT[i // 8][:, i % 8, :], qk_b[:, i, :, :].rearrange("p d t -> p (d t)"), ident[:],
                     reads=[qk_b, ident], writes=[pT[i // 8]])
            k.op("act", lambda e: e.copy(qT[:], pT[0][:]), reads=[pT[0]], writes=[qT])
            k.op("dve", lambda e: e.tensor_copy(kT[:], pT[1][:]), reads=[pT[1]], writes=[kT])
            yield
            for hb in range(2):
                psc = bank()
                for hh in range(4):
                    h = hb * 4 + hh
                    k.mm(psc[:, hh * 128:(hh + 1) * 128], kT[:, h, :], qT[:, h, :], True, True,
                         reads=[kT, qT], writes=[psc])
                k.op("dve", lambda e: e.tensor_tensor(
                    sc_b[:, hb * 4:hb * 4 + 4, :], psc[:].rearrange("p (h c) -> p h c", h=4),
                    maskT[:].unsqueeze(1).to_broadcast([128, 4, 128]), ALU.mult),
                    reads=[psc, maskT], writes=[sc_b])
                yield

        def fin(t):
            s = t % 2
            vb, gs, kd_b, qT, sc_b, y_b = vbb[s], gsb[s], kdb[s], qTb[s], scb[s], ybb[s]
            for hp in range(4):
                po = bank()
                for hh in range(2):
                    h = hp * 2 + hh
                    k.mm(po[:, hh * 256:(hh + 1) * 256], sc_b[:, h, :], vb[:, h * 256:(h + 1) * 256], True, False,
                         reads=[sc_b, vb], writes=[po])
                    k.mm(po[:, hh * 256:(hh + 1) * 256], qT[:, h, :], Sb[:, h, :], False, True,
                         reads=[qT, Sb], writes=[po])
                if hp % 2 == 0:
                    k.op("act", lambda e: e.copy(o_f[:, hp * 2:hp * 2 + 2, :],
                                                 po[:].rearrange("p (h v) -> p h v", h=2)), reads=[po], writes=[o_f])
                else:
                    k.op("dve", lambda e: e.tensor_copy(o_f[:, hp * 2:hp * 2 + 2, :],
                                                        po[:].rearrange("p (h v) -> p h v", h=2)),
                         reads=[po], writes=[o_f])
                yield
            for hp in range(4):
                pd = bank()
                for hh in range(2):
                    h = hp * 2 + hh
                    k.mm(pd[:, hh * 256:(hh + 1) * 256], kd_b[:, h, :, :].rearrange("p d t -> p (d t)"),
                         vb[:, h * 256:(h + 1) * 256], True, True, reads=[kd_b, vb], writes=[pd])
                for hh in range(2):
                    h = hp * 2 + hh
                    cc = gam[h] ** 128
                    k.op("dve", lambda e: e.scalar_tensor_tensor(Sg[:, h, :], Sg[:, h, :], cc,
                                                                 pd[:, hh * 256:(hh + 1) * 256], ALU.mult, ALU.add),
                         reads=[Sg, pd], writes=[Sg])
                yield
            k.op("act", lambda e: e.copy(Sb[:], Sg[:]), reads=[Sg], writes=[Sb])
            k.op("dve", lambda e: e.tensor_reduce(st["s1"][:], o_f[:], AX.X, ALU.add), reads=[o_f], writes=[st["s1"]])
            for h in range(8):
                k.op("act", lambda e: e.activation(junk[:, 0:256], o_f[:, h, :], AF.Square,
                                                   accum_out=st["s2"][:, h:h + 1]),
                     reads=[o_f], writes=[junk, st["s2"]])
            yield
            head_stats(k, st, 256, 1e-6)
            yield
            k.op("dve", lambda e: e.tensor_tensor(o_f[:], o_f[:], st["rstdh"][:].unsqueeze(2).to_broadcast([128, 8, 256]),
                                                  ALU.mult), reads=[o_f, st["rstdh"]], writes=[o_f])
            yield
            k.op("pool", lambda e: e.tensor_tensor(o_f[:], o_f[:], st["nmr"][:].unsqueeze(2).to_broadcast([128, 8, 256]),
                                                   ALU.add), reads=[o_f, st["nmr"]], writes=[o_f])
            yield
            k.op("pool", lambda e: e.tensor_tensor(y_b[:], o_f[:].rearrange("p h v -> p (h v)"), gs[:], ALU.mult),
                 reads=[o_f, gs], writes=[y_b])
            k.dma("pool", f"y{s}", y_out[t].t, y_b[:], reads=[y_b], writes=[y_out[t]])
            yield

        run_pipeline(NT, [prep, fin], [2, 1])
        k.barrier()


def outproj_phase(k, nc, T, hin, hout, srcs, K, w_out, g_post, ident, gscale=None):
    NT = T // 128
    NK = K // 128
    NG = NK // 4
    NPT = NK // 8
    es = ExitStack()
    with es:
        wout = [k.sb(es, f"owout{g}", [128, 4, D], BF16) for g in range(NG)]
        for g in range(NG):
            k.dma("pool", f"wl{g % 4}", wout[g][:],
                  w_out.t[g * 512:(g + 1) * 512, :].rearrange("(fc p) d -> p fc d", p=128),
                  reads=[w_out], writes=[wout[g]])
        gpost = k.sb(es, "gpost", [128, D], F32)
        k.dma("sp", "const", gpost[:], g_post.t.partition_broadcast(128), reads=[g_post], writes=[gpost])
        gsc = None
        if gscale is not None:
            gsc = k.sb(es, "gsc", [128, NK], F32)
            k.dma("sp", "const", gsc[:], gscale.t.rearrange("(kc p) -> p kc", p=128),
                  reads=[gscale], writes=[gsc], allow_slow_non_contiguous=True)
        NB = 3
        xb = [k.sb(es, f"x{i}", [128, D], F32) for i in range(NB)]
        need32 = any(not sr[3] for sr in srcs)
        oin = [k.sb(es, f"oin{i}", [128, K], F32) for i in range(NB)] if need32 else None
        oab = [k.sb(es, f"oab{i}", [128, K], BF16) for i in range(NB)]
        oT = [k.sb(es, f"oT{i}", [128, NK, 128], BF16) for i in range(NB)]
        f = [k.sb(es, f"f{i}", [128, D], F32) for i in range(NB)]
        junk = k.sb(es, "junk", [128, D], BF16)
        ss2 = [k.sb(es, f"ss2{i}", [128, 1], F32) for i in range(NB)]
        rstd2 = [k.sb(es, f"rstd2{i}", [128, 1], F32) for i in range(NB)]
        NPB = 3 if NPT == 1 else 2
        pT = [[k.ps(es, f"pT{b}_{j}", [128, 8, 128], BF16) for j in range(NPT)] for b in range(NPB)]
        pb = [k.ps(es, f"pb{b}", [128, 512], F32) for b in range(8 - NPB * NPT)]
        npb = len(pb)
        def front(t):
            s = t % NB
            k.dma("sp", f"x{s}", xb[s][:], hin[t].t, reads=[hin[t]], writes=[xb[s]])
            for j, (tiles_, c0, w, isb) in enumerate(srcs):
                if isb:
                    k.dma("sp", f"o{j}{s}", oab[s][:, c0:c0 + w], tiles_[t].t, reads=[tiles_[t]], writes=[oab[s]])
                else:
                    k.dma("sp", f"o{j}{s}", oin[s][:, c0:c0 + w], tiles_[t].t, reads=[tiles_[t]], writes=[oin[s]])
            if need32:
                k.op("act", lambda e: e.copy(oab[s][:], oin[s][:]), reads=[oin[s]], writes=[oab[s]])
            for i in range(NK):
                k.tr(pT[t % NPB][i // 8][:, i % 8, :], oab[s][:, i * 128:(i + 1) * 128], ident[:],
                     reads=[oab[s], ident], writes=[pT[t % NPB][i // 8]])
            for j in range(NPT):
                if gsc is None:
                    k.op("dve", lambda e: e.tensor_copy(oT[s][:, j * 8:j * 8 + 8, :], pT[t % NPB][j][:]),
                         reads=[pT[t % NPB][j]], writes=[oT[s]])
                else:
                    k.op("dve", lambda e: e.tensor_tensor(oT[s][:, j * 8:j * 8 + 8, :], pT[t % NPB][j][:],
                                                          gsc[:, j * 8:j * 8 + 8].unsqueeze(2).to_broadcast([128, 8, 128]),
                                                          ALU.mult), reads=[pT[t % NPB][j], gsc], writes=[oT[s]])

        def back(t):
            s = t % NB
            for nh in range(2):
                pm = pb[(2 * t + nh) % npb]
                for kc in range(NK):
                    k.mm(pm[:], oT[s][:, kc, :], wout[kc // 4][:, kc % 4, nh * 512:(nh + 1) * 512], kc == 0, kc == NK - 1,
                         reads=[oT[s], wout[kc // 4]], writes=[pm])
                if nh == 0:
                    k.op("dve", lambda e: e.tensor_copy(f[s][:, 0:512], pm[:]), reads=[pm], writes=[f[s]])
                else:
                    k.op("act", lambda e: e.copy(f[s][:, 512:1024], pm[:]), reads=[pm], writes=[f[s]])
            rms_rstd(k, "act", f[s][:], f[s], junk, ss2[s], rstd2[s], D)
            k.op("dve", lambda e: e.scalar_tensor_tensor(f[s][:], f[s][:], rstd2[s][:, 0:1], gpost[:], ALU.mult, ALU.mult),
                 reads=[f[s], rstd2[s], gpost], writes=[f[s]])
            k.op("pool", lambda e: e.tensor_tensor(f[s][:], f[s][:], xb[s][:], ALU.add), reads=[f[s], xb[s]], writes=[f[s]])
            k.dma("pool", f"y{s}", hout[t].t, f[s][:], reads=[f[s]], writes=[hout[t]])

        front(0)
        for t in range(NT):
            if t + 1 < NT:
                front(t + 1)
            back(t)
        k.barrier()


def head_stats(k, st, n, eps):
    k.op("dve", lambda e: e.tensor_scalar(st["mean"][:], st["s1"][:], 1.0 / n, None, ALU.mult),
         reads=[st["s1"]], writes=[st["mean"]])
    k.op("dve", lambda e: e.tensor_tensor(st["msq"][:], st["mean"][:], st["mean"][:], ALU.mult),
         reads=[st["mean"]], writes=[st["msq"]])
    k.op("dve", lambda e: e.scalar_tensor_tensor(st["var"][:], st["s2"][:], 1.0 / n, st["msq"][:],
                                                 ALU.mult, ALU.subtract),
         reads=[st["s2"], st["msq"]], writes=[st["var"]])
    k.op("dve", lambda e: e.tensor_scalar(st["var"][:], st["var"][:], eps, None, ALU.add),
         reads=[st["var"]], writes=[st["var"]])
    k.op("act", lambda e: e.activation(st["var"][:], st["var"][:], AF.Sqrt), reads=[st["var"]], writes=[st["var"]])
    k.op("dve", lambda e: e.reciprocal(st["rstdh"][:], st["var"][:]), reads=[st["var"]], writes=[st["rstdh"]])
    k.op("dve", lambda e: e.scalar_tensor_tensor(st["nmr"][:], st["mean"][:], -1.0, st["rstdh"][:],
                                                 ALU.mult, ALU.mult),
         reads=[st["mean"], st["rstdh"]], writes=[st["nmr"]])


DC = 32
C_MI, C_MX, C_MA, C_MITS, C_MITI, C_MTI, C_BLK = 0, 128, 256, 384, 512, 640, 768


def dplr_host_consts():
    idx = np.arange(128)
    ch = idx // DC
    same = ch[:, None] == ch[None, :]
    lt = idx[:, None] < idx[None, :]
    le = idx[:, None] <= idx[None, :]
    gt = idx[:, None] > idx[None, :]
    blk = ch[:, None] == np.arange(4)[None, :]
    pack = np.concatenate([same & le, same & lt, same & gt, same & lt, same & le, same & gt, blk], axis=1)
    return {"c_dplr": pack.astype(np.float32)}


def dplr_setup(k, es, c_dplr, ident):
    env = {"ident": ident}
    cd = k.sb(es, "cdplr", [128, 772], F32)
    k.dma("sp", "const", cd[:], c_dplr.t[:, :], reads=[c_dplr], writes=[cd])
    env["cd"] = cd
    W = 512
    for n in ("e1", "e2", "e3", "e4"):
        env[n] = k.sb(es, n, [128, W], F32)
    for n in ("rt", "kt", "bt", "at", "Xb", "U_b"):
        env[n] = k.sb(es, n, [128, W], BF16)
    env["kdec"] = [k.sb(es, f"kdec{s}", [128, W], BF16) for s in range(2)]
    env["bdec"] = [k.sb(es, f"bdec{s}", [128, W], BF16) for s in range(2)]
    env["wC"] = [k.sb(es, f"wC{s}", [128, 4, 4], F32) for s in range(2)]
    env["arT"] = [[k.sb(es, f"arT{s}{i}", [128, 4, 2, 128], BF16) for i in range(2)] for s in range(2)]
    env["bkT"] = [k.sb(es, f"bkT{i}", [128, 8, 128], BF16) for i in range(2)]
    env["NA"] = [k.sb(es, f"NA{s}", [128, 8, 2, 128], BF16) for s in range(2)]
    env["KA"] = [k.sb(es, f"KA{s}", [128, 8, 2, 128], BF16) for s in range(2)]
    env["Lm"] = k.sb(es, "Lm", [128, 8, 128], BF16)
    env["NP"] = [k.sb(es, f"NP{i}", [128, 8, 2, 128], BF16) for i in range(2)]
    env["Lp"] = [k.sb(es, f"Lp{i}", [128, 8, 128], BF16) for i in range(2)]
    env["Pf"] = k.sb(es, "Pf", [128, 8, 128], BF16)
    env["Ubar"] = [k.sb(es, f"Ubar{s}", [128, W], F32) for s in range(2)]
    env["AbT"] = [[k.sb(es, f"AbT{s}{i}", [128, 4, 128], BF16) for i in range(2)] for s in range(2)]
    env["St"] = k.sb(es, "St", [128, 4, 64], F32)
    env["Stb"] = k.sb(es, "Stb", [128, 4, 64], BF16)
    env["o"] = k.sb(es, "o_dplr", [128, W], F32)
    env["pT"] = [k.ps(es, f"pT{b}", [128, 8, 128], BF16) for b in range(2)]
    env["pU"] = k.ps(es, "pU", [128, 512], F32)
    env["pO"] = k.ps(es, "pO", [128, 512], F32)
    env["pS"] = env["pU"]
    env["pb"] = [k.ps(es, f"pb{b}", [128, 512], F32) for b in range(4)]
    env["ib"] = 0
    zero = [env["St"], env["Stb"], env["U_b"]] + env["bkT"]
    for s in range(2):
        zero += env["arT"][s] + env["AbT"][s]
    for b_ in zero:
        k.op("pool", lambda e: e.memset(b_[:], 0.0), writes=[b_])
    return env


def bank(env):
    b = env["pb"][env["ib"] % len(env["pb"])]
    env["ib"] += 1
    return b


def dplr_prep(k, env, s, R, Kp, A, B, Vb, ld):
    cd = env["cd"]
    ident = env["ident"]
    e1, e2, e3, e4 = env["e1"], env["e2"], env["e3"], env["e4"]
    ldb, lda = ld
    for (c0, dsts) in ((C_MI, ((e1, 1.0), (e2, -1.0))), (C_MX, ((e3, 1.0),)), (C_MA, ((e4, 1.0),))):
        pc = bank(env)
        k.mm(pc[:], cd[:, c0:c0 + 128], lda, True, True, reads=[cd, ldb], writes=[pc])
        for dst, sc in dsts:
            k.op("act", lambda e: e.activation(dst[:], pc[:], AF.Exp, scale=sc), reads=[pc], writes=[dst])
        yield
    pw = bank(env)
    for j in range(4):
        k.mm(pw[:, j * 4:(j + 1) * 4], lda[:, j * 128:(j + 1) * 128], cd[:, C_BLK:C_BLK + 4], True, True,
             reads=[ldb, cd], writes=[pw])
    wC = env["wC"][s]
    k.op("act", lambda e: e.activation(wC[:], pw[:, 0:16].rearrange("p (j c) -> p j c", j=4), AF.Exp),
         reads=[pw], writes=[wC])
    rt, kt, bt, at = (env[n] for n in ("rt", "kt", "bt", "at"))
    kdec, bdec = env["kdec"][s], env["bdec"][s]
    specs = ((at, A, e3), (rt, R, e1), (bt, B, e2), (kt, Kp, e2), (kdec, Kp, e4), (bdec, B, e4))
    for n, (dst, (sb_, sa), ee) in enumerate(specs):
        k.op("dve" if n % 2 == 0 else "pool", lambda e: e.tensor_tensor(dst[:], sa, ee[:], ALU.mult),
             reads=[sb_, ee], writes=[dst])
        if n % 2 == 1:
            yield
    pTa, pTb = env["pT"]
    arT, bkT = env["arT"][s], env["bkT"]
    for j in range(4):
        k.tr(pTa[:, j, :], at[:, j * 128:(j + 1) * 128], ident[:], reads=[at, ident], writes=[pTa])
        k.tr(pTa[:, 4 + j, :], rt[:, j * 128:(j + 1) * 128], ident[:], reads=[rt, ident], writes=[pTa])
    yield
    for j in range(4):
        k.tr(pTb[:, j, :], bt[:, j * 128:(j + 1) * 128], ident[:], reads=[bt, ident], writes=[pTb])
        k.tr(pTb[:, 4 + j, :], kt[:, j * 128:(j + 1) * 128], ident[:], reads=[kt, ident], writes=[pTb])
    for i, eng in ((0, "act"), (1, "dve")):
        pr = slice(64 * i, 64 * i + 64)
        k.op(eng, lambda e: (e.copy if eng == "act" else e.tensor_copy)(
            arT[i][pr].rearrange("p j s t -> p s j t"), pTa[pr].rearrange("p (s j) t -> p s j t", s=2)),
            reads=[pTa], writes=[arT[i]])
    yield
    for i, eng in ((0, "act"), (1, "dve")):
        pr = slice(64 * i, 64 * i + 64)
        k.op(eng, lambda e: (e.copy if eng == "act" else e.tensor_copy)(bkT[i][pr], pTb[pr]),
             reads=[pTb], writes=[bkT[i]])
    yield
    NA, KA, Lm = env["NA"][s], env["KA"][s], env["Lm"]
    mIT = cd[:, C_MITS:C_MITS + 256].rearrange("p (s t) -> p s t", s=2).unsqueeze(1).to_broadcast([128, 2, 2, 128])
    mTI = cd[:, C_MTI:C_MTI + 128].unsqueeze(1).to_broadcast([128, 4, 128])
    for hb in range(2):
        pl = bank(env)
        for hh in range(4):
            h = hb * 4 + hh
            k.mm(pl[:, hh * 128:(hh + 1) * 128], arT[h % 2][:, h // 2, 0, :], bkT[h % 2][:, h // 2, :], True, True,
                 reads=[arT[h % 2], bkT[h % 2]], writes=[pl])
        k.op("dve", lambda e: e.tensor_tensor(Lm[:, hb * 4:hb * 4 + 4, :], pl[:].rearrange("p (h t) -> p h t", h=4),
                                              mTI, ALU.mult), reads=[pl, cd], writes=[Lm])
        yield
    for dst, off in ((NA, 0), (KA, 4)):
        for hp in range(4):
            pk = bank(env)
            for hh in range(2):
                k.mm(pk[:, hh * 256:(hh + 1) * 256], bkT[hh][:, off + hp, :],
                     arT[hh][:, hp, :, :].rearrange("p s t -> p (s t)"), True, True,
                     reads=[bkT[hh], arT[hh]], writes=[pk])
            k.op("dve", lambda e: e.tensor_tensor(dst[:, 2 * hp:2 * hp + 2, :, :],
                                                  pk[:].rearrange("p (h s t) -> p h s t", h=2, s=2), mIT, ALU.mult),
                 reads=[pk, cd], writes=[dst])
            yield
    NP, Lp, Pf = env["NP"], env["Lp"], env["Pf"]
    idb = ident[:].unsqueeze(1).to_broadcast([128, 8, 128])

    def l_products(lhs_fn, rhs_fn, rbufs, dst):
        for hb in range(2):
            pn = bank(env)
            for hh in range(4):
                h = hb * 4 + hh
                k.mm(pn[:, hh * 128:(hh + 1) * 128], lhs_fn(h), rhs_fn(h), True, True, reads=rbufs, writes=[pn])
            k.op("act", lambda e: e.copy(dst[:, hb * 4:hb * 4 + 4, :], pn[:].rearrange("p (h t) -> p h t", h=4)),
                 reads=[pn], writes=[dst])

    k.op("pool", lambda e: e.tensor_tensor(NP[1][:, :, 1, :], NA[:, :, 0, :], idb, ALU.add),
         reads=[NA, ident], writes=[NP[1]])
    for hb in range(2):
        pn = bank(env)
        for hh in range(4):
            h = hb * 4 + hh
            k.mm(pn[:, hh * 128:(hh + 1) * 128], Lm[:, h, :], NA[:, h, 0, :], True, True, reads=[Lm, NA], writes=[pn])
        k.op("act", lambda e: e.copy(NP[1][:, hb * 4:hb * 4 + 4, 0, :], pn[:].rearrange("p (h t) -> p h t", h=4)),
             reads=[pn], writes=[NP[1]])
    yield
    l_products(lambda h: NA[:, h, 0, :], lambda h: Lm[:, h, :], [NA, Lm], Lp[1])
    yield
    for j in range(1, 4):
        cur_, nxt_ = NP[j % 2], NP[(j + 1) % 2]
        Lc, Ln = Lp[j % 2], Lp[(j + 1) % 2]
        for hp in range(4):
            pn = bank(env)
            for hh in range(2):
                h = hp * 2 + hh
                k.mm(pn[:, hh * 256:(hh + 1) * 256], Lc[:, h, :], cur_[:, h, :, :].rearrange("p s t -> p (s t)"),
                     True, True, reads=[Lc, cur_], writes=[pn])
            pv = pn[:].rearrange("p (h s t) -> p h s t", h=2, s=2)
            k.op("act", lambda e: e.copy(nxt_[:, 2 * hp:2 * hp + 2, 0, :], pv[:, :, 0, :]), reads=[pn], writes=[nxt_])
            k.op("dve", lambda e: e.tensor_tensor(nxt_[:, 2 * hp:2 * hp + 2, 1, :], pv[:, :, 1, :],
                                                  cur_[:, 2 * hp:2 * hp + 2, 1, :], ALU.add),
                 reads=[pn, cur_], writes=[nxt_])
            if hp % 2 == 1:
                yield
        l_products(lambda h: cur_[:, h, 0, :], lambda h: Lc[:, h, :], [cur_, Lc], Ln)
        yield
    for hb in range(2):
        pn = bank(env)
        for hh in range(4):
            h = hb * 4 + hh
            k.mm(pn[:, hh * 128:(hh + 1) * 128], Lp[0][:, h, :], NP[0][:, h, 1, :], True, True,
                 reads=[Lp[0], NP[0]], writes=[pn])
        k.op("dve", lambda e: e.tensor_tensor(Pf[:, hb * 4:hb * 4 + 4, :], pn[:].rearrange("p (h t) -> p h t", h=4),
                                              NP[0][:, hb * 4:hb * 4 + 4, 1, :], ALU.add),
             reads=[pn, NP[0]], writes=[Pf])
    yield
    P = Pf
    Xb, Ubar, AbT = env["Xb"], env["Ubar"][s], env["AbT"][s]
    px = bank(env)
    for h in range(8):
        k.mm(px[:, h * 64:(h + 1) * 64], KA[:, h, 0, :], Vb[:, h * 64:(h + 1) * 64], True, True,
             reads=[KA, Vb], writes=[px])
    k.op("act", lambda e: e.copy(Xb[:], px[:]), reads=[px], writes=[Xb])
    pab = bank(env)
    for h in range(8):
        pr = slice(64 * (h % 2), 64 * (h % 2) + 64)
        k.mm(pab[pr, (h // 2) * 128:(h // 2 + 1) * 128], at[:, h * 64:(h + 1) * 64], P[:, h, :], True, True,
             reads=[at, P], writes=[pab])
    k.op("act", lambda e: e.copy(AbT[0][0:64], pab[0:64].rearrange("p (j t) -> p j t", j=4)),
         reads=[pab], writes=[AbT[0]])
    k.op("dve", lambda e: e.tensor_copy(AbT[1][64:128], pab[64:128].rearrange("p (j t) -> p j t", j=4)),
         reads=[pab], writes=[AbT[1]])
    yield
    pub = bank(env)
    for h in range(8):
        k.mm(pub[:, h * 64:(h + 1) * 64], P[:, h, :], Xb[:, h * 64:(h + 1) * 64], True, True,
             reads=[P, Xb], writes=[pub])
    k.op("dve", lambda e: e.tensor_copy(Ubar[:], pub[:]), reads=[pub], writes=[Ubar])
    yield


def dplr_fin(k, env, s, Vb):
    arT, NA, KA, AbT, Ubar = env["arT"][s], env["NA"][s], env["KA"][s], env["AbT"][s], env["Ubar"][s]
    kdec, bdec, wC = env["kdec"][s], env["bdec"][s], env["wC"][s]
    U_b, St, Stb, pU, pO, pS = env["U_b"], env["St"], env["Stb"], env["pU"], env["pO"], env["pS"]
    for c in range(4):
        rc = slice(DC * c, DC * c + DC)
        cb = DC * c
        for h in range(8):
            k.mm(pU[rc, h * 64:(h + 1) * 64], AbT[h % 2][:, h // 2, rc], Stb[:, h // 2, :], True, True,
                 reads=[AbT[h % 2], Stb], writes=[pU], tp=(0, cb))
        k.op("dve", lambda e: e.tensor_tensor(U_b[rc, :], pU[rc, :], Ubar[rc, :], ALU.add),
             reads=[pU, Ubar], writes=[U_b])
        yield
        for h in range(8):
            pb_ = 64 * (h % 2)
            pr = slice(pb_, pb_ + 64)
            hc = slice(h * 64, (h + 1) * 64)
            jc = slice((h // 2) * 64, (h // 2 + 1) * 64)
            k.mm(pS[pr, jc], kdec[rc, hc], Vb[rc, hc], True, False, reads=[kdec, Vb], writes=[pS], tp=(cb, pb_))
            k.mm(pS[pr, jc], bdec[rc, hc], U_b[rc, hc], False, True, reads=[bdec, U_b], writes=[pS], tp=(cb, pb_))
        for h in range(8):
            hc = slice(h * 64, (h + 1) * 64)
            k.mm(pO[rc, hc], arT[h % 2][:, h // 2, 1, rc], Stb[:, h // 2, :], True, False,
                 reads=[arT[h % 2], Stb], writes=[pO], tp=(0, cb))
            k.mm(pO[rc, hc], NA[:, h, 1, rc], U_b[:, hc], False, False, reads=[NA, U_b], writes=[pO], tp=(0, cb))
            k.mm(pO[rc, hc], KA[:, h, 1, rc], Vb[:, hc], False, True, reads=[KA, Vb], writes=[pO], tp=(0, cb))
        k.op("pool", lambda e: e.tensor_tensor(St[:], St[:], wC[:, :, c].unsqueeze(2).to_broadcast([128, 4, 64]), ALU.mult),
             reads=[St, wC], writes=[St])
        k.op("dve", lambda e: e.tensor_tensor(St[:], St[:], pS[:, 0:256].rearrange("p (j v) -> p j v", j=4), ALU.add),
             reads=[St, pS], writes=[St])
        yield
        k.op("act", lambda e: e.copy(Stb[:], St[:]), reads=[St], writes=[Stb])
        yield
    o = env["o"]
    k.op("act", lambda e: e.copy(o[:], pO[:]), reads=[pO], writes=[o])
    yield


def run_pipeline(NT, stages, weights):
    ns = len(stages)
    for tau in range(NT + ns - 1):
        gens = []
        for i, st in enumerate(stages):
            t = tau - i
            if 0 <= t < NT:
                gens.append([st(t), weights[i]])
        while gens:
            for gw in list(gens):
                for _ in range(gw[1]):
                    try:
                        next(gw[0])
                    except StopIteration:
                        gens.remove(gw)
                        break


def norm_transpose(k, x, xs, junk, ss, rstd, pT, uT, gpre, ident):
    rms_rstd(k, "act", x[:], x, junk, ss, rstd, D)
    k.op("dve", lambda e: e.tensor_scalar(xs[:], x[:], rstd[:, 0:1], None, ALU.mult), reads=[x, rstd], writes=[xs])
    for kc in range(8):
        k.tr(pT[:, kc, :], xs[:, kc * 128:(kc + 1) * 128], ident[:], reads=[xs, ident], writes=[pT])
    k.op("dve", lambda e: e.tensor_tensor(uT[:], pT[:], gpre[:].unsqueeze(2).to_broadcast([128, 8, 128]), ALU.mult),
         reads=[pT, gpre], writes=[uT])


def inv_sqrt(k, dst, src, scale, eps):
    k.op("dve", lambda e: e.tensor_scalar(dst[:], src[:], scale, eps, ALU.mult, ALU.add), reads=[src], writes=[dst])
    k.op("act", lambda e: e.activation(dst[:], dst[:], AF.Sqrt), reads=[dst], writes=[dst])
    k.op("dve", lambda e: e.reciprocal(dst[:], dst[:]), reads=[dst], writes=[dst])


def bc(ap, n, w):
    return ap.unsqueeze(2).to_broadcast([128, n, w])


def gdn_phase(k, nc, T, hin, oa_out, w_in, conv_w, a_log, dt_bias, gnorm_w, g_pre, ident, c_dplr):
    NT = T // 128
    es = ExitStack()
    with es:
        win = [k.sb(es, f"gwin{kc}", [128, 2064], BF16) for kc in range(8)]
        for kc in range(8):
            k.dma("pool", f"wl{kc % 4}", win[kc][:], w_in.t[kc * 128:(kc + 1) * 128, 0:2064],
                  reads=[w_in], writes=[win[kc]])
        gpre = k.sb(es, "gpre", [128, 8], F32)
        k.dma("sp", "const", gpre[:], g_pre.t.rearrange("(kc p) -> p kc", p=128),
              reads=[g_pre], writes=[gpre], allow_slow_non_contiguous=True)
        cwt = k.sb(es, "cwt", [128, 4, 1536], F32)
        k.dma("sp", "const", cwt[:].rearrange("p j c -> p (j c)"),
              conv_w.t.rearrange("j c -> (j c)").partition_broadcast(128), reads=[conv_w], writes=[cwt])
        alb = k.sb(es, "alb", [128, 8], F32)
        dtb = k.sb(es, "dtb", [128, 8], F32)
        gnw = k.sb(es, "gnw", [128, 64], F32)
        k.dma("sp", "const", alb[:], a_log.t.partition_broadcast(128), reads=[a_log], writes=[alb])
        k.dma("sp", "const", dtb[:], dt_bias.t.partition_broadcast(128), reads=[dt_bias], writes=[dtb])
        k.dma("sp", "const", gnw[:], gnorm_w.t.partition_broadcast(128), reads=[gnorm_w], writes=[gnw])
        nea = k.sb(es, "nea", [128, 8], F32)
        k.op("act", lambda e: e.activation(nea[:], alb[:], AF.Exp), reads=[alb], writes=[nea])
        k.op("dve", lambda e: e.tensor_scalar(nea[:], nea[:], -1.0, None, ALU.mult), reads=[nea], writes=[nea])
        env = dplr_setup(k, es, c_dplr, ident)

        xb = [k.sb(es, f"x{i}", [128, D], F32) for i in range(2)]
        xs = k.sb(es, "xs", [128, D], BF16)
        junk = k.sb(es, "junk", [128, D], BF16)
        uT = k.sb(es, "uT", [128, 8, 128], BF16)
        ss = k.sb(es, "ss", [128, 1], F32)
        rstd = k.sb(es, "rstd", [128, 1], F32)
        pq = [k.sb(es, f"pq{i}", [128, 1536], F32) for i in range(2)]
        xsh = [k.sb(es, f"xsh{i}", [128, 1536], F32) for i in range(3)]
        cv = k.sb(es, "cv", [128, 1536], F32)
        zsb = [k.sb(es, f"zs{i}", [128, 512], F32) for i in range(3)]
        ba = k.sb(es, "ba", [128, 16], F32)
        tmp = xsh[2]
        tmp2 = k.sb(es, "tmp2", [128, 512], F32)
        sm = {n: k.sb(es, "g_" + n, [128, 16], F32) for n in ("ssq", "rn", "beta", "sp", "g", "eg", "coef", "rq",
                                                              "ss8", "rs8")}
        Rg = k.sb(es, "Rg", [128, 512], F32)
        Ag = k.sb(es, "Ag", [128, 512], F32)
        Kg = k.sb(es, "Kg", [128, 512], F32)
        Bg = k.sb(es, "Bg", [128, 512], F32)
        ldg = k.sb(es, "ldg", [128, 512], F32)
        Vbb = [k.sb(es, f"Vb{i}", [128, 512], BF16) for i in range(3)]
        oa = tmp2
        k.op("pool", lambda e: e.memset(pq[1][:], 0.0), writes=[pq[1]])
        print("gdn sbuf bytes remaining", nc.sbuf_bytes_remaining)
        v3 = lambda b_, lo: b_[:, lo:lo + 512].rearrange("p (h d) -> p h d", h=8)
        w3 = lambda b_: b_[:].rearrange("p (h d) -> p h d", h=8)

        def load_x(t):
            k.dma("sp", f"x{t % 2}", xb[t % 2][:], hin[t].t, reads=[hin[t]], writes=[xb[t % 2]])

        def prep(t):
            cur, prev = pq[t % 2], pq[(t + 1) % 2]
            x, zs, Vb = xb[t % 2], zsb[t % 3], Vbb[t % 3]
            if t == 0:
                load_x(0)
            if t + 1 < NT:
                load_x(t + 1)
            norm_transpose(k, x, xs, junk, ss, rstd, env["pT"][0], uT, gpre, ident)
            yield
            for g, (c0, c1) in enumerate(((0, 512), (512, 1024), (1024, 1536), (1536, 2048), (2048, 2064))):
                pu = bank(env)
                n = c1 - c0
                for kc in range(8):
                    k.mm(pu[:, 0:n], uT[:, kc, :], win[kc][:, c0:c1], kc == 0, kc == 7, reads=[uT, win[kc]], writes=[pu])
                if g < 3:
                    if g % 2 == 0:
                        k.op("act", lambda e: e.copy(cur[:, c0:c1], pu[:]), reads=[pu], writes=[cur])
                    else:
                        k.op("dve", lambda e: e.tensor_copy(cur[:, c0:c1], pu[:]), reads=[pu], writes=[cur])
                elif g == 3:
                    k.op("act", lambda e: e.activation(zs[:], pu[:], AF.Silu), reads=[pu], writes=[zs])
                else:
                    k.op("dve", lambda e: e.tensor_copy(ba[:], pu[:, 0:16]), reads=[pu], writes=[ba])
                yield
            for j in range(1, 4):
                sh = xsh[j - 1]
                k.dma("sp", f"sh{j}a", sh[j:128, :], cur[0:128 - j, :], reads=[cur], writes=[sh])
                k.dma("sp", f"sh{j}b", sh[0:j, :], prev[128 - j:128, :], reads=[prev], writes=[sh])
            k.op("act", lambda e: e.activation(sm["beta"][:, 0:8], ba[:, 0:8], AF.Sigmoid), reads=[ba], writes=[sm["beta"]])
            k.op("dve", lambda e: e.tensor_tensor(sm["sp"][:, 0:8], ba[:, 8:16], dtb[:], ALU.add),
                 reads=[ba, dtb], writes=[sm["sp"]])
            k.op("act", lambda e: e.activation(sm["sp"][:, 0:8], sm["sp"][:, 0:8], AF.Exp), reads=[sm["sp"]], writes=[sm["sp"]])
            k.op("dve", lambda e: e.tensor_scalar(sm["sp"][:, 0:8], sm["sp"][:, 0:8], 1.0, None, ALU.add),
                 reads=[sm["sp"]], writes=[sm["sp"]])
            k.op("act", lambda e: e.activation(sm["sp"][:, 0:8], sm["sp"][:, 0:8], AF.Ln), reads=[sm["sp"]], writes=[sm["sp"]])
            k.op("dve", lambda e: e.tensor_tensor(sm["g"][:, 0:8], sm["sp"][:, 0:8], nea[:], ALU.mult),
                 reads=[sm["sp"], nea], writes=[sm["g"]])
            k.op("act", lambda e: e.activation(sm["eg"][:, 0:8], sm["g"][:, 0:8], AF.Exp), reads=[sm["g"]], writes=[sm["eg"]])
            k.op("dve", lambda e: e.scalar_tensor_tensor(sm["coef"][:, 0:8], sm["eg"][:, 0:8], -1.0, sm["beta"][:, 0:8],
                                                         ALU.mult, ALU.mult), reads=[sm["eg"], sm["beta"]], writes=[sm["coef"]])
            k.op("act", lambda e: e.copy(w3(ldg), bc(sm["g"][:, 0:8], 8, 64)), reads=[sm["g"]], writes=[ldg])
            yield
            k.op("dve", lambda e: e.tensor_tensor(cv[:], cur[:], cwt[:, 3, :], ALU.mult), reads=[cur, cwt], writes=[cv])
            for j in range(1, 4):
                sh = xsh[j - 1]
                k.op("pool", lambda e: e.tensor_tensor(sh[:], sh[:], cwt[:, 3 - j, :], ALU.mult),
                     reads=[sh, cwt], writes=[sh])
                k.op("dve", lambda e: e.tensor_tensor(cv[:], cv[:], sh[:], ALU.add), reads=[cv, sh], writes=[cv])
                yield
            k.op("act", lambda e: e.activation(cv[:], cv[:], AF.Silu), reads=[cv], writes=[cv])
            yield
            k.op("act", lambda e: e.activation(tmp[:, 0:1024], cv[:, 0:1024], AF.Square), reads=[cv], writes=[tmp])
            k.op("dve", lambda e: e.tensor_reduce(sm["ssq"][:], tmp[:, 0:1024].rearrange("p (h d) -> p h d", h=16), AX.X, ALU.add),
                 reads=[tmp], writes=[sm["ssq"]])
            inv_sqrt(k, sm["rn"], sm["ssq"], 1.0, 1e-6)
            k.op("dve", lambda e: e.tensor_scalar(sm["rq"][:, 0:8], sm["rn"][:, 0:8], 0.125, None, ALU.mult),
                 reads=[sm["rn"]], writes=[sm["rq"]])
            yield
            k.op("dve", lambda e: e.tensor_tensor(w3(Rg), v3(cv, 0), bc(sm["rq"][:, 0:8], 8, 64), ALU.mult),
                 reads=[cv, sm["rq"]], writes=[Rg])
            k.op("pool", lambda e: e.tensor_tensor(w3(Ag), v3(cv, 512), bc(sm["rn"][:, 8:16], 8, 64), ALU.mult),
                 reads=[cv, sm["rn"]], writes=[Ag])
            k.op("dve", lambda e: e.tensor_tensor(w3(Kg), w3(Ag), bc(sm["beta"][:, 0:8], 8, 64), ALU.mult),
                 reads=[Ag, sm["beta"]], writes=[Kg])
            k.op("pool", lambda e: e.tensor_tensor(w3(Bg), w3(Ag), bc(sm["coef"][:, 0:8], 8, 64), ALU.mult),
                 reads=[Ag, sm["coef"]], writes=[Bg])
            k.op("act", lambda e: e.copy(Vb[:], cv[:, 1024:1536]), reads=[cv], writes=[Vb])
            yield

        def prep2(t):
            yield from dplr_prep(k, env, t % 2, (Rg, Rg[:]), (Kg, Kg[:]), (Ag, Ag[:]), (Bg, Bg[:]), Vbb[t % 3],
                                 (ldg, ldg[:]))

        def fin(t):
            zs, Vb = zsb[t % 3], Vbb[t % 3]
            yield from dplr_fin(k, env, t % 2, Vb)
            o = env["o"]
            k.op("act", lambda e: e.activation(tmp2[:], o[:], AF.Square), reads=[o], writes=[tmp2])
            k.op("dve", lambda e: e.tensor_reduce(sm["ss8"][:, 0:8], tmp2[:].rearrange("p (h d) -> p h d", h=8),
                                                  AX.X, ALU.add), reads=[tmp2], writes=[sm["ss8"]])
            yield
            inv_sqrt(k, sm["rs8"], sm["ss8"], 1.0 / 64, RMS_EPS)
            yield
            k.op("dve", lambda e: e.tensor_tensor(w3(oa), w3(o), bc(sm["rs8"][:, 0:8], 8, 64), ALU.mult),
                 reads=[o, sm["rs8"]], writes=[oa])
            k.op("pool", lambda e: e.tensor_tensor(w3(oa), w3(oa), gnw[:].unsqueeze(1).to_broadcast([128, 8, 64]), ALU.mult),
                 reads=[oa, gnw], writes=[oa])
            yield
            k.op("pool", lambda e: e.tensor_tensor(oa[:], oa[:], zs[:], ALU.mult), reads=[oa, zs], writes=[oa])
            k.dma("pool", "y0", oa_out[t].t, oa[:], reads=[oa], writes=[oa_out[t]])
            yield

        run_pipeline(NT, [prep, prep2, fin], [1, 4, 1])
        k.barrier()


def rwkv_phase(k, nc, T, hin, ob_out, w_in, vecs, w2, a2, g2, g_pre, ident, c_dplr):
    NT = T // 128
    es = ExitStack()
    with es:
        win = [k.sb(es, f"rwin{kc}", [128, 1792], BF16) for kc in range(8)]
        for kc in range(8):
            k.dma("pool", f"wl{kc % 4}", win[kc][:], w_in.t[kc * 128:(kc + 1) * 128, 2064:3856],
                  reads=[w_in], writes=[win[kc]])
        w2p = k.sb(es, "w2p", [128, 512], BF16)
        a2p = k.sb(es, "a2p", [128, 512], BF16)
        g2b = k.sb(es, "g2b", [128, 512], BF16)
        k.op("pool", lambda e: e.memset(w2p[:], 0.0), writes=[w2p])
        k.op("pool", lambda e: e.memset(a2p[:], 0.0), writes=[a2p])
        k.dma("pool", "wl0", w2p[0:64, :], w2.t[:, :], reads=[w2], writes=[w2p])
        k.dma("pool", "wl1", a2p[64:128, :], a2.t[:, :], reads=[a2], writes=[a2p])
        k.dma("pool", "wl2", g2b[:], g2.t[:, :], reads=[g2], writes=[g2b])
        gpre = k.sb(es, "gpre", [128, 8], F32)
        k.dma("sp", "const", gpre[:], g_pre.t.rearrange("(kc p) -> p kc", p=128),
              reads=[g_pre], writes=[gpre], allow_slow_non_contiguous=True)
        vb_ = {}
        for n, buf in vecs.items():
            w = 1792 if n == "mu" else 512
            vb_[n] = k.sb(es, "v_" + n, [128, w], F32)
            k.dma("sp", "const", vb_[n][:], buf.t.partition_broadcast(128), reads=[buf], writes=[vb_[n]])
        env = dplr_setup(k, es, c_dplr, ident)

        xb = [k.sb(es, f"x{i}", [128, D], F32) for i in range(2)]
        xs = k.sb(es, "xs", [128, D], BF16)
        junk = k.sb(es, "junk", [128, D], BF16)
        uT = k.sb(es, "uT", [128, 8, 128], BF16)
        ss = k.sb(es, "ss", [128, 1], F32)
        rstd = k.sb(es, "rstd", [128, 1], F32)
        prp = k.sb(es, "prp", [128, 1792], F32)
        rsh = k.sb(es, "rsh", [128, 1792], F32)
        lastrow = k.sb(es, "lastrow", [1, 1792], F32)
        lin = k.sb(es, "lin", [128, 256], BF16)
        linT = k.sb(es, "linT", [128, 2, 128], BF16)
        ldr = k.sb(es, "ldr", [128, 512], F32)
        aic = k.sb(es, "aic", [128, 512], F32)
        gateb = [k.sb(es, f"gate{i}", [128, 512], F32) for i in range(3)]
        tk = k.sb(es, "tk", [128, 512], F32)
        Kr = k.sb(es, "Kr", [128, 512], F32)
        Ar = k.sb(es, "Ar", [128, 512], F32)
        Br = k.sb(es, "Br", [128, 512], F32)
        Rr = k.sb(es, "Rr", [128, 512], F32)
        Vbb = [k.sb(es, f"Vb{i}", [128, 512], BF16) for i in range(3)]
        tmp = k.sb(es, "tmp", [128, 512], F32)
        tmp2 = k.sb(es, "tmp2", [128, 512], F32)
        bonusb = [k.sb(es, f"bonus{i}", [128, 512], F32) for i in range(3)]
        y = k.sb(es, "y", [128, 512], F32)
        sm = {n: k.sb(es, "r_" + n, [128, 8], F32) for n in ("ssq", "rn", "bs", "s1", "s2", "mean", "msq", "var",
                                                             "rstdh", "nmr")}
        k.op("pool", lambda e: e.memset(lastrow[:], 0.0), writes=[lastrow])
        print("rwkv sbuf bytes remaining", nc.sbuf_bytes_remaining)
        v3 = lambda ap: ap.rearrange("p (h d) -> p h d", h=8)

        def load_x(t):
            k.dma("sp", f"x{t % 2}", xb[t % 2][:], hin[t].t, reads=[hin[t]], writes=[xb[t % 2]])

        def prep(t):
            x, gate, Vb, bonus = xb[t % 2], gateb[t % 3], Vbb[t % 3], bonusb[t % 3]
            if t == 0:
                load_x(0)
            if t + 1 < NT:
                load_x(t + 1)
            norm_transpose(k, x, xs, junk, ss, rstd, env["pT"][0], uT, gpre, ident)
            yield
            for g, (c0, c1) in enumerate(((0, 512), (512, 1024), (1024, 1536), (1536, 1792))):
                pu = bank(env)
                n = c1 - c0
                for kc in range(8):
                    k.mm(pu[:, 0:n], uT[:, kc, :], win[kc][:, c0:c1], kc == 0, kc == 7, reads=[uT, win[kc]], writes=[pu])
                if g % 2 == 0:
                    k.op("act", lambda e: e.copy(prp[:, c0:c1], pu[:, 0:n]), reads=[pu], writes=[prp])
                else:
                    k.op("dve", lambda e: e.tensor_copy(prp[:, c0:c1], pu[:, 0:n]), reads=[pu], writes=[prp])
                yield
            k.dma("sp", "rsa", rsh[1:128, :], prp[0:127, :], reads=[prp], writes=[rsh])
            k.dma("sp", "rsb", rsh[0:1, :], lastrow[0:1, :], reads=[lastrow], writes=[rsh])
            k.dma("sp", "rsc", lastrow[0:1, :], prp[127:128, :], reads=[prp, rsh], writes=[lastrow])
            yield
            k.op("pool", lambda e: e.tensor_tensor(rsh[:], rsh[:], prp[:], ALU.subtract), reads=[rsh, prp, lastrow],
                 writes=[rsh])
            k.op("pool", lambda e: e.tensor_tensor(rsh[:], rsh[:], vb_["mu"][:], ALU.mult), reads=[rsh, vb_["mu"]],
                 writes=[rsh])
            yield
            k.op("dve", lambda e: e.tensor_tensor(prp[:], prp[:], rsh[:], ALU.add), reads=[prp, rsh, lastrow],
                 writes=[prp])
            r_ap, kr_ap, vr_ap = prp[:, 0:512], prp[:, 512:1024], prp[:, 1024:1536]
            k.op("act", lambda e: e.activation(lin[:, 0:64], prp[:, 1536:1600], AF.Tanh), reads=[prp], writes=[lin])
            k.op("act", lambda e: e.copy(lin[:, 64:128], prp[:, 1600:1664]), reads=[prp], writes=[lin])
            k.op("act", lambda e: e.activation(lin[:, 128:256], prp[:, 1664:1792], AF.Sigmoid), reads=[prp], writes=[lin])
            k.op("act", lambda e: e.copy(Vb[:], vr_ap), reads=[prp], writes=[Vb])
            k.op("act", lambda e: e.copy(Rr[:], r_ap), reads=[prp], writes=[Rr])
            yield
            pT0 = env["pT"][0]
            for i in range(2):
                k.tr(pT0[:, i, :], lin[:, i * 128:(i + 1) * 128], ident[:], reads=[lin, ident], writes=[pT0])
            k.op("act", lambda e: e.copy(linT[:], pT0[:, 0:2, :]), reads=[pT0], writes=[linT])
            yield
            pz = bank(env)
            k.mm(pz[:], linT[:, 0, :], w2p[:], True, True, reads=[linT, w2p], writes=[pz])
            k.op("dve", lambda e: e.tensor_tensor(ldr[:], pz[:], vb_["w0"][:], ALU.add), reads=[pz, vb_["w0"]], writes=[ldr])
            pa = bank(env)
            k.mm(pa[:], linT[:, 0, :], a2p[:], True, True, reads=[linT, a2p], writes=[pa])
            k.op("dve", lambda e: e.tensor_tensor(aic[:], pa[:], vb_["a0"][:], ALU.add), reads=[pa, vb_["a0"]], writes=[aic])
            yield
            k.op("act", lambda e: e.activation(ldr[:], ldr[:], AF.Sigmoid), reads=[ldr], writes=[ldr])
            k.op("act", lambda e: e.activation(aic[:], aic[:], AF.Sigmoid), reads=[aic], writes=[aic])
            pg = bank(env)
            k.mm(pg[:], linT[:, 1, :], g2b[:], True, True, reads=[linT, g2b], writes=[pg])
            k.op("act", lambda e: e.copy(gate[:], pg[:]), reads=[pg], writes=[gate])
            k.op("act", lambda e: e.mul(ldr[:], ldr[:], -0.6065306597126334), reads=[ldr], writes=[ldr])
            yield
            k.op("dve", lambda e: e.tensor_tensor(tk[:], kr_ap, vb_["k_k"][:], ALU.mult), reads=[prp, vb_["k_k"]], writes=[tk])
            k.op("act", lambda e: e.activation(tmp[:], tk[:], AF.Square), reads=[tk], writes=[tmp])
            k.op("dve", lambda e: e.tensor_reduce(sm["ssq"][:], v3(tmp[:]), AX.X, ALU.add), reads=[tmp], writes=[sm["ssq"]])
            yield
            inv_sqrt(k, sm["rn"], sm["ssq"], 1.0, 1e-6)
            yield
            k.op("dve", lambda e: e.tensor_tensor(v3(tk[:]), v3(tk[:]), bc(sm["rn"][:], 8, 64), ALU.mult),
                 reads=[tk, sm["rn"]], writes=[tk])
            k.op("dve", lambda e: e.scalar_tensor_tensor(Kr[:], aic[:], -1.0, vb_["k_a"][:], ALU.add, ALU.mult),
                 reads=[aic, vb_["k_a"]], writes=[Kr])
            k.op("dve", lambda e: e.scalar_tensor_tensor(Kr[:], Kr[:], 1.0, kr_ap, ALU.add, ALU.mult),
                 reads=[Kr, prp], writes=[Kr])
            k.op("act", lambda e: e.mul(Ar[:], tk[:], -1.0), reads=[tk], writes=[Ar])
            k.op("pool", lambda e: e.tensor_tensor(Br[:], tk[:], aic[:], ALU.mult), reads=[tk, aic], writes=[Br])
            yield
            k.op("pool", lambda e: e.tensor_tensor(tmp[:], r_ap, Kr[:], ALU.mult), reads=[prp, Kr], writes=[tmp])
            k.op("pool", lambda e: e.tensor_tensor(tmp[:], tmp[:], vb_["r_k"][:], ALU.mult), reads=[tmp, vb_["r_k"]],
                 writes=[tmp])
            k.op("dve", lambda e: e.tensor_reduce(sm["bs"][:], v3(tmp[:]), AX.X, ALU.add), reads=[tmp], writes=[sm["bs"]])
            k.op("dve", lambda e: e.tensor_tensor(v3(bonus[:]), v3(vr_ap), bc(sm["bs"][:], 8, 64), ALU.mult),
                 reads=[prp, sm["bs"]], writes=[bonus])
            yield

        def prep2(t):
            yield from dplr_prep(k, env, t % 2, (Rr, Rr[:]), (Kr, Kr[:]), (Ar, Ar[:]), (Br, Br[:]), Vbb[t % 3],
                                 (ldr, ldr[:]))

        def fin(t):
            gate, Vb, bonus = gateb[t % 3], Vbb[t % 3], bonusb[t % 3]
            yield from dplr_fin(k, env, t % 2, Vb)
            o = env["o"]
            k.op("dve", lambda e: e.tensor_reduce(sm["s1"][:], v3(o[:]), AX.X, ALU.add), reads=[o], writes=[sm["s1"]])
            k.op("act", lambda e: e.activation(tmp2[:], o[:], AF.Square), reads=[o], writes=[tmp2])
            k.op("dve", lambda e: e.tensor_reduce(sm["s2"][:], v3(tmp2[:]), AX.X, ALU.add), reads=[tmp2], writes=[sm["s2"]])
            yield
            head_stats(k, sm, 64, 64e-5)
            yield
            k.op("dve", lambda e: e.tensor_tensor(v3(y[:]), v3(o[:]), bc(sm["rstdh"][:], 8, 64), ALU.mult),
                 reads=[o, sm["rstdh"]], writes=[y])
            k.op("pool", lambda e: e.tensor_tensor(v3(y[:]), v3(y[:]), bc(sm["nmr"][:], 8, 64), ALU.add),
                 reads=[y, sm["nmr"]], writes=[y])
            yield
            k.op("dve", lambda e: e.tensor_tensor(y[:], y[:], vb_["ln_w"][:], ALU.mult), reads=[y, vb_["ln_w"]], writes=[y])
            k.op("pool", lambda e: e.tensor_tensor(y[:], y[:], vb_["ln_b"][:], ALU.add), reads=[y, vb_["ln_b"]], writes=[y])
            yield
            k.op("dve", lambda e: e.tensor_tensor(y[:], y[:], bonus[:], ALU.add), reads=[y, bonus], writes=[y])
            k.op("pool", lambda e: e.tensor_tensor(y[:], y[:], gate[:], ALU.mult), reads=[y, gate], writes=[y])
            k.dma("pool", "y0", ob_out[t].t, y[:], reads=[y], writes=[ob_out[t]])
            yield

        run_pipeline(NT, [prep, prep2, fin], [1, 4, 1])
        k.barrier()


def build_program(T, phases=("mlp",), dbg=None):
    nc = bass.Bass("TRN2", target_bir_lowering=False)
    k = KB(nc)

    dins = {}

    def din(name, shape):
        if name not in dins:
            dins[name] = Buf(nc.dram_tensor(name, list(shape), F32, kind="ExternalInput").ap(), name)
        return dins[name]

    NT = T // 128
    x = din("x", [T, D])
    c_ident = din("c_ident", [128, 128])
    out = Buf(nc.dram_tensor("out", [T, D], F32, kind="ExternalOutput").ap(), "out")

    def tiles(buf):
        return [Buf(buf.t[t * 128:(t + 1) * 128, :], f"{buf.name}{t}") for t in range(NT)]

    cur = tiles(x)
    with k.es:
        es = ExitStack()
        with es:
            ident = load_consts(k, es, c_ident)
            for pi, ph in enumerate(phases):
                last = pi == len(phases) - 1
                if ph in ("gdn", "rwkv", "ret"):
                    nxt = None
                elif last:
                    nxt = tiles(out)
                else:
                    scr = Buf(nc.dram_tensor(f"scr{pi}", [T, D], F32, kind="Internal").ap(), f"scr{pi}")
                    nxt = tiles(scr)
                if ph == "gdn":
                    oa_t = tiles(Buf(nc.dram_tensor("oa_scr", [T, 512], F32, kind="Internal").ap(), "oa_scr"))
                    gdn_phase(k, nc, T, cur, oa_t, din("ab_w_in", [D, 3856]), din("gdn_conv_w", [4, 1536]),
                              din("gdn_a_log", [8]), din("gdn_dt_bias", [8]), din("gdn_norm_w", [64]),
                              din("norm_mix_pre0", [D]), ident, din("c_dplr", [128, 772]))
                    continue
                if ph == "rwkv":
                    ob_t = tiles(Buf(nc.dram_tensor("ob_scr", [T, 512], F32, kind="Internal").ap(), "ob_scr"))
                    vecs = {"mu": din("rwkv_mu", [1792])}
                    for n in ("w0", "a0", "k_k", "k_a", "r_k", "ln_w", "ln_b"):
                        vecs[n] = din("rwkv_" + n, [512])
                    rwkv_phase(k, nc, T, cur, ob_t, din("ab_w_in", [D, 3856]), vecs,
                               din("rwkv_w2", [64, 512]), din("rwkv_a2", [64, 512]), din("rwkv_g2", [128, 512]),
                               din("norm_mix_pre0", [D]), ident, din("c_dplr", [128, 772]))
                    continue
                if ph == "abo":
                    outproj_phase(k, nc, T, cur, nxt, [(oa_t, 0, 512, False), (ob_t, 512, 512, False)], 1024,
                                  din("ab_w_out", [D, D]), din("norm_mix_post0", [D]), ident)
                    cur = nxt
                    continue
                if ph == "ret":
                    y_t = [Buf(a_, f"ys{t_}") for t_, a_ in enumerate(
                        (lambda yb: [yb[t_ * 128:(t_ + 1) * 128, :] for t_ in range(NT)])(
                            nc.dram_tensor("y_scr", [T, 2048], BF16, kind="Internal").ap()))]
                    ret_phase(k, nc, T, cur, y_t, din("ret_w_in", [D, 6144]), din("norm_mix_pre1", [D]), ident,
                              din("c_rot", [T, 128]), din("c_dqk", [128, 24]), din("c_maskT", [128, 128]))
                    continue
                if ph == "reto":
                    outproj_phase(k, nc, T, cur, nxt, [(y_t, 0, 2048, True)], 2048, din("ret_w_out", [2048, D]),
                                  din("norm_mix_post1", [D]), ident, gscale=din("ret_gn_w", [2048]))
                    cur = nxt
                    continue
                if ph.startswith("mlp"):
                    l = ph[3:]
                    mlp_phase(k, nc, T, cur, nxt, din("mlp_w_up" + l, [D, DFF]), din("mlp_w_down" + l, [DFF, D]),
                              din("norm_mlp_pre" + l, [D]), din("norm_mlp_post" + l, [D]), ident)
                cur = nxt
            k.barrier()
    return nc


PHASES = ("gdn", "rwkv", "abo", "mlp0", "ret", "reto", "mlp1")
SEQ = 4096
NCORES = 8


def host_consts(T):
    c = {"c_ident": np.eye(128, dtype=np.float32)}
    c.update(ret_host_consts(T))
    c.update(dplr_host_consts())
    return c


def kernel(**inputs):
    f32 = lambda a: np.ascontiguousarray(np.asarray(a, dtype=np.float32))
    x = f32(inputs["x"])
    B, T, _ = x.shape
    shared = dict(host_consts(T))
    for l in range(2):
        shared[f"mlp_w_up{l}"] = f32(inputs["mlp_w_up"][l])
        shared[f"mlp_w_down{l}"] = f32(inputs["mlp_w_down"][l])
        shared[f"norm_mlp_pre{l}"] = f32(inputs["norm_mlp_pre"][l])
        shared[f"norm_mlp_post{l}"] = f32(inputs["norm_mlp_post"][l])
        shared[f"norm_mix_pre{l}"] = f32(inputs["norm_mix_pre"][l])
        shared[f"norm_mix_post{l}"] = f32(inputs["norm_mix_post"][l])
    for n in ("ab_w_in", "gdn_conv_w", "gdn_a_log", "gdn_dt_bias", "gdn_norm_w", "rwkv_mu", "rwkv_w0", "rwkv_w2",
              "rwkv_a0", "rwkv_a2", "rwkv_g2", "rwkv_k_k", "rwkv_k_a", "rwkv_ln_w", "rwkv_ln_b", "ab_w_out",
              "ret_w_in", "ret_gn_w", "ret_w_out"):
        shared[n] = f32(inputs[n][0])
    shared["rwkv_r_k"] = f32(inputs["rwkv_r_k"][0]).reshape(512)
    nc = build_program(T, PHASES)
    in_maps = [dict(shared, x=x[b]) for b in range(B)]
    res = run_bass_kernel_spmd(nc, in_maps, core_ids=list(range(B)))
    return np.stack([np.asarray(r["out"], dtype=np.float32) for r in res.results], axis=0)
```

```python
import os
import numpy as np
from contextlib import ExitStack
import concourse.bass as bass
import concourse.mybir as mybir
from concourse.bass_utils import run_bass_kernel_spmd

F32 = mybir.dt.float32
BF16 = mybir.dt.bfloat16
AF = mybir.ActivationFunctionType
ALU = mybir.AluOpType
AX = mybir.AxisListType

SAME_ENG_SYNC = True


class Buf:
    __slots__ = ("t", "w", "r", "name")

    def __init__(self, t, name=""):
        self.t = t
        self.w = None
        self.r = {}
        self.name = name

    def __getitem__(self, key):
        return self.t[key]


class KB:
    def __init__(self, nc):
        self.nc = nc
        self.es = ExitStack()
        self.engs = {"pe": nc.tensor, "act": nc.scalar, "dve": nc.vector,
                     "pool": nc.gpsimd, "sp": nc.sync}
        self.nuniq = 0
        self.epoch = -1
        self.sem = {}
        self.cnt = {}
        self._new_epoch()

    def _new_epoch(self):
        self.epoch += 1
        self.known = {e: {} for e in self.engs}
        for e in self.engs:
            self.sem[e] = self.es.enter_context(self.nc.semaphore(f"s_{e}_{self.epoch}"))
            self.cnt[e] = 0

    def sb(self, es, name, shape, dtype):
        self.nuniq += 1
        t = es.enter_context(self.nc.sbuf_tensor(f"{name}_{self.nuniq}", list(shape), dtype))
        return Buf(t, name)

    def ps(self, es, name, shape, dtype=F32):
        self.nuniq += 1
        t = es.enter_context(self.nc.psum_tensor(f"{name}_{self.nuniq}", list(shape), dtype))
        return Buf(t, name)

    def stream(self, name):
        key = "d_" + name
        if key not in self.sem:
            self.sem[key] = self.es.enter_context(self.nc.semaphore(key))
            self.cnt[key] = 0
        return key

    def _waits(self, eng, reads, writes, extra=()):
        need = {}

        def add(ev):
            if ev is not None and ev[2] == self.epoch:
                if need.get(ev[0], 0) < ev[1]:
                    need[ev[0]] = ev[1]

        for b in reads:
            add(b.w)
        for b in writes:
            add(b.w)
            for sk, (v, ep) in b.r.items():
                add((sk, v, ep))
        for ev in extra:
            add(ev)
        e = self.engs[eng]
        kn = self.known[eng]
        for sk, v in need.items():
            if sk == eng and (eng == "pe" or eng == "sp" or not SAME_ENG_SYNC):
                continue
            if kn.get(sk, 0) >= v:
                continue
            e.wait_ge(self.sem[sk], v)
            kn[sk] = v

    def _record(self, ev, reads, writes):
        for b in reads:
            old = b.r.get(ev[0])
            if old is None or old[1] != ev[2] or old[0] < ev[1]:
                b.r[ev[0]] = (ev[1], ev[2])
        for b in writes:
            b.w = ev
            b.r = {}

    def op(self, eng, fn, reads=(), writes=()):
        self._waits(eng, reads, writes)
        ins = fn(self.engs[eng])
        self.cnt[eng] += 1
        ins.then_inc(self.sem[eng], 1)
        self._record((eng, self.cnt[eng], self.epoch), reads, writes)

    def dma(self, q, stream, out, in_, reads=(), writes=(), **kw):
        sk = self.stream(stream)
        prev = (sk, self.cnt[sk], self.epoch) if self.cnt[sk] else None
        self._waits(q, reads, writes, extra=(prev,))
        ins = self.engs[q].dma_start(out=out, in_=in_, **kw)
        self.cnt[sk] += 16
        ins.then_inc(self.sem[sk], 16)
        self._record((sk, self.cnt[sk], self.epoch), reads, writes)

    def dbg(self, name, buf, ap, shape):
        if os.environ.get("DBG", "0") != "1":
            return
        d = Buf(self.nc.dram_tensor("dbg_" + name, list(shape), F32, kind="ExternalOutput").ap(), name)
        self.dma("sp", "dbg", d.t, ap, reads=[buf], writes=[d])

    def barrier(self):
        for eng, e in self.engs.items():
            kn = self.known[eng]
            for sk, v in self.cnt.items():
                if v == 0 or kn.get(sk, 0) >= v:
                    continue
                if sk == eng and eng in ("pe", "sp"):
                    pass
                e.wait_ge(self.sem[sk], v)
                kn[sk] = v
        self._new_epoch()

    def mm(self, out, lhsT, rhs, start, stop, reads, writes, tp=None):
        kw = {}
        if tp is not None and (tp[0] == 96 or tp[1] == 96):
            kw["tile_position"] = tp
        self.op("pe", lambda e: e.matmul(out, lhsT, rhs, start=start, stop=stop, **kw), reads, writes)

    def tr(self, out, in_, ident, reads, writes):
        self.op("pe", lambda e: e.transpose(out, in_, ident), reads, writes)


D = 1024
DFF = 4096
RMS_EPS = 1e-6


def rms_rstd(k, eng_sq, x_ap, xbuf, junk, ss, rstd, n, eps=RMS_EPS):
    k.op("act", lambda e: e.activation(junk[:], x_ap, AF.Square, accum_out=ss[:]),
         reads=[xbuf], writes=[junk, ss])
    k.op("dve", lambda e: e.tensor_scalar(rstd[:], ss[:], 1.0 / n, eps, ALU.mult, ALU.add),
         reads=[ss], writes=[rstd])
    k.op("act", lambda e: e.activation(rstd[:], rstd[:], AF.Sqrt), reads=[rstd], writes=[rstd])
    k.op("dve", lambda e: e.reciprocal(rstd[:], rstd[:]), reads=[rstd], writes=[rstd])


def load_consts(k, es, c_ident):
    ident = k.sb(es, "ident", [128, 128], BF16)
    k.dma("pool", "const", ident[:], c_ident.t[:, :], reads=[c_ident], writes=[ident])
    return ident


def mlp_phase(k, nc, T, hin, hout, w_up, w_dn, g_pre, g_post, ident):
    NT = T // 128
    ST = 2
    NS = NT // ST
    es = ExitStack()
    with es:
        wup = [k.sb(es, f"wup{kc}", [128, DFF], BF16) for kc in range(8)]
        wdn = [k.sb(es, f"wdn{g}", [128, 4, D], BF16) for g in range(8)]
        for kc in range(8):
            k.dma("pool", f"wl{kc % 4}", wup[kc][:], w_up.t[kc * 128:(kc + 1) * 128, :],
                  reads=[w_up], writes=[wup[kc]])
        for g in range(8):
            k.dma("pool", f"wl{g % 4}", wdn[g][:],
                  w_dn.t[g * 512:(g + 1) * 512, :].rearrange("(fc p) d -> p fc d", p=128),
                  reads=[w_dn], writes=[wdn[g]])
        gpre = k.sb(es, "gpre", [128, 8], F32)
        k.dma("sp", "const", gpre[:], g_pre.t.rearrange("(kc p) -> p kc", p=128),
              reads=[g_pre], writes=[gpre], allow_slow_non_contiguous=True)
        gpost = k.sb(es, "gpost", [128, D], F32)
        k.dma("sp", "const", gpost[:], g_post.t.partition_broadcast(128),
              reads=[g_post], writes=[gpost])

        NB = 2
        xt = [[k.sb(es, f"xt{b}_{j}", [128, D], F32) for j in range(ST)] for b in range(NB)]
        uT = [k.sb(es, f"uT{b}", [128, 8, 128 * ST], BF16) for b in range(NB)]
        aT = [k.sb(es, f"aT{b}", [128, 32, 128 * ST], BF16) for b in range(1)]
        xs = [k.sb(es, f"xs{b}", [128, D], BF16) for b in range(2)]
        junk = k.sb(es, "junk", [128, D], BF16)
        rtmp = [k.sb(es, f"rtmp{b}", [128, 128 * ST], F32) for b in range(4)]
        ss = [k.sb(es, f"ss{b}", [128, 1], F32) for b in range(2)]
        rstd = [k.sb(es, f"rstd{b}", [128, 1], F32) for b in range(2)]
        ss2 = [k.sb(es, f"ss2{b}", [128, 1], F32) for b in range(2)]
        rstd2 = [k.sb(es, f"rstd2{b}", [128, 1], F32) for b in range(2)]
        fb = [k.sb(es, f"fb{b}", [128, D], F32) for b in range(2)]
        yb = [k.sb(es, f"yb{b}", [128, D], F32) for b in range(2)]
        pT = [k.ps(es, f"pT{b}", [128, 8, 128], BF16) for b in range(2)]
        pU = [k.ps(es, f"pU{b}", [128, 512], F32) for b in range(3)]
        pD = [k.ps(es, f"pD{b}", [128, 512], F32) for b in range(2)]

        state = {"ti": 0, "iu": 0, "idn": 0}
        NTOK = 128 * ST

        def prep(s):
            b = s % NB
            for j in range(ST):
                t = s * ST + j
                x = xt[b][j]
                ti = state["ti"]
                k.dma("sp", f"x{ti % 2}", x[:], hin[t].t, reads=[hin[t]], writes=[x])
                q = ti % 2
                rms_rstd(k, "act", x[:], x, junk, ss[q], rstd[q], D)
                k.op("dve", lambda e: e.tensor_scalar(xs[q][:], x[:], rstd[q][:, 0:1], None, ALU.mult),
                     reads=[x, rstd[q]], writes=[xs[q]])
                for kc in range(8):
                    k.tr(pT[q][:, kc, :], xs[q][:, kc * 128:(kc + 1) * 128], ident[:],
                         reads=[xs[q], ident], writes=[pT[q]])
                k.op("dve", lambda e: e.tensor_tensor(
                    uT[b][:, :, j * 128:(j + 1) * 128], pT[q][:],
                    gpre[:].unsqueeze(2).to_broadcast([128, 8, 128]), ALU.mult),
                    reads=[pT[q], gpre], writes=[uT[b]])
                state["ti"] += 1

        def up(s):
            b = s % NB
            for fc in range(32):
                pu = pU[state["iu"] % 3]
                state["iu"] += 1
                for kc in range(8):
                    k.mm(pu[:, 0:NTOK], wup[kc][:, fc * 128:(fc + 1) * 128], uT[b][:, kc, :],
                         kc == 0, kc == 7, reads=[wup[kc], uT[b]], writes=[pu])
                tm = rtmp[fc % 4]
                if fc % 2 == 0:
                    k.op("act", lambda e: e.activation(tm[:], pu[:, 0:NTOK], AF.Relu),
                         reads=[pu], writes=[tm])
                else:
                    k.op("dve", lambda e: e.tensor_scalar(tm[:], pu[:, 0:NTOK], 0.0, None, ALU.max),
                         reads=[pu], writes=[tm])
                k.op("pool", lambda e: e.tensor_tensor(aT[0][:, fc, :], tm[:], tm[:], ALU.mult),
                     reads=[tm], writes=[aT[0]])

        def down(s):
            b = s % NB
            for j in range(ST):
                t = s * ST + j
                x = xt[b][j]
                q = state["idn"] % 2
                f = fb[q]
                for nh in range(2):
                    pd = pD[nh]
                    for fc in range(32):
                        k.mm(pd[:], aT[0][:, fc, j * 128:(j + 1) * 128],
                             wdn[fc // 4][:, fc % 4, nh * 512:(nh + 1) * 512],
                             fc == 0, fc == 31, reads=[aT[0], wdn[fc // 4]], writes=[pd])
                    k.op("dve" if nh == 0 else "act",
                         (lambda e: e.tensor_copy(f[:, nh * 512:(nh + 1) * 512], pd[:])) if nh == 0 else
                         (lambda e: e.copy(f[:, nh * 512:(nh + 1) * 512], pd[:])),
                         reads=[pd], writes=[f])
                rms_rstd(k, "act", f[:], f, junk, ss2[q], rstd2[q], D)
                y = yb[q]
                k.op("dve", lambda e: e.scalar_tensor_tensor(y[:], f[:], rstd2[q][:, 0:1], gpost[:],
                                                             ALU.mult, ALU.mult),
                     reads=[f, rstd2[q], gpost], writes=[y])
                k.op("pool", lambda e: e.tensor_tensor(y[:], y[:], x[:], ALU.add),
                     reads=[y, x], writes=[y])
                k.dma("pool", f"y{q}", hout[t].t, y[:], reads=[y], writes=[hout[t]])
                state["idn"] += 1

        prep(0)
        for s in range(NS):
            up(s)
            if s + 1 < NS:
                prep(s + 1)
            down(s)
        k.barrier()


NH_RET = 8
RET_C = 128


def ret_gammas():
    return [float(1.0 - 2.0 ** (-5.0 - h)) for h in range(NH_RET)]


def ret_host_consts(T):
    pos = np.arange(T, dtype=np.float64)
    angle = 1.0 / (10000.0 ** np.linspace(0.0, 1.0, 64))
    th = pos[:, None] * angle[None, :]
    c_rot = np.concatenate([np.cos(th), np.sin(th)], axis=1).astype(np.float32)
    g = np.array(ret_gammas(), dtype=np.float64)
    idx = np.arange(128, dtype=np.float64)
    dq = g[None, :] ** idx[:, None]
    dk = g[None, :] ** (-idx[:, None]) * (128.0 ** -0.5)
    dkc = dk * (g[None, :] ** 128.0)
    c_dqk = np.concatenate([dq, dk, dkc], axis=1).astype(np.float32)
    c_maskT = (idx[:, None] <= idx[None, :]).astype(np.float32)
    return {"c_rot": c_rot, "c_dqk": c_dqk, "c_maskT": c_maskT}


def ret_phase(k, nc, T, hin, y_out, w_in, g_pre, ident, c_rot, c_dqk, c_maskT):
    NT = T // 128
    gam = ret_gammas()
    es = ExitStack()
    with es:
        win = [k.sb(es, f"rwin{kc}", [128, 6144], BF16) for kc in range(8)]
        for kc in range(8):
            k.dma("pool", f"wl{kc % 4}", win[kc][:], w_in.t[kc * 128:(kc + 1) * 128, :],
                  reads=[w_in], writes=[win[kc]])
        gpre = k.sb(es, "gpre", [128, 8], F32)
        k.dma("sp", "const", gpre[:], g_pre.t.rearrange("(kc p) -> p kc", p=128),
              reads=[g_pre], writes=[gpre], allow_slow_non_contiguous=True)
        dqk = k.sb(es, "dqk", [128, 24], F32)
        k.dma("sp", "const", dqk[:], c_dqk.t[:, :], reads=[c_dqk], writes=[dqk])
        maskT = k.sb(es, "maskT", [128, 128], F32)
        k.dma("sp", "const", maskT[:], c_maskT.t[:, :], reads=[c_maskT], writes=[maskT])

        xb = [k.sb(es, f"x{i}", [128, D], F32) for i in range(2)]
        xs = k.sb(es, "xs", [128, D], BF16)
        uT = k.sb(es, "uT", [128, 8, 128], BF16)
        cs = [k.sb(es, f"cs{b}", [128, 128], F32) for b in range(2)]
        vbb = [k.sb(es, f"vb{i}", [128, 2048], BF16) for i in range(2)]
        gsb = [k.sb(es, f"gs{i}", [128, 2048], BF16) for i in range(2)]
        tq = [[k.sb(es, f"tq{b}_{i}", [128, 4, 64], F32) for i in range(4)] for b in range(2)]
        qk_b = k.sb(es, "qk_b", [128, 16, 64, 2], BF16)
        kdb = [k.sb(es, f"kd_b{i}", [128, 8, 64, 2], BF16) for i in range(2)]
        qTb = [k.sb(es, f"qT{i}", [128, 8, 128], BF16) for i in range(2)]
        kT = k.sb(es, "kT", [128, 8, 128], BF16)
        scb = [k.sb(es, f"sc_b{i}", [128, 8, 128], BF16) for i in range(2)]
        Sg = k.sb(es, "Sg", [128, 8, 256], F32)
        Sb = k.sb(es, "Sb", [128, 8, 256], BF16)
        o_f = k.sb(es, "o_f", [128, 8, 256], F32)
        ybb = [k.sb(es, f"y_b{i}", [128, 2048], BF16) for i in range(2)]
        junk = k.sb(es, "junk", [128, D], BF16)
        st = {n: k.sb(es, n, [128, 8], F32) for n in ("s1", "s2", "mean", "msq", "var", "rstdh", "nmr")}
        ss = k.sb(es, "ss", [128, 1], F32)
        rstd = k.sb(es, "rstd", [128, 1], F32)
        pT = [k.ps(es, f"pT{b}", [128, 8, 128], BF16) for b in range(2)]
        pb = [k.ps(es, f"pb{b}", [128, 512], F32) for b in range(6)]
        ib = [0]
        print("ret sbuf bytes remaining", nc.sbuf_bytes_remaining)

        def bank():
            b = pb[ib[0] % 6]
            ib[0] += 1
            return b

        k.op("pool", lambda e: e.memset(Sg[:], 0.0), writes=[Sg])
        k.op("pool", lambda e: e.memset(Sb[:], 0.0), writes=[Sb])

        def load_x(t):
            k.dma("sp", f"x{t % 2}", xb[t % 2][:], hin[t].t, reads=[hin[t]], writes=[xb[t % 2]])
            k.dma("sp", f"cs{t % 2}", cs[t % 2][:], c_rot.t[t * 128:(t + 1) * 128, :], reads=[c_rot], writes=[cs[t % 2]])

        def prep(t):
            s = t % 2
            x, c, vb, gs, kd_b, qT, sc_b = xb[s], cs[s], vbb[s], gsb[s], kdb[s], qTb[s], scb[s]
            if t == 0:
                load_x(0)
            if t + 1 < NT:
                load_x(t + 1)
            norm_transpose(k, x, xs, junk, ss, rstd, pT[0], uT, gpre, ident)
            yield
            cosb = c[:, 0:64].unsqueeze(1).to_broadcast([128, 4, 64])
            sinb = c[:, 64:128].unsqueeze(1).to_broadcast([128, 4, 64])
            for g in range(12):
                pu = bank()
                for kc in range(8):
                    k.mm(pu[:], uT[:, kc, :], win[kc][:, g * 512:(g + 1) * 512], kc == 0, kc == 7,
                         reads=[uT, win[kc]], writes=[pu])
                if g < 4:
                    tt = tq[g % 2]
                    pv = pu[:].rearrange("p (h d t) -> p h d t", h=4, t=2)
                    x1 = pv[:, :, :, 0]
                    x2 = pv[:, :, :, 1]
                    k.op("dve", lambda e: e.tensor_tensor(tt[0][:], x1, cosb, ALU.mult), reads=[pu, c], writes=[tt[0]])
                    k.op("dve", lambda e: e.tensor_tensor(tt[1][:], x2, sinb, ALU.mult), reads=[pu, c], writes=[tt[1]])
                    k.op("dve", lambda e: e.tensor_tensor(tt[2][:], x2, cosb, ALU.mult), reads=[pu, c], writes=[tt[2]])
                    k.op("dve", lambda e: e.tensor_tensor(tt[3][:], x1, sinb, ALU.mult), reads=[pu, c], writes=[tt[3]])
                    k.op("pool", lambda e: e.tensor_tensor(tt[0][:], tt[0][:], tt[1][:], ALU.subtract),
                         reads=[tt[0], tt[1]], writes=[tt[0]])
                    k.op("pool", lambda e: e.tensor_tensor(tt[2][:], tt[2][:], tt[3][:], ALU.add),
                         reads=[tt[2], tt[3]], writes=[tt[2]])
                    h0 = 4 * g
                    sc = dqk[:, h0:h0 + 4].unsqueeze(2).to_broadcast([128, 4, 64])
                    k.op("pool", lambda e: e.tensor_tensor(qk_b[:, h0:h0 + 4, :, 0], tt[0][:], sc, ALU.mult),
                         reads=[tt[0], dqk], writes=[qk_b])
                    k.op("pool", lambda e: e.tensor_tensor(qk_b[:, h0:h0 + 4, :, 1], tt[2][:], sc, ALU.mult),
                         reads=[tt[2], dqk], writes=[qk_b])
                    if g >= 2:
                        hk = 4 * (g - 2)
                        sc2 = dqk[:, 16 + hk:16 + hk + 4].unsqueeze(2).to_broadcast([128, 4, 64])
                        k.op("pool", lambda e: e.tensor_tensor(kd_b[:, hk:hk + 4, :, 0], tt[0][:], sc2, ALU.mult),
                             reads=[tt[0], dqk], writes=[kd_b])
                        k.op("pool", lambda e: e.tensor_tensor(kd_b[:, hk:hk + 4, :, 1], tt[2][:], sc2, ALU.mult),
                             reads=[tt[2], dqk], writes=[kd_b])
                elif g < 8:
                    k.op("act", lambda e: e.copy(vb[:, (g - 4) * 512:(g - 3) * 512], pu[:]), reads=[pu], writes=[vb])
                else:
                    k.op("act", lambda e: e.activation(gs[:, (g - 8) * 512:(g - 7) * 512], pu[:], AF.Silu),
                         reads=[pu], writes=[gs])
                yield
            for i in range(16):
                k.tr(pT[i // 8][:, i % 8, :], qk_b[:, i, :, :].rearrange("p d t -> p (d t)"), ident[:],
                     reads=[qk_b, ident], writes=[pT[i // 8]])
            k.op("act", lambda e: e.copy(qT[:], pT[0][:]), reads=[pT[0]], writes=[qT])
            k.op("dve", lambda e: e.tensor_copy(kT[:], pT[1][:]), reads=[pT[1]], writes=[kT])
            yield
            for hb in range(2):
                psc = bank()
                for hh in range(4):
                    h = hb * 4 + hh
                    k.mm(psc[:, hh * 128:(hh + 1) * 128], kT[:, h, :], qT[:, h, :], True, True,
                         reads=[kT, qT], writes=[psc])
                k.op("dve", lambda e: e.tensor_tensor(
                    sc_b[:, hb * 4:hb * 4 + 4, :], psc[:].rearrange("p (h c) -> p h c", h=4),
                    maskT[:].unsqueeze(1).to_broadcast([128, 4, 128]), ALU.mult),
                    reads=[psc, maskT], writes=[sc_b])
                yield

        def fin(t):
            s = t % 2
            vb, gs, kd_b, qT, sc_b, y_b = vbb[s], gsb[s], kdb[s], qTb[s], scb[s], ybb[s]
            for hp in range(4):
                po = bank()
                for hh in range(2):
                    h = hp * 2 + hh
                    k.mm(po[:, hh * 256:(hh + 1) * 256], sc_b[:, h, :], vb[:, h * 256:(h + 1) * 256], True, False,
                         reads=[sc_b, vb], writes=[po])
                    k.mm(po[:, hh * 256:(hh + 1) * 256], qT[:, h, :], Sb[:, h, :], False, True,
                         reads=[qT, Sb], writes=[po])
                if hp % 2 == 0:
                    k.op("act", lambda e: e.copy(o_f[:, hp * 2:hp * 2 + 2, :],
                                                 po[:].rearrange("p (h v) -> p h v", h=2)), reads=[po], writes=[o_f])
                else:
                    k.op("dve", lambda e: e.tensor_copy(o_f[:, hp * 2:hp * 2 + 2, :],
                                                        po[:].rearrange("p (h v) -> p h v", h=2)),
                         reads=[po], writes=[o_f])
                yield
            for hp in range(4):
                pd = bank()
                for hh in range(2):
                    h = hp * 2 + hh
                    k.mm(pd[:, hh * 256:(hh + 1) * 256], kd_b[:, h, :, :].rearrange("p d t -> p (d t)"),
                         vb[:, h * 256:(h + 1) * 256], True, True, reads=[kd_b, vb], writes=[pd])
                for hh in range(2):
                    h = hp * 2 + hh
                    cc = gam[h] ** 128
                    k.op("dve", lambda e: e.scalar_tensor_tensor(Sg[:, h, :], Sg[:, h, :], cc,
                                                                 pd[:, hh * 256:(hh + 1) * 256], ALU.mult, ALU.add),
                         reads=[Sg, pd], writes=[Sg])
                yield
            k.op("act", lambda e: e.copy(Sb[:], Sg[:]), reads=[Sg], writes=[Sb])
            k.op("dve", lambda e: e.tensor_reduce(st["s1"][:], o_f[:], AX.X, ALU.add), reads=[o_f], writes=[st["s1"]])
            for h in range(8):
                k.op("act", lambda e: e.activation(junk[:, 0:256], o_f[:, h, :], AF.Square,
                                                   accum_out=st["s2"][:, h:h + 1]),
                     reads=[o_f], writes=[junk, st["s2"]])
            yield
            head_stats(k, st, 256, 1e-6)
            yield
            k.op("dve", lambda e: e.tensor_tensor(o_f[:], o_f[:], st["rstdh"][:].unsqueeze(2).to_broadcast([128, 8, 256]),
                                                  ALU.mult), reads=[o_f, st["rstdh"]], writes=[o_f])
            yield
            k.op("pool", lambda e: e.tensor_tensor(o_f[:], o_f[:], st["nmr"][:].unsqueeze(2).to_broadcast([128, 8, 256]),
                                                   ALU.add), reads=[o_f, st["nmr"]], writes=[o_f])
            yield
            k.op("pool", lambda e: e.tensor_tensor(y_b[:], o_f[:].rearrange("p h v -> p (h v)"), gs[:], ALU.mult),
                 reads=[o_f, gs], writes=[y_b])
            k.dma("pool", f"y{s}", y_out[t].t, y_b[:], reads=[y_b], writes=[y_out[t]])
            yield

        run_pipeline(NT, [prep, fin], [3, 2])
        k.barrier()


def outproj_phase(k, nc, T, hin, hout, srcs, K, w_out, g_post, ident, gscale=None):
    NT = T // 128
    NK = K // 128
    NG = NK // 4
    NPT = NK // 8
    es = ExitStack()
    with es:
        wout = [k.sb(es, f"owout{g}", [128, 4, D], BF16) for g in range(NG)]
        for g in range(NG):
            k.dma("pool", f"wl{g % 4}", wout[g][:],
                  w_out.t[g * 512:(g + 1) * 512, :].rearrange("(fc p) d -> p fc d", p=128),
                  reads=[w_out], writes=[wout[g]])
        gpost = k.sb(es, "gpost", [128, D], F32)
        k.dma("sp", "const", gpost[:], g_post.t.partition_broadcast(128), reads=[g_post], writes=[gpost])
        gsc = None
        if gscale is not None:
            gsc = k.sb(es, "gsc", [128, NK], F32)
            k.dma("sp", "const", gsc[:], gscale.t.rearrange("(kc p) -> p kc", p=128),
                  reads=[gscale], writes=[gsc], allow_slow_non_contiguous=True)
        NB = 3
        xb = [k.sb(es, f"x{i}", [128, D], F32) for i in range(NB)]
        need32 = any(not sr[3] for sr in srcs)
        oin = [k.sb(es, f"oin{i}", [128, K], F32) for i in range(NB)] if need32 else None
        oab = [k.sb(es, f"oab{i}", [128, K], BF16) for i in range(NB)]
        oT = [k.sb(es, f"oT{i}", [128, NK, 128], BF16) for i in range(NB)]
        f = [k.sb(es, f"f{i}", [128, D], F32) for i in range(NB)]
        junk = k.sb(es, "junk", [128, D], BF16)
        ss2 = [k.sb(es, f"ss2{i}", [128, 1], F32) for i in range(NB)]
        rstd2 = [k.sb(es, f"rstd2{i}", [128, 1], F32) for i in range(NB)]
        NPB = 3 if NPT == 1 else 2
        pT = [[k.ps(es, f"pT{b}_{j}", [128, 8, 128], BF16) for j in range(NPT)] for b in range(NPB)]
        pb = [k.ps(es, f"pb{b}", [128, 512], F32) for b in range(8 - NPB * NPT)]
        npb = len(pb)
        def front(t):
            s = t % NB
            k.dma("sp", f"x{s}", xb[s][:], hin[t].t, reads=[hin[t]], writes=[xb[s]])
            for j, (tiles_, c0, w, isb) in enumerate(srcs):
                if isb:
                    k.dma("sp", f"o{j}{s}", oab[s][:, c0:c0 + w], tiles_[t].t, reads=[tiles_[t]], writes=[oab[s]])
                else:
                    k.dma("sp", f"o{j}{s}", oin[s][:, c0:c0 + w], tiles_[t].t, reads=[tiles_[t]], writes=[oin[s]])
            if need32:
                k.op("act", lambda e: e.copy(oab[s][:], oin[s][:]), reads=[oin[s]], writes=[oab[s]])
            for i in range(NK):
                k.tr(pT[t % NPB][i // 8][:, i % 8, :], oab[s][:, i * 128:(i + 1) * 128], ident[:],
                     reads=[oab[s], ident], writes=[pT[t % NPB][i // 8]])
            for j in range(NPT):
                if gsc is None:
                    k.op("dve", lambda e: e.tensor_copy(oT[s][:, j * 8:j * 8 + 8, :], pT[t % NPB][j][:]),
                         reads=[pT[t % NPB][j]], writes=[oT[s]])
                else:
                    k.op("dve", lambda e: e.tensor_tensor(oT[s][:, j * 8:j * 8 + 8, :], pT[t % NPB][j][:],
                                                          gsc[:, j * 8:j * 8 + 8].unsqueeze(2).to_broadcast([128, 8, 128]),
                                                          ALU.mult), reads=[pT[t % NPB][j], gsc], writes=[oT[s]])

        def back(t):
            s = t % NB
            for nh in range(2):
                pm = pb[(2 * t + nh) % npb]
                for kc in range(NK):
                    k.mm(pm[:], oT[s][:, kc, :], wout[kc // 4][:, kc % 4, nh * 512:(nh + 1) * 512], kc == 0, kc == NK - 1,
                         reads=[oT[s], wout[kc // 4]], writes=[pm])
                if nh == 0:
                    k.op("dve", lambda e: e.tensor_copy(f[s][:, 0:512], pm[:]), reads=[pm], writes=[f[s]])
                else:
                    k.op("act", lambda e: e.copy(f[s][:, 512:1024], pm[:]), reads=[pm], writes=[f[s]])
            rms_rstd(k, "act", f[s][:], f[s], junk, ss2[s], rstd2[s], D)
            k.op("dve", lambda e: e.scalar_tensor_tensor(f[s][:], f[s][:], rstd2[s][:, 0:1], gpost[:], ALU.mult, ALU.mult),
                 reads=[f[s], rstd2[s], gpost], writes=[f[s]])
            k.op("pool", lambda e: e.tensor_tensor(f[s][:], f[s][:], xb[s][:], ALU.add), reads=[f[s], xb[s]], writes=[f[s]])
            k.dma("pool", f"y{s}", hout[t].t, f[s][:], reads=[f[s]], writes=[hout[t]])

        front(0)
        for t in range(NT):
            if t + 1 < NT:
                front(t + 1)
            back(t)
        k.barrier()


def head_stats(k, st, n, eps):
    k.op("dve", lambda e: e.tensor_scalar(st["mean"][:], st["s1"][:], 1.0 / n, None, ALU.mult),
         reads=[st["s1"]], writes=[st["mean"]])
    k.op("dve", lambda e: e.tensor_tensor(st["msq"][:], st["mean"][:], st["mean"][:], ALU.mult),
         reads=[st["mean"]], writes=[st["msq"]])
    k.op("dve", lambda e: e.scalar_tensor_tensor(st["var"][:], st["s2"][:], 1.0 / n, st["msq"][:],
                                                 ALU.mult, ALU.subtract),
         reads=[st["s2"], st["msq"]], writes=[st["var"]])
    k.op("dve", lambda e: e.tensor_scalar(st["var"][:], st["var"][:], eps, None, ALU.add),
         reads=[st["var"]], writes=[st["var"]])
    k.op("act", lambda e: e.activation(st["var"][:], st["var"][:], AF.Sqrt), reads=[st["var"]], writes=[st["var"]])
    k.op("dve", lambda e: e.reciprocal(st["rstdh"][:], st["var"][:]), reads=[st["var"]], writes=[st["rstdh"]])
    k.op("dve", lambda e: e.scalar_tensor_tensor(st["nmr"][:], st["mean"][:], -1.0, st["rstdh"][:],
                                                 ALU.mult, ALU.mult),
         reads=[st["mean"], st["rstdh"]], writes=[st["nmr"]])


DC = 32
C_MI, C_MX, C_MA, C_MITS, C_MITI, C_MTI, C_BLK = 0, 128, 256, 384, 512, 640, 768


def dplr_host_consts():
    idx = np.arange(128)
    ch = idx // DC
    same = ch[:, None] == ch[None, :]
    lt = idx[:, None] < idx[None, :]
    le = idx[:, None] <= idx[None, :]
    gt = idx[:, None] > idx[None, :]
    blk = ch[:, None] == np.arange(4)[None, :]
    pack = np.concatenate([same & le, same & lt, same & gt, same & lt, same & le, same & gt, blk], axis=1)
    return {"c_dplr": pack.astype(np.float32)}


def dplr_setup(k, es, c_dplr, ident):
    env = {"ident": ident}
    cd = k.sb(es, "cdplr", [128, 772], F32)
    k.dma("sp", "const", cd[:], c_dplr.t[:, :], reads=[c_dplr], writes=[cd])
    env["cd"] = cd
    W = 512
    for n in ("e1", "e2", "e3", "e4"):
        env[n] = k.sb(es, n, [128, W], F32)
    for n in ("rt", "kt", "bt", "at", "Xb", "U_b"):
        env[n] = k.sb(es, n, [128, W], BF16)
    env["kdec"] = [k.sb(es, f"kdec{s}", [128, W], BF16) for s in range(2)]
    env["bdec"] = [k.sb(es, f"bdec{s}", [128, W], BF16) for s in range(2)]
    env["wC"] = [k.sb(es, f"wC{s}", [128, 4, 4], F32) for s in range(2)]
    env["arT"] = [[k.sb(es, f"arT{s}{i}", [128, 4, 2, 128], BF16) for i in range(2)] for s in range(2)]
    env["bkT"] = [k.sb(es, f"bkT{i}", [128, 8, 128], BF16) for i in range(2)]
    env["NA"] = [k.sb(es, f"NA{s}", [128, 8, 2, 128], BF16) for s in range(2)]
    env["KA"] = [k.sb(es, f"KA{s}", [128, 8, 2, 128], BF16) for s in range(2)]
    env["Lm"] = k.sb(es, "Lm", [128, 8, 128], BF16)
    env["NP"] = [k.sb(es, f"NP{i}", [128, 8, 2, 128], BF16) for i in range(2)]
    env["Lp"] = [k.sb(es, f"Lp{i}", [128, 8, 128], BF16) for i in range(2)]
    env["Pf"] = k.sb(es, "Pf", [128, 8, 128], BF16)
    env["Ubar"] = [k.sb(es, f"Ubar{s}", [128, W], F32) for s in range(2)]
    env["AbT"] = [[k.sb(es, f"AbT{s}{i}", [128, 4, 128], BF16) for i in range(2)] for s in range(2)]
    env["St"] = k.sb(es, "St", [128, 4, 64], F32)
    env["Stb"] = k.sb(es, "Stb", [128, 4, 64], BF16)
    env["o"] = k.sb(es, "o_dplr", [128, W], F32)
    env["pT"] = [k.ps(es, f"pT{b}", [128, 8, 128], BF16) for b in range(2)]
    env["pU"] = k.ps(es, "pU", [128, 512], F32)
    env["pO"] = k.ps(es, "pO", [128, 512], F32)
    env["pS"] = env["pU"]
    env["pb"] = [k.ps(es, f"pb{b}", [128, 512], F32) for b in range(4)]
    env["ib"] = 0
    zero = [env["St"], env["Stb"], env["U_b"]] + env["bkT"]
    for s in range(2):
        zero += env["arT"][s] + env["AbT"][s]
    for b_ in zero:
        k.op("pool", lambda e: e.memset(b_[:], 0.0), writes=[b_])
    return env


def bank(env):
    b = env["pb"][env["ib"] % len(env["pb"])]
    env["ib"] += 1
    return b


def dplr_prep(k, env, s, R, Kp, A, B, Vb, ld):
    cd = env["cd"]
    ident = env["ident"]
    e1, e2, e3, e4 = env["e1"], env["e2"], env["e3"], env["e4"]
    ldb, lda = ld
    for (c0, dsts) in ((C_MI, ((e1, 1.0), (e2, -1.0))), (C_MX, ((e3, 1.0),)), (C_MA, ((e4, 1.0),))):
        pc = bank(env)
        k.mm(pc[:], cd[:, c0:c0 + 128], lda, True, True, reads=[cd, ldb], writes=[pc])
        for dst, sc in dsts:
            k.op("act", lambda e: e.activation(dst[:], pc[:], AF.Exp, scale=sc), reads=[pc], writes=[dst])
        yield
    pw = bank(env)
    for j in range(4):
        k.mm(pw[:, j * 4:(j + 1) * 4], lda[:, j * 128:(j + 1) * 128], cd[:, C_BLK:C_BLK + 4], True, True,
             reads=[ldb, cd], writes=[pw])
    wC = env["wC"][s]
    k.op("act", lambda e: e.activation(wC[:], pw[:, 0:16].rearrange("p (j c) -> p j c", j=4), AF.Exp),
         reads=[pw], writes=[wC])
    rt, kt, bt, at = (env[n] for n in ("rt", "kt", "bt", "at"))
    kdec, bdec = env["kdec"][s], env["bdec"][s]
    specs = ((at, A, e3), (rt, R, e1), (bt, B, e2), (kt, Kp, e2), (kdec, Kp, e4), (bdec, B, e4))
    for n, (dst, (sb_, sa), ee) in enumerate(specs):
        k.op("dve" if n % 2 == 0 else "pool", lambda e: e.tensor_tensor(dst[:], sa, ee[:], ALU.mult),
             reads=[sb_, ee], writes=[dst])
        if n % 2 == 1:
            yield
    pTa, pTb = env["pT"]
    arT, bkT = env["arT"][s], env["bkT"]
    for j in range(4):
        k.tr(pTa[:, j, :], at[:, j * 128:(j + 1) * 128], ident[:], reads=[at, ident], writes=[pTa])
        k.tr(pTa[:, 4 + j, :], rt[:, j * 128:(j + 1) * 128], ident[:], reads=[rt, ident], writes=[pTa])
    yield
    for j in range(4):
        k.tr(pTb[:, j, :], bt[:, j * 128:(j + 1) * 128], ident[:], reads=[bt, ident], writes=[pTb])
        k.tr(pTb[:, 4 + j, :], kt[:, j * 128:(j + 1) * 128], ident[:], reads=[kt, ident], writes=[pTb])
    for i, eng in ((0, "act"), (1, "dve")):
        pr = slice(64 * i, 64 * i + 64)
        k.op(eng, lambda e: (e.copy if eng == "act" else e.tensor_copy)(
            arT[i][pr].rearrange("p j s t -> p s j t"), pTa[pr].rearrange("p (s j) t -> p s j t", s=2)),
            reads=[pTa], writes=[arT[i]])
    yield
    for i, eng in ((0, "act"), (1, "dve")):
        pr = slice(64 * i, 64 * i + 64)
        k.op(eng, lambda e: (e.copy if eng == "act" else e.tensor_copy)(bkT[i][pr], pTb[pr]),
             reads=[pTb], writes=[bkT[i]])
    yield
    NA, KA, Lm = env["NA"][s], env["KA"][s], env["Lm"]
    mIT = cd[:, C_MITS:C_MITS + 256].rearrange("p (s t) -> p s t", s=2).unsqueeze(1).to_broadcast([128, 2, 2, 128])
    mTI = cd[:, C_MTI:C_MTI + 128].unsqueeze(1).to_broadcast([128, 4, 128])
    for hb in range(2):
        pl = bank(env)
        for hh in range(4):
            h = hb * 4 + hh
            k.mm(pl[:, hh * 128:(hh + 1) * 128], arT[h % 2][:, h // 2, 0, :], bkT[h % 2][:, h // 2, :], True, True,
                 reads=[arT[h % 2], bkT[h % 2]], writes=[pl])
        k.op("dve", lambda e: e.tensor_tensor(Lm[:, hb * 4:hb * 4 + 4, :], pl[:].rearrange("p (h t) -> p h t", h=4),
                                              mTI, ALU.mult), reads=[pl, cd], writes=[Lm])
        yield
    for dst, off in ((NA, 0), (KA, 4)):
        for hp in range(4):
            pk = bank(env)
            for hh in range(2):
                k.mm(pk[:, hh * 256:(hh + 1) * 256], bkT[hh][:, off + hp, :],
                     arT[hh][:, hp, :, :].rearrange("p s t -> p (s t)"), True, True,
                     reads=[bkT[hh], arT[hh]], writes=[pk])
            k.op("dve", lambda e: e.tensor_tensor(dst[:, 2 * hp:2 * hp + 2, :, :],
                                                  pk[:].rearrange("p (h s t) -> p h s t", h=2, s=2), mIT, ALU.mult),
                 reads=[pk, cd], writes=[dst])
            yield
    NP, Lp, Pf = env["NP"], env["Lp"], env["Pf"]
    idb = ident[:].unsqueeze(1).to_broadcast([128, 8, 128])

    def l_products(lhs_fn, rhs_fn, rbufs, dst):
        for hb in range(2):
            pn = bank(env)
            for hh in range(4):
                h = hb * 4 + hh
                k.mm(pn[:, hh * 128:(hh + 1) * 128], lhs_fn(h), rhs_fn(h), True, True, reads=rbufs, writes=[pn])
            k.op("act", lambda e: e.copy(dst[:, hb * 4:hb * 4 + 4, :], pn[:].rearrange("p (h t) -> p h t", h=4)),
                 reads=[pn], writes=[dst])

    k.op("pool", lambda e: e.tensor_tensor(NP[1][:, :, 1, :], NA[:, :, 0, :], idb, ALU.add),
         reads=[NA, ident], writes=[NP[1]])
    for hb in range(2):
        pn = bank(env)
        for hh in range(4):
            h = hb * 4 + hh
            k.mm(pn[:, hh * 128:(hh + 1) * 128], Lm[:, h, :], NA[:, h, 0, :], True, True, reads=[Lm, NA], writes=[pn])
        k.op("act", lambda e: e.copy(NP[1][:, hb * 4:hb * 4 + 4, 0, :], pn[:].rearrange("p (h t) -> p h t", h=4)),
             reads=[pn], writes=[NP[1]])
    yield
    l_products(lambda h: NA[:, h, 0, :], lambda h: Lm[:, h, :], [NA, Lm], Lp[1])
    yield
    for j in range(1, 4):
        cur_, nxt_ = NP[j % 2], NP[(j + 1) % 2]
        Lc, Ln = Lp[j % 2], Lp[(j + 1) % 2]
        for hp in range(4):
            pn = bank(env)
            for hh in range(2):
                h = hp * 2 + hh
                k.mm(pn[:, hh * 256:(hh + 1) * 256], Lc[:, h, :], cur_[:, h, :, :].rearrange("p s t -> p (s t)"),
                     True, True, reads=[Lc, cur_], writes=[pn])
            pv = pn[:].rearrange("p (h s t) -> p h s t", h=2, s=2)
            k.op("act", lambda e: e.copy(nxt_[:, 2 * hp:2 * hp + 2, 0, :], pv[:, :, 0, :]), reads=[pn], writes=[nxt_])
            k.op("dve", lambda e: e.tensor_tensor(nxt_[:, 2 * hp:2 * hp + 2, 1, :], pv[:, :, 1, :],
                                                  cur_[:, 2 * hp:2 * hp + 2, 1, :], ALU.add),
                 reads=[pn, cur_], writes=[nxt_])
            if hp % 2 == 1:
                yield
        l_products(lambda h: cur_[:, h, 0, :], lambda h: Lc[:, h, :], [cur_, Lc], Ln)
        yield
    for hb in range(2):
        pn = bank(env)
        for hh in range(4):
            h = hb * 4 + hh
            k.mm(pn[:, hh * 128:(hh + 1) * 128], Lp[0][:, h, :], NP[0][:, h, 1, :], True, True,
                 reads=[Lp[0], NP[0]], writes=[pn])
        k.op("dve", lambda e: e.tensor_tensor(Pf[:, hb * 4:hb * 4 + 4, :], pn[:].rearrange("p (h t) -> p h t", h=4),
                                              NP[0][:, hb * 4:hb * 4 + 4, 1, :], ALU.add),
             reads=[pn, NP[0]], writes=[Pf])
    yield
    P = Pf
    Xb, Ubar, AbT = env["Xb"], env["Ubar"][s], env["AbT"][s]
    px = bank(env)
    for h in range(8):
        k.mm(px[:, h * 64:(h + 1) * 64], KA[:, h, 0, :], Vb[:, h * 64:(h + 1) * 64], True, True,
             reads=[KA, Vb], writes=[px])
    k.op("act", lambda e: e.copy(Xb[:], px[:]), reads=[px], writes=[Xb])
    pab = bank(env)
    for h in range(8):
        pr = slice(64 * (h % 2), 64 * (h % 2) + 64)
        k.mm(pab[pr, (h // 2) * 128:(h // 2 + 1) * 128], at[:, h * 64:(h + 1) * 64], P[:, h, :], True, True,
             reads=[at, P], writes=[pab])
    k.op("act", lambda e: e.copy(AbT[0][0:64], pab[0:64].rearrange("p (j t) -> p j t", j=4)),
         reads=[pab], writes=[AbT[0]])
    k.op("dve", lambda e: e.tensor_copy(AbT[1][64:128], pab[64:128].rearrange("p (j t) -> p j t", j=4)),
         reads=[pab], writes=[AbT[1]])
    yield
    pub = bank(env)
    for h in range(8):
        k.mm(pub[:, h * 64:(h + 1) * 64], P[:, h, :], Xb[:, h * 64:(h + 1) * 64], True, True,
             reads=[P, Xb], writes=[pub])
    k.op("dve", lambda e: e.tensor_copy(Ubar[:], pub[:]), reads=[pub], writes=[Ubar])
    yield


def dplr_fin(k, env, s, Vb):
    arT, NA, KA, AbT, Ubar = env["arT"][s], env["NA"][s], env["KA"][s], env["AbT"][s], env["Ubar"][s]
    kdec, bdec, wC = env["kdec"][s], env["bdec"][s], env["wC"][s]
    U_b, St, Stb, pU, pO, pS = env["U_b"], env["St"], env["Stb"], env["pU"], env["pO"], env["pS"]
    for c in range(4):
        rc = slice(DC * c, DC * c + DC)
        cb = DC * c
        for h in range(8):
            k.mm(pU[rc, h * 64:(h + 1) * 64], AbT[h % 2][:, h // 2, rc], Stb[:, h // 2, :], True, True,
                 reads=[AbT[h % 2], Stb], writes=[pU], tp=(0, cb))
        k.op("dve", lambda e: e.tensor_tensor(U_b[rc, :], pU[rc, :], Ubar[rc, :], ALU.add),
             reads=[pU, Ubar], writes=[U_b])
        yield
        for h in range(8):
            pb_ = 64 * (h % 2)
            pr = slice(pb_, pb_ + 64)
            hc = slice(h * 64, (h + 1) * 64)
            jc = slice((h // 2) * 64, (h // 2 + 1) * 64)
            k.mm(pS[pr, jc], kdec[rc, hc], Vb[rc, hc], True, False, reads=[kdec, Vb], writes=[pS], tp=(cb, pb_))
            k.mm(pS[pr, jc], bdec[rc, hc], U_b[rc, hc], False, True, reads=[bdec, U_b], writes=[pS], tp=(cb, pb_))
        for h in range(8):
            hc = slice(h * 64, (h + 1) * 64)
            k.mm(pO[rc, hc], arT[h % 2][:, h // 2, 1, rc], Stb[:, h // 2, :], True, False,
                 reads=[arT[h % 2], Stb], writes=[pO], tp=(0, cb))
            k.mm(pO[rc, hc], NA[:, h, 1, rc], U_b[:, hc], False, False, reads=[NA, U_b], writes=[pO], tp=(0, cb))
            k.mm(pO[rc, hc], KA[:, h, 1, rc], Vb[:, hc], False, True, reads=[KA, Vb], writes=[pO], tp=(0, cb))
        k.op("pool", lambda e: e.tensor_tensor(St[:], St[:], wC[:, :, c].unsqueeze(2).to_broadcast([128, 4, 64]), ALU.mult),
             reads=[St, wC], writes=[St])
        k.op("dve", lambda e: e.tensor_tensor(St[:], St[:], pS[:, 0:256].rearrange("p (j v) -> p j v", j=4), ALU.add),
             reads=[St, pS], writes=[St])
        yield
        k.op("act", lambda e: e.copy(Stb[:], St[:]), reads=[St], writes=[Stb])
        yield
    o = env["o"]
    k.op("act", lambda e: e.copy(o[:], pO[:]), reads=[pO], writes=[o])
    yield


def run_pipeline(NT, stages, weights):
    ns = len(stages)
    for tau in range(NT + ns - 1):
        gens = []
        for i, st in enumerate(stages):
            t = tau - i
            if 0 <= t < NT:
                gens.append([st(t), weights[i]])
        while gens:
            for gw in list(gens):
                for _ in range(gw[1]):
                    try:
                        next(gw[0])
                    except StopIteration:
                        gens.remove(gw)
                        break


def norm_transpose(k, x, xs, junk, ss, rstd, pT, uT, gpre, ident):
    rms_rstd(k, "act", x[:], x, junk, ss, rstd, D)
    k.op("dve", lambda e: e.tensor_scalar(xs[:], x[:], rstd[:, 0:1], None, ALU.mult), reads=[x, rstd], writes=[xs])
    for kc in range(8):
        k.tr(pT[:, kc, :], xs[:, kc * 128:(kc + 1) * 128], ident[:], reads=[xs, ident], writes=[pT])
    k.op("dve", lambda e: e.tensor_tensor(uT[:], pT[:], gpre[:].unsqueeze(2).to_broadcast([128, 8, 128]), ALU.mult),
         reads=[pT, gpre], writes=[uT])


def inv_sqrt(k, dst, src, scale, eps):
    k.op("dve", lambda e: e.tensor_scalar(dst[:], src[:], scale, eps, ALU.mult, ALU.add), reads=[src], writes=[dst])
    k.op("act", lambda e: e.activation(dst[:], dst[:], AF.Sqrt), reads=[dst], writes=[dst])
    k.op("dve", lambda e: e.reciprocal(dst[:], dst[:]), reads=[dst], writes=[dst])


def bc(ap, n, w):
    return ap.unsqueeze(2).to_broadcast([128, n, w])


def gdn_phase(k, nc, T, hin, oa_out, w_in, conv_w, a_log, dt_bias, gnorm_w, g_pre, ident, c_dplr):
    NT = T // 128
    es = ExitStack()
    with es:
        win = [k.sb(es, f"gwin{kc}", [128, 2064], BF16) for kc in range(8)]
        for kc in range(8):
            k.dma("pool", f"wl{kc % 4}", win[kc][:], w_in.t[kc * 128:(kc + 1) * 128, 0:2064],
                  reads=[w_in], writes=[win[kc]])
        gpre = k.sb(es, "gpre", [128, 8], F32)
        k.dma("sp", "const", gpre[:], g_pre.t.rearrange("(kc p) -> p kc", p=128),
              reads=[g_pre], writes=[gpre], allow_slow_non_contiguous=True)
        cwt = k.sb(es, "cwt", [128, 4, 1536], F32)
        k.dma("sp", "const", cwt[:].rearrange("p j c -> p (j c)"),
              conv_w.t.rearrange("j c -> (j c)").partition_broadcast(128), reads=[conv_w], writes=[cwt])
        alb = k.sb(es, "alb", [128, 8], F32)
        dtb = k.sb(es, "dtb", [128, 8], F32)
        gnw = k.sb(es, "gnw", [128, 64], F32)
        k.dma("sp", "const", alb[:], a_log.t.partition_broadcast(128), reads=[a_log], writes=[alb])
        k.dma("sp", "const", dtb[:], dt_bias.t.partition_broadcast(128), reads=[dt_bias], writes=[dtb])
        k.dma("sp", "const", gnw[:], gnorm_w.t.partition_broadcast(128), reads=[gnorm_w], writes=[gnw])
        nea = k.sb(es, "nea", [128, 8], F32)
        k.op("act", lambda e: e.activation(nea[:], alb[:], AF.Exp), reads=[alb], writes=[nea])
        k.op("dve", lambda e: e.tensor_scalar(nea[:], nea[:], -1.0, None, ALU.mult), reads=[nea], writes=[nea])
        env = dplr_setup(k, es, c_dplr, ident)

        xb = [k.sb(es, f"x{i}", [128, D], F32) for i in range(2)]
        xs = k.sb(es, "xs", [128, D], BF16)
        junk = k.sb(es, "junk", [128, D], BF16)
        uT = k.sb(es, "uT", [128, 8, 128], BF16)
        ss = k.sb(es, "ss", [128, 1], F32)
        rstd = k.sb(es, "rstd", [128, 1], F32)
        pq = [k.sb(es, f"pq{i}", [128, 1536], F32) for i in range(2)]
        xsh = [k.sb(es, f"xsh{i}", [128, 1536], F32) for i in range(3)]
        cv = k.sb(es, "cv", [128, 1536], F32)
        zsb = [k.sb(es, f"zs{i}", [128, 512], F32) for i in range(3)]
        ba = k.sb(es, "ba", [128, 16], F32)
        tmp = xsh[2]
        tmp2 = k.sb(es, "tmp2", [128, 512], F32)
        sm = {n: k.sb(es, "g_" + n, [128, 16], F32) for n in ("ssq", "rn", "beta", "sp", "g", "eg", "coef", "rq",
                                                              "ss8", "rs8")}
        Rg = k.sb(es, "Rg", [128, 512], F32)
        Ag = k.sb(es, "Ag", [128, 512], F32)
        Kg = k.sb(es, "Kg", [128, 512], F32)
        Bg = k.sb(es, "Bg", [128, 512], F32)
        ldg = k.sb(es, "ldg", [128, 512], F32)
        Vbb = [k.sb(es, f"Vb{i}", [128, 512], BF16) for i in range(3)]
        oa = tmp2
        k.op("pool", lambda e: e.memset(pq[1][:], 0.0), writes=[pq[1]])
        print("gdn sbuf bytes remaining", nc.sbuf_bytes_remaining)
        v3 = lambda b_, lo: b_[:, lo:lo + 512].rearrange("p (h d) -> p h d", h=8)
        w3 = lambda b_: b_[:].rearrange("p (h d) -> p h d", h=8)

        def load_x(t):
            k.dma("sp", f"x{t % 2}", xb[t % 2][:], hin[t].t, reads=[hin[t]], writes=[xb[t % 2]])

        def prep(t):
            cur, prev = pq[t % 2], pq[(t + 1) % 2]
            x, zs, Vb = xb[t % 2], zsb[t % 3], Vbb[t % 3]
            if t == 0:
                load_x(0)
            if t + 1 < NT:
                load_x(t + 1)
            norm_transpose(k, x, xs, junk, ss, rstd, env["pT"][0], uT, gpre, ident)
            yield
            for g, (c0, c1) in enumerate(((0, 512), (512, 1024), (1024, 1536), (1536, 2048), (2048, 2064))):
                pu = bank(env)
                n = c1 - c0
                for kc in range(8):
                    k.mm(pu[:, 0:n], uT[:, kc, :], win[kc][:, c0:c1], kc == 0, kc == 7, reads=[uT, win[kc]], writes=[pu])
                if g < 3:
                    if g % 2 == 0:
                        k.op("act", lambda e: e.copy(cur[:, c0:c1], pu[:]), reads=[pu], writes=[cur])
                    else:
                        k.op("dve", lambda e: e.tensor_copy(cur[:, c0:c1], pu[:]), reads=[pu], writes=[cur])
                elif g == 3:
                    k.op("act", lambda e: e.activation(zs[:], pu[:], AF.Silu), reads=[pu], writes=[zs])
                else:
                    k.op("dve", lambda e: e.tensor_copy(ba[:], pu[:, 0:16]), reads=[pu], writes=[ba])
                yield
            for j in range(1, 4):
                sh = xsh[j - 1]
                k.dma("sp", f"sh{j}a", sh[j:128, :], cur[0:128 - j, :], reads=[cur], writes=[sh])
                k.dma("sp", f"sh{j}b", sh[0:j, :], prev[128 - j:128, :], reads=[prev], writes=[sh])
            k.op("act", lambda e: e.activation(sm["beta"][:, 0:8], ba[:, 0:8], AF.Sigmoid), reads=[ba], writes=[sm["beta"]])
            k.op("dve", lambda e: e.tensor_tensor(sm["sp"][:, 0:8], ba[:, 8:16], dtb[:], ALU.add),
                 reads=[ba, dtb], writes=[sm["sp"]])
            k.op("act", lambda e: e.activation(sm["sp"][:, 0:8], sm["sp"][:, 0:8], AF.Exp), reads=[sm["sp"]], writes=[sm["sp"]])
            k.op("dve", lambda e: e.tensor_scalar(sm["sp"][:, 0:8], sm["sp"][:, 0:8], 1.0, None, ALU.add),
                 reads=[sm["sp"]], writes=[sm["sp"]])
            k.op("act", lambda e: e.activation(sm["sp"][:, 0:8], sm["sp"][:, 0:8], AF.Ln), reads=[sm["sp"]], writes=[sm["sp"]])
            k.op("dve", lambda e: e.tensor_tensor(sm["g"][:, 0:8], sm["sp"][:, 0:8], nea[:], ALU.mult),
                 reads=[sm["sp"], nea], writes=[sm["g"]])
            k.op("act", lambda e: e.activation(sm["eg"][:, 0:8], sm["g"][:, 0:8], AF.Exp), reads=[sm["g"]], writes=[sm["eg"]])
            k.op("dve", lambda e: e.scalar_tensor_tensor(sm["coef"][:, 0:8], sm["eg"][:, 0:8], -1.0, sm["beta"][:, 0:8],
                                                         ALU.mult, ALU.mult), reads=[sm["eg"], sm["beta"]], writes=[sm["coef"]])
            k.op("act", lambda e: e.copy(w3(ldg), bc(sm["g"][:, 0:8], 8, 64)), reads=[sm["g"]], writes=[ldg])
            yield
            k.op("dve", lambda e: e.tensor_tensor(cv[:], cur[:], cwt[:, 3, :], ALU.mult), reads=[cur, cwt], writes=[cv])
            for j in range(1, 4):
                sh = xsh[j - 1]
                k.op("pool", lambda e: e.tensor_tensor(sh[:], sh[:], cwt[:, 3 - j, :], ALU.mult),
                     reads=[sh, cwt], writes=[sh])
                k.op("dve", lambda e: e.tensor_tensor(cv[:], cv[:], sh[:], ALU.add), reads=[cv, sh], writes=[cv])
                yield
            k.op("act", lambda e: e.activation(cv[:], cv[:], AF.Silu), reads=[cv], writes=[cv])
            yield
            k.op("act", lambda e: e.activation(tmp[:, 0:1024], cv[:, 0:1024], AF.Square), reads=[cv], writes=[tmp])
            k.op("dve", lambda e: e.tensor_reduce(sm["ssq"][:], tmp[:, 0:1024].rearrange("p (h d) -> p h d", h=16), AX.X, ALU.add),
                 reads=[tmp], writes=[sm["ssq"]])
            inv_sqrt(k, sm["rn"], sm["ssq"], 1.0, 1e-6)
            k.op("dve", lambda e: e.tensor_scalar(sm["rq"][:, 0:8], sm["rn"][:, 0:8], 0.125, None, ALU.mult),
                 reads=[sm["rn"]], writes=[sm["rq"]])
            yield
            k.op("dve", lambda e: e.tensor_tensor(w3(Rg), v3(cv, 0), bc(sm["rq"][:, 0:8], 8, 64), ALU.mult),
                 reads=[cv, sm["rq"]], writes=[Rg])
            k.op("pool", lambda e: e.tensor_tensor(w3(Ag), v3(cv, 512), bc(sm["rn"][:, 8:16], 8, 64), ALU.mult),
                 reads=[cv, sm["rn"]], writes=[Ag])
            k.op("dve", lambda e: e.tensor_tensor(w3(Kg), w3(Ag), bc(sm["beta"][:, 0:8], 8, 64), ALU.mult),
                 reads=[Ag, sm["beta"]], writes=[Kg])
            k.op("pool", lambda e: e.tensor_tensor(w3(Bg), w3(Ag), bc(sm["coef"][:, 0:8], 8, 64), ALU.mult),
                 reads=[Ag, sm["coef"]], writes=[Bg])
            k.op("act", lambda e: e.copy(Vb[:], cv[:, 1024:1536]), reads=[cv], writes=[Vb])
            yield

        def prep2(t):
            yield from dplr_prep(k, env, t % 2, (Rg, Rg[:]), (Kg, Kg[:]), (Ag, Ag[:]), (Bg, Bg[:]), Vbb[t % 3],
                                 (ldg, ldg[:]))

        def fin(t):
            zs, Vb = zsb[t % 3], Vbb[t % 3]
            yield from dplr_fin(k, env, t % 2, Vb)
            o = env["o"]
            k.op("act", lambda e: e.activation(tmp2[:], o[:], AF.Square), reads=[o], writes=[tmp2])
            k.op("dve", lambda e: e.tensor_reduce(sm["ss8"][:, 0:8], tmp2[:].rearrange("p (h d) -> p h d", h=8),
                                                  AX.X, ALU.add), reads=[tmp2], writes=[sm["ss8"]])
            yield
            inv_sqrt(k, sm["rs8"], sm["ss8"], 1.0 / 64, RMS_EPS)
            yield
            k.op("dve", lambda e: e.tensor_tensor(w3(oa), w3(o), bc(sm["rs8"][:, 0:8], 8, 64), ALU.mult),
                 reads=[o, sm["rs8"]], writes=[oa])
            k.op("pool", lambda e: e.tensor_tensor(w3(oa), w3(oa), gnw[:].unsqueeze(1).to_broadcast([128, 8, 64]), ALU.mult),
                 reads=[oa, gnw], writes=[oa])
            yield
            k.op("pool", lambda e: e.tensor_tensor(oa[:], oa[:], zs[:], ALU.mult), reads=[oa, zs], writes=[oa])
            k.dma("pool", "y0", oa_out[t].t, oa[:], reads=[oa], writes=[oa_out[t]])
            yield

        run_pipeline(NT, [prep, prep2, fin], [1, 4, 1])
        k.barrier()


def rwkv_phase(k, nc, T, hin, ob_out, w_in, vecs, w2, a2, g2, g_pre, ident, c_dplr):
    NT = T // 128
    es = ExitStack()
    with es:
        win = [k.sb(es, f"rwin{kc}", [128, 1792], BF16) for kc in range(8)]
        for kc in range(8):
            k.dma("pool", f"wl{kc % 4}", win[kc][:], w_in.t[kc * 128:(kc + 1) * 128, 2064:3856],
                  reads=[w_in], writes=[win[kc]])
        w2p = k.sb(es, "w2p", [128, 512], BF16)
        a2p = k.sb(es, "a2p", [128, 512], BF16)
        g2b = k.sb(es, "g2b", [128, 512], BF16)
        k.op("pool", lambda e: e.memset(w2p[:], 0.0), writes=[w2p])
        k.op("pool", lambda e: e.memset(a2p[:], 0.0), writes=[a2p])
        k.dma("pool", "wl0", w2p[0:64, :], w2.t[:, :], reads=[w2], writes=[w2p])
        k.dma("pool", "wl1", a2p[64:128, :], a2.t[:, :], reads=[a2], writes=[a2p])
        k.dma("pool", "wl2", g2b[:], g2.t[:, :], reads=[g2], writes=[g2b])
        gpre = k.sb(es, "gpre", [128, 8], F32)
        k.dma("sp", "const", gpre[:], g_pre.t.rearrange("(kc p) -> p kc", p=128),
              reads=[g_pre], writes=[gpre], allow_slow_non_contiguous=True)
        vb_ = {}
        for n, buf in vecs.items():
            w = 1792 if n == "mu" else 512
            vb_[n] = k.sb(es, "v_" + n, [128, w], F32)
            k.dma("sp", "const", vb_[n][:], buf.t.partition_broadcast(128), reads=[buf], writes=[vb_[n]])
        env = dplr_setup(k, es, c_dplr, ident)

        xb = [k.sb(es, f"x{i}", [128, D], F32) for i in range(2)]
        xs = k.sb(es, "xs", [128, D], BF16)
        junk = k.sb(es, "junk", [128, D], BF16)
        uT = k.sb(es, "uT", [128, 8, 128], BF16)
        ss = k.sb(es, "ss", [128, 1], F32)
        rstd = k.sb(es, "rstd", [128, 1], F32)
        prp = k.sb(es, "prp", [128, 1792], F32)
        rsh = k.sb(es, "rsh", [128, 1792], F32)
        lastrow = k.sb(es, "lastrow", [1, 1792], F32)
        lin = k.sb(es, "lin", [128, 256], BF16)
        linT = k.sb(es, "linT", [128, 2, 128], BF16)
        ldr = k.sb(es, "ldr", [128, 512], F32)
        aic = k.sb(es, "aic", [128, 512], F32)
        gateb = [k.sb(es, f"gate{i}", [128, 512], F32) for i in range(3)]
        tk = k.sb(es, "tk", [128, 512], F32)
        Kr = k.sb(es, "Kr", [128, 512], F32)
        Ar = k.sb(es, "Ar", [128, 512], F32)
        Br = k.sb(es, "Br", [128, 512], F32)
        Rr = k.sb(es, "Rr", [128, 512], F32)
        Vbb = [k.sb(es, f"Vb{i}", [128, 512], BF16) for i in range(3)]
        tmp = k.sb(es, "tmp", [128, 512], F32)
        tmp2 = k.sb(es, "tmp2", [128, 512], F32)
        bonusb = [k.sb(es, f"bonus{i}", [128, 512], F32) for i in range(3)]
        y = k.sb(es, "y", [128, 512], F32)
        sm = {n: k.sb(es, "r_" + n, [128, 8], F32) for n in ("ssq", "rn", "bs", "s1", "s2", "mean", "msq", "var",
                                                             "rstdh", "nmr")}
        k.op("pool", lambda e: e.memset(lastrow[:], 0.0), writes=[lastrow])
        print("rwkv sbuf bytes remaining", nc.sbuf_bytes_remaining)
        v3 = lambda ap: ap.rearrange("p (h d) -> p h d", h=8)

        def load_x(t):
            k.dma("sp", f"x{t % 2}", xb[t % 2][:], hin[t].t, reads=[hin[t]], writes=[xb[t % 2]])

        def prep(t):
            x, gate, Vb, bonus = xb[t % 2], gateb[t % 3], Vbb[t % 3], bonusb[t % 3]
            if t == 0:
                load_x(0)
            if t + 1 < NT:
                load_x(t + 1)
            norm_transpose(k, x, xs, junk, ss, rstd, env["pT"][0], uT, gpre, ident)
            yield
            for g, (c0, c1) in enumerate(((0, 512), (512, 1024), (1024, 1536), (1536, 1792))):
                pu = bank(env)
                n = c1 - c0
                for kc in range(8):
                    k.mm(pu[:, 0:n], uT[:, kc, :], win[kc][:, c0:c1], kc == 0, kc == 7, reads=[uT, win[kc]], writes=[pu])
                if g % 2 == 0:
                    k.op("act", lambda e: e.copy(prp[:, c0:c1], pu[:, 0:n]), reads=[pu], writes=[prp])
                else:
                    k.op("dve", lambda e: e.tensor_copy(prp[:, c0:c1], pu[:, 0:n]), reads=[pu], writes=[prp])
                yield
            k.dma("sp", "rsa", rsh[1:128, :], prp[0:127, :], reads=[prp], writes=[rsh])
            k.dma("sp", "rsb", rsh[0:1, :], lastrow[0:1, :], reads=[lastrow], writes=[rsh])
            k.dma("sp", "rsc", lastrow[0:1, :], prp[127:128, :], reads=[prp, rsh], writes=[lastrow])
            yield
            k.op("pool", lambda e: e.tensor_tensor(rsh[:], rsh[:], prp[:], ALU.subtract), reads=[rsh, prp, lastrow],
                 writes=[rsh])
            k.op("pool", lambda e: e.tensor_tensor(rsh[:], rsh[:], vb_["mu"][:], ALU.mult), reads=[rsh, vb_["mu"]],
                 writes=[rsh])
            yield
            k.op("dve", lambda e: e.tensor_tensor(prp[:], prp[:], rsh[:], ALU.add), reads=[prp, rsh, lastrow],
                 writes=[prp])
            r_ap, kr_ap, vr_ap = prp[:, 0:512], prp[:, 512:1024], prp[:, 1024:1536]
            k.op("act", lambda e: e.activation(lin[:, 0:64], prp[:, 1536:1600], AF.Tanh), reads=[prp], writes=[lin])
            k.op("act", lambda e: e.copy(lin[:, 64:128], prp[:, 1600:1664]), reads=[prp], writes=[lin])
            k.op("act", lambda e: e.activation(lin[:, 128:256], prp[:, 1664:1792], AF.Sigmoid), reads=[prp], writes=[lin])
            k.op("act", lambda e: e.copy(Vb[:], vr_ap), reads=[prp], writes=[Vb])
            k.op("act", lambda e: e.copy(Rr[:], r_ap), reads=[prp], writes=[Rr])
            yield
            pT0 = env["pT"][0]
            for i in range(2):
                k.tr(pT0[:, i, :], lin[:, i * 128:(i + 1) * 128], ident[:], reads=[lin, ident], writes=[pT0])
            k.op("act", lambda e: e.copy(linT[:], pT0[:, 0:2, :]), reads=[pT0], writes=[linT])
            yield
            pz = bank(env)
            k.mm(pz[:], linT[:, 0, :], w2p[:], True, True, reads=[linT, w2p], writes=[pz])
            k.op("dve", lambda e: e.tensor_tensor(ldr[:], pz[:], vb_["w0"][:], ALU.add), reads=[pz, vb_["w0"]], writes=[ldr])
            pa = bank(env)
            k.mm(pa[:], linT[:, 0, :], a2p[:], True, True, reads=[linT, a2p], writes=[pa])
            k.op("dve", lambda e: e.tensor_tensor(aic[:], pa[:], vb_["a0"][:], ALU.add), reads=[pa, vb_["a0"]], writes=[aic])
            yield
            k.op("act", lambda e: e.activation(ldr[:], ldr[:], AF.Sigmoid), reads=[ldr], writes=[ldr])
            k.op("act", lambda e: e.activation(aic[:], aic[:], AF.Sigmoid), reads=[aic], writes=[aic])
            pg = bank(env)
            k.mm(pg[:], linT[:, 1, :], g2b[:], True, True, reads=[linT, g2b], writes=[pg])
            k.op("act", lambda e: e.copy(gate[:], pg[:]), reads=[pg], writes=[gate])
            k.op("act", lambda e: e.mul(ldr[:], ldr[:], -0.6065306597126334), reads=[ldr], writes=[ldr])
            yield
            k.op("dve", lambda e: e.tensor_tensor(tk[:], kr_ap, vb_["k_k"][:], ALU.mult), reads=[prp, vb_["k_k"]], writes=[tk])
            k.op("act", lambda e: e.activation(tmp[:], tk[:], AF.Square), reads=[tk], writes=[tmp])
            k.op("dve", lambda e: e.tensor_reduce(sm["ssq"][:], v3(tmp[:]), AX.X, ALU.add), reads=[tmp], writes=[sm["ssq"]])
            yield
            inv_sqrt(k, sm["rn"], sm["ssq"], 1.0, 1e-6)
            yield
            k.op("dve", lambda e: e.tensor_tensor(v3(tk[:]), v3(tk[:]), bc(sm["rn"][:], 8, 64), ALU.mult),
                 reads=[tk, sm["rn"]], writes=[tk])
            k.op("dve", lambda e: e.scalar_tensor_tensor(Kr[:], aic[:], -1.0, vb_["k_a"][:], ALU.add, ALU.mult),
                 reads=[aic, vb_["k_a"]], writes=[Kr])
            k.op("dve", lambda e: e.scalar_tensor_tensor(Kr[:], Kr[:], 1.0, kr_ap, ALU.add, ALU.mult),
                 reads=[Kr, prp], writes=[Kr])
            k.op("act", lambda e: e.mul(Ar[:], tk[:], -1.0), reads=[tk], writes=[Ar])
            k.op("pool", lambda e: e.tensor_tensor(Br[:], tk[:], aic[:], ALU.mult), reads=[tk, aic], writes=[Br])
            yield
            k.op("pool", lambda e: e.tensor_tensor(tmp[:], r_ap, Kr[:], ALU.mult), reads=[prp, Kr], writes=[tmp])
            k.op("pool", lambda e: e.tensor_tensor(tmp[:], tmp[:], vb_["r_k"][:], ALU.mult), reads=[tmp, vb_["r_k"]],
                 writes=[tmp])
            k.op("dve", lambda e: e.tensor_reduce(sm["bs"][:], v3(tmp[:]), AX.X, ALU.add), reads=[tmp], writes=[sm["bs"]])
            k.op("dve", lambda e: e.tensor_tensor(v3(bonus[:]), v3(vr_ap), bc(sm["bs"][:], 8, 64), ALU.mult),
                 reads=[prp, sm["bs"]], writes=[bonus])
            yield

        def prep2(t):
            yield from dplr_prep(k, env, t % 2, (Rr, Rr[:]), (Kr, Kr[:]), (Ar, Ar[:]), (Br, Br[:]), Vbb[t % 3],
                                 (ldr, ldr[:]))

        def fin(t):
            gate, Vb, bonus = gateb[t % 3], Vbb[t % 3], bonusb[t % 3]
            yield from dplr_fin(k, env, t % 2, Vb)
            o = env["o"]
            k.op("dve", lambda e: e.tensor_reduce(sm["s1"][:], v3(o[:]), AX.X, ALU.add), reads=[o], writes=[sm["s1"]])
            k.op("act", lambda e: e.activation(tmp2[:], o[:], AF.Square), reads=[o], writes=[tmp2])
            k.op("dve", lambda e: e.tensor_reduce(sm["s2"][:], v3(tmp2[:]), AX.X, ALU.add), reads=[tmp2], writes=[sm["s2"]])
            yield
            head_stats(k, sm, 64, 64e-5)
            yield
            k.op("dve", lambda e: e.tensor_tensor(v3(y[:]), v3(o[:]), bc(sm["rstdh"][:], 8, 64), ALU.mult),
                 reads=[o, sm["rstdh"]], writes=[y])
            k.op("pool", lambda e: e.tensor_tensor(v3(y[:]), v3(y[:]), bc(sm["nmr"][:], 8, 64), ALU.add),
                 reads=[y, sm["nmr"]], writes=[y])
            yield
            k.op("dve", lambda e: e.tensor_tensor(y[:], y[:], vb_["ln_w"][:], ALU.mult), reads=[y, vb_["ln_w"]], writes=[y])
            k.op("pool", lambda e: e.tensor_tensor(y[:], y[:], vb_["ln_b"][:], ALU.add), reads=[y, vb_["ln_b"]], writes=[y])
            yield
            k.op("dve", lambda e: e.tensor_tensor(y[:], y[:], bonus[:], ALU.add), reads=[y, bonus], writes=[y])
            k.op("pool", lambda e: e.tensor_tensor(y[:], y[:], gate[:], ALU.mult), reads=[y, gate], writes=[y])
            k.dma("pool", "y0", ob_out[t].t, y[:], reads=[y], writes=[ob_out[t]])
            yield

        run_pipeline(NT, [prep, prep2, fin], [1, 4, 1])
        k.barrier()


def build_program(T, phases=("mlp",), dbg=None):
    nc = bass.Bass("TRN2", target_bir_lowering=False)
    k = KB(nc)

    dins = {}

    def din(name, shape):
        if name not in dins:
            dins[name] = Buf(nc.dram_tensor(name, list(shape), F32, kind="ExternalInput").ap(), name)
        return dins[name]

    NT = T // 128
    x = din("x", [T, D])
    c_ident = din("c_ident", [128, 128])
    out = Buf(nc.dram_tensor("out", [T, D], F32, kind="ExternalOutput").ap(), "out")

    def tiles(buf):
        return [Buf(buf.t[t * 128:(t + 1) * 128, :], f"{buf.name}{t}") for t in range(NT)]

    cur = tiles(x)
    with k.es:
        es = ExitStack()
        with es:
            ident = load_consts(k, es, c_ident)
            for pi, ph in enumerate(phases):
                last = pi == len(phases) - 1
                if ph in ("gdn", "rwkv", "ret"):
                    nxt = None
                elif last:
                    nxt = tiles(out)
                else:
                    scr = Buf(nc.dram_tensor(f"scr{pi}", [T, D], F32, kind="Internal").ap(), f"scr{pi}")
                    nxt = tiles(scr)
                if ph == "gdn":
                    oa_t = tiles(Buf(nc.dram_tensor("oa_scr", [T, 512], F32, kind="Internal").ap(), "oa_scr"))
                    gdn_phase(k, nc, T, cur, oa_t, din("ab_w_in", [D, 3856]), din("gdn_conv_w", [4, 1536]),
                              din("gdn_a_log", [8]), din("gdn_dt_bias", [8]), din("gdn_norm_w", [64]),
                              din("norm_mix_pre0", [D]), ident, din("c_dplr", [128, 772]))
                    continue
                if ph == "rwkv":
                    ob_t = tiles(Buf(nc.dram_tensor("ob_scr", [T, 512], F32, kind="Internal").ap(), "ob_scr"))
                    vecs = {"mu": din("rwkv_mu", [1792])}
                    for n in ("w0", "a0", "k_k", "k_a", "r_k", "ln_w", "ln_b"):
                        vecs[n] = din("rwkv_" + n, [512])
                    rwkv_phase(k, nc, T, cur, ob_t, din("ab_w_in", [D, 3856]), vecs,
                               din("rwkv_w2", [64, 512]), din("rwkv_a2", [64, 512]), din("rwkv_g2", [128, 512]),
                               din("norm_mix_pre0", [D]), ident, din("c_dplr", [128, 772]))
                    continue
                if ph == "abo":
                    outproj_phase(k, nc, T, cur, nxt, [(oa_t, 0, 512, False), (ob_t, 512, 512, False)], 1024,
                                  din("ab_w_out", [D, D]), din("norm_mix_post0", [D]), ident)
                    cur = nxt
                    continue
                if ph == "ret":
                    y_t = [Buf(a_, f"ys{t_}") for t_, a_ in enumerate(
                        (lambda yb: [yb[t_ * 128:(t_ + 1) * 128, :] for t_ in range(NT)])(
                            nc.dram_tensor("y_scr", [T, 2048], BF16, kind="Internal").ap()))]
                    ret_phase(k, nc, T, cur, y_t, din("ret_w_in", [D, 6144]), din("norm_mix_pre1", [D]), ident,
                              din("c_rot", [T, 128]), din("c_dqk", [128, 24]), din("c_maskT", [128, 128]))
                    continue
                if ph == "reto":
                    outproj_phase(k, nc, T, cur, nxt, [(y_t, 0, 2048, True)], 2048, din("ret_w_out", [2048, D]),
                                  din("norm_mix_post1", [D]), ident, gscale=din("ret_gn_w", [2048]))
                    cur = nxt
                    continue
                if ph.startswith("mlp"):
                    l = ph[3:]
                    mlp_phase(k, nc, T, cur, nxt, din("mlp_w_up" + l, [D, DFF]), din("mlp_w_down" + l, [DFF, D]),
                              din("norm_mlp_pre" + l, [D]), din("norm_mlp_post" + l, [D]), ident)
                cur = nxt
            k.barrier()
    return nc


PHASES = ("gdn", "rwkv", "abo", "mlp0", "ret", "reto", "mlp1")
SEQ = 4096
NCORES = 8


def host_consts(T):
    c = {"c_ident": np.eye(128, dtype=np.float32)}
    c.update(ret_host_consts(T))
    c.update(dplr_host_consts())
    return c


def kernel(**inputs):
    f32 = lambda a: np.ascontiguousarray(np.asarray(a, dtype=np.float32))
    x = f32(inputs["x"])
    B, T, _ = x.shape
    shared = dict(host_consts(T))
    for l in range(2):
        shared[f"mlp_w_up{l}"] = f32(inputs["mlp_w_up"][l])
        shared[f"mlp_w_down{l}"] = f32(inputs["mlp_w_down"][l])
        shared[f"norm_mlp_pre{l}"] = f32(inputs["norm_mlp_pre"][l])
        shared[f"norm_mlp_post{l}"] = f32(inputs["norm_mlp_post"][l])
        shared[f"norm_mix_pre{l}"] = f32(inputs["norm_mix_pre"][l])
        shared[f"norm_mix_post{l}"] = f32(inputs["norm_mix_post"][l])
    for n in ("ab_w_in", "gdn_conv_w", "gdn_a_log", "gdn_dt_bias", "gdn_norm_w", "rwkv_mu", "rwkv_w0", "rwkv_w2",
              "rwkv_a0", "rwkv_a2", "rwkv_g2", "rwkv_k_k", "rwkv_k_a", "rwkv_ln_w", "rwkv_ln_b", "ab_w_out",
              "ret_w_in", "ret_gn_w", "ret_w_out"):
        shared[n] = f32(inputs[n][0])
    shared["rwkv_r_k"] = f32(inputs["rwkv_r_k"][0]).reshape(512)
    nc = build_program(T, PHASES)
    in_maps = [dict(shared, x=x[b]) for b in range(B)]
    res = run_bass_kernel_spmd(nc, in_maps, core_ids=list(range(B)))
    return np.stack([np.asarray(r["out"], dtype=np.float32) for r in res.results], axis=0)
```

```python
import os
import numpy as np
from contextlib import ExitStack
import concourse.bass as bass
import concourse.mybir as mybir
from concourse.bass_utils import run_bass_kernel_spmd

F32 = mybir.dt.float32
BF16 = mybir.dt.bfloat16
AF = mybir.ActivationFunctionType
ALU = mybir.AluOpType
AX = mybir.AxisListType

SAME_ENG_SYNC = True


class Buf:
    __slots__ = ("t", "w", "r", "name")

    def __init__(self, t, name=""):
        self.t = t
        self.w = None
        self.r = {}
        self.name = name

    def __getitem__(self, key):
        return self.t[key]


class KB:
    def __init__(self, nc):
        self.nc = nc
        self.es = ExitStack()
        self.engs = {"pe": nc.tensor, "act": nc.scalar, "dve": nc.vector,
                     "pool": nc.gpsimd, "sp": nc.sync}
        self.nuniq = 0
        self.epoch = -1
        self.sem = {}
        self.cnt = {}
        self._new_epoch()

    def _new_epoch(self):
        self.epoch += 1
        self.known = {e: {} for e in self.engs}
        for e in self.engs:
            self.sem[e] = self.es.enter_context(self.nc.semaphore(f"s_{e}_{self.epoch}"))
            self.cnt[e] = 0

    def sb(self, es, name, shape, dtype):
        self.nuniq += 1
        t = es.enter_context(self.nc.sbuf_tensor(f"{name}_{self.nuniq}", list(shape), dtype))
        return Buf(t, name)

    def ps(self, es, name, shape, dtype=F32):
        self.nuniq += 1
        t = es.enter_context(self.nc.psum_tensor(f"{name}_{self.nuniq}", list(shape), dtype))
        return Buf(t, name)

    def stream(self, name):
        key = "d_" + name
        if key not in self.sem:
            self.sem[key] = self.es.enter_context(self.nc.semaphore(key))
            self.cnt[key] = 0
        return key

    def _waits(self, eng, reads, writes, extra=()):
        need = {}

        def add(ev):
            if ev is not None and ev[2] == self.epoch:
                if need.get(ev[0], 0) < ev[1]:
                    need[ev[0]] = ev[1]

        for b in reads:
            add(b.w)
        for b in writes:
            add(b.w)
            for sk, (v, ep) in b.r.items():
                add((sk, v, ep))
        for ev in extra:
            add(ev)
        e = self.engs[eng]
        kn = self.known[eng]
        for sk, v in need.items():
            if sk == eng and (eng == "pe" or eng == "sp" or not SAME_ENG_SYNC):
                continue
            if kn.get(sk, 0) >= v:
                continue
            e.wait_ge(self.sem[sk], v)
            kn[sk] = v

    def _record(self, ev, reads, writes):
        for b in reads:
            old = b.r.get(ev[0])
            if old is None or old[1] != ev[2] or old[0] < ev[1]:
                b.r[ev[0]] = (ev[1], ev[2])
        for b in writes:
            b.w = ev
            b.r = {}

    def op(self, eng, fn, reads=(), writes=()):
        self._waits(eng, reads, writes)
        ins = fn(self.engs[eng])
        self.cnt[eng] += 1
        ins.then_inc(self.sem[eng], 1)
        self._record((eng, self.cnt[eng], self.epoch), reads, writes)

    def dma(self, q, stream, out, in_, reads=(), writes=(), **kw):
        sk = self.stream(stream)
        prev = (sk, self.cnt[sk], self.epoch) if self.cnt[sk] else None
        self._waits(q, reads, writes, extra=(prev,))
        ins = self.engs[q].dma_start(out=out, in_=in_, **kw)
        self.cnt[sk] += 16
        ins.then_inc(self.sem[sk], 16)
        self._record((sk, self.cnt[sk], self.epoch), reads, writes)

    def dbg(self, name, buf, ap, shape):
        if os.environ.get("DBG", "0") != "1":
            return
        d = Buf(self.nc.dram_tensor("dbg_" + name, list(shape), F32, kind="ExternalOutput").ap(), name)
        self.dma("sp", "dbg", d.t, ap, reads=[buf], writes=[d])

    def barrier(self):
        for eng, e in self.engs.items():
            kn = self.known[eng]
            for sk, v in self.cnt.items():
                if v == 0 or kn.get(sk, 0) >= v:
                    continue
                if sk == eng and eng in ("pe", "sp"):
                    pass
                e.wait_ge(self.sem[sk], v)
                kn[sk] = v
        self._new_epoch()

    def mm(self, out, lhsT, rhs, start, stop, reads, writes, tp=None):
        kw = {}
        if tp is not None and (tp[0] == 96 or tp[1] == 96):
            kw["tile_position"] = tp
        self.op("pe", lambda e: e.matmul(out, lhsT, rhs, start=start, stop=stop, **kw), reads, writes)

    def tr(self, out, in_, ident, reads, writes):
        self.op("pe", lambda e: e.transpose(out, in_, ident), reads, writes)


D = 1024
DFF = 4096
RMS_EPS = 1e-6


def rms_rstd(k, eng_sq, x_ap, xbuf, junk, ss, rstd, n, eps=RMS_EPS):
    k.op("act", lambda e: e.activation(junk[:], x_ap, AF.Square, accum_out=ss[:]),
         reads=[xbuf], writes=[junk, ss])
    k.op("dve", lambda e: e.tensor_scalar(rstd[:], ss[:], 1.0 / n, eps, ALU.mult, ALU.add),
         reads=[ss], writes=[rstd])
    k.op("act", lambda e: e.activation(rstd[:], rstd[:], AF.Sqrt), reads=[rstd], writes=[rstd])
    k.op("dve", lambda e: e.reciprocal(rstd[:], rstd[:]), reads=[rstd], writes=[rstd])


def load_consts(k, es, c_ident):
    ident = k.sb(es, "ident", [128, 128], BF16)
    k.dma("pool", "const", ident[:], c_ident.t[:, :], reads=[c_ident], writes=[ident])
    return ident


def mlp_phase(k, nc, T, hin, hout, w_up, w_dn, g_pre, g_post, ident):
    NT = T // 128
    ST = 2
    NS = NT // ST
    es = ExitStack()
    with es:
        wup = [k.sb(es, f"wup{kc}", [128, DFF], BF16) for kc in range(8)]
        wdn = [k.sb(es, f"wdn{g}", [128, 4, D], BF16) for g in range(8)]
        for kc in range(8):
            k.dma("pool", f"wl{kc % 8}", wup[kc][:], w_up.t[kc * 128:(kc + 1) * 128, :],
                  reads=[w_up], writes=[wup[kc]])
        for g in range(8):
            k.dma("pool", f"wl{g % 8}", wdn[g][:],
                  w_dn.t[g * 512:(g + 1) * 512, :].rearrange("(fc p) d -> p fc d", p=128),
                  reads=[w_dn], writes=[wdn[g]])
        gpre = k.sb(es, "gpre", [128, 8], F32)
        k.dma("sp", "const", gpre[:], g_pre.t.rearrange("(kc p) -> p kc", p=128),
              reads=[g_pre], writes=[gpre], allow_slow_non_contiguous=True)
        gpost = k.sb(es, "gpost", [128, D], F32)
        k.dma("sp", "const", gpost[:], g_post.t.partition_broadcast(128),
              reads=[g_post], writes=[gpost])

        NB = 2
        xt = [[k.sb(es, f"xt{b}_{j}", [128, D], F32) for j in range(ST)] for b in range(NB)]
        uT = [k.sb(es, f"uT{b}", [128, 8, 128 * ST], BF16) for b in range(NB)]
        aT = [k.sb(es, f"aT{b}", [128, 32, 128 * ST], BF16) for b in range(1)]
        xs = [k.sb(es, f"xs{b}", [128, D], BF16) for b in range(2)]
        junk = k.sb(es, "junk", [128, D], BF16)
        rtmp = [k.sb(es, f"rtmp{b}", [128, 128 * ST], F32) for b in range(4)]
        ss = [k.sb(es, f"ss{b}", [128, 1], F32) for b in range(2)]
        rstd = [k.sb(es, f"rstd{b}", [128, 1], F32) for b in range(2)]
        ss2 = [k.sb(es, f"ss2{b}", [128, 1], F32) for b in range(2)]
        rstd2 = [k.sb(es, f"rstd2{b}", [128, 1], F32) for b in range(2)]
        fb = [k.sb(es, f"fb{b}", [128, D], F32) for b in range(2)]
        yb = [k.sb(es, f"yb{b}", [128, D], F32) for b in range(2)]
        pT = [k.ps(es, f"pT{b}", [128, 8, 128], BF16) for b in range(2)]
        pU = [k.ps(es, f"pU{b}", [128, 512], F32) for b in range(3)]
        pD = [k.ps(es, f"pD{b}", [128, 512], F32) for b in range(2)]

        state = {"ti": 0, "iu": 0, "idn": 0}
        NTOK = 128 * ST

        def prep(s):
            b = s % NB
            for j in range(ST):
                t = s * ST + j
                x = xt[b][j]
                ti = state["ti"]
                k.dma("sp", f"x{ti % 2}", x[:], hin[t].t, reads=[hin[t]], writes=[x])
                q = ti % 2
                rms_rstd(k, "act", x[:], x, junk, ss[q], rstd[q], D)
                k.op("dve", lambda e: e.tensor_scalar(xs[q][:], x[:], rstd[q][:, 0:1], None, ALU.mult),
                     reads=[x, rstd[q]], writes=[xs[q]])
                for kc in range(8):
                    k.tr(pT[q][:, kc, :], xs[q][:, kc * 128:(kc + 1) * 128], ident[:],
                         reads=[xs[q], ident], writes=[pT[q]])
                k.op("dve", lambda e: e.tensor_tensor(
                    uT[b][:, :, j * 128:(j + 1) * 128], pT[q][:],
                    gpre[:].unsqueeze(2).to_broadcast([128, 8, 128]), ALU.mult),
                    reads=[pT[q], gpre], writes=[uT[b]])
                state["ti"] += 1

        def up(s):
            b = s % NB
            for fc in range(32):
                pu = pU[state["iu"] % 3]
                state["iu"] += 1
                for kc in range(8):
                    k.mm(pu[:, 0:NTOK], wup[kc][:, fc * 128:(fc + 1) * 128], uT[b][:, kc, :],
                         kc == 0, kc == 7, reads=[wup[kc], uT[b]], writes=[pu])
                tm = rtmp[fc % 4]
                if fc % 2 == 0:
                    k.op("act", lambda e: e.activation(tm[:], pu[:, 0:NTOK], AF.Relu),
                         reads=[pu], writes=[tm])
                else:
                    k.op("dve", lambda e: e.tensor_scalar(tm[:], pu[:, 0:NTOK], 0.0, None, ALU.max),
                         reads=[pu], writes=[tm])
                k.op("pool", lambda e: e.tensor_tensor(aT[0][:, fc, :], tm[:], tm[:], ALU.mult),
                     reads=[tm], writes=[aT[0]])

        def down(s):
            b = s % NB
            for j in range(ST):
                t = s * ST + j
                x = xt[b][j]
                q = state["idn"] % 2
                f = fb[q]
                for nh in range(2):
                    pd = pD[nh]
                    for fc in range(32):
                        k.mm(pd[:], aT[0][:, fc, j * 128:(j + 1) * 128],
                             wdn[fc // 4][:, fc % 4, nh * 512:(nh + 1) * 512],
                             fc == 0, fc == 31, reads=[aT[0], wdn[fc // 4]], writes=[pd])
                    k.op("dve" if nh == 0 else "act",
                         (lambda e: e.tensor_copy(f[:, nh * 512:(nh + 1) * 512], pd[:])) if nh == 0 else
                         (lambda e: e.copy(f[:, nh * 512:(nh + 1) * 512], pd[:])),
                         reads=[pd], writes=[f])
                rms_rstd(k, "act", f[:], f, junk, ss2[q], rstd2[q], D)
                y = yb[q]
                k.op("dve", lambda e: e.scalar_tensor_tensor(y[:], f[:], rstd2[q][:, 0:1], gpost[:],
                                                             ALU.mult, ALU.mult),
                     reads=[f, rstd2[q], gpost], writes=[y])
                k.op("pool", lambda e: e.tensor_tensor(y[:], y[:], x[:], ALU.add),
                     reads=[y, x], writes=[y])
                k.dma("pool", f"y{q}", hout[t].t, y[:], reads=[y], writes=[hout[t]])
                state["idn"] += 1

        prep(0)
        for s in range(NS):
            up(s)
            if s + 1 < NS:
                prep(s + 1)
            down(s)
        k.barrier()


NH_RET = 8
RET_C = 128


def ret_gammas():
    return [float(1.0 - 2.0 ** (-5.0 - h)) for h in range(NH_RET)]


def ret_host_consts(T):
    pos = np.arange(T, dtype=np.float64)
    angle = 1.0 / (10000.0 ** np.linspace(0.0, 1.0, 64))
    th = pos[:, None] * angle[None, :]
    c_rot = np.concatenate([np.cos(th), np.sin(th)], axis=1).astype(np.float32)
    g = np.array(ret_gammas(), dtype=np.float64)
    idx = np.arange(128, dtype=np.float64)
    dq = g[None, :] ** idx[:, None]
    dk = g[None, :] ** (-idx[:, None]) * (128.0 ** -0.5)
    dkc = dk * (g[None, :] ** 128.0)
    c_dqk = np.concatenate([dq, dk, dkc], axis=1).astype(np.float32)
    c_maskT = (idx[:, None] <= idx[None, :]).astype(np.float32)
    return {"c_rot": c_rot, "c_dqk": c_dqk, "c_maskT": c_maskT}


def ret_phase(k, nc, T, hin, y_out, w_in, g_pre, ident, c_rot, c_dqk, c_maskT):
    NT = T // 128
    gam = ret_gammas()
    es = ExitStack()
    with es:
        win = [k.sb(es, f"rwin{kc}", [128, 6144], BF16) for kc in range(8)]
        for kc in range(8):
            k.dma("pool", f"wl{kc % 8}", win[kc][:], w_in.t[kc * 128:(kc + 1) * 128, :],
                  reads=[w_in], writes=[win[kc]])
        gpre = k.sb(es, "gpre", [128, 8], F32)
        k.dma("sp", "const", gpre[:], g_pre.t.rearrange("(kc p) -> p kc", p=128),
              reads=[g_pre], writes=[gpre], allow_slow_non_contiguous=True)
        dqk = k.sb(es, "dqk", [128, 24], F32)
        k.dma("sp", "const", dqk[:], c_dqk.t[:, :], reads=[c_dqk], writes=[dqk])
        maskT = k.sb(es, "maskT", [128, 128], F32)
        k.dma("sp", "const", maskT[:], c_maskT.t[:, :], reads=[c_maskT], writes=[maskT])

        xb = [k.sb(es, f"x{i}", [128, D], F32) for i in range(2)]
        xs = k.sb(es, "xs", [128, D], BF16)
        uT = k.sb(es, "uT", [128, 8, 128], BF16)
        cs = [k.sb(es, f"cs{b}", [128, 128], F32) for b in range(2)]
        vbb = [k.sb(es, f"vb{i}", [128, 2048], BF16) for i in range(2)]
        gsb = [k.sb(es, f"gs{i}", [128, 2048], BF16) for i in range(2)]
        tq = [[k.sb(es, f"tq{b}_{i}", [128, 4, 64], F32) for i in range(4)] for b in range(2)]
        qk_b = k.sb(es, "qk_b", [128, 16, 64, 2], BF16)
        kdb = [k.sb(es, f"kd_b{i}", [128, 8, 64, 2], BF16) for i in range(2)]
        qTb = [k.sb(es, f"qT{i}", [128, 8, 128], BF16) for i in range(2)]
        kT = k.sb(es, "kT", [128, 8, 128], BF16)
        scb = [k.sb(es, f"sc_b{i}", [128, 8, 128], BF16) for i in range(2)]
        Sg = k.sb(es, "Sg", [128, 8, 256], F32)
        Sb = k.sb(es, "Sb", [128, 8, 256], BF16)
        o_f = k.sb(es, "o_f", [128, 8, 256], F32)
        ybb = [k.sb(es, f"y_b{i}", [128, 2048], BF16) for i in range(2)]
        junk = k.sb(es, "junk", [128, D], BF16)
        st = {n: k.sb(es, n, [128, 8], F32) for n in ("s1", "s2", "mean", "msq", "var", "rstdh", "nmr")}
        ss = k.sb(es, "ss", [128, 1], F32)
        rstd = k.sb(es, "rstd", [128, 1], F32)
        pT = [k.ps(es, f"pT{b}", [128, 8, 128], BF16) for b in range(2)]
        pb = [k.ps(es, f"pb{b}", [128, 512], F32) for b in range(6)]
        ib = [0]
        print("ret sbuf bytes remaining", nc.sbuf_bytes_remaining)

        def bank():
            b = pb[ib[0] % 6]
            ib[0] += 1
            return b

        k.op("pool", lambda e: e.memset(Sg[:], 0.0), writes=[Sg])
        k.op("pool", lambda e: e.memset(Sb[:], 0.0), writes=[Sb])

        def load_x(t):
            k.dma("sp", f"x{t % 2}", xb[t % 2][:], hin[t].t, reads=[hin[t]], writes=[xb[t % 2]])
            k.dma("sp", f"cs{t % 2}", cs[t % 2][:], c_rot.t[t * 128:(t + 1) * 128, :], reads=[c_rot], writes=[cs[t % 2]])

        def prep(t):
            s = t % 2
            x, c, vb, gs, kd_b, qT, sc_b = xb[s], cs[s], vbb[s], gsb[s], kdb[s], qTb[s], scb[s]
            if t == 0:
                load_x(0)
            if t + 1 < NT:
                load_x(t + 1)
            norm_transpose(k, x, xs, junk, ss, rstd, pT[0], uT, gpre, ident)
            yield
            cosb = c[:, 0:64].unsqueeze(1).to_broadcast([128, 4, 64])
            sinb = c[:, 64:128].unsqueeze(1).to_broadcast([128, 4, 64])
            for g in range(12):
                pu = bank()
                for kc in range(8):
                    k.mm(pu[:], uT[:, kc, :], win[kc][:, g * 512:(g + 1) * 512], kc == 0, kc == 7,
                         reads=[uT, win[kc]], writes=[pu])
                if g < 4:
                    tt = tq[g % 2]
                    pv = pu[:].rearrange("p (h d t) -> p h d t", h=4, t=2)
                    x1 = pv[:, :, :, 0]
                    x2 = pv[:, :, :, 1]
                    k.op("dve", lambda e: e.tensor_tensor(tt[0][:], x1, cosb, ALU.mult), reads=[pu, c], writes=[tt[0]])
                    k.op("dve", lambda e: e.tensor_tensor(tt[1][:], x2, sinb, ALU.mult), reads=[pu, c], writes=[tt[1]])
                    k.op("dve", lambda e: e.tensor_tensor(tt[2][:], x2, cosb, ALU.mult), reads=[pu, c], writes=[tt[2]])
                    k.op("dve", lambda e: e.tensor_tensor(tt[3][:], x1, sinb, ALU.mult), reads=[pu, c], writes=[tt[3]])
                    k.op("pool", lambda e: e.tensor_tensor(tt[0][:], tt[0][:], tt[1][:], ALU.subtract),
                         reads=[tt[0], tt[1]], writes=[tt[0]])
                    k.op("pool", lambda e: e.tensor_tensor(tt[2][:], tt[2][:], tt[3][:], ALU.add),
                         reads=[tt[2], tt[3]], writes=[tt[2]])
                    h0 = 4 * g
                    sc = dqk[:, h0:h0 + 4].unsqueeze(2).to_broadcast([128, 4, 64])
                    k.op("pool", lambda e: e.tensor_tensor(qk_b[:, h0:h0 + 4, :, 0], tt[0][:], sc, ALU.mult),
                         reads=[tt[0], dqk], writes=[qk_b])
                    k.op("pool", lambda e: e.tensor_tensor(qk_b[:, h0:h0 + 4, :, 1], tt[2][:], sc, ALU.mult),
                         reads=[tt[2], dqk], writes=[qk_b])
                    if g >= 2:
                        hk = 4 * (g - 2)
                        sc2 = dqk[:, 16 + hk:16 + hk + 4].unsqueeze(2).to_broadcast([128, 4, 64])
                        k.op("pool", lambda e: e.tensor_tensor(kd_b[:, hk:hk + 4, :, 0], tt[0][:], sc2, ALU.mult),
                             reads=[tt[0], dqk], writes=[kd_b])
                        k.op("pool", lambda e: e.tensor_tensor(kd_b[:, hk:hk + 4, :, 1], tt[2][:], sc2, ALU.mult),
                             reads=[tt[2], dqk], writes=[kd_b])
                elif g < 8:
                    k.op("act", lambda e: e.copy(vb[:, (g - 4) * 512:(g - 3) * 512], pu[:]), reads=[pu], writes=[vb])
                else:
                    k.op("act", lambda e: e.activation(gs[:, (g - 8) * 512:(g - 7) * 512], pu[:], AF.Silu),
                         reads=[pu], writes=[gs])
                yield
            for i in range(16):
                k.tr(pT[i // 8][:, i % 8, :], qk_b[:, i, :, :].rearrange("p d t -> p (d t)"), ident[:],
                     reads=[qk_b, ident], writes=[pT[i // 8]])
            k.op("act", lambda e: e.copy(qT[:], pT[0][:]), reads=[pT[0]], writes=[qT])
            k.op("dve", lambda e: e.tensor_copy(kT[:], pT[1][:]), reads=[pT[1]], writes=[kT])
            yield
            for hb in range(2):
                psc = bank()
                for hh in range(4):
                    h = hb * 4 + hh
                    k.mm(psc[:, hh * 128:(hh + 1) * 128], kT[:, h, :], qT[:, h, :], True, True,
                         reads=[kT, qT], writes=[psc])
                k.op("dve", lambda e: e.tensor_tensor(
                    sc_b[:, hb * 4:hb * 4 + 4, :], psc[:].rearrange("p (h c) -> p h c", h=4),
                    maskT[:].unsqueeze(1).to_broadcast([128, 4, 128]), ALU.mult),
                    reads=[psc, maskT], writes=[sc_b])
                yield

        def fin(t):
            s = t % 2
            vb, gs, kd_b, qT, sc_b, y_b = vbb[s], gsb[s], kdb[s], qTb[s], scb[s], ybb[s]
            for hp in range(4):
                po = bank()
                for hh in range(2):
                    h = hp * 2 + hh
                    k.mm(po[:, hh * 256:(hh + 1) * 256], sc_b[:, h, :], vb[:, h * 256:(h + 1) * 256], True, False,
                         reads=[sc_b, vb], writes=[po])
                    k.mm(po[:, hh * 256:(hh + 1) * 256], qT[:, h, :], Sb[:, h, :], False, True,
                         reads=[qT, Sb], writes=[po])
                if hp % 2 == 0:
                    k.op("act", lambda e: e.copy(o_f[:, hp * 2:hp * 2 + 2, :],
                                                 po[:].rearrange("p (h v) -> p h v", h=2)), reads=[po], writes=[o_f])
                else:
                    k.op("dve", lambda e: e.tensor_copy(o_f[:, hp * 2:hp * 2 + 2, :],
                                                        po[:].rearrange("p (h v) -> p h v", h=2)),
                         reads=[po], writes=[o_f])
                yield
            for hp in range(4):
                pd = bank()
                for hh in range(2):
                    h = hp * 2 + hh
                    k.mm(pd[:, hh * 256:(hh + 1) * 256], kd_b[:, h, :, :].rearrange("p d t -> p (d t)"),
                         vb[:, h * 256:(h + 1) * 256], True, True, reads=[kd_b, vb], writes=[pd])
                for hh in range(2):
                    h = hp * 2 + hh
                    cc = gam[h] ** 128
                    k.op("dve", lambda e: e.scalar_tensor_tensor(Sg[:, h, :], Sg[:, h, :], cc,
                                                                 pd[:, hh * 256:(hh + 1) * 256], ALU.mult, ALU.add),
                         reads=[Sg, pd], writes=[Sg])
                yield
            k.op("act", lambda e: e.copy(Sb[:], Sg[:]), reads=[Sg], writes=[Sb])
            k.op("dve", lambda e: e.tensor_reduce(st["s1"][:], o_f[:], AX.X, ALU.add), reads=[o_f], writes=[st["s1"]])
            for h in range(8):
                k.op("act", lambda e: e.activation(junk[:, 0:256], o_f[:, h, :], AF.Square,
                                                   accum_out=st["s2"][:, h:h + 1]),
                     reads=[o_f], writes=[junk, st["s2"]])
            yield
            head_stats(k, st, 256, 1e-6)
            yield
            k.op("dve", lambda e: e.tensor_tensor(o_f[:], o_f[:], st["rstdh"][:].unsqueeze(2).to_broadcast([128, 8, 256]),
                                                  ALU.mult), reads=[o_f, st["rstdh"]], writes=[o_f])
            yield
            k.op("pool", lambda e: e.tensor_tensor(o_f[:], o_f[:], st["nmr"][:].unsqueeze(2).to_broadcast([128, 8, 256]),
                                                   ALU.add), reads=[o_f, st["nmr"]], writes=[o_f])
            yield
            k.op("pool", lambda e: e.tensor_tensor(y_b[:], o_f[:].rearrange("p h v -> p (h v)"), gs[:], ALU.mult),
                 reads=[o_f, gs], writes=[y_b])
            k.dma("pool", f"y{s}", y_out[t].t, y_b[:], reads=[y_b], writes=[y_out[t]])
            yield

        run_pipeline(NT, [prep, fin], [1, 1])
        k.barrier()


def outproj_phase(k, nc, T, hin, hout, srcs, K, w_out, g_post, ident, gscale=None):
    NT = T // 128
    NK = K // 128
    NG = NK // 4
    NPT = NK // 8
    es = ExitStack()
    with es:
        wout = [k.sb(es, f"owout{g}", [128, 4, D], BF16) for g in range(NG)]
        for g in range(NG):
            k.dma("pool", f"wl{g % 8}", wout[g][:],
                  w_out.t[g * 512:(g + 1) * 512, :].rearrange("(fc p) d -> p fc d", p=128),
                  reads=[w_out], writes=[wout[g]])
        gpost = k.sb(es, "gpost", [128, D], F32)
        k.dma("sp", "const", gpost[:], g_post.t.partition_broadcast(128), reads=[g_post], writes=[gpost])
        gsc = None
        if gscale is not None:
            gsc = k.sb(es, "gsc", [128, NK], F32)
            k.dma("sp", "const", gsc[:], gscale.t.rearrange("(kc p) -> p kc", p=128),
                  reads=[gscale], writes=[gsc], allow_slow_non_contiguous=True)
        NB = 3
        xb = [k.sb(es, f"x{i}", [128, D], F32) for i in range(NB)]
        need32 = any(not sr[3] for sr in srcs)
        oin = [k.sb(es, f"oin{i}", [128, K], F32) for i in range(NB)] if need32 else None
        oab = [k.sb(es, f"oab{i}", [128, K], BF16) for i in range(NB)]
        oT = [k.sb(es, f"oT{i}", [128, NK, 128], BF16) for i in range(NB)]
        f = [k.sb(es, f"f{i}", [128, D], F32) for i in range(NB)]
        junk = k.sb(es, "junk", [128, D], BF16)
        ss2 = [k.sb(es, f"ss2{i}", [128, 1], F32) for i in range(NB)]
        rstd2 = [k.sb(es, f"rstd2{i}", [128, 1], F32) for i in range(NB)]
        NPB = 3 if NPT == 1 else 2
        pT = [[k.ps(es, f"pT{b}_{j}", [128, 8, 128], BF16) for j in range(NPT)] for b in range(NPB)]
        pb = [k.ps(es, f"pb{b}", [128, 512], F32) for b in range(8 - NPB * NPT)]
        npb = len(pb)
        def front(t):
            s = t % NB
            k.dma("sp", f"x{s}", xb[s][:], hin[t].t, reads=[hin[t]], writes=[xb[s]])
            for j, (tiles_, c0, w, isb) in enumerate(srcs):
                if isb:
                    k.dma("sp", f"o{j}{s}", oab[s][:, c0:c0 + w], tiles_[t].t, reads=[tiles_[t]], writes=[oab[s]])
                else:
                    k.dma("sp", f"o{j}{s}", oin[s][:, c0:c0 + w], tiles_[t].t, reads=[tiles_[t]], writes=[oin[s]])
            if need32:
                k.op("act", lambda e: e.copy(oab[s][:], oin[s][:]), reads=[oin[s]], writes=[oab[s]])
            for i in range(NK):
                k.tr(pT[t % NPB][i // 8][:, i % 8, :], oab[s][:, i * 128:(i + 1) * 128], ident[:],
                     reads=[oab[s], ident], writes=[pT[t % NPB][i // 8]])
            for j in range(NPT):
                if gsc is None:
                    k.op("dve", lambda e: e.tensor_copy(oT[s][:, j * 8:j * 8 + 8, :], pT[t % NPB][j][:]),
                         reads=[pT[t % NPB][j]], writes=[oT[s]])
                else:
                    k.op("dve", lambda e: e.tensor_tensor(oT[s][:, j * 8:j * 8 + 8, :], pT[t % NPB][j][:],
                                                          gsc[:, j * 8:j * 8 + 8].unsqueeze(2).to_broadcast([128, 8, 128]),
                                                          ALU.mult), reads=[pT[t % NPB][j], gsc], writes=[oT[s]])

        def back(t):
            s = t % NB
            for nh in range(2):
                pm = pb[(2 * t + nh) % npb]
                for kc in range(NK):
                    k.mm(pm[:], oT[s][:, kc, :], wout[kc // 4][:, kc % 4, nh * 512:(nh + 1) * 512], kc == 0, kc == NK - 1,
                         reads=[oT[s], wout[kc // 4]], writes=[pm])
                if nh == 0:
                    k.op("dve", lambda e: e.tensor_copy(f[s][:, 0:512], pm[:]), reads=[pm], writes=[f[s]])
                else:
                    k.op("act", lambda e: e.copy(f[s][:, 512:1024], pm[:]), reads=[pm], writes=[f[s]])
            rms_rstd(k, "act", f[s][:], f[s], junk, ss2[s], rstd2[s], D)
            k.op("dve", lambda e: e.scalar_tensor_tensor(f[s][:], f[s][:], rstd2[s][:, 0:1], gpost[:], ALU.mult, ALU.mult),
                 reads=[f[s], rstd2[s], gpost], writes=[f[s]])
            k.op("pool", lambda e: e.tensor_tensor(f[s][:], f[s][:], xb[s][:], ALU.add), reads=[f[s], xb[s]], writes=[f[s]])
            k.dma("pool", f"y{s}", hout[t].t, f[s][:], reads=[f[s]], writes=[hout[t]])

        front(0)
        for t in range(NT):
            if t + 1 < NT:
                front(t + 1)
            back(t)
        k.barrier()


def head_stats(k, st, n, eps):
    k.op("dve", lambda e: e.tensor_scalar(st["mean"][:], st["s1"][:], 1.0 / n, None, ALU.mult),
         reads=[st["s1"]], writes=[st["mean"]])
    k.op("dve", lambda e: e.tensor_tensor(st["msq"][:], st["mean"][:], st["mean"][:], ALU.mult),
         reads=[st["mean"]], writes=[st["msq"]])
    k.op("dve", lambda e: e.scalar_tensor_tensor(st["var"][:], st["s2"][:], 1.0 / n, st["msq"][:],
                                                 ALU.mult, ALU.subtract),
         reads=[st["s2"], st["msq"]], writes=[st["var"]])
    k.op("dve", lambda e: e.tensor_scalar(st["var"][:], st["var"][:], eps, None, ALU.add),
         reads=[st["var"]], writes=[st["var"]])
    k.op("act", lambda e: e.activation(st["var"][:], st["var"][:], AF.Sqrt), reads=[st["var"]], writes=[st["var"]])
    k.op("dve", lambda e: e.reciprocal(st["rstdh"][:], st["var"][:]), reads=[st["var"]], writes=[st["rstdh"]])
    k.op("dve", lambda e: e.scalar_tensor_tensor(st["nmr"][:], st["mean"][:], -1.0, st["rstdh"][:],
                                                 ALU.mult, ALU.mult),
         reads=[st["mean"], st["rstdh"]], writes=[st["nmr"]])


DC = 32
C_MI, C_MX, C_MA, C_MITS, C_MITI, C_MTI, C_BLK = 0, 128, 256, 384, 512, 640, 768


def dplr_host_consts():
    idx = np.arange(128)
    ch = idx // DC
    same = ch[:, None] == ch[None, :]
    lt = idx[:, None] < idx[None, :]
    le = idx[:, None] <= idx[None, :]
    gt = idx[:, None] > idx[None, :]
    blk = ch[:, None] == np.arange(4)[None, :]
    pack = np.concatenate([same & le, same & lt, same & gt, same & lt, same & le, same & gt, blk], axis=1)
    return {"c_dplr": pack.astype(np.float32)}


def dplr_setup(k, es, c_dplr, ident):
    env = {"ident": ident}
    cd = k.sb(es, "cdplr", [128, 772], F32)
    k.dma("sp", "const", cd[:], c_dplr.t[:, :], reads=[c_dplr], writes=[cd])
    env["cd"] = cd
    W = 512
    for n in ("e1", "e2", "e3", "e4"):
        env[n] = k.sb(es, n, [128, W], F32)
    for n in ("rt", "kt", "bt", "at", "Xb", "U_b"):
        env[n] = k.sb(es, n, [128, W], BF16)
    env["kdec"] = [k.sb(es, f"kdec{s}", [128, W], BF16) for s in range(2)]
    env["bdec"] = [k.sb(es, f"bdec{s}", [128, W], BF16) for s in range(2)]
    env["wC"] = [k.sb(es, f"wC{s}", [128, 4, 4], F32) for s in range(2)]
    env["arT"] = [[k.sb(es, f"arT{s}{i}", [128, 4, 2, 128], BF16) for i in range(2)] for s in range(2)]
    env["bkT"] = [k.sb(es, f"bkT{i}", [128, 8, 128], BF16) for i in range(2)]
    env["NA"] = [k.sb(es, f"NA{s}", [128, 8, 2, 128], BF16) for s in range(2)]
    env["KA"] = [k.sb(es, f"KA{s}", [128, 8, 2, 128], BF16) for s in range(2)]
    env["Lm"] = k.sb(es, "Lm", [128, 8, 128], BF16)
    env["NP"] = [k.sb(es, f"NP{i}", [128, 8, 2, 128], BF16) for i in range(2)]
    env["Lp"] = [k.sb(es, f"Lp{i}", [128, 8, 128], BF16) for i in range(2)]
    env["Pf"] = k.sb(es, "Pf", [128, 8, 128], BF16)
    env["Ubar"] = [k.sb(es, f"Ubar{s}", [128, W], F32) for s in range(2)]
    env["AbT"] = [[k.sb(es, f"AbT{s}{i}", [128, 4, 128], BF16) for i in range(2)] for s in range(2)]
    env["St"] = k.sb(es, "St", [128, 4, 64], F32)
    env["Stb"] = k.sb(es, "Stb", [128, 4, 64], BF16)
    env["o"] = k.sb(es, "o_dplr", [128, W], F32)
    env["pT"] = [k.ps(es, f"pT{b}", [128, 8, 128], BF16) for b in range(2)]
    env["pU"] = k.ps(es, "pU", [128, 512], F32)
    env["pO"] = k.ps(es, "pO", [128, 512], F32)
    env["pS"] = env["pU"]
    env["pb"] = [k.ps(es, f"pb{b}", [128, 512], F32) for b in range(4)]
    env["ib"] = 0
    zero = [env["St"], env["Stb"], env["U_b"]] + env["bkT"]
    for s in range(2):
        zero += env["arT"][s] + env["AbT"][s]
    for b_ in zero:
        k.op("pool", lambda e: e.memset(b_[:], 0.0), writes=[b_])
    return env


def bank(env):
    b = env["pb"][env["ib"] % len(env["pb"])]
    env["ib"] += 1
    return b


def dplr_prep(k, env, s, R, Kp, A, B, Vb, ld):
    cd = env["cd"]
    ident = env["ident"]
    e1, e2, e3, e4 = env["e1"], env["e2"], env["e3"], env["e4"]
    ldb, lda = ld
    for (c0, dsts) in ((C_MI, ((e1, 1.0), (e2, -1.0))), (C_MX, ((e3, 1.0),)), (C_MA, ((e4, 1.0),))):
        pc = bank(env)
        k.mm(pc[:], cd[:, c0:c0 + 128], lda, True, True, reads=[cd, ldb], writes=[pc])
        for dst, sc in dsts:
            k.op("act", lambda e: e.activation(dst[:], pc[:], AF.Exp, scale=sc), reads=[pc], writes=[dst])
        yield
    pw = bank(env)
    for j in range(4):
        k.mm(pw[:, j * 4:(j + 1) * 4], lda[:, j * 128:(j + 1) * 128], cd[:, C_BLK:C_BLK + 4], True, True,
             reads=[ldb, cd], writes=[pw])
    wC = env["wC"][s]
    k.op("act", lambda e: e.activation(wC[:], pw[:, 0:16].rearrange("p (j c) -> p j c", j=4), AF.Exp),
         reads=[pw], writes=[wC])
    rt, kt, bt, at = (env[n] for n in ("rt", "kt", "bt", "at"))
    kdec, bdec = env["kdec"][s], env["bdec"][s]
    specs = ((at, A, e3), (rt, R, e1), (bt, B, e2), (kt, Kp, e2), (kdec, Kp, e4), (bdec, B, e4))
    for n, (dst, (sb_, sa), ee) in enumerate(specs):
        k.op("dve" if n % 2 == 0 else "pool", lambda e: e.tensor_tensor(dst[:], sa, ee[:], ALU.mult),
             reads=[sb_, ee], writes=[dst])
        if n % 2 == 1:
            yield
    pTa, pTb = env["pT"]
    arT, bkT = env["arT"][s], env["bkT"]
    for j in range(4):
        k.tr(pTa[:, j, :], at[:, j * 128:(j + 1) * 128], ident[:], reads=[at, ident], writes=[pTa])
        k.tr(pTa[:, 4 + j, :], rt[:, j * 128:(j + 1) * 128], ident[:], reads=[rt, ident], writes=[pTa])
    yield
    for j in range(4):
        k.tr(pTb[:, j, :], bt[:, j * 128:(j + 1) * 128], ident[:], reads=[bt, ident], writes=[pTb])
        k.tr(pTb[:, 4 + j, :], kt[:, j * 128:(j + 1) * 128], ident[:], reads=[kt, ident], writes=[pTb])
    for i, eng in ((0, "act"), (1, "dve")):
        pr = slice(64 * i, 64 * i + 64)
        k.op(eng, lambda e: (e.copy if eng == "act" else e.tensor_copy)(
            arT[i][pr].rearrange("p j s t -> p s j t"), pTa[pr].rearrange("p (s j) t -> p s j t", s=2)),
            reads=[pTa], writes=[arT[i]])
    yield
    for i, eng in ((0, "act"), (1, "dve")):
        pr = slice(64 * i, 64 * i + 64)
        k.op(eng, lambda e: (e.copy if eng == "act" else e.tensor_copy)(bkT[i][pr], pTb[pr]),
             reads=[pTb], writes=[bkT[i]])
    yield
    NA, KA, Lm = env["NA"][s], env["KA"][s], env["Lm"]
    mIT = cd[:, C_MITS:C_MITS + 256].rearrange("p (s t) -> p s t", s=2).unsqueeze(1).to_broadcast([128, 2, 2, 128])
    mTI = cd[:, C_MTI:C_MTI + 128].unsqueeze(1).to_broadcast([128, 4, 128])
    for hb in range(2):
        pl = bank(env)
        for hh in range(4):
            h = hb * 4 + hh
            k.mm(pl[:, hh * 128:(hh + 1) * 128], arT[h % 2][:, h // 2, 0, :], bkT[h % 2][:, h // 2, :], True, True,
                 reads=[arT[h % 2], bkT[h % 2]], writes=[pl])
        k.op("dve", lambda e: e.tensor_tensor(Lm[:, hb * 4:hb * 4 + 4, :], pl[:].rearrange("p (h t) -> p h t", h=4),
                                              mTI, ALU.mult), reads=[pl, cd], writes=[Lm])
        yield
    for dst, off in ((NA, 0), (KA, 4)):
        for hp in range(4):
            pk = bank(env)
            for hh in range(2):
                k.mm(pk[:, hh * 256:(hh + 1) * 256], bkT[hh][:, off + hp, :],
                     arT[hh][:, hp, :, :].rearrange("p s t -> p (s t)"), True, True,
                     reads=[bkT[hh], arT[hh]], writes=[pk])
            k.op("dve", lambda e: e.tensor_tensor(dst[:, 2 * hp:2 * hp + 2, :, :],
                                                  pk[:].rearrange("p (h s t) -> p h s t", h=2, s=2), mIT, ALU.mult),
                 reads=[pk, cd], writes=[dst])
            yield
    NP, Lp, Pf = env["NP"], env["Lp"], env["Pf"]
    idb = ident[:].unsqueeze(1).to_broadcast([128, 8, 128])

    def l_products(lhs_fn, rhs_fn, rbufs, dst):
        for hb in range(2):
            pn = bank(env)
            for hh in range(4):
                h = hb * 4 + hh
                k.mm(pn[:, hh * 128:(hh + 1) * 128], lhs_fn(h), rhs_fn(h), True, True, reads=rbufs, writes=[pn])
            k.op("act", lambda e: e.copy(dst[:, hb * 4:hb * 4 + 4, :], pn[:].rearrange("p (h t) -> p h t", h=4)),
                 reads=[pn], writes=[dst])

    k.op("pool", lambda e: e.tensor_tensor(NP[1][:, :, 1, :], NA[:, :, 0, :], idb, ALU.add),
         reads=[NA, ident], writes=[NP[1]])
    for hb in range(2):
        pn = bank(env)
        for hh in range(4):
            h = hb * 4 + hh
            k.mm(pn[:, hh * 128:(hh + 1) * 128], Lm[:, h, :], NA[:, h, 0, :], True, True, reads=[Lm, NA], writes=[pn])
        k.op("act", lambda e: e.copy(NP[1][:, hb * 4:hb * 4 + 4, 0, :], pn[:].rearrange("p (h t) -> p h t", h=4)),
             reads=[pn], writes=[NP[1]])
    yield
    l_products(lambda h: NA[:, h, 0, :], lambda h: Lm[:, h, :], [NA, Lm], Lp[1])
    yield
    for j in range(1, 4):
        cur_, nxt_ = NP[j % 2], NP[(j + 1) % 2]
        Lc, Ln = Lp[j % 2], Lp[(j + 1) % 2]
        for hp in range(4):
            pn = bank(env)
            for hh in range(2):
                h = hp * 2 + hh
                k.mm(pn[:, hh * 256:(hh + 1) * 256], Lc[:, h, :], cur_[:, h, :, :].rearrange("p s t -> p (s t)"),
                     True, True, reads=[Lc, cur_], writes=[pn])
            pv = pn[:].rearrange("p (h s t) -> p h s t", h=2, s=2)
            k.op("act", lambda e: e.copy(nxt_[:, 2 * hp:2 * hp + 2, 0, :], pv[:, :, 0, :]), reads=[pn], writes=[nxt_])
            k.op("dve", lambda e: e.tensor_tensor(nxt_[:, 2 * hp:2 * hp + 2, 1, :], pv[:, :, 1, :],
                                                  cur_[:, 2 * hp:2 * hp + 2, 1, :], ALU.add),
                 reads=[pn, cur_], writes=[nxt_])
            if hp % 2 == 1:
                yield
        l_products(lambda h: cur_[:, h, 0, :], lambda h: Lc[:, h, :], [cur_, Lc], Ln)
        yield
    for hb in range(2):
        pn = bank(env)
        for hh in range(4):
            h = hb * 4 + hh
            k.mm(pn[:, hh * 128:(hh + 1) * 128], Lp[0][:, h, :], NP[0][:, h, 1, :], True, True,
                 reads=[Lp[0], NP[0]], writes=[pn])
        k.op("dve", lambda e: e.tensor_tensor(Pf[:, hb * 4:hb * 4 + 4, :], pn[:].rearrange("p (h t) -> p h t", h=4),
                                              NP[0][:, hb * 4:hb * 4 + 4, 1, :], ALU.add),
             reads=[pn, NP[0]], writes=[Pf])
    yield
    P = Pf
    Xb, Ubar, AbT = env["Xb"], env["Ubar"][s], env["AbT"][s]
    px = bank(env)
    for h in range(8):
        k.mm(px[:, h * 64:(h + 1) * 64], KA[:, h, 0, :], Vb[:, h * 64:(h + 1) * 64], True, True,
             reads=[KA, Vb], writes=[px])
    k.op("act", lambda e: e.copy(Xb[:], px[:]), reads=[px], writes=[Xb])
    pab = bank(env)
    for h in range(8):
        pr = slice(64 * (h % 2), 64 * (h % 2) + 64)
        k.mm(pab[pr, (h // 2) * 128:(h // 2 + 1) * 128], at[:, h * 64:(h + 1) * 64], P[:, h, :], True, True,
             reads=[at, P], writes=[pab])
    k.op("act", lambda e: e.copy(AbT[0][0:64], pab[0:64].rearrange("p (j t) -> p j t", j=4)),
         reads=[pab], writes=[AbT[0]])
    k.op("dve", lambda e: e.tensor_copy(AbT[1][64:128], pab[64:128].rearrange("p (j t) -> p j t", j=4)),
         reads=[pab], writes=[AbT[1]])
    yield
    pub = bank(env)
    for h in range(8):
        k.mm(pub[:, h * 64:(h + 1) * 64], P[:, h, :], Xb[:, h * 64:(h + 1) * 64], True, True,
             reads=[P, Xb], writes=[pub])
    k.op("dve", lambda e: e.tensor_copy(Ubar[:], pub[:]), reads=[pub], writes=[Ubar])
    yield


def dplr_fin(k, env, s, Vb):
    arT, NA, KA, AbT, Ubar = env["arT"][s], env["NA"][s], env["KA"][s], env["AbT"][s], env["Ubar"][s]
    kdec, bdec, wC = env["kdec"][s], env["bdec"][s], env["wC"][s]
    U_b, St, Stb, pU, pO, pS = env["U_b"], env["St"], env["Stb"], env["pU"], env["pO"], env["pS"]
    for c in range(4):
        rc = slice(DC * c, DC * c + DC)
        cb = DC * c
        for h in range(8):
            k.mm(pU[rc, h * 64:(h + 1) * 64], AbT[h % 2][:, h // 2, rc], Stb[:, h // 2, :], True, True,
                 reads=[AbT[h % 2], Stb], writes=[pU], tp=(0, cb))
        k.op("dve", lambda e: e.tensor_tensor(U_b[rc, :], pU[rc, :], Ubar[rc, :], ALU.add),
             reads=[pU, Ubar], writes=[U_b])
        yield
        for h in range(8):
            pb_ = 64 * (h % 2)
            pr = slice(pb_, pb_ + 64)
            hc = slice(h * 64, (h + 1) * 64)
            jc = slice((h // 2) * 64, (h // 2 + 1) * 64)
            k.mm(pS[pr, jc], kdec[rc, hc], Vb[rc, hc], True, False, reads=[kdec, Vb], writes=[pS], tp=(cb, pb_))
            k.mm(pS[pr, jc], bdec[rc, hc], U_b[rc, hc], False, True, reads=[bdec, U_b], writes=[pS], tp=(cb, pb_))
        for h in range(8):
            hc = slice(h * 64, (h + 1) * 64)
            k.mm(pO[rc, hc], arT[h % 2][:, h // 2, 1, rc], Stb[:, h // 2, :], True, False,
                 reads=[arT[h % 2], Stb], writes=[pO], tp=(0, cb))
            k.mm(pO[rc, hc], NA[:, h, 1, rc], U_b[:, hc], False, False, reads=[NA, U_b], writes=[pO], tp=(0, cb))
            k.mm(pO[rc, hc], KA[:, h, 1, rc], Vb[:, hc], False, True, reads=[KA, Vb], writes=[pO], tp=(0, cb))
        k.op("pool", lambda e: e.tensor_tensor(St[:], St[:], wC[:, :, c].unsqueeze(2).to_broadcast([128, 4, 64]), ALU.mult),
             reads=[St, wC], writes=[St])
        k.op("dve", lambda e: e.tensor_tensor(St[:], St[:], pS[:, 0:256].rearrange("p (j v) -> p j v", j=4), ALU.add),
             reads=[St, pS], writes=[St])
        yield
        k.op("act", lambda e: e.copy(Stb[:], St[:]), reads=[St], writes=[Stb])
        yield
    o = env["o"]
    k.op("act", lambda e: e.copy(o[:], pO[:]), reads=[pO], writes=[o])
    yield


def run_pipeline(NT, stages, weights):
    ns = len(stages)
    for tau in range(NT + ns - 1):
        gens = []
        for i, st in enumerate(stages):
            t = tau - i
            if 0 <= t < NT:
                gens.append([st(t), weights[i]])
        while gens:
            for gw in list(gens):
                for _ in range(gw[1]):
                    try:
                        next(gw[0])
                    except StopIteration:
                        gens.remove(gw)
                        break


def norm_transpose(k, x, xs, junk, ss, rstd, pT, uT, gpre, ident):
    rms_rstd(k, "act", x[:], x, junk, ss, rstd, D)
    k.op("dve", lambda e: e.tensor_scalar(xs[:], x[:], rstd[:, 0:1], None, ALU.mult), reads=[x, rstd], writes=[xs])
    for kc in range(8):
        k.tr(pT[:, kc, :], xs[:, kc * 128:(kc + 1) * 128], ident[:], reads=[xs, ident], writes=[pT])
    k.op("dve", lambda e: e.tensor_tensor(uT[:], pT[:], gpre[:].unsqueeze(2).to_broadcast([128, 8, 128]), ALU.mult),
         reads=[pT, gpre], writes=[uT])


def inv_sqrt(k, dst, src, scale, eps):
    k.op("dve", lambda e: e.tensor_scalar(dst[:], src[:], scale, eps, ALU.mult, ALU.add), reads=[src], writes=[dst])
    k.op("act", lambda e: e.activation(dst[:], dst[:], AF.Sqrt), reads=[dst], writes=[dst])
    k.op("dve", lambda e: e.reciprocal(dst[:], dst[:]), reads=[dst], writes=[dst])


def bc(ap, n, w):
    return ap.unsqueeze(2).to_broadcast([128, n, w])


def gdn_phase(k, nc, T, hin, oa_out, w_in, conv_w, a_log, dt_bias, gnorm_w, g_pre, ident, c_dplr):
    NT = T // 128
    es = ExitStack()
    with es:
        win = [k.sb(es, f"gwin{kc}", [128, 2064], BF16) for kc in range(8)]
        for kc in range(8):
            k.dma("pool", f"wl{kc % 8}", win[kc][:], w_in.t[kc * 128:(kc + 1) * 128, 0:2064],
                  reads=[w_in], writes=[win[kc]])
        gpre = k.sb(es, "gpre", [128, 8], F32)
        k.dma("sp", "const", gpre[:], g_pre.t.rearrange("(kc p) -> p kc", p=128),
              reads=[g_pre], writes=[gpre], allow_slow_non_contiguous=True)
        cwt = k.sb(es, "cwt", [128, 4, 1536], F32)
        k.dma("sp", "const", cwt[:].rearrange("p j c -> p (j c)"),
              conv_w.t.rearrange("j c -> (j c)").partition_broadcast(128), reads=[conv_w], writes=[cwt])
        alb = k.sb(es, "alb", [128, 8], F32)
        dtb = k.sb(es, "dtb", [128, 8], F32)
        gnw = k.sb(es, "gnw", [128, 64], F32)
        k.dma("sp", "const", alb[:], a_log.t.partition_broadcast(128), reads=[a_log], writes=[alb])
        k.dma("sp", "const", dtb[:], dt_bias.t.partition_broadcast(128), reads=[dt_bias], writes=[dtb])
        k.dma("sp", "const", gnw[:], gnorm_w.t.partition_broadcast(128), reads=[gnorm_w], writes=[gnw])
        nea = k.sb(es, "nea", [128, 8], F32)
        k.op("act", lambda e: e.activation(nea[:], alb[:], AF.Exp), reads=[alb], writes=[nea])
        k.op("dve", lambda e: e.tensor_scalar(nea[:], nea[:], -1.0, None, ALU.mult), reads=[nea], writes=[nea])
        env = dplr_setup(k, es, c_dplr, ident)

        xb = [k.sb(es, f"x{i}", [128, D], F32) for i in range(2)]
        xs = k.sb(es, "xs", [128, D], BF16)
        junk = k.sb(es, "junk", [128, D], BF16)
        uT = k.sb(es, "uT", [128, 8, 128], BF16)
        ss = k.sb(es, "ss", [128, 1], F32)
        rstd = k.sb(es, "rstd", [128, 1], F32)
        pq = [k.sb(es, f"pq{i}", [128, 1536], F32) for i in range(2)]
        xsh = [k.sb(es, f"xsh{i}", [128, 1536], F32) for i in range(3)]
        cv = k.sb(es, "cv", [128, 1536], F32)
        zsb = [k.sb(es, f"zs{i}", [128, 512], F32) for i in range(3)]
        ba = k.sb(es, "ba", [128, 16], F32)
        tmp = xsh[2]
        tmp2 = k.sb(es, "tmp2", [128, 512], F32)
        sm = {n: k.sb(es, "g_" + n, [128, 16], F32) for n in ("ssq", "rn", "beta", "sp", "g", "eg", "coef", "rq",
                                                              "ss8", "rs8")}
        Rg = k.sb(es, "Rg", [128, 512], F32)
        Ag = k.sb(es, "Ag", [128, 512], F32)
        Kg = k.sb(es, "Kg", [128, 512], F32)
        Bg = k.sb(es, "Bg", [128, 512], F32)
        ldg = k.sb(es, "ldg", [128, 512], F32)
        Vbb = [k.sb(es, f"Vb{i}", [128, 512], BF16) for i in range(3)]
        oa = tmp2
        k.op("pool", lambda e: e.memset(pq[1][:], 0.0), writes=[pq[1]])
        print("gdn sbuf bytes remaining", nc.sbuf_bytes_remaining)
        v3 = lambda b_, lo: b_[:, lo:lo + 512].rearrange("p (h d) -> p h d", h=8)
        w3 = lambda b_: b_[:].rearrange("p (h d) -> p h d", h=8)

        def load_x(t):
            k.dma("sp", f"x{t % 2}", xb[t % 2][:], hin[t].t, reads=[hin[t]], writes=[xb[t % 2]])

        def prep(t):
            cur, prev = pq[t % 2], pq[(t + 1) % 2]
            x, zs, Vb = xb[t % 2], zsb[t % 3], Vbb[t % 3]
            if t == 0:
                load_x(0)
            if t + 1 < NT:
                load_x(t + 1)
            norm_transpose(k, x, xs, junk, ss, rstd, env["pT"][0], uT, gpre, ident)
            yield
            for g, (c0, c1) in enumerate(((0, 512), (512, 1024), (1024, 1536), (1536, 2048), (2048, 2064))):
                pu = bank(env)
                n = c1 - c0
                for kc in range(8):
                    k.mm(pu[:, 0:n], uT[:, kc, :], win[kc][:, c0:c1], kc == 0, kc == 7, reads=[uT, win[kc]], writes=[pu])
                if g < 3:
                    if g % 2 == 0:
                        k.op("act", lambda e: e.copy(cur[:, c0:c1], pu[:]), reads=[pu], writes=[cur])
                    else:
                        k.op("dve", lambda e: e.tensor_copy(cur[:, c0:c1], pu[:]), reads=[pu], writes=[cur])
                elif g == 3:
                    k.op("act", lambda e: e.activation(zs[:], pu[:], AF.Silu), reads=[pu], writes=[zs])
                else:
                    k.op("dve", lambda e: e.tensor_copy(ba[:], pu[:, 0:16]), reads=[pu], writes=[ba])
                yield
            for j in range(1, 4):
                sh = xsh[j - 1]
                k.dma("sp", f"sh{j}a", sh[j:128, :], cur[0:128 - j, :], reads=[cur], writes=[sh])
                k.dma("sp", f"sh{j}b", sh[0:j, :], prev[128 - j:128, :], reads=[prev], writes=[sh])
            k.op("act", lambda e: e.activation(sm["beta"][:, 0:8], ba[:, 0:8], AF.Sigmoid), reads=[ba], writes=[sm["beta"]])
            k.op("dve", lambda e: e.tensor_tensor(sm["sp"][:, 0:8], ba[:, 8:16], dtb[:], ALU.add),
                 reads=[ba, dtb], writes=[sm["sp"]])
            k.op("act", lambda e: e.activation(sm["sp"][:, 0:8], sm["sp"][:, 0:8], AF.Exp), reads=[sm["sp"]], writes=[sm["sp"]])
            k.op("dve", lambda e: e.tensor_scalar(sm["sp"][:, 0:8], sm["sp"][:, 0:8], 1.0, None, ALU.add),
                 reads=[sm["sp"]], writes=[sm["sp"]])
            k.op("act", lambda e: e.activation(sm["sp"][:, 0:8], sm["sp"][:, 0:8], AF.Ln), reads=[sm["sp"]], writes=[sm["sp"]])
            k.op("dve", lambda e: e.tensor_tensor(sm["g"][:, 0:8], sm["sp"][:, 0:8], nea[:], ALU.mult),
                 reads=[sm["sp"], nea], writes=[sm["g"]])
            k.op("act", lambda e: e.activation(sm["eg"][:, 0:8], sm["g"][:, 0:8], AF.Exp), reads=[sm["g"]], writes=[sm["eg"]])
            k.op("dve", lambda e: e.scalar_tensor_tensor(sm["coef"][:, 0:8], sm["eg"][:, 0:8], -1.0, sm["beta"][:, 0:8],
                                                         ALU.mult, ALU.mult), reads=[sm["eg"], sm["beta"]], writes=[sm["coef"]])
            k.op("act", lambda e: e.copy(w3(ldg), bc(sm["g"][:, 0:8], 8, 64)), reads=[sm["g"]], writes=[ldg])
            yield
            k.op("dve", lambda e: e.tensor_tensor(cv[:], cur[:], cwt[:, 3, :], ALU.mult), reads=[cur, cwt], writes=[cv])
            for j in range(1, 4):
                sh = xsh[j - 1]
                k.op("pool", lambda e: e.tensor_tensor(sh[:], sh[:], cwt[:, 3 - j, :], ALU.mult),
                     reads=[sh, cwt], writes=[sh])
                k.op("dve", lambda e: e.tensor_tensor(cv[:], cv[:], sh[:], ALU.add), reads=[cv, sh], writes=[cv])
                yield
            k.op("act", lambda e: e.activation(cv[:], cv[:], AF.Silu), reads=[cv], writes=[cv])
            yield
            k.op("act", lambda e: e.activation(tmp[:, 0:1024], cv[:, 0:1024], AF.Square), reads=[cv], writes=[tmp])
            k.op("dve", lambda e: e.tensor_reduce(sm["ssq"][:], tmp[:, 0:1024].rearrange("p (h d) -> p h d", h=16), AX.X, ALU.add),
                 reads=[tmp], writes=[sm["ssq"]])
            inv_sqrt(k, sm["rn"], sm["ssq"], 1.0, 1e-6)
            k.op("dve", lambda e: e.tensor_scalar(sm["rq"][:, 0:8], sm["rn"][:, 0:8], 0.125, None, ALU.mult),
                 reads=[sm["rn"]], writes=[sm["rq"]])
            yield
            k.op("dve", lambda e: e.tensor_tensor(w3(Rg), v3(cv, 0), bc(sm["rq"][:, 0:8], 8, 64), ALU.mult),
                 reads=[cv, sm["rq"]], writes=[Rg])
            k.op("pool", lambda e: e.tensor_tensor(w3(Ag), v3(cv, 512), bc(sm["rn"][:, 8:16], 8, 64), ALU.mult),
                 reads=[cv, sm["rn"]], writes=[Ag])
            k.op("dve", lambda e: e.tensor_tensor(w3(Kg), w3(Ag), bc(sm["beta"][:, 0:8], 8, 64), ALU.mult),
                 reads=[Ag, sm["beta"]], writes=[Kg])
            k.op("pool", lambda e: e.tensor_tensor(w3(Bg), w3(Ag), bc(sm["coef"][:, 0:8], 8, 64), ALU.mult),
                 reads=[Ag, sm["coef"]], writes=[Bg])
            k.op("act", lambda e: e.copy(Vb[:], cv[:, 1024:1536]), reads=[cv], writes=[Vb])
            yield

        def prep2(t):
            yield from dplr_prep(k, env, t % 2, (Rg, Rg[:]), (Kg, Kg[:]), (Ag, Ag[:]), (Bg, Bg[:]), Vbb[t % 3],
                                 (ldg, ldg[:]))

        def fin(t):
            zs, Vb = zsb[t % 3], Vbb[t % 3]
            yield from dplr_fin(k, env, t % 2, Vb)
            o = env["o"]
            k.op("act", lambda e: e.activation(tmp2[:], o[:], AF.Square), reads=[o], writes=[tmp2])
            k.op("dve", lambda e: e.tensor_reduce(sm["ss8"][:, 0:8], tmp2[:].rearrange("p (h d) -> p h d", h=8),
                                                  AX.X, ALU.add), reads=[tmp2], writes=[sm["ss8"]])
            yield
            inv_sqrt(k, sm["rs8"], sm["ss8"], 1.0 / 64, RMS_EPS)
            yield
            k.op("dve", lambda e: e.tensor_tensor(w3(oa), w3(o), bc(sm["rs8"][:, 0:8], 8, 64), ALU.mult),
                 reads=[o, sm["rs8"]], writes=[oa])
            k.op("pool", lambda e: e.tensor_tensor(w3(oa), w3(oa), gnw[:].unsqueeze(1).to_broadcast([128, 8, 64]), ALU.mult),
                 reads=[oa, gnw], writes=[oa])
            yield
            k.op("pool", lambda e: e.tensor_tensor(oa[:], oa[:], zs[:], ALU.mult), reads=[oa, zs], writes=[oa])
            k.dma("pool", "y0", oa_out[t].t, oa[:], reads=[oa], writes=[oa_out[t]])
            yield

        run_pipeline(NT, [prep, prep2, fin], [1, 4, 1])
        k.barrier()


def rwkv_phase(k, nc, T, hin, ob_out, w_in, vecs, w2, a2, g2, g_pre, ident, c_dplr):
    NT = T // 128
    es = ExitStack()
    with es:
        win = [k.sb(es, f"rwin{kc}", [128, 1792], BF16) for kc in range(8)]
        for kc in range(8):
            k.dma("pool", f"wl{kc % 8}", win[kc][:], w_in.t[kc * 128:(kc + 1) * 128, 2064:3856],
                  reads=[w_in], writes=[win[kc]])
        w2p = k.sb(es, "w2p", [128, 512], BF16)
        a2p = k.sb(es, "a2p", [128, 512], BF16)
        g2b = k.sb(es, "g2b", [128, 512], BF16)
        k.op("pool", lambda e: e.memset(w2p[:], 0.0), writes=[w2p])
        k.op("pool", lambda e: e.memset(a2p[:], 0.0), writes=[a2p])
        k.dma("pool", "wl0", w2p[0:64, :], w2.t[:, :], reads=[w2], writes=[w2p])
        k.dma("pool", "wl1", a2p[64:128, :], a2.t[:, :], reads=[a2], writes=[a2p])
        k.dma("pool", "wl2", g2b[:], g2.t[:, :], reads=[g2], writes=[g2b])
        gpre = k.sb(es, "gpre", [128, 8], F32)
        k.dma("sp", "const", gpre[:], g_pre.t.rearrange("(kc p) -> p kc", p=128),
              reads=[g_pre], writes=[gpre], allow_slow_non_contiguous=True)
        vb_ = {}
        for n, buf in vecs.items():
            w = 1792 if n == "mu" else 512
            vb_[n] = k.sb(es, "v_" + n, [128, w], F32)
            k.dma("sp", "const", vb_[n][:], buf.t.partition_broadcast(128), reads=[buf], writes=[vb_[n]])
        env = dplr_setup(k, es, c_dplr, ident)

        xb = [k.sb(es, f"x{i}", [128, D], F32) for i in range(2)]
        xs = k.sb(es, "xs", [128, D], BF16)
        junk = k.sb(es, "junk", [128, D], BF16)
        uT = k.sb(es, "uT", [128, 8, 128], BF16)
        ss = k.sb(es, "ss", [128, 1], F32)
        rstd = k.sb(es, "rstd", [128, 1], F32)
        prp = k.sb(es, "prp", [128, 1792], F32)
        rsh = k.sb(es, "rsh", [128, 1792], F32)
        lastrow = k.sb(es, "lastrow", [1, 1792], F32)
        lin = k.sb(es, "lin", [128, 256], BF16)
        linT = k.sb(es, "linT", [128, 2, 128], BF16)
        ldr = k.sb(es, "ldr", [128, 512], F32)
        aic = k.sb(es, "aic", [128, 512], F32)
        gateb = [k.sb(es, f"gate{i}", [128, 512], F32) for i in range(3)]
        tk = k.sb(es, "tk", [128, 512], F32)
        Kr = k.sb(es, "Kr", [128, 512], F32)
        Ar = k.sb(es, "Ar", [128, 512], F32)
        Br = k.sb(es, "Br", [128, 512], F32)
        Rr = k.sb(es, "Rr", [128, 512], F32)
        Vbb = [k.sb(es, f"Vb{i}", [128, 512], BF16) for i in range(3)]
        tmp = k.sb(es, "tmp", [128, 512], F32)
        tmp2 = k.sb(es, "tmp2", [128, 512], F32)
        bonusb = [k.sb(es, f"bonus{i}", [128, 512], F32) for i in range(3)]
        y = k.sb(es, "y", [128, 512], F32)
        sm = {n: k.sb(es, "r_" + n, [128, 8], F32) for n in ("ssq", "rn", "bs", "s1", "s2", "mean", "msq", "var",
                                                             "rstdh", "nmr")}
        k.op("pool", lambda e: e.memset(lastrow[:], 0.0), writes=[lastrow])
        print("rwkv sbuf bytes remaining", nc.sbuf_bytes_remaining)
        v3 = lambda ap: ap.rearrange("p (h d) -> p h d", h=8)

        def load_x(t):
            k.dma("sp", f"x{t % 2}", xb[t % 2][:], hin[t].t, reads=[hin[t]], writes=[xb[t % 2]])

        def prep(t):
            x, gate, Vb, bonus = xb[t % 2], gateb[t % 3], Vbb[t % 3], bonusb[t % 3]
            if t == 0:
                load_x(0)
            if t + 1 < NT:
                load_x(t + 1)
            norm_transpose(k, x, xs, junk, ss, rstd, env["pT"][0], uT, gpre, ident)
            yield
            for g, (c0, c1) in enumerate(((0, 512), (512, 1024), (1024, 1536), (1536, 1792))):
                pu = bank(env)
                n = c1 - c0
                for kc in range(8):
                    k.mm(pu[:, 0:n], uT[:, kc, :], win[kc][:, c0:c1], kc == 0, kc == 7, reads=[uT, win[kc]], writes=[pu])
                if g % 2 == 0:
                    k.op("act", lambda e: e.copy(prp[:, c0:c1], pu[:, 0:n]), reads=[pu], writes=[prp])
                else:
                    k.op("dve", lambda e: e.tensor_copy(prp[:, c0:c1], pu[:, 0:n]), reads=[pu], writes=[prp])
                yield
            k.dma("sp", "rsa", rsh[1:128, :], prp[0:127, :], reads=[prp], writes=[rsh])
            k.dma("sp", "rsb", rsh[0:1, :], lastrow[0:1, :], reads=[lastrow], writes=[rsh])
            k.dma("sp", "rsc", lastrow[0:1, :], prp[127:128, :], reads=[prp, rsh], writes=[lastrow])
            yield
            k.op("pool", lambda e: e.tensor_tensor(rsh[:], rsh[:], prp[:], ALU.subtract), reads=[rsh, prp, lastrow],
                 writes=[rsh])
            k.op("pool", lambda e: e.tensor_tensor(rsh[:], rsh[:], vb_["mu"][:], ALU.mult), reads=[rsh, vb_["mu"]],
                 writes=[rsh])
            yield
            k.op("dve", lambda e: e.tensor_tensor(prp[:], prp[:], rsh[:], ALU.add), reads=[prp, rsh, lastrow],
                 writes=[prp])
            r_ap, kr_ap, vr_ap = prp[:, 0:512], prp[:, 512:1024], prp[:, 1024:1536]
            k.op("act", lambda e: e.activation(lin[:, 0:64], prp[:, 1536:1600], AF.Tanh), reads=[prp], writes=[lin])
            k.op("act", lambda e: e.copy(lin[:, 64:128], prp[:, 1600:1664]), reads=[prp], writes=[lin])
            k.op("act", lambda e: e.activation(lin[:, 128:256], prp[:, 1664:1792], AF.Sigmoid), reads=[prp], writes=[lin])
            k.op("act", lambda e: e.copy(Vb[:], vr_ap), reads=[prp], writes=[Vb])
            k.op("act", lambda e: e.copy(Rr[:], r_ap), reads=[prp], writes=[Rr])
            yield
            pT0 = env["pT"][0]
            for i in range(2):
                k.tr(pT0[:, i, :], lin[:, i * 128:(i + 1) * 128], ident[:], reads=[lin, ident], writes=[pT0])
            k.op("act", lambda e: e.copy(linT[:], pT0[:, 0:2, :]), reads=[pT0], writes=[linT])
            yield
            pz = bank(env)
            k.mm(pz[:], linT[:, 0, :], w2p[:], True, True, reads=[linT, w2p], writes=[pz])
            k.op("dve", lambda e: e.tensor_tensor(ldr[:], pz[:], vb_["w0"][:], ALU.add), reads=[pz, vb_["w0"]], writes=[ldr])
            pa = bank(env)
            k.mm(pa[:], linT[:, 0, :], a2p[:], True, True, reads=[linT, a2p], writes=[pa])
            k.op("dve", lambda e: e.tensor_tensor(aic[:], pa[:], vb_["a0"][:], ALU.add), reads=[pa, vb_["a0"]], writes=[aic])
            yield
            k.op("act", lambda e: e.activation(ldr[:], ldr[:], AF.Sigmoid), reads=[ldr], writes=[ldr])
            k.op("act", lambda e: e.activation(aic[:], aic[:], AF.Sigmoid), reads=[aic], writes=[aic])
            pg = bank(env)
            k.mm(pg[:], linT[:, 1, :], g2b[:], True, True, reads=[linT, g2b], writes=[pg])
            k.op("act", lambda e: e.copy(gate[:], pg[:]), reads=[pg], writes=[gate])
            k.op("act", lambda e: e.mul(ldr[:], ldr[:], -0.6065306597126334), reads=[ldr], writes=[ldr])
            yield
            k.op("dve", lambda e: e.tensor_tensor(tk[:], kr_ap, vb_["k_k"][:], ALU.mult), reads=[prp, vb_["k_k"]], writes=[tk])
            k.op("act", lambda e: e.activation(tmp[:], tk[:], AF.Square), reads=[tk], writes=[tmp])
            k.op("dve", lambda e: e.tensor_reduce(sm["ssq"][:], v3(tmp[:]), AX.X, ALU.add), reads=[tmp], writes=[sm["ssq"]])
            yield
            inv_sqrt(k, sm["rn"], sm["ssq"], 1.0, 1e-6)
            yield
            k.op("dve", lambda e: e.tensor_tensor(v3(tk[:]), v3(tk[:]), bc(sm["rn"][:], 8, 64), ALU.mult),
                 reads=[tk, sm["rn"]], writes=[tk])
            k.op("dve", lambda e: e.scalar_tensor_tensor(Kr[:], aic[:], -1.0, vb_["k_a"][:], ALU.add, ALU.mult),
                 reads=[aic, vb_["k_a"]], writes=[Kr])
            k.op("dve", lambda e: e.scalar_tensor_tensor(Kr[:], Kr[:], 1.0, kr_ap, ALU.add, ALU.mult),
                 reads=[Kr, prp], writes=[Kr])
            k.op("act", lambda e: e.mul(Ar[:], tk[:], -1.0), reads=[tk], writes=[Ar])
            k.op("pool", lambda e: e.tensor_tensor(Br[:], tk[:], aic[:], ALU.mult), reads=[tk, aic], writes=[Br])
            yield
            k.op("pool", lambda e: e.tensor_tensor(tmp[:], r_ap, Kr[:], ALU.mult), reads=[prp, Kr], writes=[tmp])
            k.op("pool", lambda e: e.tensor_tensor(tmp[:], tmp[:], vb_["r_k"][:], ALU.mult), reads=[tmp, vb_["r_k"]],
                 writes=[tmp])
            k.op("dve", lambda e: e.tensor_reduce(sm["bs"][:], v3(tmp[:]), AX.X, ALU.add), reads=[tmp], writes=[sm["bs"]])
            k.op("dve", lambda e: e.tensor_tensor(v3(bonus[:]), v3(vr_ap), bc(sm["bs"][:], 8, 64), ALU.mult),
                 reads=[prp, sm["bs"]], writes=[bonus])
            yield

        def prep2(t):
            yield from dplr_prep(k, env, t % 2, (Rr, Rr[:]), (Kr, Kr[:]), (Ar, Ar[:]), (Br, Br[:]), Vbb[t % 3],
                                 (ldr, ldr[:]))

        def fin(t):
            gate, Vb, bonus = gateb[t % 3], Vbb[t % 3], bonusb[t % 3]
            yield from dplr_fin(k, env, t % 2, Vb)
            o = env["o"]
            k.op("dve", lambda e: e.tensor_reduce(sm["s1"][:], v3(o[:]), AX.X, ALU.add), reads=[o], writes=[sm["s1"]])
            k.op("act", lambda e: e.activation(tmp2[:], o[:], AF.Square), reads=[o], writes=[tmp2])
            k.op("dve", lambda e: e.tensor_reduce(sm["s2"][:], v3(tmp2[:]), AX.X, ALU.add), reads=[tmp2], writes=[sm["s2"]])
            yield
            head_stats(k, sm, 64, 64e-5)
            yield
            k.op("dve", lambda e: e.tensor_tensor(v3(y[:]), v3(o[:]), bc(sm["rstdh"][:], 8, 64), ALU.mult),
                 reads=[o, sm["rstdh"]], writes=[y])
            k.op("pool", lambda e: e.tensor_tensor(v3(y[:]), v3(y[:]), bc(sm["nmr"][:], 8, 64), ALU.add),
                 reads=[y, sm["nmr"]], writes=[y])
            yield
            k.op("dve", lambda e: e.tensor_tensor(y[:], y[:], vb_["ln_w"][:], ALU.mult), reads=[y, vb_["ln_w"]], writes=[y])
            k.op("pool", lambda e: e.tensor_tensor(y[:], y[:], vb_["ln_b"][:], ALU.add), reads=[y, vb_["ln_b"]], writes=[y])
            yield
            k.op("dve", lambda e: e.tensor_tensor(y[:], y[:], bonus[:], ALU.add), reads=[y, bonus], writes=[y])
            k.op("pool", lambda e: e.tensor_tensor(y[:], y[:], gate[:], ALU.mult), reads=[y, gate], writes=[y])
            k.dma("pool", "y0", ob_out[t].t, y[:], reads=[y], writes=[ob_out[t]])
            yield

        run_pipeline(NT, [prep, prep2, fin], [1, 4, 1])
        k.barrier()


def build_program(T, phases=("mlp",), dbg=None):
    nc = bass.Bass("TRN2", target_bir_lowering=False)
    k = KB(nc)

    dins = {}

    def din(name, shape):
        if name not in dins:
            dins[name] = Buf(nc.dram_tensor(name, list(shape), F32, kind="ExternalInput").ap(), name)
        return dins[name]

    NT = T // 128
    x = din("x", [T, D])
    c_ident = din("c_ident", [128, 128])
    out = Buf(nc.dram_tensor("out", [T, D], F32, kind="ExternalOutput").ap(), "out")

    def tiles(buf):
        return [Buf(buf.t[t * 128:(t + 1) * 128, :], f"{buf.name}{t}") for t in range(NT)]

    cur = tiles(x)
    with k.es:
        es = ExitStack()
        with es:
            ident = load_consts(k, es, c_ident)
            for pi, ph in enumerate(phases):
                last = pi == len(phases) - 1
                if ph in ("gdn", "rwkv", "ret"):
                    nxt = None
                elif last:
                    nxt = tiles(out)
                else:
                    scr = Buf(nc.dram_tensor(f"scr{pi}", [T, D], F32, kind="Internal").ap(), f"scr{pi}")
                    nxt = tiles(scr)
                if ph == "gdn":
                    oa_t = tiles(Buf(nc.dram_tensor("oa_scr", [T, 512], F32, kind="Internal").ap(), "oa_scr"))
                    gdn_phase(k, nc, T, cur, oa_t, din("ab_w_in", [D, 3856]), din("gdn_conv_w", [4, 1536]),
                              din("gdn_a_log", [8]), din("gdn_dt_bias", [8]), din("gdn_norm_w", [64]),
                              din("norm_mix_pre0", [D]), ident, din("c_dplr", [128, 772]))
                    continue
                if ph == "rwkv":
                    ob_t = tiles(Buf(nc.dram_tensor("ob_scr", [T, 512], F32, kind="Internal").ap(), "ob_scr"))
                    vecs = {"mu": din("rwkv_mu", [1792])}
                    for n in ("w0", "a0", "k_k", "k_a", "r_k", "ln_w", "ln_b"):
                        vecs[n] = din("rwkv_" + n, [512])
                    rwkv_phase(k, nc, T, cur, ob_t, din("ab_w_in", [D, 3856]), vecs,
                               din("rwkv_w2", [64, 512]), din("rwkv_a2", [64, 512]), din("rwkv_g2", [128, 512]),
                               din("norm_mix_pre0", [D]), ident, din("c_dplr", [128, 772]))
                    continue
                if ph == "abo":
                    outproj_phase(k, nc, T, cur, nxt, [(oa_t, 0, 512, False), (ob_t, 512, 512, False)], 1024,
                                  din("ab_w_out", [D, D]), din("norm_mix_post0", [D]), ident)
                    cur = nxt
                    continue
                if ph == "ret":
                    y_t = [Buf(a_, f"ys{t_}") for t_, a_ in enumerate(
                        (lambda yb: [yb[t_ * 128:(t_ + 1) * 128, :] for t_ in range(NT)])(
                            nc.dram_tensor("y_scr", [T, 2048], BF16, kind="Internal").ap()))]
                    ret_phase(k, nc, T, cur, y_t, din("ret_w_in", [D, 6144]), din("norm_mix_pre1", [D]), ident,
                              din("c_rot", [T, 128]), din("c_dqk", [128, 24]), din("c_maskT", [128, 128]))
                    continue
                if ph == "reto":
                    outproj_phase(k, nc, T, cur, nxt, [(y_t, 0, 2048, True)], 2048, din("ret_w_out", [2048, D]),
                                  din("norm_mix_post1", [D]), ident, gscale=din("ret_gn_w", [2048]))
                    cur = nxt
                    continue
                if ph.startswith("mlp"):
                    l = ph[3:]
                    mlp_phase(k, nc, T, cur, nxt, din("mlp_w_up" + l, [D, DFF]), din("mlp_w_down" + l, [DFF, D]),
                              din("norm_mlp_pre" + l, [D]), din("norm_mlp_post" + l, [D]), ident)
                cur = nxt
            k.barrier()
    return nc


PHASES = ("gdn", "rwkv", "abo", "mlp0", "ret", "reto", "mlp1")
SEQ = 4096
NCORES = 8


def host_consts(T):
    c = {"c_ident": np.eye(128, dtype=np.float32)}
    c.update(ret_host_consts(T))
    c.update(dplr_host_consts())
    return c


def kernel(**inputs):
    f32 = lambda a: np.ascontiguousarray(np.asarray(a, dtype=np.float32))
    x = f32(inputs["x"])
    B, T, _ = x.shape
    shared = dict(host_consts(T))
    for l in range(2):
        shared[f"mlp_w_up{l}"] = f32(inputs["mlp_w_up"][l])
        shared[f"mlp_w_down{l}"] = f32(inputs["mlp_w_down"][l])
        shared[f"norm_mlp_pre{l}"] = f32(inputs["norm_mlp_pre"][l])
        shared[f"norm_mlp_post{l}"] = f32(inputs["norm_mlp_post"][l])
        shared[f"norm_mix_pre{l}"] = f32(inputs["norm_mix_pre"][l])
        shared[f"norm_mix_post{l}"] = f32(inputs["norm_mix_post"][l])
    for n in ("ab_w_in", "gdn_conv_w", "gdn_a_log", "gdn_dt_bias", "gdn_norm_w", "rwkv_mu", "rwkv_w0", "rwkv_w2",
              "rwkv_a0", "rwkv_a2", "rwkv_g2", "rwkv_k_k", "rwkv_k_a", "rwkv_ln_w", "rwkv_ln_b", "ab_w_out",
              "ret_w_in", "ret_gn_w", "ret_w_out"):
        shared[n] = f32(inputs[n][0])
    shared["rwkv_r_k"] = f32(inputs["rwkv_r_k"][0]).reshape(512)
    nc = build_program(T, PHASES)
    in_maps = [dict(shared, x=x[b]) for b in range(B)]
    res = run_bass_kernel_spmd(nc, in_maps, core_ids=list(range(B)))
    return np.stack([np.asarray(r["out"], dtype=np.float32) for r in res.results], axis=0)
```

```python
import os
import numpy as np
from contextlib import ExitStack
import concourse.bass as bass
import concourse.mybir as mybir
from concourse.bass_utils import run_bass_kernel_spmd

F32 = mybir.dt.float32
BF16 = mybir.dt.bfloat16
AF = mybir.ActivationFunctionType
ALU = mybir.AluOpType
AX = mybir.AxisListType

SAME_ENG_SYNC = True


class Buf:
    __slots__ = ("t", "w", "r", "name")

    def __init__(self, t, name=""):
        self.t = t
        self.w = None
        self.r = {}
        self.name = name

    def __getitem__(self, key):
        return self.t[key]


class KB:
    def __init__(self, nc):
        self.nc = nc
        self.es = ExitStack()
        self.engs = {"pe": nc.tensor, "act": nc.scalar, "dve": nc.vector,
                     "pool": nc.gpsimd, "sp": nc.sync}
        self.nuniq = 0
        self.epoch = -1
        self.sem = {}
        self.cnt = {}
        self._new_epoch()

    def _new_epoch(self):
        self.epoch += 1
        self.known = {e: {} for e in self.engs}
        for e in self.engs:
            self.sem[e] = self.es.enter_context(self.nc.semaphore(f"s_{e}_{self.epoch}"))
            self.cnt[e] = 0

    def sb(self, es, name, shape, dtype):
        self.nuniq += 1
        t = es.enter_context(self.nc.sbuf_tensor(f"{name}_{self.nuniq}", list(shape), dtype))
        return Buf(t, name)

    def ps(self, es, name, shape, dtype=F32):
        self.nuniq += 1
        t = es.enter_context(self.nc.psum_tensor(f"{name}_{self.nuniq}", list(shape), dtype))
        return Buf(t, name)

    def stream(self, name):
        key = "d_" + name
        if key not in self.sem:
            self.sem[key] = self.es.enter_context(self.nc.semaphore(key))
            self.cnt[key] = 0
        return key

    def _waits(self, eng, reads, writes, extra=()):
        need = {}

        def add(ev):
            if ev is not None and ev[2] == self.epoch:
                if need.get(ev[0], 0) < ev[1]:
                    need[ev[0]] = ev[1]

        for b in reads:
            add(b.w)
        for b in writes:
            add(b.w)
            for sk, (v, ep) in b.r.items():
                add((sk, v, ep))
        for ev in extra:
            add(ev)
        e = self.engs[eng]
        kn = self.known[eng]
        for sk, v in need.items():
            if sk == eng and (eng == "pe" or eng == "sp" or not SAME_ENG_SYNC):
                continue
            if kn.get(sk, 0) >= v:
                continue
            e.wait_ge(self.sem[sk], v)
            kn[sk] = v

    def _record(self, ev, reads, writes):
        for b in reads:
            old = b.r.get(ev[0])
            if old is None or old[1] != ev[2] or old[0] < ev[1]:
                b.r[ev[0]] = (ev[1], ev[2])
        for b in writes:
            b.w = ev
            b.r = {}

    def op(self, eng, fn, reads=(), writes=()):
        self._waits(eng, reads, writes)
        ins = fn(self.engs[eng])
        self.cnt[eng] += 1
        ins.then_inc(self.sem[eng], 1)
        self._record((eng, self.cnt[eng], self.epoch), reads, writes)

    def dma(self, q, stream, out, in_, reads=(), writes=(), **kw):
        sk = self.stream(stream)
        prev = (sk, self.cnt[sk], self.epoch) if self.cnt[sk] else None
        self._waits(q, reads, writes, extra=(prev,))
        ins = self.engs[q].dma_start(out=out, in_=in_, **kw)
        self.cnt[sk] += 16
        ins.then_inc(self.sem[sk], 16)
        self._record((sk, self.cnt[sk], self.epoch), reads, writes)

    def dbg(self, name, buf, ap, shape):
        if os.environ.get("DBG", "0") != "1":
            return
        d = Buf(self.nc.dram_tensor("dbg_" + name, list(shape), F32, kind="ExternalOutput").ap(), name)
        self.dma("sp", "dbg", d.t, ap, reads=[buf], writes=[d])

    def barrier(self):
        for eng, e in self.engs.items():
            kn = self.known[eng]
            for sk, v in self.cnt.items():
                if v == 0 or kn.get(sk, 0) >= v:
                    continue
                if sk == eng and eng in ("pe", "sp"):
                    pass
                e.wait_ge(self.sem[sk], v)
                kn[sk] = v
        self._new_epoch()

    def mm(self, out, lhsT, rhs, start, stop, reads, writes, tp=None):
        kw = {}
        if tp is not None and (tp[0] == 96 or tp[1] == 96):
            kw["tile_position"] = tp
        self.op("pe", lambda e: e.matmul(out, lhsT, rhs, start=start, stop=stop, **kw), reads, writes)

    def tr(self, out, in_, ident, reads, writes):
        self.op("pe", lambda e: e.transpose(out, in_, ident), reads, writes)


D = 1024
DFF = 4096
RMS_EPS = 1e-6


def rms_rstd(k, eng_sq, x_ap, xbuf, junk, ss, rstd, n, eps=RMS_EPS):
    k.op("act", lambda e: e.activation(junk[:], x_ap, AF.Square, accum_out=ss[:]),
         reads=[xbuf], writes=[junk, ss])
    k.op("dve", lambda e: e.tensor_scalar(rstd[:], ss[:], 1.0 / n, eps, ALU.mult, ALU.add),
         reads=[ss], writes=[rstd])
    k.op("act", lambda e: e.activation(rstd[:], rstd[:], AF.Sqrt), reads=[rstd], writes=[rstd])
    k.op("dve", lambda e: e.reciprocal(rstd[:], rstd[:]), reads=[rstd], writes=[rstd])


def load_consts(k, es, c_ident):
    ident = k.sb(es, "ident", [128, 128], BF16)
    k.dma("pool", "const", ident[:], c_ident.t[:, :], reads=[c_ident], writes=[ident])
    return ident


def mlp_phase(k, nc, T, hin, hout, w_up, w_dn, g_pre, g_post, ident):
    NT = T // 128
    ST = 2
    NS = NT // ST
    es = ExitStack()
    with es:
        wup = [k.sb(es, f"wup{kc}", [128, DFF], BF16) for kc in range(8)]
        wdn = [k.sb(es, f"wdn{g}", [128, 4, D], BF16) for g in range(8)]
        for kc in range(8):
            k.dma("pool", f"wl{kc % 8}", wup[kc][:], w_up.t[kc * 128:(kc + 1) * 128, :],
                  reads=[w_up], writes=[wup[kc]])
        for g in range(8):
            k.dma("pool", f"wl{g % 8}", wdn[g][:],
                  w_dn.t[g * 512:(g + 1) * 512, :].rearrange("(fc p) d -> p fc d", p=128),
                  reads=[w_dn], writes=[wdn[g]])
        gpre = k.sb(es, "gpre", [128, 8], F32)
        k.dma("sp", "const", gpre[:], g_pre.t.rearrange("(kc p) -> p kc", p=128),
              reads=[g_pre], writes=[gpre], allow_slow_non_contiguous=True)
        gpost = k.sb(es, "gpost", [128, D], F32)
        k.dma("sp", "const", gpost[:], g_post.t.partition_broadcast(128),
              reads=[g_post], writes=[gpost])

        NB = 2
        xt = [[k.sb(es, f"xt{b}_{j}", [128, D], F32) for j in range(ST)] for b in range(NB)]
        uT = [k.sb(es, f"uT{b}", [128, 8, 128 * ST], BF16) for b in range(NB)]
        aT = [k.sb(es, f"aT{b}", [128, 32, 128 * ST], BF16) for b in range(1)]
        xs = [k.sb(es, f"xs{b}", [128, D], BF16) for b in range(2)]
        junk = k.sb(es, "junk", [128, D], BF16)
        rtmp = [k.sb(es, f"rtmp{b}", [128, 128 * ST], F32) for b in range(4)]
        ss = [k.sb(es, f"ss{b}", [128, 1], F32) for b in range(2)]
        rstd = [k.sb(es, f"rstd{b}", [128, 1], F32) for b in range(2)]
        ss2 = [k.sb(es, f"ss2{b}", [128, 1], F32) for b in range(2)]
        rstd2 = [k.sb(es, f"rstd2{b}", [128, 1], F32) for b in range(2)]
        fb = [k.sb(es, f"fb{b}", [128, D], F32) for b in range(2)]
        yb = [k.sb(es, f"yb{b}", [128, D], F32) for b in range(2)]
        pT = [k.ps(es, f"pT{b}", [128, 8, 128], BF16) for b in range(2)]
        pU = [k.ps(es, f"pU{b}", [128, 512], F32) for b in range(4)]
        pD = [k.ps(es, f"pD{b}", [128, 512], F32) for b in range(2)]

        state = {"ti": 0, "iu": 0, "idn": 0}
        NTOK = 128 * ST

        def prep(s):
            b = s % NB
            for j in range(ST):
                t = s * ST + j
                x = xt[b][j]
                ti = state["ti"]
                k.dma("sp", f"x{ti % 2}", x[:], hin[t].t, reads=[hin[t]], writes=[x])
                q = ti % 2
                rms_rstd(k, "act", x[:], x, junk, ss[q], rstd[q], D)
                k.op("dve", lambda e: e.tensor_scalar(xs[q][:], x[:], rstd[q][:, 0:1], None, ALU.mult),
                     reads=[x, rstd[q]], writes=[xs[q]])
                for kc in range(8):
                    k.tr(pT[q][:, kc, :], xs[q][:, kc * 128:(kc + 1) * 128], ident[:],
                         reads=[xs[q], ident], writes=[pT[q]])
                k.op("dve", lambda e: e.tensor_tensor(
                    uT[b][:, :, j * 128:(j + 1) * 128], pT[q][:],
                    gpre[:].unsqueeze(2).to_broadcast([128, 8, 128]), ALU.mult),
                    reads=[pT[q], gpre], writes=[uT[b]])
                state["ti"] += 1

        def up(s):
            b = s % NB
            for fc in range(32):
                pu = pU[state["iu"] % 4]
                state["iu"] += 1
                for kc in range(8):
                    k.mm(pu[:, 0:NTOK], wup[kc][:, fc * 128:(fc + 1) * 128], uT[b][:, kc, :],
                         kc == 0, kc == 7, reads=[wup[kc], uT[b]], writes=[pu])
                tm = rtmp[fc % 4]
                if fc % 2 == 0:
                    k.op("act", lambda e: e.activation(tm[:], pu[:, 0:NTOK], AF.Relu),
                         reads=[pu], writes=[tm])
                else:
                    k.op("dve", lambda e: e.tensor_scalar(tm[:], pu[:, 0:NTOK], 0.0, None, ALU.max),
                         reads=[pu], writes=[tm])
                k.op("pool", lambda e: e.tensor_tensor(aT[0][:, fc, :], tm[:], tm[:], ALU.mult),
                     reads=[tm], writes=[aT[0]])

        def down(s):
            b = s % NB
            for j in range(ST):
                t = s * ST + j
                x = xt[b][j]
                q = state["idn"] % 2
                f = fb[q]
                for nh in range(2):
                    pd = pD[nh]
                    for fc in range(32):
                        k.mm(pd[:], aT[0][:, fc, j * 128:(j + 1) * 128],
                             wdn[fc // 4][:, fc % 4, nh * 512:(nh + 1) * 512],
                             fc == 0, fc == 31, reads=[aT[0], wdn[fc // 4]], writes=[pd])
                    k.op("dve" if nh == 0 else "act",
                         (lambda e: e.tensor_copy(f[:, nh * 512:(nh + 1) * 512], pd[:])) if nh == 0 else
                         (lambda e: e.copy(f[:, nh * 512:(nh + 1) * 512], pd[:])),
                         reads=[pd], writes=[f])
                rms_rstd(k, "act", f[:], f, junk, ss2[q], rstd2[q], D)
                y = yb[q]
                k.op("dve", lambda e: e.scalar_tensor_tensor(y[:], f[:], rstd2[q][:, 0:1], gpost[:],
                                                             ALU.mult, ALU.mult),
                     reads=[f, rstd2[q], gpost], writes=[y])
                k.op("pool", lambda e: e.tensor_tensor(y[:], y[:], x[:], ALU.add),
                     reads=[y, x], writes=[y])
                k.dma("pool", f"y{q}", hout[t].t, y[:], reads=[y], writes=[hout[t]])
                state["idn"] += 1

        prep(0)
        for s in range(NS):
            up(s)
            if s + 1 < NS:
                prep(s + 1)
            down(s)
        k.barrier()


NH_RET = 8
RET_C = 128


def ret_gammas():
    return [float(1.0 - 2.0 ** (-5.0 - h)) for h in range(NH_RET)]


def ret_host_consts(T):
    pos = np.arange(T, dtype=np.float64)
    angle = 1.0 / (10000.0 ** np.linspace(0.0, 1.0, 64))
    th = pos[:, None] * angle[None, :]
    c_rot = np.concatenate([np.cos(th), np.sin(th)], axis=1).astype(np.float32)
    g = np.array(ret_gammas(), dtype=np.float64)
    idx = np.arange(128, dtype=np.float64)
    dq = g[None, :] ** idx[:, None]
    dk = g[None, :] ** (-idx[:, None]) * (128.0 ** -0.5)
    dkc = dk * (g[None, :] ** 128.0)
    c_dqk = np.concatenate([dq, dk, dkc], axis=1).astype(np.float32)
    c_maskT = (idx[:, None] <= idx[None, :]).astype(np.float32)
    return {"c_rot": c_rot, "c_dqk": c_dqk, "c_maskT": c_maskT}


def ret_phase(k, nc, T, hin, y_out, w_in, g_pre, ident, c_rot, c_dqk, c_maskT):
    NT = T // 128
    gam = ret_gammas()
    es = ExitStack()
    with es:
        win = [k.sb(es, f"rwin{kc}", [128, 6144], BF16) for kc in range(8)]
        for kc in range(8):
            k.dma("pool", f"wl{kc % 8}", win[kc][:], w_in.t[kc * 128:(kc + 1) * 128, :],
                  reads=[w_in], writes=[win[kc]])
        gpre = k.sb(es, "gpre", [128, 8], F32)
        k.dma("sp", "const", gpre[:], g_pre.t.rearrange("(kc p) -> p kc", p=128),
              reads=[g_pre], writes=[gpre], allow_slow_non_contiguous=True)
        dqk = k.sb(es, "dqk", [128, 24], F32)
        k.dma("sp", "const", dqk[:], c_dqk.t[:, :], reads=[c_dqk], writes=[dqk])
        maskT = k.sb(es, "maskT", [128, 128], F32)
        k.dma("sp", "const", maskT[:], c_maskT.t[:, :], reads=[c_maskT], writes=[maskT])

        xb = [k.sb(es, f"x{i}", [128, D], F32) for i in range(2)]
        xs = k.sb(es, "xs", [128, D], BF16)
        uT = k.sb(es, "uT", [128, 8, 128], BF16)
        cs = [k.sb(es, f"cs{b}", [128, 128], F32) for b in range(2)]
        vbb = [k.sb(es, f"vb{i}", [128, 2048], BF16) for i in range(2)]
        gsb = [k.sb(es, f"gs{i}", [128, 2048], BF16) for i in range(2)]
        tq = [[k.sb(es, f"tq{b}_{i}", [128, 4, 64], F32) for i in range(4)] for b in range(2)]
        qk_b = k.sb(es, "qk_b", [128, 16, 64, 2], BF16)
        kdb = [k.sb(es, f"kd_b{i}", [128, 8, 64, 2], BF16) for i in range(2)]
        qTb = [k.sb(es, f"qT{i}", [128, 8, 128], BF16) for i in range(2)]
        kT = k.sb(es, "kT", [128, 8, 128], BF16)
        scb = [k.sb(es, f"sc_b{i}", [128, 8, 128], BF16) for i in range(2)]
        Sg = k.sb(es, "Sg", [128, 8, 256], F32)
        Sb = k.sb(es, "Sb", [128, 8, 256], BF16)
        o_f = k.sb(es, "o_f", [128, 8, 256], F32)
        ybb = [k.sb(es, f"y_b{i}", [128, 2048], BF16) for i in range(2)]
        junk = k.sb(es, "junk", [128, D], BF16)
        st = {n: k.sb(es, n, [128, 8], F32) for n in ("s1", "s2", "mean", "msq", "var", "rstdh", "nmr")}
        ss = k.sb(es, "ss", [128, 1], F32)
        rstd = k.sb(es, "rstd", [128, 1], F32)
        pT = [k.ps(es, f"pT{b}", [128, 8, 128], BF16) for b in range(2)]
        pb = [k.ps(es, f"pb{b}", [128, 512], F32) for b in range(6)]
        ib = [0]
        print("ret sbuf bytes remaining", nc.sbuf_bytes_remaining)

        def bank():
            b = pb[ib[0] % 6]
            ib[0] += 1
            return b

        k.op("pool", lambda e: e.memset(Sg[:], 0.0), writes=[Sg])
        k.op("pool", lambda e: e.memset(Sb[:], 0.0), writes=[Sb])

        def load_x(t):
            k.dma("sp", f"x{t % 2}", xb[t % 2][:], hin[t].t, reads=[hin[t]], writes=[xb[t % 2]])
            k.dma("sp", f"cs{t % 2}", cs[t % 2][:], c_rot.t[t * 128:(t + 1) * 128, :], reads=[c_rot], writes=[cs[t % 2]])

        def prep(t):
            s = t % 2
            x, c, vb, gs, kd_b, qT, sc_b = xb[s], cs[s], vbb[s], gsb[s], kdb[s], qTb[s], scb[s]
            if t == 0:
                load_x(0)
            if t + 1 < NT:
                load_x(t + 1)
            norm_transpose(k, x, xs, junk, ss, rstd, pT[0], uT, gpre, ident)
            yield
            cosb = c[:, 0:64].unsqueeze(1).to_broadcast([128, 4, 64])
            sinb = c[:, 64:128].unsqueeze(1).to_broadcast([128, 4, 64])
            for g in range(12):
                pu = bank()
                for kc in range(8):
                    k.mm(pu[:], uT[:, kc, :], win[kc][:, g * 512:(g + 1) * 512], kc == 0, kc == 7,
                         reads=[uT, win[kc]], writes=[pu])
                if g < 4:
                    tt = tq[g % 2]
                    pv = pu[:].rearrange("p (h d t) -> p h d t", h=4, t=2)
                    x1 = pv[:, :, :, 0]
                    x2 = pv[:, :, :, 1]
                    k.op("dve", lambda e: e.tensor_tensor(tt[0][:], x1, cosb, ALU.mult), reads=[pu, c], writes=[tt[0]])
                    k.op("dve", lambda e: e.tensor_tensor(tt[1][:], x2, sinb, ALU.mult), reads=[pu, c], writes=[tt[1]])
                    k.op("dve", lambda e: e.tensor_tensor(tt[2][:], x2, cosb, ALU.mult), reads=[pu, c], writes=[tt[2]])
                    k.op("dve", lambda e: e.tensor_tensor(tt[3][:], x1, sinb, ALU.mult), reads=[pu, c], writes=[tt[3]])
                    k.op("pool", lambda e: e.tensor_tensor(tt[0][:], tt[0][:], tt[1][:], ALU.subtract),
                         reads=[tt[0], tt[1]], writes=[tt[0]])
                    k.op("pool", lambda e: e.tensor_tensor(tt[2][:], tt[2][:], tt[3][:], ALU.add),
                         reads=[tt[2], tt[3]], writes=[tt[2]])
                    h0 = 4 * g
                    sc = dqk[:, h0:h0 + 4].unsqueeze(2).to_broadcast([128, 4, 64])
                    k.op("pool", lambda e: e.tensor_tensor(qk_b[:, h0:h0 + 4, :, 0], tt[0][:], sc, ALU.mult),
                         reads=[tt[0], dqk], writes=[qk_b])
                    k.op("pool", lambda e: e.tensor_tensor(qk_b[:, h0:h0 + 4, :, 1], tt[2][:], sc, ALU.mult),
                         reads=[tt[2], dqk], writes=[qk_b])
                    if g >= 2:
                        hk = 4 * (g - 2)
                        sc2 = dqk[:, 16 + hk:16 + hk + 4].unsqueeze(2).to_broadcast([128, 4, 64])
                        k.op("pool", lambda e: e.tensor_tensor(kd_b[:, hk:hk + 4, :, 0], tt[0][:], sc2, ALU.mult),
                             reads=[tt[0], dqk], writes=[kd_b])
                        k.op("pool", lambda e: e.tensor_tensor(kd_b[:, hk:hk + 4, :, 1], tt[2][:], sc2, ALU.mult),
                             reads=[tt[2], dqk], writes=[kd_b])
                elif g < 8:
                    k.op("act", lambda e: e.copy(vb[:, (g - 4) * 512:(g - 3) * 512], pu[:]), reads=[pu], writes=[vb])
                else:
                    k.op("act", lambda e: e.activation(gs[:, (g - 8) * 512:(g - 7) * 512], pu[:], AF.Silu),
                         reads=[pu], writes=[gs])
                yield
            for i in range(16):
                k.tr(pT[i // 8][:, i % 8, :], qk_b[:, i, :, :].rearrange("p d t -> p (d t)"), ident[:],
                     reads=[qk_b, ident], writes=[pT[i // 8]])
            k.op("act", lambda e: e.copy(qT[:], pT[0][:]), reads=[pT[0]], writes=[qT])
            k.op("dve", lambda e: e.tensor_copy(kT[:], pT[1][:]), reads=[pT[1]], writes=[kT])
            yield
            for hb in range(2):
                psc = bank()
                for hh in range(4):
                    h = hb * 4 + hh
                    k.mm(psc[:, hh * 128:(hh + 1) * 128], kT[:, h, :], qT[:, h, :], True, True,
                         reads=[kT, qT], writes=[psc])
                k.op("dve", lambda e: e.tensor_tensor(
                    sc_b[:, hb * 4:hb * 4 + 4, :], psc[:].rearrange("p (h c) -> p h c", h=4),
                    maskT[:].unsqueeze(1).to_broadcast([128, 4, 128]), ALU.mult),
                    reads=[psc, maskT], writes=[sc_b])
                yield

        def fin(t):
            s = t % 2
            vb, gs, kd_b, qT, sc_b, y_b = vbb[s], gsb[s], kdb[s], qTb[s], scb[s], ybb[s]
            for hp in range(4):
                po = bank()
                for hh in range(2):
                    h = hp * 2 + hh
                    k.mm(po[:, hh * 256:(hh + 1) * 256], sc_b[:, h, :], vb[:, h * 256:(h + 1) * 256], True, False,
                         reads=[sc_b, vb], writes=[po])
                    k.mm(po[:, hh * 256:(hh + 1) * 256], qT[:, h, :], Sb[:, h, :], False, True,
                         reads=[qT, Sb], writes=[po])
                if hp % 2 == 0:
                    k.op("act", lambda e: e.copy(o_f[:, hp * 2:hp * 2 + 2, :],
                                                 po[:].rearrange("p (h v) -> p h v", h=2)), reads=[po], writes=[o_f])
                else:
                    k.op("dve", lambda e: e.tensor_copy(o_f[:, hp * 2:hp * 2 + 2, :],
                                                        po[:].rearrange("p (h v) -> p h v", h=2)),
                         reads=[po], writes=[o_f])
                yield
            for hp in range(4):
                pd = bank()
                for hh in range(2):
                    h = hp * 2 + hh
                    k.mm(pd[:, hh * 256:(hh + 1) * 256], kd_b[:, h, :, :].rearrange("p d t -> p (d t)"),
                         vb[:, h * 256:(h + 1) * 256], True, True, reads=[kd_b, vb], writes=[pd])
                for hh in range(2):
                    h = hp * 2 + hh
                    cc = gam[h] ** 128
                    k.op("dve", lambda e: e.scalar_tensor_tensor(Sg[:, h, :], Sg[:, h, :], cc,
                                                                 pd[:, hh * 256:(hh + 1) * 256], ALU.mult, ALU.add),
                         reads=[Sg, pd], writes=[Sg])
                yield
            k.op("act", lambda e: e.copy(Sb[:], Sg[:]), reads=[Sg], writes=[Sb])
            k.op("dve", lambda e: e.tensor_reduce(st["s1"][:], o_f[:], AX.X, ALU.add), reads=[o_f], writes=[st["s1"]])
            for h in range(8):
                k.op("act", lambda e: e.activation(junk[:, 0:256], o_f[:, h, :], AF.Square,
                                                   accum_out=st["s2"][:, h:h + 1]),
                     reads=[o_f], writes=[junk, st["s2"]])
            yield
            head_stats(k, st, 256, 1e-6)
            yield
            k.op("dve", lambda e: e.tensor_tensor(o_f[:], o_f[:], st["rstdh"][:].unsqueeze(2).to_broadcast([128, 8, 256]),
                                                  ALU.mult), reads=[o_f, st["rstdh"]], writes=[o_f])
            yield
            k.op("pool", lambda e: e.tensor_tensor(o_f[:], o_f[:], st["nmr"][:].unsqueeze(2).to_broadcast([128, 8, 256]),
                                                   ALU.add), reads=[o_f, st["nmr"]], writes=[o_f])
            yield
            k.op("pool", lambda e: e.tensor_tensor(y_b[:], o_f[:].rearrange("p h v -> p (h v)"), gs[:], ALU.mult),
                 reads=[o_f, gs], writes=[y_b])
            k.dma("pool", f"y{s}", y_out[t].t, y_b[:], reads=[y_b], writes=[y_out[t]])
            yield

        run_pipeline(NT, [prep, fin], [1, 1])
        k.barrier()


def outproj_phase(k, nc, T, hin, hout, srcs, K, w_out, g_post, ident, gscale=None):
    NT = T // 128
    NK = K // 128
    NG = NK // 4
    NPT = NK // 8
    es = ExitStack()
    with es:
        wout = [k.sb(es, f"owout{g}", [128, 4, D], BF16) for g in range(NG)]
        for g in range(NG):
            k.dma("pool", f"wl{g % 8}", wout[g][:],
                  w_out.t[g * 512:(g + 1) * 512, :].rearrange("(fc p) d -> p fc d", p=128),
                  reads=[w_out], writes=[wout[g]])
        gpost = k.sb(es, "gpost", [128, D], F32)
        k.dma("sp", "const", gpost[:], g_post.t.partition_broadcast(128), reads=[g_post], writes=[gpost])
        gsc = None
        if gscale is not None:
            gsc = k.sb(es, "gsc", [128, NK], F32)
            k.dma("sp", "const", gsc[:], gscale.t.rearrange("(kc p) -> p kc", p=128),
                  reads=[gscale], writes=[gsc], allow_slow_non_contiguous=True)
        NB = 3
        xb = [k.sb(es, f"x{i}", [128, D], F32) for i in range(NB)]
        need32 = any(not sr[3] for sr in srcs)
        oin = [k.sb(es, f"oin{i}", [128, K], F32) for i in range(NB)] if need32 else None
        oab = [k.sb(es, f"oab{i}", [128, K], BF16) for i in range(NB)]
        oT = [k.sb(es, f"oT{i}", [128, NK, 128], BF16) for i in range(NB)]
        f = [k.sb(es, f"f{i}", [128, D], F32) for i in range(NB)]
        junk = k.sb(es, "junk", [128, D], BF16)
        ss2 = [k.sb(es, f"ss2{i}", [128, 1], F32) for i in range(NB)]
        rstd2 = [k.sb(es, f"rstd2{i}", [128, 1], F32) for i in range(NB)]
        NPB = 3 if NPT == 1 else 2
        pT = [[k.ps(es, f"pT{b}_{j}", [128, 8, 128], BF16) for j in range(NPT)] for b in range(NPB)]
        pb = [k.ps(es, f"pb{b}", [128, 512], F32) for b in range(8 - NPB * NPT)]
        npb = len(pb)
        def front(t):
            s = t % NB
            k.dma("sp", f"x{s}", xb[s][:], hin[t].t, reads=[hin[t]], writes=[xb[s]])
            for j, (tiles_, c0, w, isb) in enumerate(srcs):
                if isb:
                    k.dma("sp", f"o{j}{s}", oab[s][:, c0:c0 + w], tiles_[t].t, reads=[tiles_[t]], writes=[oab[s]])
                else:
                    k.dma("sp", f"o{j}{s}", oin[s][:, c0:c0 + w], tiles_[t].t, reads=[tiles_[t]], writes=[oin[s]])
            if need32:
                k.op("act", lambda e: e.copy(oab[s][:], oin[s][:]), reads=[oin[s]], writes=[oab[s]])
            for i in range(NK):
                k.tr(pT[t % NPB][i // 8][:, i % 8, :], oab[s][:, i * 128:(i + 1) * 128], ident[:],
                     reads=[oab[s], ident], writes=[pT[t % NPB][i // 8]])
            for j in range(NPT):
                if gsc is None:
                    k.op("dve", lambda e: e.tensor_copy(oT[s][:, j * 8:j * 8 + 8, :], pT[t % NPB][j][:]),
                         reads=[pT[t % NPB][j]], writes=[oT[s]])
                else:
                    k.op("dve", lambda e: e.tensor_tensor(oT[s][:, j * 8:j * 8 + 8, :], pT[t % NPB][j][:],
                                                          gsc[:, j * 8:j * 8 + 8].unsqueeze(2).to_broadcast([128, 8, 128]),
                                                          ALU.mult), reads=[pT[t % NPB][j], gsc], writes=[oT[s]])

        def back(t):
            s = t % NB
            for nh in range(2):
                pm = pb[(2 * t + nh) % npb]
                for kc in range(NK):
                    k.mm(pm[:], oT[s][:, kc, :], wout[kc // 4][:, kc % 4, nh * 512:(nh + 1) * 512], kc == 0, kc == NK - 1,
                         reads=[oT[s], wout[kc // 4]], writes=[pm])
                if nh == 0:
                    k.op("dve", lambda e: e.tensor_copy(f[s][:, 0:512], pm[:]), reads=[pm], writes=[f[s]])
                else:
                    k.op("act", lambda e: e.copy(f[s][:, 512:1024], pm[:]), reads=[pm], writes=[f[s]])
            rms_rstd(k, "act", f[s][:], f[s], junk, ss2[s], rstd2[s], D)
            k.op("dve", lambda e: e.scalar_tensor_tensor(f[s][:], f[s][:], rstd2[s][:, 0:1], gpost[:], ALU.mult, ALU.mult),
                 reads=[f[s], rstd2[s], gpost], writes=[f[s]])
            k.op("pool", lambda e: e.tensor_tensor(f[s][:], f[s][:], xb[s][:], ALU.add), reads=[f[s], xb[s]], writes=[f[s]])
            k.dma("pool", f"y{s}", hout[t].t, f[s][:], reads=[f[s]], writes=[hout[t]])

        front(0)
        for t in range(NT):
            if t + 1 < NT:
                front(t + 1)
            back(t)
        k.barrier()


def head_stats(k, st, n, eps):
    k.op("dve", lambda e: e.tensor_scalar(st["mean"][:], st["s1"][:], 1.0 / n, None, ALU.mult),
         reads=[st["s1"]], writes=[st["mean"]])
    k.op("dve", lambda e: e.tensor_tensor(st["msq"][:], st["mean"][:], st["mean"][:], ALU.mult),
         reads=[st["mean"]], writes=[st["msq"]])
    k.op("dve", lambda e: e.scalar_tensor_tensor(st["var"][:], st["s2"][:], 1.0 / n, st["msq"][:],
                                                 ALU.mult, ALU.subtract),
         reads=[st["s2"], st["msq"]], writes=[st["var"]])
    k.op("dve", lambda e: e.tensor_scalar(st["var"][:], st["var"][:], eps, None, ALU.add),
         reads=[st["var"]], writes=[st["var"]])
    k.op("act", lambda e: e.activation(st["var"][:], st["var"][:], AF.Sqrt), reads=[st["var"]], writes=[st["var"]])
    k.op("dve", lambda e: e.reciprocal(st["rstdh"][:], st["var"][:]), reads=[st["var"]], writes=[st["rstdh"]])
    k.op("dve", lambda e: e.scalar_tensor_tensor(st["nmr"][:], st["mean"][:], -1.0, st["rstdh"][:],
                                                 ALU.mult, ALU.mult),
         reads=[st["mean"], st["rstdh"]], writes=[st["nmr"]])


DC = 32
C_MI, C_MX, C_MA, C_MITS, C_MITI, C_MTI, C_BLK = 0, 128, 256, 384, 512, 640, 768


def dplr_host_consts():
    idx = np.arange(128)
    ch = idx // DC
    same = ch[:, None] == ch[None, :]
    lt = idx[:, None] < idx[None, :]
    le = idx[:, None] <= idx[None, :]
    gt = idx[:, None] > idx[None, :]
    blk = ch[:, None] == np.arange(4)[None, :]
    pack = np.concatenate([same & le, same & lt, same & gt, same & lt, same & le, same & gt, blk], axis=1)
    return {"c_dplr": pack.astype(np.float32)}


def dplr_setup(k, es, c_dplr, ident):
    env = {"ident": ident}
    cd = k.sb(es, "cdplr", [128, 772], F32)
    k.dma("sp", "const", cd[:], c_dplr.t[:, :], reads=[c_dplr], writes=[cd])
    env["cd"] = cd
    W = 512
    for n in ("e1", "e2", "e3", "e4"):
        env[n] = k.sb(es, n, [128, W], F32)
    for n in ("rt", "kt", "bt", "at", "Xb", "U_b"):
        env[n] = k.sb(es, n, [128, W], BF16)
    env["kdec"] = [k.sb(es, f"kdec{s}", [128, W], BF16) for s in range(2)]
    env["bdec"] = [k.sb(es, f"bdec{s}", [128, W], BF16) for s in range(2)]
    env["wC"] = [k.sb(es, f"wC{s}", [128, 4, 4], F32) for s in range(2)]
    env["arT"] = [[k.sb(es, f"arT{s}{i}", [128, 4, 2, 128], BF16) for i in range(2)] for s in range(2)]
    env["bkT"] = [k.sb(es, f"bkT{i}", [128, 8, 128], BF16) for i in range(2)]
    env["NA"] = [k.sb(es, f"NA{s}", [128, 8, 2, 128], BF16) for s in range(2)]
    env["KA"] = [k.sb(es, f"KA{s}", [128, 8, 2, 128], BF16) for s in range(2)]
    env["Lm"] = k.sb(es, "Lm", [128, 8, 128], BF16)
    env["NP"] = [k.sb(es, f"NP{i}", [128, 8, 2, 128], BF16) for i in range(2)]
    env["Lp"] = [k.sb(es, f"Lp{i}", [128, 8, 128], BF16) for i in range(2)]
    env["Pf"] = k.sb(es, "Pf", [128, 8, 128], BF16)
    env["Ubar"] = [k.sb(es, f"Ubar{s}", [128, W], F32) for s in range(2)]
    env["AbT"] = [[k.sb(es, f"AbT{s}{i}", [128, 4, 128], BF16) for i in range(2)] for s in range(2)]
    env["St"] = k.sb(es, "St", [128, 4, 64], F32)
    env["Stb"] = k.sb(es, "Stb", [128, 4, 64], BF16)
    env["o"] = k.sb(es, "o_dplr", [128, W], F32)
    env["pT"] = [k.ps(es, f"pT{b}", [128, 8, 128], BF16) for b in range(2)]
    env["pU"] = k.ps(es, "pU", [128, 512], F32)
    env["pO"] = k.ps(es, "pO", [128, 512], F32)
    env["pS"] = env["pU"]
    env["pb"] = [k.ps(es, f"pb{b}", [128, 512], F32) for b in range(4)]
    env["ib"] = 0
    zero = [env["St"], env["Stb"], env["U_b"]] + env["bkT"]
    for s in range(2):
        zero += env["arT"][s] + env["AbT"][s]
    for b_ in zero:
        k.op("pool", lambda e: e.memset(b_[:], 0.0), writes=[b_])
    return env


def bank(env):
    b = env["pb"][env["ib"] % len(env["pb"])]
    env["ib"] += 1
    return b


def dplr_prep(k, env, s, R, Kp, A, B, Vb, ld):
    cd = env["cd"]
    ident = env["ident"]
    e1, e2, e3, e4 = env["e1"], env["e2"], env["e3"], env["e4"]
    ldb, lda = ld
    for (c0, dsts) in ((C_MI, ((e1, 1.0), (e2, -1.0))), (C_MX, ((e3, 1.0),)), (C_MA, ((e4, 1.0),))):
        pc = bank(env)
        k.mm(pc[:], cd[:, c0:c0 + 128], lda, True, True, reads=[cd, ldb], writes=[pc])
        for dst, sc in dsts:
            k.op("act", lambda e: e.activation(dst[:], pc[:], AF.Exp, scale=sc), reads=[pc], writes=[dst])
        yield
    pw = bank(env)
    for j in range(4):
        k.mm(pw[:, j * 4:(j + 1) * 4], lda[:, j * 128:(j + 1) * 128], cd[:, C_BLK:C_BLK + 4], True, True,
             reads=[ldb, cd], writes=[pw])
    wC = env["wC"][s]
    k.op("act", lambda e: e.activation(wC[:], pw[:, 0:16].rearrange("p (j c) -> p j c", j=4), AF.Exp),
         reads=[pw], writes=[wC])
    rt, kt, bt, at = (env[n] for n in ("rt", "kt", "bt", "at"))
    kdec, bdec = env["kdec"][s], env["bdec"][s]
    specs = ((at, A, e3), (rt, R, e1), (bt, B, e2), (kt, Kp, e2), (kdec, Kp, e4), (bdec, B, e4))
    for n, (dst, (sb_, sa), ee) in enumerate(specs):
        k.op("dve" if n % 2 == 0 else "pool", lambda e: e.tensor_tensor(dst[:], sa, ee[:], ALU.mult),
             reads=[sb_, ee], writes=[dst])
        if n % 2 == 1:
            yield
    pTa, pTb = env["pT"]
    arT, bkT = env["arT"][s], env["bkT"]
    for j in range(4):
        k.tr(pTa[:, j, :], at[:, j * 128:(j + 1) * 128], ident[:], reads=[at, ident], writes=[pTa])
        k.tr(pTa[:, 4 + j, :], rt[:, j * 128:(j + 1) * 128], ident[:], reads=[rt, ident], writes=[pTa])
    yield
    for j in range(4):
        k.tr(pTb[:, j, :], bt[:, j * 128:(j + 1) * 128], ident[:], reads=[bt, ident], writes=[pTb])
        k.tr(pTb[:, 4 + j, :], kt[:, j * 128:(j + 1) * 128], ident[:], reads=[kt, ident], writes=[pTb])
    for i, eng in ((0, "act"), (1, "dve")):
        pr = slice(64 * i, 64 * i + 64)
        k.op(eng, lambda e: (e.copy if eng == "act" else e.tensor_copy)(
            arT[i][pr].rearrange("p j s t -> p s j t"), pTa[pr].rearrange("p (s j) t -> p s j t", s=2)),
            reads=[pTa], writes=[arT[i]])
    yield
    for i, eng in ((0, "act"), (1, "dve")):
        pr = slice(64 * i, 64 * i + 64)
        k.op(eng, lambda e: (e.copy if eng == "act" else e.tensor_copy)(bkT[i][pr], pTb[pr]),
             reads=[pTb], writes=[bkT[i]])
    yield
    NA, KA, Lm = env["NA"][s], env["KA"][s], env["Lm"]
    mIT = cd[:, C_MITS:C_MITS + 256].rearrange("p (s t) -> p s t", s=2).unsqueeze(1).to_broadcast([128, 2, 2, 128])
    mTI = cd[:, C_MTI:C_MTI + 128].unsqueeze(1).to_broadcast([128, 4, 128])
    for hb in range(2):
        pl = bank(env)
        for hh in range(4):
            h = hb * 4 + hh
            k.mm(pl[:, hh * 128:(hh + 1) * 128], arT[h % 2][:, h // 2, 0, :], bkT[h % 2][:, h // 2, :], True, True,
                 reads=[arT[h % 2], bkT[h % 2]], writes=[pl])
        k.op("dve", lambda e: e.tensor_tensor(Lm[:, hb * 4:hb * 4 + 4, :], pl[:].rearrange("p (h t) -> p h t", h=4),
                                              mTI, ALU.mult), reads=[pl, cd], writes=[Lm])
        yield
    for dst, off in ((NA, 0), (KA, 4)):
        for hp in range(4):
            pk = bank(env)
            for hh in range(2):
                k.mm(pk[:, hh * 256:(hh + 1) * 256], bkT[hh][:, off + hp, :],
                     arT[hh][:, hp, :, :].rearrange("p s t -> p (s t)"), True, True,
                     reads=[bkT[hh], arT[hh]], writes=[pk])
            k.op("dve", lambda e: e.tensor_tensor(dst[:, 2 * hp:2 * hp + 2, :, :],
                                                  pk[:].rearrange("p (h s t) -> p h s t", h=2, s=2), mIT, ALU.mult),
                 reads=[pk, cd], writes=[dst])
            yield
    NP, Lp, Pf = env["NP"], env["Lp"], env["Pf"]
    idb = ident[:].unsqueeze(1).to_broadcast([128, 8, 128])

    def l_products(lhs_fn, rhs_fn, rbufs, dst):
        for hb in range(2):
            pn = bank(env)
            for hh in range(4):
                h = hb * 4 + hh
                k.mm(pn[:, hh * 128:(hh + 1) * 128], lhs_fn(h), rhs_fn(h), True, True, reads=rbufs, writes=[pn])
            k.op("act", lambda e: e.copy(dst[:, hb * 4:hb * 4 + 4, :], pn[:].rearrange("p (h t) -> p h t", h=4)),
                 reads=[pn], writes=[dst])

    k.op("pool", lambda e: e.tensor_tensor(NP[1][:, :, 1, :], NA[:, :, 0, :], idb, ALU.add),
         reads=[NA, ident], writes=[NP[1]])
    for hb in range(2):
        pn = bank(env)
        for hh in range(4):
            h = hb * 4 + hh
            k.mm(pn[:, hh * 128:(hh + 1) * 128], Lm[:, h, :], NA[:, h, 0, :], True, True, reads=[Lm, NA], writes=[pn])
        k.op("act", lambda e: e.copy(NP[1][:, hb * 4:hb * 4 + 4, 0, :], pn[:].rearrange("p (h t) -> p h t", h=4)),
             reads=[pn], writes=[NP[1]])
    yield
    l_products(lambda h: NA[:, h, 0, :], lambda h: Lm[:, h, :], [NA, Lm], Lp[1])
    yield
    for j in range(1, 4):
        cur_, nxt_ = NP[j % 2], NP[(j + 1) % 2]
        Lc, Ln = Lp[j % 2], Lp[(j + 1) % 2]
        for hp in range(4):
            pn = bank(env)
            for hh in range(2):
                h = hp * 2 + hh
                k.mm(pn[:, hh * 256:(hh + 1) * 256], Lc[:, h, :], cur_[:, h, :, :].rearrange("p s t -> p (s t)"),
                     True, True, reads=[Lc, cur_], writes=[pn])
            pv = pn[:].rearrange("p (h s t) -> p h s t", h=2, s=2)
            k.op("act", lambda e: e.copy(nxt_[:, 2 * hp:2 * hp + 2, 0, :], pv[:, :, 0, :]), reads=[pn], writes=[nxt_])
            k.op("dve", lambda e: e.tensor_tensor(nxt_[:, 2 * hp:2 * hp + 2, 1, :], pv[:, :, 1, :],
                                                  cur_[:, 2 * hp:2 * hp + 2, 1, :], ALU.add),
                 reads=[pn, cur_], writes=[nxt_])
            if hp % 2 == 1:
                yield
        l_products(lambda h: cur_[:, h, 0, :], lambda h: Lc[:, h, :], [cur_, Lc], Ln)
        yield
    for hb in range(2):
        pn = bank(env)
        for hh in range(4):
            h = hb * 4 + hh
            k.mm(pn[:, hh * 128:(hh + 1) * 128], Lp[0][:, h, :], NP[0][:, h, 1, :], True, True,
                 reads=[Lp[0], NP[0]], writes=[pn])
        k.op("dve", lambda e: e.tensor_tensor(Pf[:, hb * 4:hb * 4 + 4, :], pn[:].rearrange("p (h t) -> p h t", h=4),
                                              NP[0][:, hb * 4:hb * 4 + 4, 1, :], ALU.add),
             reads=[pn, NP[0]], writes=[Pf])
    yield
    P = Pf
    Xb, Ubar, AbT = env["Xb"], env["Ubar"][s], env["AbT"][s]
    px = bank(env)
    for h in range(8):
        k.mm(px[:, h * 64:(h + 1) * 64], KA[:, h, 0, :], Vb[:, h * 64:(h + 1) * 64], True, True,
             reads=[KA, Vb], writes=[px])
    k.op("act", lambda e: e.copy(Xb[:], px[:]), reads=[px], writes=[Xb])
    pab = bank(env)
    for h in range(8):
        pr = slice(64 * (h % 2), 64 * (h % 2) + 64)
        k.mm(pab[pr, (h // 2) * 128:(h // 2 + 1) * 128], at[:, h * 64:(h + 1) * 64], P[:, h, :], True, True,
             reads=[at, P], writes=[pab])
    k.op("act", lambda e: e.copy(AbT[0][0:64], pab[0:64].rearrange("p (j t) -> p j t", j=4)),
         reads=[pab], writes=[AbT[0]])
    k.op("dve", lambda e: e.tensor_copy(AbT[1][64:128], pab[64:128].rearrange("p (j t) -> p j t", j=4)),
         reads=[pab], writes=[AbT[1]])
    yield
    pub = bank(env)
    for h in range(8):
        k.mm(pub[:, h * 64:(h + 1) * 64], P[:, h, :], Xb[:, h * 64:(h + 1) * 64], True, True,
             reads=[P, Xb], writes=[pub])
    k.op("dve", lambda e: e.tensor_copy(Ubar[:], pub[:]), reads=[pub], writes=[Ubar])
    yield


def dplr_fin(k, env, s, Vb):
    arT, NA, KA, AbT, Ubar = env["arT"][s], env["NA"][s], env["KA"][s], env["AbT"][s], env["Ubar"][s]
    kdec, bdec, wC = env["kdec"][s], env["bdec"][s], env["wC"][s]
    U_b, St, Stb, pU, pO, pS = env["U_b"], env["St"], env["Stb"], env["pU"], env["pO"], env["pS"]
    for c in range(4):
        rc = slice(DC * c, DC * c + DC)
        cb = DC * c
        for h in range(8):
            k.mm(pU[rc, h * 64:(h + 1) * 64], AbT[h % 2][:, h // 2, rc], Stb[:, h // 2, :], True, True,
                 reads=[AbT[h % 2], Stb], writes=[pU], tp=(0, cb))
        k.op("dve", lambda e: e.tensor_tensor(U_b[rc, :], pU[rc, :], Ubar[rc, :], ALU.add),
             reads=[pU, Ubar], writes=[U_b])
        yield
        for h in range(8):
            pb_ = 64 * (h % 2)
            pr = slice(pb_, pb_ + 64)
            hc = slice(h * 64, (h + 1) * 64)
            jc = slice((h // 2) * 64, (h // 2 + 1) * 64)
            k.mm(pS[pr, jc], kdec[rc, hc], Vb[rc, hc], True, False, reads=[kdec, Vb], writes=[pS], tp=(cb, pb_))
            k.mm(pS[pr, jc], bdec[rc, hc], U_b[rc, hc], False, True, reads=[bdec, U_b], writes=[pS], tp=(cb, pb_))
        for h in range(8):
            hc = slice(h * 64, (h + 1) * 64)
            k.mm(pO[rc, hc], arT[h % 2][:, h // 2, 1, rc], Stb[:, h // 2, :], True, False,
                 reads=[arT[h % 2], Stb], writes=[pO], tp=(0, cb))
            k.mm(pO[rc, hc], NA[:, h, 1, rc], U_b[:, hc], False, False, reads=[NA, U_b], writes=[pO], tp=(0, cb))
            k.mm(pO[rc, hc], KA[:, h, 1, rc], Vb[:, hc], False, True, reads=[KA, Vb], writes=[pO], tp=(0, cb))
        k.op("pool", lambda e: e.tensor_tensor(St[:], St[:], wC[:, :, c].unsqueeze(2).to_broadcast([128, 4, 64]), ALU.mult),
             reads=[St, wC], writes=[St])
        k.op("dve", lambda e: e.tensor_tensor(St[:], St[:], pS[:, 0:256].rearrange("p (j v) -> p j v", j=4), ALU.add),
             reads=[St, pS], writes=[St])
        yield
        k.op("act", lambda e: e.copy(Stb[:], St[:]), reads=[St], writes=[Stb])
        yield
    o = env["o"]
    k.op("act", lambda e: e.copy(o[:], pO[:]), reads=[pO], writes=[o])
    yield


def run_pipeline(NT, stages, weights):
    ns = len(stages)
    for tau in range(NT + ns - 1):
        gens = []
        for i, st in enumerate(stages):
            t = tau - i
            if 0 <= t < NT:
                gens.append([st(t), weights[i]])
        while gens:
            for gw in list(gens):
                for _ in range(gw[1]):
                    try:
                        next(gw[0])
                    except StopIteration:
                        gens.remove(gw)
                        break


def norm_transpose(k, x, xs, junk, ss, rstd, pT, uT, gpre, ident):
    rms_rstd(k, "act", x[:], x, junk, ss, rstd, D)
    k.op("dve", lambda e: e.tensor_scalar(xs[:], x[:], rstd[:, 0:1], None, ALU.mult), reads=[x, rstd], writes=[xs])
    for kc in range(8):
        k.tr(pT[:, kc, :], xs[:, kc * 128:(kc + 1) * 128], ident[:], reads=[xs, ident], writes=[pT])
    k.op("dve", lambda e: e.tensor_tensor(uT[:], pT[:], gpre[:].unsqueeze(2).to_broadcast([128, 8, 128]), ALU.mult),
         reads=[pT, gpre], writes=[uT])


def inv_sqrt(k, dst, src, scale, eps):
    k.op("dve", lambda e: e.tensor_scalar(dst[:], src[:], scale, eps, ALU.mult, ALU.add), reads=[src], writes=[dst])
    k.op("act", lambda e: e.activation(dst[:], dst[:], AF.Sqrt), reads=[dst], writes=[dst])
    k.op("dve", lambda e: e.reciprocal(dst[:], dst[:]), reads=[dst], writes=[dst])


def bc(ap, n, w):
    return ap.unsqueeze(2).to_broadcast([128, n, w])


def gdn_phase(k, nc, T, hin, oa_out, w_in, conv_w, a_log, dt_bias, gnorm_w, g_pre, ident, c_dplr):
    NT = T // 128
    es = ExitStack()
    with es:
        win = [k.sb(es, f"gwin{kc}", [128, 2064], BF16) for kc in range(8)]
        for kc in range(8):
            k.dma("pool", f"wl{kc % 8}", win[kc][:], w_in.t[kc * 128:(kc + 1) * 128, 0:2064],
                  reads=[w_in], writes=[win[kc]])
        gpre = k.sb(es, "gpre", [128, 8], F32)
        k.dma("sp", "const", gpre[:], g_pre.t.rearrange("(kc p) -> p kc", p=128),
              reads=[g_pre], writes=[gpre], allow_slow_non_contiguous=True)
        cwt = k.sb(es, "cwt", [128, 4, 1536], F32)
        k.dma("sp", "const", cwt[:].rearrange("p j c -> p (j c)"),
              conv_w.t.rearrange("j c -> (j c)").partition_broadcast(128), reads=[conv_w], writes=[cwt])
        alb = k.sb(es, "alb", [128, 8], F32)
        dtb = k.sb(es, "dtb", [128, 8], F32)
        gnw = k.sb(es, "gnw", [128, 64], F32)
        k.dma("sp", "const", alb[:], a_log.t.partition_broadcast(128), reads=[a_log], writes=[alb])
        k.dma("sp", "const", dtb[:], dt_bias.t.partition_broadcast(128), reads=[dt_bias], writes=[dtb])
        k.dma("sp", "const", gnw[:], gnorm_w.t.partition_broadcast(128), reads=[gnorm_w], writes=[gnw])
        nea = k.sb(es, "nea", [128, 8], F32)
        k.op("act", lambda e: e.activation(nea[:], alb[:], AF.Exp), reads=[alb], writes=[nea])
        k.op("dve", lambda e: e.tensor_scalar(nea[:], nea[:], -1.0, None, ALU.mult), reads=[nea], writes=[nea])
        env = dplr_setup(k, es, c_dplr, ident)

        xb = [k.sb(es, f"x{i}", [128, D], F32) for i in range(2)]
        xs = k.sb(es, "xs", [128, D], BF16)
        junk = k.sb(es, "junk", [128, D], BF16)
        uT = k.sb(es, "uT", [128, 8, 128], BF16)
        ss = k.sb(es, "ss", [128, 1], F32)
        rstd = k.sb(es, "rstd", [128, 1], F32)
        pq = [k.sb(es, f"pq{i}", [128, 1536], F32) for i in range(2)]
        xsh = [k.sb(es, f"xsh{i}", [128, 1536], F32) for i in range(3)]
        cv = k.sb(es, "cv", [128, 1536], F32)
        zsb = [k.sb(es, f"zs{i}", [128, 512], F32) for i in range(3)]
        ba = k.sb(es, "ba", [128, 16], F32)
        tmp = xsh[2]
        tmp2 = k.sb(es, "tmp2", [128, 512], F32)
        sm = {n: k.sb(es, "g_" + n, [128, 16], F32) for n in ("ssq", "rn", "beta", "sp", "g", "eg", "coef", "rq",
                                                              "ss8", "rs8")}
        Rg = k.sb(es, "Rg", [128, 512], F32)
        Ag = k.sb(es, "Ag", [128, 512], F32)
        Kg = k.sb(es, "Kg", [128, 512], F32)
        Bg = k.sb(es, "Bg", [128, 512], F32)
        ldg = k.sb(es, "ldg", [128, 512], F32)
        Vbb = [k.sb(es, f"Vb{i}", [128, 512], BF16) for i in range(3)]
        oa = tmp2
        k.op("pool", lambda e: e.memset(pq[1][:], 0.0), writes=[pq[1]])
        print("gdn sbuf bytes remaining", nc.sbuf_bytes_remaining)
        v3 = lambda b_, lo: b_[:, lo:lo + 512].rearrange("p (h d) -> p h d", h=8)
        w3 = lambda b_: b_[:].rearrange("p (h d) -> p h d", h=8)

        def load_x(t):
            k.dma("sp", f"x{t % 2}", xb[t % 2][:], hin[t].t, reads=[hin[t]], writes=[xb[t % 2]])

        def prep(t):
            cur, prev = pq[t % 2], pq[(t + 1) % 2]
            x, zs, Vb = xb[t % 2], zsb[t % 3], Vbb[t % 3]
            if t == 0:
                load_x(0)
            if t + 1 < NT:
                load_x(t + 1)
            norm_transpose(k, x, xs, junk, ss, rstd, env["pT"][0], uT, gpre, ident)
            yield
            for g, (c0, c1) in enumerate(((0, 512), (512, 1024), (1024, 1536), (1536, 2048), (2048, 2064))):
                pu = bank(env)
                n = c1 - c0
                for kc in range(8):
                    k.mm(pu[:, 0:n], uT[:, kc, :], win[kc][:, c0:c1], kc == 0, kc == 7, reads=[uT, win[kc]], writes=[pu])
                if g < 3:
                    if g % 2 == 0:
                        k.op("act", lambda e: e.copy(cur[:, c0:c1], pu[:]), reads=[pu], writes=[cur])
                    else:
                        k.op("dve", lambda e: e.tensor_copy(cur[:, c0:c1], pu[:]), reads=[pu], writes=[cur])
                elif g == 3:
                    k.op("act", lambda e: e.activation(zs[:], pu[:], AF.Silu), reads=[pu], writes=[zs])
                else:
                    k.op("dve", lambda e: e.tensor_copy(ba[:], pu[:, 0:16]), reads=[pu], writes=[ba])
                yield
            for j in range(1, 4):
                sh = xsh[j - 1]
                k.dma("sp", f"sh{j}a", sh[j:128, :], cur[0:128 - j, :], reads=[cur], writes=[sh])
                k.dma("sp", f"sh{j}b", sh[0:j, :], prev[128 - j:128, :], reads=[prev], writes=[sh])
            k.op("act", lambda e: e.activation(sm["beta"][:, 0:8], ba[:, 0:8], AF.Sigmoid), reads=[ba], writes=[sm["beta"]])
            k.op("dve", lambda e: e.tensor_tensor(sm["sp"][:, 0:8], ba[:, 8:16], dtb[:], ALU.add),
                 reads=[ba, dtb], writes=[sm["sp"]])
            k.op("act", lambda e: e.activation(sm["sp"][:, 0:8], sm["sp"][:, 0:8], AF.Exp), reads=[sm["sp"]], writes=[sm["sp"]])
            k.op("dve", lambda e: e.tensor_scalar(sm["sp"][:, 0:8], sm["sp"][:, 0:8], 1.0, None, ALU.add),
                 reads=[sm["sp"]], writes=[sm["sp"]])
            k.op("act", lambda e: e.activation(sm["sp"][:, 0:8], sm["sp"][:, 0:8], AF.Ln), reads=[sm["sp"]], writes=[sm["sp"]])
            k.op("dve", lambda e: e.tensor_tensor(sm["g"][:, 0:8], sm["sp"][:, 0:8], nea[:], ALU.mult),
                 reads=[sm["sp"], nea], writes=[sm["g"]])
            k.op("act", lambda e: e.activation(sm["eg"][:, 0:8], sm["g"][:, 0:8], AF.Exp), reads=[sm["g"]], writes=[sm["eg"]])
            k.op("dve", lambda e: e.scalar_tensor_tensor(sm["coef"][:, 0:8], sm["eg"][:, 0:8], -1.0, sm["beta"][:, 0:8],
                                                         ALU.mult, ALU.mult), reads=[sm["eg"], sm["beta"]], writes=[sm["coef"]])
            k.op("act", lambda e: e.copy(w3(ldg), bc(sm["g"][:, 0:8], 8, 64)), reads=[sm["g"]], writes=[ldg])
            yield
            k.op("dve", lambda e: e.tensor_tensor(cv[:], cur[:], cwt[:, 3, :], ALU.mult), reads=[cur, cwt], writes=[cv])
            for j in range(1, 4):
                sh = xsh[j - 1]
                k.op("pool", lambda e: e.tensor_tensor(sh[:], sh[:], cwt[:, 3 - j, :], ALU.mult),
                     reads=[sh, cwt], writes=[sh])
                k.op("dve", lambda e: e.tensor_tensor(cv[:], cv[:], sh[:], ALU.add), reads=[cv, sh], writes=[cv])
                yield
            k.op("act", lambda e: e.activation(cv[:], cv[:], AF.Silu), reads=[cv], writes=[cv])
            yield
            k.op("act", lambda e: e.activation(tmp[:, 0:1024], cv[:, 0:1024], AF.Square), reads=[cv], writes=[tmp])
            k.op("dve", lambda e: e.tensor_reduce(sm["ssq"][:], tmp[:, 0:1024].rearrange("p (h d) -> p h d", h=16), AX.X, ALU.add),
                 reads=[tmp], writes=[sm["ssq"]])
            inv_sqrt(k, sm["rn"], sm["ssq"], 1.0, 1e-6)
            k.op("dve", lambda e: e.tensor_scalar(sm["rq"][:, 0:8], sm["rn"][:, 0:8], 0.125, None, ALU.mult),
                 reads=[sm["rn"]], writes=[sm["rq"]])
            yield
            k.op("dve", lambda e: e.tensor_tensor(w3(Rg), v3(cv, 0), bc(sm["rq"][:, 0:8], 8, 64), ALU.mult),
                 reads=[cv, sm["rq"]], writes=[Rg])
            k.op("pool", lambda e: e.tensor_tensor(w3(Ag), v3(cv, 512), bc(sm["rn"][:, 8:16], 8, 64), ALU.mult),
                 reads=[cv, sm["rn"]], writes=[Ag])
            k.op("dve", lambda e: e.tensor_tensor(w3(Kg), w3(Ag), bc(sm["beta"][:, 0:8], 8, 64), ALU.mult),
                 reads=[Ag, sm["beta"]], writes=[Kg])
            k.op("pool", lambda e: e.tensor_tensor(w3(Bg), w3(Ag), bc(sm["coef"][:, 0:8], 8, 64), ALU.mult),
                 reads=[Ag, sm["coef"]], writes=[Bg])
            k.op("act", lambda e: e.copy(Vb[:], cv[:, 1024:1536]), reads=[cv], writes=[Vb])
            yield

        def prep2(t):
            yield from dplr_prep(k, env, t % 2, (Rg, Rg[:]), (Kg, Kg[:]), (Ag, Ag[:]), (Bg, Bg[:]), Vbb[t % 3],
                                 (ldg, ldg[:]))

        def fin(t):
            zs, Vb = zsb[t % 3], Vbb[t % 3]
            yield from dplr_fin(k, env, t % 2, Vb)
            o = env["o"]
            k.op("act", lambda e: e.activation(tmp2[:], o[:], AF.Square), reads=[o], writes=[tmp2])
            k.op("dve", lambda e: e.tensor_reduce(sm["ss8"][:, 0:8], tmp2[:].rearrange("p (h d) -> p h d", h=8),
                                                  AX.X, ALU.add), reads=[tmp2], writes=[sm["ss8"]])
            yield
            inv_sqrt(k, sm["rs8"], sm["ss8"], 1.0 / 64, RMS_EPS)
            yield
            k.op("dve", lambda e: e.tensor_tensor(w3(oa), w3(o), bc(sm["rs8"][:, 0:8], 8, 64), ALU.mult),
                 reads=[o, sm["rs8"]], writes=[oa])
            k.op("pool", lambda e: e.tensor_tensor(w3(oa), w3(oa), gnw[:].unsqueeze(1).to_broadcast([128, 8, 64]), ALU.mult),
                 reads=[oa, gnw], writes=[oa])
            yield
            k.op("pool", lambda e: e.tensor_tensor(oa[:], oa[:], zs[:], ALU.mult), reads=[oa, zs], writes=[oa])
            k.dma("pool", "y0", oa_out[t].t, oa[:], reads=[oa], writes=[oa_out[t]])
            yield

        run_pipeline(NT, [prep, prep2, fin], [1, 4, 1])
        k.barrier()


def rwkv_phase(k, nc, T, hin, ob_out, w_in, vecs, w2, a2, g2, g_pre, ident, c_dplr):
    NT = T // 128
    es = ExitStack()
    with es:
        win = [k.sb(es, f"rwin{kc}", [128, 1792], BF16) for kc in range(8)]
        for kc in range(8):
            k.dma("pool", f"wl{kc % 8}", win[kc][:], w_in.t[kc * 128:(kc + 1) * 128, 2064:3856],
                  reads=[w_in], writes=[win[kc]])
        w2p = k.sb(es, "w2p", [128, 512], BF16)
        a2p = k.sb(es, "a2p", [128, 512], BF16)
        g2b = k.sb(es, "g2b", [128, 512], BF16)
        k.op("pool", lambda e: e.memset(w2p[:], 0.0), writes=[w2p])
        k.op("pool", lambda e: e.memset(a2p[:], 0.0), writes=[a2p])
        k.dma("pool", "wl0", w2p[0:64, :], w2.t[:, :], reads=[w2], writes=[w2p])
        k.dma("pool", "wl1", a2p[64:128, :], a2.t[:, :], reads=[a2], writes=[a2p])
        k.dma("pool", "wl2", g2b[:], g2.t[:, :], reads=[g2], writes=[g2b])
        gpre = k.sb(es, "gpre", [128, 8], F32)
        k.dma("sp", "const", gpre[:], g_pre.t.rearrange("(kc p) -> p kc", p=128),
              reads=[g_pre], writes=[gpre], allow_slow_non_contiguous=True)
        vb_ = {}
        for n, buf in vecs.items():
            w = 1792 if n == "mu" else 512
            vb_[n] = k.sb(es, "v_" + n, [128, w], F32)
            k.dma("sp", "const", vb_[n][:], buf.t.partition_broadcast(128), reads=[buf], writes=[vb_[n]])
        env = dplr_setup(k, es, c_dplr, ident)

        xb = [k.sb(es, f"x{i}", [128, D], F32) for i in range(2)]
        xs = k.sb(es, "xs", [128, D], BF16)
        junk = k.sb(es, "junk", [128, D], BF16)
        uT = k.sb(es, "uT", [128, 8, 128], BF16)
        ss = k.sb(es, "ss", [128, 1], F32)
        rstd = k.sb(es, "rstd", [128, 1], F32)
        prp = k.sb(es, "prp", [128, 1792], F32)
        rsh = k.sb(es, "rsh", [128, 1792], F32)
        lastrow = k.sb(es, "lastrow", [1, 1792], F32)
        lin = k.sb(es, "lin", [128, 256], BF16)
        linT = k.sb(es, "linT", [128, 2, 128], BF16)
        ldr = k.sb(es, "ldr", [128, 512], F32)
        aic = k.sb(es, "aic", [128, 512], F32)
        gateb = [k.sb(es, f"gate{i}", [128, 512], F32) for i in range(3)]
        tk = k.sb(es, "tk", [128, 512], F32)
        Kr = k.sb(es, "Kr", [128, 512], F32)
        Ar = k.sb(es, "Ar", [128, 512], F32)
        Br = k.sb(es, "Br", [128, 512], F32)
        Rr = k.sb(es, "Rr", [128, 512], F32)
        Vbb = [k.sb(es, f"Vb{i}", [128, 512], BF16) for i in range(3)]
        tmp = k.sb(es, "tmp", [128, 512], F32)
        tmp2 = k.sb(es, "tmp2", [128, 512], F32)
        bonusb = [k.sb(es, f"bonus{i}", [128, 512], F32) for i in range(3)]
        y = k.sb(es, "y", [128, 512], F32)
        sm = {n: k.sb(es, "r_" + n, [128, 8], F32) for n in ("ssq", "rn", "bs", "s1", "s2", "mean", "msq", "var",
                                                             "rstdh", "nmr")}
        k.op("pool", lambda e: e.memset(lastrow[:], 0.0), writes=[lastrow])
        print("rwkv sbuf bytes remaining", nc.sbuf_bytes_remaining)
        v3 = lambda ap: ap.rearrange("p (h d) -> p h d", h=8)

        def load_x(t):
            k.dma("sp", f"x{t % 2}", xb[t % 2][:], hin[t].t, reads=[hin[t]], writes=[xb[t % 2]])

        def prep(t):
            x, gate, Vb, bonus = xb[t % 2], gateb[t % 3], Vbb[t % 3], bonusb[t % 3]
            if t == 0:
                load_x(0)
            if t + 1 < NT:
                load_x(t + 1)
            norm_transpose(k, x, xs, junk, ss, rstd, env["pT"][0], uT, gpre, ident)
            yield
            for g, (c0, c1) in enumerate(((0, 512), (512, 1024), (1024, 1536), (1536, 1792))):
                pu = bank(env)
                n = c1 - c0
                for kc in range(8):
                    k.mm(pu[:, 0:n], uT[:, kc, :], win[kc][:, c0:c1], kc == 0, kc == 7, reads=[uT, win[kc]], writes=[pu])
                if g % 2 == 0:
                    k.op("act", lambda e: e.copy(prp[:, c0:c1], pu[:, 0:n]), reads=[pu], writes=[prp])
                else:
                    k.op("dve", lambda e: e.tensor_copy(prp[:, c0:c1], pu[:, 0:n]), reads=[pu], writes=[prp])
                yield
            k.dma("sp", "rsa", rsh[1:128, :], prp[0:127, :], reads=[prp], writes=[rsh])
            k.dma("sp", "rsb", rsh[0:1, :], lastrow[0:1, :], reads=[lastrow], writes=[rsh])
            k.dma("sp", "rsc", lastrow[0:1, :], prp[127:128, :], reads=[prp, rsh], writes=[lastrow])
            yield
            k.op("pool", lambda e: e.tensor_tensor(rsh[:], rsh[:], prp[:], ALU.subtract), reads=[rsh, prp, lastrow],
                 writes=[rsh])
            k.op("pool", lambda e: e.tensor_tensor(rsh[:], rsh[:], vb_["mu"][:], ALU.mult), reads=[rsh, vb_["mu"]],
                 writes=[rsh])
            yield
            k.op("dve", lambda e: e.tensor_tensor(prp[:], prp[:], rsh[:], ALU.add), reads=[prp, rsh, lastrow],
                 writes=[prp])
            r_ap, kr_ap, vr_ap = prp[:, 0:512], prp[:, 512:1024], prp[:, 1024:1536]
            k.op("act", lambda e: e.activation(lin[:, 0:64], prp[:, 1536:1600], AF.Tanh), reads=[prp], writes=[lin])
            k.op("act", lambda e: e.copy(lin[:, 64:128], prp[:, 1600:1664]), reads=[prp], writes=[lin])
            k.op("act", lambda e: e.activation(lin[:, 128:256], prp[:, 1664:1792], AF.Sigmoid), reads=[prp], writes=[lin])
            k.op("act", lambda e: e.copy(Vb[:], vr_ap), reads=[prp], writes=[Vb])
            k.op("act", lambda e: e.copy(Rr[:], r_ap), reads=[prp], writes=[Rr])
            yield
            pT0 = env["pT"][0]
            for i in range(2):
                k.tr(pT0[:, i, :], lin[:, i * 128:(i + 1) * 128], ident[:], reads=[lin, ident], writes=[pT0])
            k.op("act", lambda e: e.copy(linT[:], pT0[:, 0:2, :]), reads=[pT0], writes=[linT])
            yield
            pz = bank(env)
            k.mm(pz[:], linT[:, 0, :], w2p[:], True, True, reads=[linT, w2p], writes=[pz])
            k.op("dve", lambda e: e.tensor_tensor(ldr[:], pz[:], vb_["w0"][:], ALU.add), reads=[pz, vb_["w0"]], writes=[ldr])
            pa = bank(env)
            k.mm(pa[:], linT[:, 0, :], a2p[:], True, True, reads=[linT, a2p], writes=[pa])
            k.op("dve", lambda e: e.tensor_tensor(aic[:], pa[:], vb_["a0"][:], ALU.add), reads=[pa, vb_["a0"]], writes=[aic])
            yield
            k.op("act", lambda e: e.activation(ldr[:], ldr[:], AF.Sigmoid), reads=[ldr], writes=[ldr])
            k.op("act", lambda e: e.activation(aic[:], aic[:], AF.Sigmoid), reads=[aic], writes=[aic])
            pg = bank(env)
            k.mm(pg[:], linT[:, 1, :], g2b[:], True, True, reads=[linT, g2b], writes=[pg])
            k.op("act", lambda e: e.copy(gate[:], pg[:]), reads=[pg], writes=[gate])
            k.op("act", lambda e: e.mul(ldr[:], ldr[:], -0.6065306597126334), reads=[ldr], writes=[ldr])
            yield
            k.op("dve", lambda e: e.tensor_tensor(tk[:], kr_ap, vb_["k_k"][:], ALU.mult), reads=[prp, vb_["k_k"]], writes=[tk])
            k.op("act", lambda e: e.activation(tmp[:], tk[:], AF.Square), reads=[tk], writes=[tmp])
            k.op("dve", lambda e: e.tensor_reduce(sm["ssq"][:], v3(tmp[:]), AX.X, ALU.add), reads=[tmp], writes=[sm["ssq"]])
            yield
            inv_sqrt(k, sm["rn"], sm["ssq"], 1.0, 1e-6)
            yield
            k.op("dve", lambda e: e.tensor_tensor(v3(tk[:]), v3(tk[:]), bc(sm["rn"][:], 8, 64), ALU.mult),
                 reads=[tk, sm["rn"]], writes=[tk])
            k.op("dve", lambda e: e.scalar_tensor_tensor(Kr[:], aic[:], -1.0, vb_["k_a"][:], ALU.add, ALU.mult),
                 reads=[aic, vb_["k_a"]], writes=[Kr])
            k.op("dve", lambda e: e.scalar_tensor_tensor(Kr[:], Kr[:], 1.0, kr_ap, ALU.add, ALU.mult),
                 reads=[Kr, prp], writes=[Kr])
            k.op("act", lambda e: e.mul(Ar[:], tk[:], -1.0), reads=[tk], writes=[Ar])
            k.op("pool", lambda e: e.tensor_tensor(Br[:], tk[:], aic[:], ALU.mult), reads=[tk, aic], writes=[Br])
            yield
            k.op("pool", lambda e: e.tensor_tensor(tmp[:], r_ap, Kr[:], ALU.mult), reads=[prp, Kr], writes=[tmp])
            k.op("pool", lambda e: e.tensor_tensor(tmp[:], tmp[:], vb_["r_k"][:], ALU.mult), reads=[tmp, vb_["r_k"]],
                 writes=[tmp])
            k.op("dve", lambda e: e.tensor_reduce(sm["bs"][:], v3(tmp[:]), AX.X, ALU.add), reads=[tmp], writes=[sm["bs"]])
            k.op("dve", lambda e: e.tensor_tensor(v3(bonus[:]), v3(vr_ap), bc(sm["bs"][:], 8, 64), ALU.mult),
                 reads=[prp, sm["bs"]], writes=[bonus])
            yield

        def prep2(t):
            yield from dplr_prep(k, env, t % 2, (Rr, Rr[:]), (Kr, Kr[:]), (Ar, Ar[:]), (Br, Br[:]), Vbb[t % 3],
                                 (ldr, ldr[:]))

        def fin(t):
            gate, Vb, bonus = gateb[t % 3], Vbb[t % 3], bonusb[t % 3]
            yield from dplr_fin(k, env, t % 2, Vb)
            o = env["o"]
            k.op("dve", lambda e: e.tensor_reduce(sm["s1"][:], v3(o[:]), AX.X, ALU.add), reads=[o], writes=[sm["s1"]])
            k.op("act", lambda e: e.activation(tmp2[:], o[:], AF.Square), reads=[o], writes=[tmp2])
            k.op("dve", lambda e: e.tensor_reduce(sm["s2"][:], v3(tmp2[:]), AX.X, ALU.add), reads=[tmp2], writes=[sm["s2"]])
            yield
            head_stats(k, sm, 64, 64e-5)
            yield
            k.op("dve", lambda e: e.tensor_tensor(v3(y[:]), v3(o[:]), bc(sm["rstdh"][:], 8, 64), ALU.mult),
                 reads=[o, sm["rstdh"]], writes=[y])
            k.op("pool", lambda e: e.tensor_tensor(v3(y[:]), v3(y[:]), bc(sm["nmr"][:], 8, 64), ALU.add),
                 reads=[y, sm["nmr"]], writes=[y])
            yield
            k.op("dve", lambda e: e.tensor_tensor(y[:], y[:], vb_["ln_w"][:], ALU.mult), reads=[y, vb_["ln_w"]], writes=[y])
            k.op("pool", lambda e: e.tensor_tensor(y[:], y[:], vb_["ln_b"][:], ALU.add), reads=[y, vb_["ln_b"]], writes=[y])
            yield
            k.op("dve", lambda e: e.tensor_tensor(y[:], y[:], bonus[:], ALU.add), reads=[y, bonus], writes=[y])
            k.op("pool", lambda e: e.tensor_tensor(y[:], y[:], gate[:], ALU.mult), reads=[y, gate], writes=[y])
            k.dma("pool", "y0", ob_out[t].t, y[:], reads=[y], writes=[ob_out[t]])
            yield

        run_pipeline(NT, [prep, prep2, fin], [1, 4, 1])
        k.barrier()


def build_program(T, phases=("mlp",), dbg=None):
    nc = bass.Bass("TRN2", target_bir_lowering=False)
    k = KB(nc)

    dins = {}

    def din(name, shape):
        if name not in dins:
            dins[name] = Buf(nc.dram_tensor(name, list(shape), F32, kind="ExternalInput").ap(), name)
        return dins[name]

    NT = T // 128
    x = din("x", [T, D])
    c_ident = din("c_ident", [128, 128])
    out = Buf(nc.dram_tensor("out", [T, D], F32, kind="ExternalOutput").ap(), "out")

    def tiles(buf):
        return [Buf(buf.t[t * 128:(t + 1) * 128, :], f"{buf.name}{t}") for t in range(NT)]

    cur = tiles(x)
    with k.es:
        es = ExitStack()
        with es:
            ident = load_consts(k, es, c_ident)
            for pi, ph in enumerate(phases):
                last = pi == len(phases) - 1
                if ph in ("gdn", "rwkv", "ret"):
                    nxt = None
                elif last:
                    nxt = tiles(out)
                else:
                    scr = Buf(nc.dram_tensor(f"scr{pi}", [T, D], F32, kind="Internal").ap(), f"scr{pi}")
                    nxt = tiles(scr)
                if ph == "gdn":
                    oa_t = tiles(Buf(nc.dram_tensor("oa_scr", [T, 512], F32, kind="Internal").ap(), "oa_scr"))
                    gdn_phase(k, nc, T, cur, oa_t, din("ab_w_in", [D, 3856]), din("gdn_conv_w", [4, 1536]),
                              din("gdn_a_log", [8]), din("gdn_dt_bias", [8]), din("gdn_norm_w", [64]),
                              din("norm_mix_pre0", [D]), ident, din("c_dplr", [128, 772]))
                    continue
                if ph == "rwkv":
                    ob_t = tiles(Buf(nc.dram_tensor("ob_scr", [T, 512], F32, kind="Internal").ap(), "ob_scr"))
                    vecs = {"mu": din("rwkv_mu", [1792])}
                    for n in ("w0", "a0", "k_k", "k_a", "r_k", "ln_w", "ln_b"):
                        vecs[n] = din("rwkv_" + n, [512])
                    rwkv_phase(k, nc, T, cur, ob_t, din("ab_w_in", [D, 3856]), vecs,
                               din("rwkv_w2", [64, 512]), din("rwkv_a2", [64, 512]), din("rwkv_g2", [128, 512]),
                               din("norm_mix_pre0", [D]), ident, din("c_dplr", [128, 772]))
                    continue
                if ph == "abo":
                    outproj_phase(k, nc, T, cur, nxt, [(oa_t, 0, 512, False), (ob_t, 512, 512, False)], 1024,
                                  din("ab_w_out", [D, D]), din("norm_mix_post0", [D]), ident)
                    cur = nxt
                    continue
                if ph == "ret":
                    y_t = [Buf(a_, f"ys{t_}") for t_, a_ in enumerate(
                        (lambda yb: [yb[t_ * 128:(t_ + 1) * 128, :] for t_ in range(NT)])(
                            nc.dram_tensor("y_scr", [T, 2048], BF16, kind="Internal").ap()))]
                    ret_phase(k, nc, T, cur, y_t, din("ret_w_in", [D, 6144]), din("norm_mix_pre1", [D]), ident,
                              din("c_rot", [T, 128]), din("c_dqk", [128, 24]), din("c_maskT", [128, 128]))
                    continue
                if ph == "reto":
                    outproj_phase(k, nc, T, cur, nxt, [(y_t, 0, 2048, True)], 2048, din("ret_w_out", [2048, D]),
                                  din("norm_mix_post1", [D]), ident, gscale=din("ret_gn_w", [2048]))
                    cur = nxt
                    continue
                if ph.startswith("mlp"):
                    l = ph[3:]
                    mlp_phase(k, nc, T, cur, nxt, din("mlp_w_up" + l, [D, DFF]), din("mlp_w_down" + l, [DFF, D]),
                              din("norm_mlp_pre" + l, [D]), din("norm_mlp_post" + l, [D]), ident)
                cur = nxt
            k.barrier()
    return nc


PHASES = ("gdn", "rwkv", "abo", "mlp0", "ret", "reto", "mlp1")
SEQ = 4096
NCORES = 8


def host_consts(T):
    c = {"c_ident": np.eye(128, dtype=np.float32)}
    c.update(ret_host_consts(T))
    c.update(dplr_host_consts())
    return c


def kernel(**inputs):
    f32 = lambda a: np.ascontiguousarray(np.asarray(a, dtype=np.float32))
    x = f32(inputs["x"])
    B, T, _ = x.shape
    shared = dict(host_consts(T))
    for l in range(2):
        shared[f"mlp_w_up{l}"] = f32(inputs["mlp_w_up"][l])
        shared[f"mlp_w_down{l}"] = f32(inputs["mlp_w_down"][l])
        shared[f"norm_mlp_pre{l}"] = f32(inputs["norm_mlp_pre"][l])
        shared[f"norm_mlp_post{l}"] = f32(inputs["norm_mlp_post"][l])
        shared[f"norm_mix_pre{l}"] = f32(inputs["norm_mix_pre"][l])
        shared[f"norm_mix_post{l}"] = f32(inputs["norm_mix_post"][l])
    for n in ("ab_w_in", "gdn_conv_w", "gdn_a_log", "gdn_dt_bias", "gdn_norm_w", "rwkv_mu", "rwkv_w0", "rwkv_w2",
              "rwkv_a0", "rwkv_a2", "rwkv_g2", "rwkv_k_k", "rwkv_k_a", "rwkv_ln_w", "rwkv_ln_b", "ab_w_out",
              "ret_w_in", "ret_gn_w", "ret_w_out"):
        shared[n] = f32(inputs[n][0])
    shared["rwkv_r_k"] = f32(inputs["rwkv_r_k"][0]).reshape(512)
    nc = build_program(T, PHASES)
    in_maps = [dict(shared, x=x[b]) for b in range(B)]
    res = run_bass_kernel_spmd(nc, in_maps, core_ids=list(range(B)))
    return np.stack([np.asarray(r["out"], dtype=np.float32) for r in res.results], axis=0)
```
